# Optimizing a Trainium2 kernel written in Bass

```python
import math
import jax, jax.numpy as jnp
from jax import lax
import numpy as np

D_MODEL = 1024
BATCH = 2
SEQ = 8192
DEPTH = 1

CHUNK = 64
Q_BLOCK = 128
MEM_LEN = 256
EPS = 1e-6
SSM_WIDTH = D_MODEL
SSM_GROUP = 16
SSM_GROUPS = SSM_WIDTH // SSM_GROUP
SSM_STATE = 64
DT_MIN = 1e-3
DT_MAX = 1e-1
DA_HEADS = 8
DA_HEAD_DIM = D_MODEL // (2 * DA_HEADS)
DA_V_DIM = 2 * DA_HEAD_DIM
DA_QK_WIDTH = DA_HEADS * 2 * DA_HEAD_DIM
DA_WIDTH = DA_HEADS * DA_V_DIM
ROPE_THETA = 10000.0
XA_HEADS = 4
XA_HEAD_DIM = D_MODEL // XA_HEADS
D_FF = 4 * D_MODEL
COL_SSM = SSM_WIDTH
COL_Q = COL_SSM + DA_QK_WIDTH
COL_K = COL_Q + DA_QK_WIDTH
COL_V = COL_K + DA_WIDTH
IN_COLS = COL_V + 2 * D_MODEL

kernel_name = "hybrid_s5_diffattn_gated_stream_encoder"

F32 = jnp.float32


def rms_norm(x, g):
    xf = x.astype(F32)
    y = xf * lax.rsqrt(jnp.mean(xf * xf, axis=-1, keepdims=True) + EPS)
    return (y * g.astype(F32)).astype(x.dtype)


def rope_tables(seqlen, dim):
    inv = ROPE_THETA ** (-jnp.arange(0, dim, 2, dtype=F32) / dim)
    ang = jnp.arange(seqlen, dtype=F32)[:, None] * inv[None, :]
    return jnp.cos(ang), jnp.sin(ang)


def apply_rope(t, cos, sin):
    half = t.shape[-1] // 2
    c = cos[:, None, None, :]
    s = sin[:, None, None, :]
    t1, t2 = t[..., :half], t[..., half:]
    return jnp.concatenate([t1 * c - t2 * s, t2 * c + t1 * s], axis=-1)


def s5_branch(u, lam_re, lam_im, log_dt, b_re, b_im, c_re, c_im, d_skip):
    bsz, seqlen, _ = u.shape
    ug = u.astype(F32).reshape(bsz, seqlen, SSM_GROUPS, SSM_GROUP).astype(jnp.complex64)
    lam = lax.complex(lam_re.astype(F32), lam_im.astype(F32))
    dt = jnp.exp(log_dt.astype(F32))[:, None]
    lam_bar = jnp.exp(lam * dt)
    b = lax.complex(b_re.astype(F32), b_im.astype(F32))
    c = lax.complex(c_re.astype(F32), c_im.astype(F32))
    b_bar = ((lam_bar - 1.0) / lam)[..., None] * b
    bu = jnp.einsum('gph,blgh->lbgp', b_bar, ug)
    a = jnp.broadcast_to(lam_bar, (seqlen,) + lam_bar.shape)

    def combine(e_i, e_j):
        a_i, s_i = e_i
        a_j, s_j = e_j
        return a_j * a_i, a_j[:, None] * s_i + s_j

    _, states = lax.associative_scan(combine, (a, bu), axis=0)
    y = jnp.einsum('ghp,lbgp->blgh', c, states).real
    y = y.reshape(bsz, seqlen, SSM_WIDTH) + d_skip.astype(F32) * u.astype(F32)
    return y.astype(u.dtype)


def diff_attention(q, k, v, lq1, lk1, lq2, lk2, head_norm, lam_init):
    bsz, seqlen, _ = q.shape
    nb = seqlen // Q_BLOCK
    cos, sin = rope_tables(seqlen, DA_HEAD_DIM)
    q = apply_rope(q.astype(F32).reshape(bsz, seqlen, DA_HEADS, 2, DA_HEAD_DIM), cos, sin)
    k = apply_rope(k.astype(F32).reshape(bsz, seqlen, DA_HEADS, 2, DA_HEAD_DIM), cos, sin)
    q = jnp.transpose(q, (0, 2, 3, 1, 4))
    k = jnp.transpose(k, (0, 2, 3, 1, 4))
    vv = jnp.transpose(v.astype(F32).reshape(bsz, seqlen, DA_HEADS, DA_V_DIM), (0, 2, 1, 3))
    qblocks = jnp.moveaxis(q.reshape(bsz, DA_HEADS, 2, nb, Q_BLOCK, DA_HEAD_DIM), 3, 0)
    lam = (jnp.exp(jnp.sum(lq1.astype(F32) * lk1.astype(F32)))
           - jnp.exp(jnp.sum(lq2.astype(F32) * lk2.astype(F32))) + lam_init)
    scale = DA_HEAD_DIM ** -0.5
    k_chunk = jnp.arange(seqlen) // CHUNK

    def one_block(args):
        qb, bi = args
        s = jnp.einsum('bhcqd,bhckd->bhcqk', qb, k) * scale
        q_chunk = (bi * Q_BLOCK + jnp.arange(Q_BLOCK)) // CHUNK
        mask = k_chunk[None, :] <= q_chunk[:, None]
        s = jnp.where(mask, s, -jnp.inf)
        p = jax.nn.softmax(s, axis=-1)
        w = p[:, :, 0] - lam * p[:, :, 1]
        return jnp.einsum('bhqk,bhke->bhqe', w, vv)

    o = lax.map(one_block, (qblocks, jnp.arange(nb, dtype=jnp.int32)))
    o = jnp.transpose(o, (1, 0, 3, 2, 4)).reshape(bsz, seqlen, DA_HEADS, DA_V_DIM)
    o = o * lax.rsqrt(jnp.mean(o * o, axis=-1, keepdims=True) + EPS) * head_norm.astype(F32)
    o = o * (1.0 - lam_init)
    return o.reshape(bsz, seqlen, DA_WIDTH).astype(v.dtype)


def memory_cross_attention(h, mem_n, w_xq, w_xkv, w_xo):
    bsz, seqlen, _ = h.shape
    q = (h @ w_xq).reshape(bsz, seqlen, XA_HEADS, XA_HEAD_DIM)
    kv = mem_n @ w_xkv
    k, v = jnp.split(kv, 2, axis=-1)
    k = k.reshape(bsz, -1, XA_HEADS, XA_HEAD_DIM)
    v = v.reshape(bsz, -1, XA_HEADS, XA_HEAD_DIM)
    s = jnp.einsum('blhd,bmhd->bhlm', q, k).astype(F32) * (XA_HEAD_DIM ** -0.5)
    p = jax.nn.softmax(s, axis=-1)
    o = jnp.einsum('bhlm,bmhd->blhd', p, v.astype(F32)).astype(h.dtype)
    return o.reshape(bsz, seqlen, D_MODEL) @ w_xo


def setup_inputs(seed: int = 0) -> dict:
    key = jax.random.key(seed)
    ks = iter(jax.random.split(key, 64))
    L = DEPTH

    def nrm(shape, scale):
        return scale * jax.random.normal(next(ks), shape, F32)

    def gain(width=D_MODEL):
        return 1.0 + nrm((L, width), 0.05)

    G, P, Hg = SSM_GROUPS, SSM_STATE, SSM_GROUP
    lam_im = jnp.broadcast_to(jnp.pi * jnp.arange(P, dtype=F32), (L, G, P))
    lam_re = -0.5 + jnp.clip(nrm((L, G, P), 0.01), -0.05, 0.05)
    log_dt = jax.random.uniform(next(ks), (L, G), F32, math.log(DT_MIN), math.log(DT_MAX))
    return {
        "x": nrm((BATCH, SEQ, D_MODEL), 1.0),
        "mem": nrm((BATCH, MEM_LEN, D_MODEL), 1.0),
        "norm_mix_pre": gain(),
        "w_in": nrm((L, D_MODEL, IN_COLS), D_MODEL ** -0.5),
        "b_gate": nrm((L, 2 * D_MODEL), 0.02),
        "ssm_lambda_re": lam_re,
        "ssm_lambda_im": lam_im,
        "ssm_log_dt": log_dt,
        "ssm_b_re": nrm((L, G, P, Hg), (2.0 * Hg) ** -0.5),
        "ssm_b_im": nrm((L, G, P, Hg), (2.0 * Hg) ** -0.5),
        "ssm_c_re": nrm((L, G, Hg, P), (2.0 * P) ** -0.5),
        "ssm_c_im": nrm((L, G, Hg, P), (2.0 * P) ** -0.5),
        "ssm_d": nrm((L, SSM_WIDTH), 1.0),
        "w_glu": nrm((L, SSM_WIDTH, SSM_WIDTH), SSM_WIDTH ** -0.5),
        "b_glu": nrm((L, SSM_WIDTH), 0.02),
        "w_ssm_proj": nrm((L, SSM_WIDTH, D_MODEL), SSM_WIDTH ** -0.5),
        "da_lambda_q1": nrm((L, DA_HEAD_DIM), 0.1),
        "da_lambda_k1": nrm((L, DA_HEAD_DIM), 0.1),
        "da_lambda_q2": nrm((L, DA_HEAD_DIM), 0.1),
        "da_lambda_k2": nrm((L, DA_HEAD_DIM), 0.1),
        "da_head_norm": gain(DA_V_DIM),
        "w_da_proj": nrm((L, DA_WIDTH, D_MODEL), DA_WIDTH ** -0.5),
        "w_mix_out": nrm((L, D_MODEL, D_MODEL), D_MODEL ** -0.5),
        "norm_mix_post": gain(),
        "norm_x_pre": gain(),
        "norm_mem": gain(),
        "w_xq": nrm((L, D_MODEL, D_MODEL), D_MODEL ** -0.5),
        "w_xkv": nrm((L, D_MODEL, 2 * D_MODEL), D_MODEL ** -0.5),
        "w_xo": nrm((L, D_MODEL, D_MODEL), D_MODEL ** -0.5),
        "norm_x_post": gain(),
        "norm_ff_pre": gain(),
        "w_ff1": nrm((L, D_MODEL, D_FF), D_MODEL ** -0.5),
        "w_ff2": nrm((L, D_FF, D_MODEL), D_FF ** -0.5),
        "norm_ff_post": gain(),
    }


def reference(x, mem, norm_mix_pre, w_in, b_gate, ssm_lambda_re, ssm_lambda_im, ssm_log_dt,
              ssm_b_re, ssm_b_im, ssm_c_re, ssm_c_im, ssm_d, w_glu, b_glu, w_ssm_proj,
              da_lambda_q1, da_lambda_k1, da_lambda_q2, da_lambda_k2, da_head_norm, w_da_proj,
              w_mix_out, norm_mix_post, norm_x_pre, norm_mem, w_xq, w_xkv, w_xo, norm_x_post,
              norm_ff_pre, w_ff1, w_ff2, norm_ff_post):
    for l in range(DEPTH):
        lam_init = 0.8 - 0.6 * math.exp(-0.3 * l)
        h = rms_norm(x, norm_mix_pre[l])
        z = h @ w_in[l]
        u_s, q, k, v, g = jnp.split(z, [COL_SSM, COL_Q, COL_K, COL_V], axis=-1)
        ys = s5_branch(u_s, ssm_lambda_re[l], ssm_lambda_im[l], ssm_log_dt[l],
                       ssm_b_re[l], ssm_b_im[l], ssm_c_re[l], ssm_c_im[l], ssm_d[l])
        ys = jax.nn.gelu(ys)
        ys = ys * jax.nn.sigmoid(ys @ w_glu[l] + b_glu[l])
        branch_a = ys @ w_ssm_proj[l]
        ya = diff_attention(q, k, v, da_lambda_q1[l], da_lambda_k1[l], da_lambda_q2[l],
                            da_lambda_k2[l], da_head_norm[l], lam_init)
        branch_b = ya @ w_da_proj[l]
        gates = jax.nn.sigmoid((g + b_gate[l]).astype(F32))
        g_a, g_b = jnp.split(gates, 2, axis=-1)
        merged = (g_a * branch_a.astype(F32) + g_b * branch_b.astype(F32)).astype(x.dtype)
        x = x + rms_norm(merged @ w_mix_out[l], norm_mix_post[l])
        h = rms_norm(x, norm_x_pre[l])
        mem_n = rms_norm(mem, norm_mem[l])
        x = x + rms_norm(memory_cross_attention(h, mem_n, w_xq[l], w_xkv[l], w_xo[l]), norm_x_post[l])
        h = rms_norm(x, norm_ff_pre[l])
        f = jnp.square(jax.nn.relu(h @ w_ff1[l])) @ w_ff2[l]
        x = x + rms_norm(f, norm_ff_post[l])
    return x
```

```python
import contextlib
import math
import numpy as np
import concourse.bass as bass
import concourse.mybir as mybir
from concourse.bass_utils import run_bass_kernel_spmd

F32 = mybir.dt.float32
BF16 = mybir.dt.bfloat16
ALU = mybir.AluOpType
AF = mybir.ActivationFunctionType
AX = mybir.AxisListType

D = 1024
SEQ = 8192
NB = 64
NOWN = 2048
EPS = 1e-6
NLEV = 13
ENGS = ["pe", "act", "dve", "pool", "sp"]


class Op:
    __slots__ = ("eng", "idx", "fn", "deps", "is_dma", "needs_inc", "semval", "dsem", "dval")

    def __init__(self, eng, idx, fn, is_dma):
        self.eng, self.idx, self.fn, self.is_dma = eng, idx, fn, is_dma
        self.deps = []
        self.needs_inc = False
        self.semval = None
        self.dsem = None
        self.dval = None


class Prog:
    def __init__(self, nc, n_dma_sems=16):
        self.nc = nc
        self.ops = {e: [] for e in ENGS}
        self.state = {}
        self.rings = {"sp": (0, 12), "pool": (12, 10), "act": (22, 4), "dve": (26, 2), "pe": (26, 2)}
        n_dma_sems = 28
        self.n_dma_sems = n_dma_sems
        self.dma_rr = {q: 0 for q in self.rings}
        self.dma_counts = [0] * n_dma_sems
        self.waited = {}
        self.waited_dma = {}

    def _st(self, key):
        s = self.state.get(key)
        if s is None:
            s = {"w": {}, "r": {}}
            if isinstance(key, str) and ":" in key:
                base = self.state.get(key.split(":")[0])
                if base is not None:
                    s["w"] = dict(base["w"])
            self.state[key] = s
        return s

    def _add_dep(self, op, dep):
        if dep is None or dep is op:
            return
        if dep.is_dma:
            k = (op.eng, dep.dsem)
            if self.waited_dma.get(k, -1) >= dep.dval:
                return
            self.waited_dma[k] = dep.dval
            op.deps.append(dep)
            return
        k = (op.eng, dep.eng)
        if self.waited.get(k, -1) >= dep.idx:
            return
        self.waited[k] = dep.idx
        dep.needs_inc = True
        op.deps.append(dep)

    def op(self, eng, fn, reads=(), writes=(), dma=False):
        lst = self.ops[eng]
        o = Op(eng, len(lst), fn, dma)
        if dma:
            base, cnt_ = self.rings[eng]
            i = base + self.dma_rr[eng]
            self.dma_rr[eng] = (self.dma_rr[eng] + 1) % cnt_
            self.dma_counts[i] += 16
            o.dsem, o.dval = i, self.dma_counts[i]
        for key in reads:
            for e, w in self._st(key)["w"].items():
                if (not w.is_dma) and w.eng == eng and eng == "pe":
                    continue
                self._add_dep(o, w)
        for key in writes:
            s = self._st(key)
            for e, r in s["r"].items():
                if (not r.is_dma) and r.eng == eng and not dma:
                    continue
                self._add_dep(o, r)
            for e, w in s["w"].items():
                if (not w.is_dma) and w.eng == eng and not dma:
                    continue
                self._add_dep(o, w)
        me = ("dma%d" % o.dsem) if dma else eng
        for key in reads:
            self._st(key)["r"][me] = o
        for key in writes:
            s = self._st(key)
            s["w"][me] = o
            s["r"] = {}
        lst.append(o)
        return o

    def alias(self, newkey, oldkeys):
        ns = self._st(newkey)
        for ok in oldkeys:
            os_ = self.state.get(ok)
            if os_ is None:
                continue
            for kind in ("w", "r"):
                for e, o in os_[kind].items():
                    cur = ns["w"].get(e)
                    if cur is None or (o.is_dma and o.dval > cur.dval) or ((not o.is_dma) and o.idx > cur.idx):
                        ns["w"][e] = o

    def emit(self):
        nc = self.nc
        with contextlib.ExitStack() as es:
            sems = {e: es.enter_context(nc.semaphore("s_" + e)) for e in ["pe", "act", "dve", "pool"]}
            dsems = [es.enter_context(nc.semaphore("d%d" % i)) for i in range(self.n_dma_sems)]
            for e in ENGS:
                c = 0
                for o in self.ops[e]:
                    if (not o.is_dma) and o.needs_inc:
                        c += 1
                        o.semval = c
            block = es.enter_context(nc.Block())

            def run(name, eng):
                for o in self.ops[name]:
                    for d in o.deps:
                        if d.is_dma:
                            eng.wait_ge(dsems[d.dsem], d.dval)
                        else:
                            eng.wait_ge(sems[d.eng], d.semval)
                    ins = o.fn(eng)
                    if o.is_dma:
                        ins.then_inc(dsems[o.dsem], 16)
                    elif o.needs_inc:
                        ins.then_inc(sems[o.eng], 1)

            @block.tensor
            def _(eng):
                run("pe", eng)

            @block.scalar
            def _(eng):
                run("act", eng)

            @block.vector
            def _(eng):
                run("dve", eng)

            @block.gpsimd
            def _(eng):
                run("pool", eng)

            @block.sync
            def _(eng):
                run("sp", eng)
                for i in range(self.n_dma_sems):
                    if self.dma_counts[i] > 0:
                        eng.wait_ge(dsems[i], self.dma_counts[i])


class Arena:
    def __init__(self, P, base_ap, nbytes):
        self.P = P
        self.base = base_ap
        self.nbytes = nbytes
        self.live = {}
        self.freed = []

    def alloc(self, name, nelem, dt):
        esz = 4 if dt == F32 else 2
        size = (nelem * esz + 63) // 64 * 64
        segs = sorted(self.live.values())
        off = 0
        for (o, s) in segs:
            if off + size <= o:
                break
            off = max(off, o + s)
        assert off + size <= self.nbytes, "SBUF arena overflow for %s (%d): %s" % (name, size, sorted((o, sz, n) for n, (o, sz) in self.live.items()))
        self.live[name] = (off, size)
        olds = [n for (o, s, n) in self.freed if o < off + size and off < o + s]
        oldkeys = [k for k in self.P.state if any(k == n or (isinstance(k, str) and k.startswith(n + ":")) for n in olds)]
        self.P.alias(name, oldkeys)
        self._aliaskeys = oldkeys
        ap = self.base[:, off // 4:(off + size) // 4]
        if dt != F32:
            ap = ap.bitcast(dt)
        return ap[:, 0:nelem]

    def free(self, name):
        o, s = self.live.pop(name)
        self.freed.append((o, s, name))


def build_program(debug=False):
    nc = bass.Bass("TRN2", target_bir_lowering=False)

    def din(name, shape, dt=F32):
        return nc.dram_tensor(name, list(shape), dt, kind="ExternalInput").ap()

    def dscr(name, shape, dt=BF16):
        return nc.dram_tensor(name, list(shape), dt, kind="ExternalOutput" if debug else "Internal").ap()

    xseq = din("xseq", [SEQ, D])
    xown = din("xown", [NOWN, D])
    mem = din("mem", [256, D])
    w_in = din("w_in", [D, 6144])
    w_glu = din("w_glu", [D, D]); w_ssm = din("w_ssm_proj", [D, D]); w_da = din("w_da_proj", [D, D])
    w_mix = din("w_mix_out", [D, D]); w_xq = din("w_xq", [D, D]); w_xkv = din("w_xkv", [D, 2 * D])
    w_xo = din("w_xo", [D, D]); w_ff1 = din("w_ff1", [D, 4 * D]); w_ff2 = din("w_ff2", [4 * D, D])
    gains = {n: din(n, [D]) for n in ["norm_mix_pre", "norm_mix_post", "norm_x_pre", "norm_mem", "norm_x_post",
                                      "norm_ff_pre", "norm_ff_post"]}
    b_gate = din("b_gate", [2 * D]); b_glu = din("b_glu", [D]); ssm_d = din("ssm_d", [D])
    lam_re = din("ssm_lambda_re", [64, 64]); lam_im = din("ssm_lambda_im", [64, 64]); log_dt = din("ssm_log_dt", [64])
    b_re = din("ssm_b_re", [64, 64, 16]); b_im = din("ssm_b_im", [64, 64, 16])
    c_re = din("ssm_c_re", [64, 16, 64]); c_im = din("ssm_c_im", [64, 16, 64])
    lqk = [din(n, [64]) for n in ["da_lambda_q1", "da_lambda_k1", "da_lambda_q2", "da_lambda_k2"]]
    head_norm = din("da_head_norm", [128])
    cos_seq = din("cos_seq", [128, SEQ]); sin_seq = din("sin_seq", [128, SEQ])
    cos_own = din("cos_own", [128, NOWN]); sin_own = din("sin_own", [128, NOWN])
    kvalid = din("kvalid", [SEQ])
    esel_d = din("esel", [128, 256])
    ident_d = din("ident", [128, 128]); swap_d = din("swapm", [128, 128]); gmask_d = din("gmask", [128, 8, 128])
    out_d = nc.dram_tensor("out", [NOWN, D], F32, kind="ExternalOutput").ap()

    kT_d = dscr("kT_d", [8, 128, SEQ]); v_d = dscr("v_d", [SEQ, D]); uT_d = dscr("uT_d", [8, 128, SEQ])
    qT_d = dscr("qT_d", [8, 128, NOWN]); g_d = dscr("g_d", [16, 128, NOWN])

    P = Prog(nc)
    es = contextlib.ExitStack()
    ARENA_BYTES = 190 * 1024
    arena_t = es.enter_context(nc.sbuf_tensor("arena", [128, ARENA_BYTES // 4], F32))
    A = Arena(P, arena_t[:], ARENA_BYTES)
    banks = [es.enter_context(nc.psum_tensor("bank%d" % i, [128, 512], F32)) for i in range(8)]

    def bank(i):
        return banks[i][:]

    def bank_bf(i):
        return banks[i][:].bitcast(BF16)

    def bkey(i):
        return "bank%d" % i

    def dma(out, in_, reads=(), writes=(), q="sp", slow=False):
        if slow:
            return P.op(q, lambda e: e.dma_start(out=out, in_=in_, allow_slow_non_contiguous=True), reads, writes, dma=True)
        return P.op(q, lambda e: e.dma_start(out=out, in_=in_), reads, writes, dma=True)

    ident_f = A.alloc("ident_f", 128, F32); ident_b = A.alloc("ident_b", 128, BF16)
    swap_f = A.alloc("swap_f", 128, F32)
    gmask = A.alloc("gmask", 1024, F32)
    epst = A.alloc("epst", 1, F32); zerot = A.alloc("zerot", 1, F32)
    dma(ident_f, ident_d, writes=["ident_f"]); dma(swap_f, swap_d, writes=["swap_f"])
    dma(gmask, gmask_d.rearrange("p g c -> p (g c)"), writes=["gmask"])
    P.op("pool", lambda e: e.tensor_copy(out=ident_b, in_=ident_f), ["ident_f"], ["ident_b"])
    P.op("pool", lambda e: e.memset(epst, EPS), [], ["epst"])
    P.op("pool", lambda e: e.memset(zerot, 0.0), [], ["zerot"])

    def load_pp(name, src, n):
        t = A.alloc(name, n, F32)
        dma(t, src.rearrange("(k p) -> p k", p=128), writes=[name], slow=True)
        return t

    g_mix_pre = load_pp("g_mix_pre", gains["norm_mix_pre"], 8)
    g_x_pre = load_pp("g_x_pre", gains["norm_x_pre"], 8)
    g_mem = load_pp("g_mem", gains["norm_mem"], 8)
    g_ff_pre = load_pp("g_ff_pre", gains["norm_ff_pre"], 8)
    bgate_pp = load_pp("bgate_pp", b_gate, 16)
    bglu_pp = load_pp("bglu_pp", b_glu, 8)
    ssmd_pp = load_pp("ssmd_pp", ssm_d, 8)

    wctr = [0]

    def load_weight(name, src, K, N, gain=None, col0=0, rot=False):
        KC = K // 128
        wt = A.alloc(name, KC * N, BF16)
        wv = wt.rearrange("p (k n) -> p k n", k=KC)
        CH = min(N, 2048)
        for kc in range(KC):
            for c0 in range(0, N, CH):
                i = wctr[0]; wctr[0] += 1
                sname = "wstage%d" % (i % 2)
                if sname not in A.live:
                    A.alloc(sname, 2048, F32)
                o, s = A.live[sname]
                st = A.base[:, o // 4:o // 4 + CH]
                dma(st, src[kc * 128:(kc + 1) * 128, col0 + c0:col0 + c0 + CH], writes=[sname], q="sp")
                dst = wv[:, kc, c0:c0 + CH]
                eng = "pool" if (i % 2 == 0) else "dve"
                if rot:
                    sv = st.rearrange("p (m t d) -> p m t d", t=2, d=32)
                    dv = dst.rearrange("p (m t d) -> p m t d", t=2, d=32)
                    if gain is not None:
                        P.op(eng, lambda e, dv=dv, sv=sv, kc=kc: e.tensor_scalar(out=dv[:, :, 0, :], in0=sv[:, :, 1, :], scalar1=gain[:, kc:kc + 1], scalar2=-1.0, op0=ALU.mult, op1=ALU.mult), [sname, "gains"], [name])
                        P.op(eng, lambda e, dv=dv, sv=sv, kc=kc: e.tensor_scalar(out=dv[:, :, 1, :], in0=sv[:, :, 0, :], scalar1=gain[:, kc:kc + 1], scalar2=None, op0=ALU.mult), [sname, "gains"], [name])
                else:
                    if gain is not None:
                        P.op(eng, lambda e, dst=dst, st=st, kc=kc: e.tensor_scalar(out=dst, in0=st, scalar1=gain[:, kc:kc + 1], scalar2=None, op0=ALU.mult), [sname, "gains"], [name])
                    else:
                        P.op(eng, lambda e, dst=dst, st=st: e.tensor_copy(out=dst, in_=st), [sname], [name])
        return wv

    P.op("pool", lambda e: e.engine_nop(), ["g_mix_pre", "g_x_pre", "g_mem", "g_ff_pre"], ["gains"])

    nctr = [0]

    def norm_block_T(x_src_ap, x_is_dram, hT_dst, hT_key, xkey=None, tp_bank=0):
        i = nctr[0]; nctr[0] += 1
        if x_is_dram:
            xs_name = "xs%d" % (i % 2)
            if xs_name not in A.live:
                A.alloc(xs_name, 1024, F32)
            o, s = A.live[xs_name]
            xs = A.base[:, o // 4:o // 4 + 1024]
            dma(xs, x_src_ap, writes=[xs_name], q="sp")
            rkeys = [xs_name]
        else:
            xs = x_src_ap
            rkeys = [xkey]
        for nm, n, dt in (("nsq%d" % (i % 2), 1024, BF16), ("nst%d" % (i % 2), 4, F32), ("nh%d" % (i % 2), 1024, BF16)):
            if nm not in A.live:
                A.alloc(nm, n, dt)
        o, s = A.live["nsq%d" % (i % 2)]; sq = A.base[:, o // 4:o // 4 + 512].bitcast(BF16)
        o, s = A.live["nst%d" % (i % 2)]; st = A.base[:, o // 4:o // 4 + 4]
        o, s = A.live["nh%d" % (i % 2)]; hb = A.base[:, o // 4:o // 4 + 512].bitcast(BF16)
        ks, kt, kh = "nsq%d" % (i % 2), "nst%d" % (i % 2), "nh%d" % (i % 2)
        P.op("act", lambda e: e.activation(out=sq, in_=xs, func=AF.Square, accum_out=st[:, 0:1]), rkeys, [ks, kt])
        P.op("act", lambda e: e.activation(out=st[:, 1:2], in_=st[:, 0:1], func=AF.Ln, scale=1.0 / D, bias=epst), [kt, "epst"], [kt + ":1"])
        P.op("act", lambda e: e.activation(out=st[:, 2:3], in_=st[:, 1:2], func=AF.Exp, scale=-0.5), [kt + ":1"], [kt + ":2"])
        P.op("dve", lambda e: e.tensor_scalar(out=hb, in0=xs, scalar1=st[:, 2:3], scalar2=None, op0=ALU.mult), rkeys + [kt + ":2"], [kh])
        tp = bank_bf(tp_bank).rearrange("p (k t) -> p k t", k=8)
        for kc in range(8):
            P.op("pe", lambda e, kc=kc: e.transpose(out=tp[:, kc, :], in_=hb[:, kc * 128:(kc + 1) * 128], identity=ident_b), [kh, "ident_b"], [bkey(tp_bank)])
        P.op("act", lambda e: e.activation(out=hT_dst, in_=tp, func=AF.Copy), [bkey(tp_bank)], [hT_key])

    x1_d = dscr("x1_d", [NOWN, D], F32)
    x2_d = dscr("x2_d", [NOWN, D], F32)
    wf1_d = nc.dram_tensor("wf1_d", [D, 4 * D], BF16, kind="Internal").ap()
    wf2_d = nc.dram_tensor("wf2_d", [4 * D, D], BF16, kind="Internal").ap()
    wsc = {}
    for nm_, (src_, K_, N_, gn_) in {"wglu": (w_glu, D, D, None), "wssm": (w_ssm, D, D, None), "wda": (w_da, D, D, None),
                                       "wmix": (w_mix, D, D, None), "wxkv": (w_xkv, D, 2 * D, g_mem), "wxq": (w_xq, D, D, g_x_pre),
                                       "wxo": (w_xo, D, D, None)}.items():
        wsc[nm_] = (nc.dram_tensor(nm_ + "_d", [K_, N_], BF16, kind="Internal").ap(), src_, K_, N_, gn_)
    cst_ = [A.alloc("cstg%d" % i, 1024, F32) for i in range(2)]
    cb = [A.alloc("cb%d" % i, 1024, BF16) for i in range(2)]
    cctr = [0]

    def cast_gen():
        jobs = [(v_[1], v_[2], v_[3], v_[4], v_[0], k_ + "_d") for k_, v_ in wsc.items()]
        jobs.append((w_ff1, D, 4 * D, g_ff_pre, wf1_d, "wf_d"))
        jobs.append((w_ff2, 4 * D, D, None, wf2_d, "wf_d"))
        for (src, K_, N_, gain, dst, dkey) in jobs:
            for kc in range(K_ // 128):
                for c0 in range(0, N_, 1024):
                    i = cctr[0]; cctr[0] += 1
                    st = cst_[i % 2]; sname = "cstg%d" % (i % 2)
                    dma(st, src[kc * 128:(kc + 1) * 128, c0:c0 + 1024], writes=[sname], q="pool")
                    cbt = cb[i % 2]; cbk = "cb%d" % (i % 2)
                    if gain is not None:
                        P.op("pool", lambda e, cbt=cbt, st=st, kc=kc, gain=gain: e.tensor_scalar(out=cbt, in0=st, scalar1=gain[:, kc:kc + 1], scalar2=None, op0=ALU.mult), [sname, "gains"], [cbk])
                    else:
                        P.op("pool", lambda e, cbt=cbt, st=st: e.tensor_copy(out=cbt, in_=st), [sname], [cbk])
                    dma(dst[kc * 128:(kc + 1) * 128, c0:c0 + 1024], cbt, reads=[cbk], writes=[dkey], q="pool")
                    yield

    cgen = cast_gen()
    cg_alive = [True]

    def cast_step():
        if cg_alive[0]:
            try:
                next(cgen)
            except StopIteration:
                cg_alive[0] = False

    wk = load_weight("wk", w_in, D, 1024, gain=g_mix_pre, col0=2048)
    wkr = load_weight("wkr", w_in, D, 1024, gain=g_mix_pre, col0=2048, rot=True)
    wu = load_weight("wu", w_in, D, 1024, gain=g_mix_pre, col0=0)
    wv_ = load_weight("wv", w_in, D, 1024, gain=g_mix_pre, col0=3072)
    hTa = [A.alloc("hTa%d" % i, 8 * 512, BF16).rearrange("p (k t) -> p k t", k=8) for i in range(2)]
    cst = [A.alloc("cst%d" % i, 1024, F32) for i in range(2)]
    kst = [A.alloc("kst%d" % i, 512, BF16) for i in range(2)]
    kt1 = [A.alloc("kt1_%d" % i, 512, F32) for i in range(2)]
    kt2 = [A.alloc("kt2_%d" % i, 512, F32) for i in range(2)]
    vst = [A.alloc("vst%d" % i, 1024, BF16) for i in range(2)]
    ust = [A.alloc("ust%d" % i, 512, BF16) for i in range(2)]
    cnt = [0]
    for tt in range(16):
        hb_i = tt % 2
        hT = hTa[hb_i]; hk = "hTa%d" % hb_i
        for bl in range(4):
            blk = tt * 4 + bl
            norm_block_T(xseq[blk * 128:(blk + 1) * 128, :], True, hT[:, :, bl * 128:(bl + 1) * 128], hk)
        cs = cst[hb_i]; ck = "cst%d" % hb_i
        dma(cs[:, 0:512], cos_seq[:, tt * 512:(tt + 1) * 512], writes=[ck])
        dma(cs[:, 512:1024], sin_seq[:, tt * 512:(tt + 1) * 512], writes=[ck])
        for h in range(8):
            cast_step()
            i = cnt[0]; cnt[0] += 1
            pb = 1 + 2 * (i % 2)
            for kc in range(8):
                P.op("pe", lambda e, kc=kc, h=h, pb=pb, hT=hT: e.matmul(out=bank(pb), lhsT=wk[:, kc, h * 128:(h + 1) * 128], rhs=hT[:, kc, :], start=(kc == 0), stop=(kc == 7)), ["wk", hk], [bkey(pb)])
            for kc in range(8):
                P.op("pe", lambda e, kc=kc, h=h, pb=pb, hT=hT: e.matmul(out=bank(pb + 1), lhsT=wkr[:, kc, h * 128:(h + 1) * 128], rhs=hT[:, kc, :], start=(kc == 0), stop=(kc == 7)), ["wkr", hk], [bkey(pb + 1)])
            j = i % 2
            P.op("dve", lambda e, pb=pb, j=j, cs=cs: e.tensor_tensor(out=kt1[j], in0=bank(pb), in1=cs[:, 0:512], op=ALU.mult), [bkey(pb), ck], ["kt1_%d" % j])
            P.op("dve", lambda e, pb=pb, j=j, cs=cs: e.tensor_tensor(out=kt2[j], in0=bank(pb + 1), in1=cs[:, 512:1024], op=ALU.mult), [bkey(pb + 1), ck], ["kt2_%d" % j])
            P.op("pool", lambda e, j=j: e.tensor_tensor(out=kst[j], in0=kt1[j], in1=kt2[j], op=ALU.add), ["kt1_%d" % j, "kt2_%d" % j], ["kst%d" % j])
            dma(kT_d[h, :, tt * 512:(tt + 1) * 512], kst[j], reads=["kst%d" % j], writes=["kT_d"], q="sp")
        for fc in range(8):
            i = cnt[0]; cnt[0] += 1
            pb = 5 + (i % 2)
            for kc in range(8):
                P.op("pe", lambda e, kc=kc, fc=fc, pb=pb, hT=hT: e.matmul(out=bank(pb), lhsT=wu[:, kc, fc * 128:(fc + 1) * 128], rhs=hT[:, kc, :], start=(kc == 0), stop=(kc == 7)), ["wu", hk], [bkey(pb)])
            j = i % 2
            P.op("act", lambda e, pb=pb, j=j: e.activation(out=ust[j], in_=bank(pb), func=AF.Copy), [bkey(pb)], ["ust%d" % j])
            dma(uT_d[fc, :, tt * 512:(tt + 1) * 512], ust[j], reads=["ust%d" % j], writes=["uT_d"], q="sp")
        for bl in range(4):
            blk = tt * 4 + bl
            i = cnt[0]; cnt[0] += 1
            j = i % 2
            for half in range(2):
                pb = 1 + 2 * (i % 2) + half
                for kc in range(8):
                    P.op("pe", lambda e, kc=kc, bl=bl, half=half, pb=pb, hT=hT: e.matmul(out=bank(pb), lhsT=hT[:, kc, bl * 128:(bl + 1) * 128], rhs=wv_[:, kc, half * 512:(half + 1) * 512], start=(kc == 0), stop=(kc == 7)), ["wv", hk], [bkey(pb)])
                P.op("act", lambda e, pb=pb, j=j, half=half: e.activation(out=vst[j][:, half * 512:(half + 1) * 512], in_=bank(pb), func=AF.Copy), [bkey(pb)], ["vst%d" % j])
            dma(v_d[blk * 128:(blk + 1) * 128, :], vst[j], reads=["vst%d" % j], writes=["v_d"], q="sp")
    while cg_alive[0]:
        cast_step()
    for n in ["cstg0", "cstg1", "cb0", "cb1"]:
        A.free(n)
    for n in ["wu", "wk", "wkr", "wv", "hTa0", "hTa1", "cst0", "cst1", "kst0", "kst1", "kt1_0", "kt1_1", "kt2_0", "kt2_1", "vst0", "vst1", "ust0", "ust1"]:
        A.free(n)

    wq = load_weight("wq", w_in, D, 1024, gain=g_mix_pre, col0=1024)
    wqr = load_weight("wqr", w_in, D, 1024, gain=g_mix_pre, col0=1024, rot=True)
    wg = load_weight("wg", w_in, D, 2048, gain=g_mix_pre, col0=4096)
    hTo = [A.alloc("hTo%d" % i, 8 * 512, BF16).rearrange("p (k t) -> p k t", k=8) for i in range(2)]
    cso = [A.alloc("cso%d" % i, 1024, F32) for i in range(2)]
    qst = [A.alloc("qst%d" % i, 512, BF16) for i in range(2)]
    qt1 = [A.alloc("qt1_%d" % i, 512, F32) for i in range(2)]
    qt2 = [A.alloc("qt2_%d" % i, 512, F32) for i in range(2)]
    gst = [A.alloc("gst%d" % i, 512, BF16) for i in range(2)]
    for tt in range(4):
        hb_i = tt % 2
        hT = hTo[hb_i]; hk = "hTo%d" % hb_i
        for bl in range(4):
            blk = tt * 4 + bl
            norm_block_T(xown[blk * 128:(blk + 1) * 128, :], True, hT[:, :, bl * 128:(bl + 1) * 128], hk)
        cs = cso[hb_i]; ck = "cso%d" % hb_i
        dma(cs[:, 0:512], cos_own[:, tt * 512:(tt + 1) * 512], writes=[ck])
        dma(cs[:, 512:1024], sin_own[:, tt * 512:(tt + 1) * 512], writes=[ck])
        for h in range(8):
            i = cnt[0]; cnt[0] += 1
            pb = 1 + 2 * (i % 2)
            for kc in range(8):
                P.op("pe", lambda e, kc=kc, h=h, pb=pb, hT=hT: e.matmul(out=bank(pb), lhsT=wq[:, kc, h * 128:(h + 1) * 128], rhs=hT[:, kc, :], start=(kc == 0), stop=(kc == 7)), ["wq", hk], [bkey(pb)])
            for kc in range(8):
                P.op("pe", lambda e, kc=kc, h=h, pb=pb, hT=hT: e.matmul(out=bank(pb + 1), lhsT=wqr[:, kc, h * 128:(h + 1) * 128], rhs=hT[:, kc, :], start=(kc == 0), stop=(kc == 7)), ["wqr", hk], [bkey(pb + 1)])
            j = i % 2
            P.op("dve", lambda e, pb=pb, j=j, cs=cs: e.tensor_tensor(out=qt1[j], in0=bank(pb), in1=cs[:, 0:512], op=ALU.mult), [bkey(pb), ck], ["qt1_%d" % j])
            P.op("dve", lambda e, pb=pb, j=j, cs=cs: e.tensor_tensor(out=qt2[j], in0=bank(pb + 1), in1=cs[:, 512:1024], op=ALU.mult), [bkey(pb + 1), ck], ["qt2_%d" % j])
            P.op("pool", lambda e, j=j: e.tensor_tensor(out=qst[j], in0=qt1[j], in1=qt2[j], op=ALU.add), ["qt1_%d" % j, "qt2_%d" % j], ["qst%d" % j])
            dma(qT_d[h, :, tt * 512:(tt + 1) * 512], qst[j], reads=["qst%d" % j], writes=["qT_d"], q="sp")
        for gc in range(16):
            i = cnt[0]; cnt[0] += 1
            pb = 5 + (i % 2)
            for kc in range(8):
                P.op("pe", lambda e, kc=kc, gc=gc, pb=pb, hT=hT: e.matmul(out=bank(pb), lhsT=wg[:, kc, gc * 128:(gc + 1) * 128], rhs=hT[:, kc, :], start=(kc == 0), stop=(kc == 7)), ["wg", hk], [bkey(pb)])
            j = i % 2
            P.op("act", lambda e, pb=pb, j=j, gc=gc: e.activation(out=gst[j], in_=bank(pb), func=AF.Sigmoid, bias=bgate_pp[:, gc:gc + 1]), [bkey(pb), "bgate_pp"], ["gst%d" % j])
            dma(g_d[gc, :, tt * 512:(tt + 1) * 512], gst[j], reads=["gst%d" % j], writes=["g_d"], q="sp")
    for n in ["wq", "wqr", "wg", "hTo0", "hTo1", "cso0", "cso1", "qst0", "qst1", "qt1_0", "qt1_1", "qt2_0", "qt2_1", "gst0", "gst1"]:
        A.free(n)

    for n in ["wstage0", "wstage1", "xs0", "xs1", "nsq0", "nsq1", "nh0", "nh1", "nst0", "nst1"]:
        if n in A.live:
            A.free(n)
    def a64(name, n, dt=F32):
        return A.alloc(name, n, dt)[0:64, :]

    lre = a64("lre", 64); lim = a64("lim", 64); ldt = a64("ldt", 64)
    dma(lre, lam_re.rearrange("g p -> p g"), writes=["lre"], slow=True)
    dma(lim, lam_im.rearrange("g p -> p g"), writes=["lim"], slow=True)
    dma(ldt, log_dt.partition_broadcast(64), writes=["ldt"])
    dtt = a64("dtt", 64); zr = a64("zr", 64); zi = a64("zi", 64)
    P.op("act", lambda e: e.activation(out=dtt, in_=ldt, func=AF.Exp), ["ldt"], ["dtt"])
    P.op("dve", lambda e: e.tensor_tensor(out=zr, in0=lre, in1=dtt, op=ALU.mult), ["lre", "dtt"], ["zr"])
    P.op("dve", lambda e: e.tensor_tensor(out=zi, in0=lim, in1=dtt, op=ALU.mult), ["lim", "dtt"], ["zi"])
    mag = a64("mag", 64); cr = a64("cr", 64); ci = a64("ci", 64); halfpi = a64("halfpi", 1)
    P.op("pool", lambda e: e.memset(halfpi, math.pi / 2), [], ["halfpi"])
    P.op("act", lambda e: e.activation(out=mag, in_=zr, func=AF.Exp, scale=1.0 / 32), ["zr"], ["mag"])
    P.op("act", lambda e: e.activation(out=ci, in_=zi, func=AF.Sin, scale=1.0 / 32), ["zi"], ["ci"])
    P.op("act", lambda e: e.activation(out=cr, in_=zi, func=AF.Sin, scale=1.0 / 32, bias=halfpi), ["zi", "halfpi"], ["cr"])
    P.op("dve", lambda e: e.tensor_tensor(out=cr, in0=cr, in1=mag, op=ALU.mult), ["cr", "mag"], ["cr"])
    P.op("dve", lambda e: e.tensor_tensor(out=ci, in0=ci, in1=mag, op=ALU.mult), ["ci", "mag"], ["ci"])
    PW = a64("PW", NLEV * 128).rearrange("p (l r g) -> p l r g", l=NLEV, r=2)
    t1 = a64("sq_t1", 64); t2 = a64("sq_t2", 64); t3 = a64("sq_t3", 64)

    def csquare(sr, si, dr, di, keys_in, key_out):
        P.op("dve", lambda e: e.tensor_tensor(out=t1, in0=sr, in1=sr, op=ALU.mult), keys_in, ["sq_t1"])
        P.op("dve", lambda e: e.tensor_tensor(out=t2, in0=si, in1=si, op=ALU.mult), keys_in, ["sq_t2"])
        P.op("dve", lambda e: e.tensor_tensor(out=t3, in0=sr, in1=si, op=ALU.mult), keys_in, ["sq_t3"])
        P.op("dve", lambda e: e.tensor_tensor(out=dr, in0=t1, in1=t2, op=ALU.subtract), ["sq_t1", "sq_t2"], [key_out])
        P.op("dve", lambda e: e.tensor_tensor(out=di, in0=t3, in1=t3, op=ALU.add), ["sq_t3"], [key_out + ":i"])

    wr = [a64("wr%d" % i, 64) for i in range(2)]; wi = [a64("wi%d" % i, 64) for i in range(2)]
    csquare(cr, ci, wr[0], wi[0], ["cr", "ci"], "wr0")
    csquare(wr[0], wi[0], wr[1], wi[1], ["wr0", "wr0:i"], "wr1")
    csquare(wr[1], wi[1], wr[0], wi[0], ["wr1", "wr1:i"], "wr0")
    csquare(wr[0], wi[0], wr[1], wi[1], ["wr0", "wr0:i"], "wr1")
    csquare(wr[1], wi[1], PW[:, 0, 0, :], PW[:, 0, 1, :], ["wr1", "wr1:i"], "PW:0")
    for l in range(1, NLEV):
        csquare(PW[:, l - 1, 0, :], PW[:, l - 1, 1, :], PW[:, l, 0, :], PW[:, l, 1, :], ["PW:%d" % (l - 1), "PW:%d:i" % (l - 1)], "PW:%d" % l)
    den = a64("den", 64); cfr = a64("cfr", 64); cfi = a64("cfi", 64); lm1 = a64("lm1", 64)
    P.op("dve", lambda e: e.tensor_tensor(out=t1, in0=lre, in1=lre, op=ALU.mult), ["lre", "PW:%d:i" % (NLEV - 1)], ["sq_t1"])
    P.op("dve", lambda e: e.tensor_tensor(out=t2, in0=lim, in1=lim, op=ALU.mult), ["lim"], ["sq_t2"])
    P.op("dve", lambda e: e.tensor_tensor(out=den, in0=t1, in1=t2, op=ALU.add), ["sq_t1", "sq_t2"], ["den"])
    P.op("dve", lambda e: e.reciprocal(out=den, in_=den), ["den"], ["den"])
    P.op("dve", lambda e: e.tensor_scalar(out=lm1, in0=PW[:, 0, 0, :], scalar1=-1.0, scalar2=None, op0=ALU.add), ["PW:0"], ["lm1"])
    P.op("dve", lambda e: e.tensor_tensor(out=t1, in0=lm1, in1=lre, op=ALU.mult), ["lm1", "lre", "den"], ["sq_t1"])
    P.op("dve", lambda e: e.tensor_tensor(out=t2, in0=PW[:, 0, 1, :], in1=lim, op=ALU.mult), ["PW:0:i", "lim"], ["sq_t2"])
    P.op("dve", lambda e: e.tensor_tensor(out=cfr, in0=t1, in1=t2, op=ALU.add), ["sq_t1", "sq_t2"], ["cfr"])
    P.op("dve", lambda e: e.tensor_tensor(out=t1, in0=PW[:, 0, 1, :], in1=lre, op=ALU.mult), ["PW:0:i", "lre", "cfr"], ["sq_t1"])
    P.op("dve", lambda e: e.tensor_tensor(out=t2, in0=lm1, in1=lim, op=ALU.mult), ["lm1", "lim", "cfr"], ["sq_t2"])
    P.op("dve", lambda e: e.tensor_tensor(out=cfi, in0=t1, in1=t2, op=ALU.subtract), ["sq_t1", "sq_t2"], ["cfi"])
    P.op("dve", lambda e: e.tensor_tensor(out=cfr, in0=cfr, in1=den, op=ALU.mult), ["cfr", "den"], ["cfr"])
    P.op("dve", lambda e: e.tensor_tensor(out=cfi, in0=cfi, in1=den, op=ALU.mult), ["cfi", "den"], ["cfi"])
    braw = a64("braw", 2048).rearrange("p (r g h) -> p r g h", r=2, g=64)
    for gh in range(2):
        dma(braw[:, 0, gh * 32:(gh + 1) * 32, :], b_re[gh * 32:(gh + 1) * 32].rearrange("g p h -> p g h"), writes=["braw"], slow=True)
        dma(braw[:, 1, gh * 32:(gh + 1) * 32, :], b_im[gh * 32:(gh + 1) * 32].rearrange("g p h -> p g h"), writes=["braw"], slow=True)
    Bb = a64("Bb", 2048).rearrange("p (r g h) -> p r g h", r=2, g=64)
    bt1 = a64("bt1", 1024).rearrange("p (g h) -> p g h", g=64); bt2 = a64("bt2", 1024).rearrange("p (g h) -> p g h", g=64)
    cfr_b = cfr.unsqueeze(2).broadcast_to([64, 64, 16]); cfi_b = cfi.unsqueeze(2).broadcast_to([64, 64, 16])
    P.op("dve", lambda e: e.tensor_tensor(out=bt1, in0=braw[:, 0], in1=cfr_b, op=ALU.mult), ["braw", "cfr"], ["bt1"])
    P.op("dve", lambda e: e.tensor_tensor(out=bt2, in0=braw[:, 1], in1=cfi_b, op=ALU.mult), ["braw", "cfi"], ["bt2"])
    P.op("dve", lambda e: e.tensor_tensor(out=Bb[:, 0], in0=bt1, in1=bt2, op=ALU.subtract), ["bt1", "bt2"], ["Bb"])
    P.op("dve", lambda e: e.tensor_tensor(out=bt1, in0=braw[:, 0], in1=cfi_b, op=ALU.mult), ["braw", "cfi", "Bb"], ["bt1"])
    P.op("dve", lambda e: e.tensor_tensor(out=bt2, in0=braw[:, 1], in1=cfr_b, op=ALU.mult), ["braw", "cfr", "Bb"], ["bt2"])
    P.op("dve", lambda e: e.tensor_tensor(out=Bb[:, 1], in0=bt1, in1=bt2, op=ALU.add), ["bt1", "bt2"], ["Bb:i"])
    Cst = A.alloc("Cst", 1024, F32).rearrange("p (g h) -> p g h", g=64)
    for gh in range(2):
        dma(Cst[0:64, gh * 32:(gh + 1) * 32, :], c_re[gh * 32:(gh + 1) * 32].rearrange("g h p -> p g h"), writes=["Cst"], slow=True)
        dma(Cst[64:128, gh * 32:(gh + 1) * 32, :], c_im[gh * 32:(gh + 1) * 32].rearrange("g h p -> p g h"), writes=["Cst"], slow=True)
    P.op("pool", lambda e: e.tensor_scalar(out=Cst[64:128], in0=Cst[64:128], scalar1=-1.0, scalar2=None, op0=ALU.mult), ["Cst"], ["Cst"])
    S1 = A.alloc("S1", NLEV * 64, F32).rearrange("p (l g) -> p l g", l=NLEV)
    S2 = A.alloc("S2", NLEV * 64, F32).rearrange("p (l g) -> p l g", l=NLEV)
    pwkeys = ["PW:%d" % l for l in range(NLEV)] + ["PW:%d:i" % l for l in range(NLEV)]
    dma(S1[0:64], PW[:, :, 0, :], reads=pwkeys, writes=["S1"]); dma(S1[64:128], PW[:, :, 0, :], reads=pwkeys, writes=["S1"])
    dma(S2[0:64], PW[:, :, 1, :], reads=pwkeys, writes=["S2"]); dma(S2[64:128], PW[:, :, 1, :], reads=pwkeys, writes=["S2"])
    P.op("pool", lambda e: e.tensor_scalar(out=S2[64:128], in0=S2[64:128], scalar1=-1.0, scalar2=None, op0=ALU.mult), ["S2"], ["S2"])

    if debug:
        pw_dbg = dscr("pw_dbg", [64, NLEV * 128], F32)
        dma(pw_dbg, PW.rearrange("p l r g -> p (l r g)"), reads=pwkeys, writes=["pw_dbg"])
        bb_dbg = dscr("bb_dbg", [64, 2048], F32)
        dma(bb_dbg, Bb.rearrange("p r g h -> p (r g h)"), reads=["Bb", "Bb:i"], writes=["bb_dbg"])
    for n in ["braw", "bt1", "bt2", "PW", "sq_t1", "sq_t2", "sq_t3", "wr0", "wr1", "wi0", "wi1", "mag", "cr", "ci", "lm1", "den", "cfr", "cfi", "zr", "zi", "dtt", "ldt", "lre", "lim"]:
        A.free(n)
    uT = [A.alloc("uT%d" % i, SEQ, BF16) for i in range(1)]
    X0p = [A.alloc("X0p%d" % i, SEQ, BF16) for i in range(2)]
    Tp = [A.alloc("Tp%d" % i, SEQ, BF16) for i in range(2)]
    Xop = [[A.alloc("Xo%d_%d" % (p_, i), NOWN, BF16).rearrange("p (s t) -> p s t", s=16) for i in range(2)] for p_ in range(2)]
    XBp = [[A.alloc("XB%d_%d" % (p_, i), 64, BF16) for i in range(2)] for p_ in range(2)]
    ysg = A.alloc("ysg", 8 * NOWN, BF16).rearrange("p (k t) -> p k t", k=8)
    WB = [A.alloc("WB%d" % i, 128, BF16) for i in range(4)]
    WC = [A.alloc("WC%d" % i, 128, BF16) for i in range(4)]
    R = [A.alloc("R%d" % i, 128, BF16) for i in range(52)]
    Rt = [A.alloc("Rt%d" % i, 128, F32) for i in range(4)]
    bm = [A.alloc("bm%d" % i, 256, F32)[0:64, :] for i in range(2)]
    gel = [A.alloc("gel%d" % i, NOWN, F32) for i in range(2)]
    evc = [0]
    toff = [0, 4096, 6144, 7168, 7680, 7936, 8064]

    def prep_gen(fc, gl, par, st_):
        g = fc * 8 + gl
        wi_ = st_ * 2 + par
        bmt = bm[par]; bmk = "bm%d" % par
        Bfc = Bb[:, :, fc * 8:(fc + 1) * 8, :]
        gmv64 = gmask[0:64, gl * 128:(gl + 1) * 128].rearrange("p (g h) -> p g h", g=8)
        P.op("dve", lambda e: e.tensor_tensor(out=bmt[:, 0:128].rearrange("p (g h) -> p g h", g=8), in0=Bfc[:, 0], in1=gmv64, op=ALU.mult), ["Bb", "Bb:i", "gmask"], [bmk])
        P.op("dve", lambda e: e.tensor_tensor(out=bmt[:, 128:256].rearrange("p (g h) -> p g h", g=8), in0=Bfc[:, 1], in1=gmv64, op=ALU.mult), ["Bb", "Bb:i", "gmask"], [bmk + ":i"])
        pbw = 7
        P.op("pe", lambda e: e.matmul(out=bank(pbw)[:, 0:64], lhsT=bmt[:, 0:128], rhs=ident_f[0:64, 0:64], start=True, stop=True), [bmk, "ident_f"], [bkey(pbw)])
        P.op("pe", lambda e: e.matmul(out=bank(pbw)[:, 64:128], lhsT=bmt[:, 128:256], rhs=ident_f[0:64, 0:64], start=True, stop=True), [bmk + ":i", "ident_f"], [bkey(pbw)])
        P.op("act", lambda e: e.activation(out=WB[wi_], in_=bank(pbw)[:, 0:128], func=AF.Copy), [bkey(pbw)], ["WB%d" % wi_])
        P.op("pool", lambda e: e.tensor_tensor(out=WC[wi_].rearrange("p (g h) -> p g h", g=8), in0=Cst[:, fc * 8:(fc + 1) * 8, :], in1=gmask[:, gl * 128:(gl + 1) * 128].rearrange("p (g h) -> p g h", g=8), op=ALU.mult), ["Cst", "gmask"], ["WC%d" % wi_])
        yield
        for lev in range(NLEV):
            Rm = R[wi_ * 13 + lev]; Rk = "R%d" % (wi_ * 13 + lev)
            rt = Rt[(lev % 2) * 2 + par]; rtk = "Rt%d" % ((lev % 2) * 2 + par)
            P.op("pool", lambda e, lev=lev, rt=rt: e.tensor_scalar(out=rt, in0=swap_f, scalar1=S2[:, lev, g:g + 1], scalar2=None, op0=ALU.mult), ["swap_f", "S2"], [rtk])
            P.op("dve", lambda e, Rm=Rm, lev=lev, rt=rt: e.scalar_tensor_tensor(out=Rm, in0=ident_f, scalar=S1[:, lev, g:g + 1], in1=rt, op0=ALU.mult, op1=ALU.add), ["ident_f", "S1", rtk], [Rk])
            yield

    def group_gen(fc, gl, par, u, uk, st_):
        g = fc * 8 + gl
        wi_ = st_ * 2 + par
        X0 = X0p[par]; Tb = Tp[par]; Xo = Xop[par]; XB = XBp[par]
        xn = "X0p%d" % par; tn = "Tp%d" % par
        bc = [0]

        def nextbank():
            bc[0] += 1
            return 2 * par + (bc[0] % 2)

        def evac(pb, dst_ap, dkey, n=None):
            src_ap = bank(pb) if n is None else bank(pb)[:, 0:n]
            evc[0] += 1
            if evc[0] % 3 != 0:
                P.op("act", lambda e: e.activation(out=dst_ap, in_=src_ap, func=AF.Copy), [bkey(pb)], [dkey])
            else:
                P.op("dve", lambda e: e.tensor_copy(out=dst_ap, in_=src_ap), [bkey(pb)], [dkey])

        Rg = [(R[wi_ * 13 + lev], "R%d" % (wi_ * 13 + lev)) for lev in range(NLEV)]
        for tt in range(16):
            pb = nextbank()
            P.op("pe", lambda e, pb=pb, tt=tt: e.matmul(out=bank(pb), lhsT=WB[wi_], rhs=u[:, tt * 512:(tt + 1) * 512], start=True, stop=True), ["WB%d" % wi_, uk], [bkey(pb)])
            evac(pb, X0[:, tt * 512:(tt + 1) * 512], xn + ":%d" % tt)
            if tt % 4 == 3:
                yield
        for lev in range(7):
            n_l = 4096 >> lev
            if lev == 0:
                srcv = X0.rearrange("p (i two) -> p i two", two=2); sbase = xn + ":"
            else:
                srcv = Tb[:, toff[lev - 1]:toff[lev - 1] + 2 * n_l].rearrange("p (i two) -> p i two", two=2); sbase = tn + ":%d_" % (lev - 1)
            Rm, Rk = Rg[lev]
            for c0 in range(0, n_l, 512):
                n = min(512, n_l - c0)
                pb = nextbank()
                skeys = sorted(set([sbase + "%d" % ((2 * c0) // 512), sbase + "%d" % ((2 * c0 + 2 * n - 1) // 512)]))
                P.op("pe", lambda e, pb=pb, srcv=srcv, c0=c0, n=n: e.matmul(out=bank(pb)[:, 0:n], lhsT=ident_b, rhs=srcv[:, c0:c0 + n, 1], start=True, stop=False), ["ident_b"] + skeys, [bkey(pb)])
                P.op("pe", lambda e, pb=pb, srcv=srcv, c0=c0, n=n, Rm=Rm: e.matmul(out=bank(pb)[:, 0:n], lhsT=Rm, rhs=srcv[:, c0:c0 + n, 0], start=False, stop=True), [Rk] + skeys, [bkey(pb)])
                evac(pb, Tb[:, toff[lev] + c0:toff[lev] + c0 + n], tn + ":%d_%d" % (lev, c0 // 512), n)
                if (c0 // 512) % 2 == 1:
                    yield
            yield
        xbk = tn + ":6_0"
        xb_src = Tb[:, toff[6]:toff[6] + 64]
        for m_ in range(6):
            sh = 1 << m_
            Rm, Rk = Rg[7 + m_]
            pb = nextbank()
            P.op("pe", lambda e, pb=pb, xb_src=xb_src: e.matmul(out=bank(pb)[:, 0:64], lhsT=ident_b, rhs=xb_src, start=True, stop=False), ["ident_b", xbk], [bkey(pb)])
            P.op("pe", lambda e, pb=pb, xb_src=xb_src, sh=sh, Rm=Rm: e.matmul(out=bank(pb)[:, sh:64], lhsT=Rm, rhs=xb_src[:, 0:64 - sh], start=False, stop=True), [Rk, xbk], [bkey(pb)])
            dstb = XB[m_ % 2]
            xbk = "XB%d_%d" % (par, m_ % 2)
            evac(pb, dstb, xbk, 64)
            xb_src = dstb
            yield
        x0own = X0.rearrange("p (s b t) -> p s b t", s=16, b=4)[:, :, 3, :]
        x0keys = [xn + ":%d" % t for t in range(16)]
        xo0keys = ["Xo%d_0:%d" % (par, t) for t in range(4)]
        P.op("pool", lambda e: e.tensor_copy(out=Xo[0], in_=x0own), x0keys, xo0keys)
        pb = nextbank()
        Rm, Rk = Rg[0]
        P.op("pe", lambda e, pb=pb: e.matmul(out=bank(pb)[:, 0:16], lhsT=ident_b, rhs=x0own[:, :, 0], start=True, stop=False), ["ident_b"] + x0keys, [bkey(pb)])
        P.op("pe", lambda e, pb=pb, xb_src=xb_src, Rm=Rm: e.matmul(out=bank(pb)[:, 0:16], lhsT=Rm, rhs=xb_src.rearrange("p (s b) -> p s b", b=4)[:, :, 2], start=False, stop=True), [Rk, xbk], [bkey(pb)])
        P.op("dve", lambda e, pb=pb: e.tensor_copy(out=Xo[0][:, :, 0], in_=bank(pb)[:, 0:16]), [bkey(pb)] + xo0keys, xo0keys)
        yield
        cur = 0
        for lev in range(7):
            sh = 1 << lev
            Rm, Rk = Rg[lev]
            src = Xo[cur]; dst = Xo[1 - cur]
            for q4 in range(4):
                sk = "Xo%d_%d:%d" % (par, cur, q4); dk = "Xo%d_%d:%d" % (par, 1 - cur, q4)
                pb = nextbank()
                pv = bank(pb).rearrange("p (s t) -> p s t", s=4)
                P.op("pe", lambda e, pb=pb, src=src, q4=q4: e.matmul(out=bank(pb), lhsT=ident_b, rhs=src[:, q4 * 4:(q4 + 1) * 4, :].rearrange("p s t -> p (s t)"), start=True, stop=False), ["ident_b", sk], [bkey(pb)])
                P.op("pe", lambda e, pv=pv, src=src, q4=q4, sh=sh, Rm=Rm: e.matmul(out=pv[:, :, sh:128], lhsT=Rm, rhs=src[:, q4 * 4:(q4 + 1) * 4, 0:128 - sh], start=False, stop=True), [Rk, sk], [bkey(pb)])
                evac(pb, dst[:, q4 * 4:(q4 + 1) * 4, :].rearrange("p s t -> p (s t)"), dk)
                if q4 % 2 == 1:
                    yield
            cur = 1 - cur
        fin = Xo[cur].rearrange("p s t -> p (s t)"); fkb = "Xo%d_%d" % (par, cur)
        if debug and g == 63:
            xf_dbg = dscr("xf_dbg", [128, NOWN])
            dma(xf_dbg, fin, reads=[fkb + ":%d" % t for t in range(4)], writes=["xf_dbg"])
        for ot in range(4):
            yb = 4 + (2 * par + ot) % 3
            P.op("pe", lambda e, ot=ot, yb=yb: e.matmul(out=bank(yb), lhsT=WC[wi_], rhs=fin[:, ot * 512:(ot + 1) * 512], start=True, stop=True), ["WC%d" % wi_, fkb + ":%d" % ot], [bkey(yb)])
            pbk = bkey(yb); pbb = bank(yb)
            yacc = gel[0][:, ot * 512:(ot + 1) * 512]
            if gl == 0:
                uo = u.rearrange("p (s b t) -> p s b t", s=16, b=4)[:, ot * 4:(ot + 1) * 4, 3, :]
                P.op("dve", lambda e, yacc=yacc, uo=uo, pbb=pbb: e.scalar_tensor_tensor(out=yacc.rearrange("p (s t) -> p s t", s=4), in0=uo, scalar=ssmd_pp[:, fc:fc + 1], in1=pbb.rearrange("p (s t) -> p s t", s=4), op0=ALU.mult, op1=ALU.add), [uk, "ssmd_pp", pbk], ["gel0:%d" % ot])
            else:
                P.op("dve", lambda e, yacc=yacc, pbb=pbb: e.tensor_tensor(out=yacc, in0=pbb, in1=yacc, op=ALU.add), [pbk, "gel0:%d" % ot], ["gel0:%d" % ot])
            if ot % 2 == 1:
                yield

    def run_lockstep(gens):
        alive = [True] * len(gens)
        while any(alive):
            for i_ in range(len(gens)):
                if alive[i_]:
                    try:
                        next(gens[i_])
                    except StopIteration:
                        alive[i_] = False

    run_lockstep([prep_gen(0, 0, 0, 0), prep_gen(0, 1, 1, 0)])
    for fc in range(8):
        u = uT[0]; uk = "uT0"
        dma(u, uT_d[fc], reads=["uT_d"], writes=[uk], q="pool")
        for gp in range(4):
            pk = fc * 4 + gp
            st_ = pk % 2
            gens = [group_gen(fc, 2 * gp, 0, u, uk, st_), group_gen(fc, 2 * gp + 1, 1, u, uk, st_)]
            if pk + 1 < 32:
                nfc, ngp = (pk + 1) // 4, (pk + 1) % 4
                gens.append(prep_gen(nfc, 2 * ngp, 0, 1 - st_))
                gens.append(prep_gen(nfc, 2 * ngp + 1, 1, 1 - st_))
            run_lockstep(gens)
        yk = ["gel0:%d" % ot for ot in range(4)]
        P.op("act", lambda e: e.activation(out=gel[1], in_=gel[0], func=AF.Square), yk, ["gel1"])
        P.op("dve", lambda e: e.tensor_scalar(out=gel[1], in0=gel[1], scalar1=0.044715 * 1.5957691216, scalar2=1.5957691216, op0=ALU.mult, op1=ALU.add), ["gel1"], ["gel1"])
        P.op("dve", lambda e: e.tensor_tensor(out=gel[1], in0=gel[1], in1=gel[0], op=ALU.mult), ["gel1"] + yk, ["gel1"])
        P.op("act", lambda e: e.activation(out=gel[1], in_=gel[1], func=AF.Sigmoid), ["gel1"], ["gel1"])
        P.op("dve", lambda e, fc=fc: e.tensor_tensor(out=ysg[:, fc, :], in0=gel[1], in1=gel[0], op=ALU.mult), ["gel1"] + yk, ["ysg"])
    for n in ["uT0", "X0p0", "X0p1", "Tp0", "Tp1", "Xo0_0", "Xo0_1", "Xo1_0", "Xo1_1", "XB0_0", "XB0_1", "XB1_0", "XB1_1", "WB0", "WB1", "WB2", "WB3", "WC0", "WC1", "WC2", "WC3"] + ["R%d" % i for i in range(52)] + ["Rt0", "Rt1", "Rt2", "Rt3", "bm0", "bm1",
              "gel0", "gel1", "S1", "S2", "Cst", "Bb"]:
        A.free(n)

    if debug:
        ysg_dbg = dscr("ysg_dbg", [128, 8 * NOWN])
        dma(ysg_dbg, ysg.rearrange("p k t -> p (k t)"), reads=["ysg"], writes=["ysg_dbg"])
    lq = A.alloc("lq", 256, F32).rearrange("p (a d) -> p a d", a=4)
    for a in range(4):
        dma(lq[:, a, :], lqk[a].partition_broadcast(128), writes=["lq"])
    lamt = A.alloc("lamt", 8, F32)
    lqp = A.alloc("lqp", 128, F32).rearrange("p (a d) -> p a d", a=2)
    P.op("dve", lambda e: e.tensor_tensor(out=lqp[:, 0, :], in0=lq[:, 0, :], in1=lq[:, 1, :], op=ALU.mult), ["lq"], ["lqp"])
    P.op("dve", lambda e: e.tensor_tensor(out=lqp[:, 1, :], in0=lq[:, 2, :], in1=lq[:, 3, :], op=ALU.mult), ["lq"], ["lqp"])
    P.op("dve", lambda e: e.tensor_reduce(out=lamt[:, 0:2], in_=lqp, axis=AX.X, op=ALU.add), ["lqp"], ["lamt"])
    P.op("act", lambda e: e.activation(out=lamt[:, 2:4], in_=lamt[:, 0:2], func=AF.Exp), ["lamt"], ["lamt:e"])
    P.op("dve", lambda e: e.tensor_tensor(out=lamt[:, 4:5], in0=lamt[:, 3:4], in1=lamt[:, 2:3], op=ALU.subtract), ["lamt:e"], ["lamt:d"])
    P.op("dve", lambda e: e.tensor_scalar(out=lamt[:, 5:6], in0=lamt[:, 4:5], scalar1=-0.2, scalar2=None, op0=ALU.add), ["lamt:d"], ["neglam"])
    hn = A.alloc("hn", 128, F32)
    dma(hn, head_norm.partition_broadcast(128), writes=["hn"])
    P.op("dve", lambda e: e.tensor_scalar(out=hn, in0=hn, scalar1=0.8, scalar2=None, op0=ALU.mult), ["hn"], ["hn"])
    kvf = A.alloc("kvf", 64, F32)
    dma(kvf, kvalid.rearrange("(b p) -> p b", p=128), writes=["kvf"], slow=True)

    ya = A.alloc("ya", 16 * 1024, BF16).rearrange("p (s f) -> p s f", s=16)
    Kh = [A.alloc("Kh%d" % i, 2 * SEQ, BF16)[0:64, :].rearrange("p (m t) -> p m t", m=2) for i in range(2)]
    Vh = [A.alloc("Vh%d" % i, 64 * 128, BF16).rearrange("p (b d) -> p b d", b=64) for i in range(1)]
    Qh = [A.alloc("Qh%d" % i, 2 * NOWN, BF16)[0:64, :].rearrange("p (m t) -> p m t", m=2) for i in range(1)]
    PT = [A.alloc("PT%d" % i, 1024, BF16).rearrange("p (m q) -> p m q", m=2) for i in range(2)]
    Esel = A.alloc("Esel", 256, BF16).rearrange("p (m c) -> p m c", m=2)
    Esel_f = A.alloc("Esel_f", 256, F32)
    dma(Esel_f, esel_d, writes=["Esel_f"])
    P.op("pool", lambda e: e.tensor_copy(out=Esel.rearrange("p m c -> p (m c)"), in_=Esel_f), ["Esel_f"], ["Esel"])
    denrow = [A.alloc("denrow%d" % i, 512, F32) for i in range(2)]
    rcol = [A.alloc("rcol%d" % i, 128, F32) for i in range(2)]
    OT = [A.alloc("OT%d" % i, 1024, BF16).rearrange("p (m q) -> p m q", m=2) for i in range(2)]
    ones_f = A.alloc("ones_f", 1, F32)
    P.op("pool", lambda e: e.memset(ones_f, 1.0), [], ["ones_f"])
    ep = [A.alloc("ep%d" % i, 8, F32) for i in range(2)]
    eo = [A.alloc("eo%d" % i, 384, F32) for i in range(2)]
    v_dh = v_d.rearrange("(b p) (h d) -> p b h d", p=128, h=8)
    actr = [0]
    tpv = bank_bf(7).rearrange("p (i m d) -> p i m d", i=4, m=2)
    dcol = bank(6)[:, 0:128]
    for h in range(8):
        K = Kh[h % 2]; V = Vh[0]; Q = Qh[0]
        kk_ = "Kh%d" % (h % 2)
        for m in range(2):
            dma(K[:, m, :], kT_d[h, m * 64:(m + 1) * 64, :], reads=["kT_d"], writes=[kk_], q="sp")
            dma(Q[:, m, :], qT_d[h, m * 64:(m + 1) * 64, :], reads=["qT_d"], writes=["Qh0"], q="sp")
        for vq in range(4):
            dma(V[:, vq * 16:(vq + 1) * 16, :], v_dh[:, vq * 16:(vq + 1) * 16, h, :], reads=["v_d"], writes=["Vh0"], q="sp")
        for G in range(4):
            gj = (h * 4 + G) % 2
            dr = denrow[gj]; drk = "denrow%d" % gj
            ot_ = OT[gj]; otk = "OT%d" % gj
            nkb = 16 * G + 16
            base_i = actr[0]
            actr[0] += nkb

            def emit_scores(kb, G=G, K=K, kk_=kk_, base_i=base_i):
                rel_ = kb - 16 * G - 3
                i0_ = 0 if rel_ <= 0 else (rel_ + 3) // 4
                c0 = i0_ * 128
                idiag = rel_ // 4 if (rel_ >= 0 and rel_ % 4 == 0) else -1
                pj = (base_i + kb) % 2
                pt = PT[pj]; ptk = "PT%d" % pj
                for m in range(2):
                    pb = 2 * pj + m
                    P.op("pe", lambda e, pb=pb, kb=kb, m=m, c0=c0: e.matmul(out=bank(pb)[:, c0:512], lhsT=K[:, m, kb * 128:(kb + 1) * 128], rhs=Q[:, m, G * 512 + c0:(G + 1) * 512], start=True, stop=True), [kk_, "Qh0"], [bkey(pb)])
                    P.op("act", lambda e, pb=pb, pt=pt, m=m, c0=c0: e.activation(out=pt[:, m, c0:512], in_=bank(pb)[:, c0:512], func=AF.Exp, scale=0.125), [bkey(pb)], [ptk + ":%d" % m])
                    if idiag >= 0:
                        P.op("dve", lambda e, pt=pt, m=m, idiag=idiag: e.memset(pt[64:128, m, idiag * 128:idiag * 128 + 64], 0.0), [ptk + ":%d" % m], [ptk + ":%d" % m])
                    if kb < 3:
                        P.op("dve", lambda e, pt=pt, m=m, kb=kb: e.tensor_scalar(out=pt[:, m, :], in0=pt[:, m, :], scalar1=kvf[:, kb:kb + 1], scalar2=None, op0=ALU.mult), [ptk + ":%d" % m, "kvf"], [ptk + ":%d" % m])

            def emit_pv(kb, G=G, V=V, nkb=nkb, base_i=base_i):
                rel_ = kb - 16 * G - 3
                i0_ = 0 if rel_ <= 0 else (rel_ + 3) // 4
                c0 = i0_ * 128
                pj = (base_i + kb) % 2
                pt = PT[pj]; ptk = "PT%d" % pj
                for m in range(2):
                    P.op("pe", lambda e, kb=kb, m=m, pt=pt, c0=c0: e.matmul(out=bank(4 + m)[:, c0:512], lhsT=V[:, kb, :], rhs=pt[:, m, c0:512], start=(kb == 0), stop=(kb == nkb - 1)), [ptk + ":%d" % m, "Vh0"], [bkey(4 + m)])
                    P.op("pe", lambda e, kb=kb, m=m, pt=pt, c0=c0: e.matmul(out=bank(6)[:, c0:512], lhsT=Esel[:, m, :], rhs=pt[:, m, c0:512], start=(kb == 0 and m == 0), stop=(kb == nkb - 1 and m == 1)), [ptk + ":%d" % m, "Esel"], [bkey(6)])

            emit_scores(0)
            for kb in range(nkb):
                if kb + 1 < nkb:
                    emit_scores(kb + 1)
                emit_pv(kb)
            P.op("act", lambda e, ot_=ot_: e.activation(out=ot_[:, 0, :], in_=bank(4), func=AF.Copy), [bkey(4)], [otk + ":0"])
            P.op("dve", lambda e, ot_=ot_: e.tensor_copy(out=ot_[:, 1, :], in_=bank(5)), [bkey(5)], [otk + ":1"])
            P.op("dve", lambda e, dr=dr: e.tensor_copy(out=dr, in_=bank(6)), [bkey(6)], [drk])
            for isl in range(4):
                P.op("pe", lambda e, isl=isl, dr=dr: e.matmul(out=dcol[:, isl * 32:(isl + 1) * 32], lhsT=dr[:, isl * 128:(isl + 1) * 128], rhs=ident_f[:, 0:32], start=True, stop=True), [drk, "ident_f"], [bkey(6)])
            rc = rcol[gj]; rck = "rcol%d" % gj
            P.op("dve", lambda e, rc=rc: e.reciprocal(out=rc, in_=dcol), [bkey(6)], [rck])
            for isl in range(4):
                for m in range(2):
                    P.op("pe", lambda e, isl=isl, m=m, ot_=ot_: e.transpose(out=tpv[:, isl, m, :], in_=ot_[:, m, isl * 128:(isl + 1) * 128], identity=ident_b), [otk + ":%d" % m, "ident_b"], [bkey(7)])
            for isl in range(4):
                s_ = G * 4 + isl
                j = (h * 16 + s_) % 2
                e_ = ep[j]; o_ = eo[j]; ek = "ep%d" % j; ok_ = "eo%d" % j
                P.op("dve", lambda e, e_=e_, isl=isl, rc=rc: e.tensor_copy(out=e_[:, 0:2], in_=rc[:, isl * 32:isl * 32 + 2]), [rck], [ek])
                P.op("dve", lambda e, e_=e_: e.tensor_tensor(out=e_[:, 2:3], in0=e_[:, 1:2], in1=lamt[:, 5:6], op=ALU.mult), [ek, "neglam"], [ek + ":2"])
                P.op("dve", lambda e, e_=e_, o_=o_, isl=isl: e.tensor_scalar(out=o_[:, 0:128], in0=tpv[:, isl, 1, :], scalar1=e_[:, 2:3], scalar2=None, op0=ALU.mult), [bkey(7), ek + ":2"], [ok_])
                P.op("dve", lambda e, e_=e_, o_=o_, isl=isl: e.scalar_tensor_tensor(out=o_[:, 128:256], in0=tpv[:, isl, 0, :], scalar=e_[:, 0:1], in1=o_[:, 0:128], op0=ALU.mult, op1=ALU.add), [bkey(7), ek, ok_], [ok_ + ":1"])
                P.op("act", lambda e, e_=e_, o_=o_: e.activation(out=o_[:, 256:384], in_=o_[:, 128:256], func=AF.Square, accum_out=e_[:, 3:4]), [ok_ + ":1"], [ok_ + ":2", ek + ":3"])
                P.op("act", lambda e, e_=e_: e.activation(out=e_[:, 4:5], in_=e_[:, 3:4], func=AF.Ln, scale=1.0 / 128, bias=epst), [ek + ":3", "epst"], [ek + ":4"])
                P.op("act", lambda e, e_=e_: e.activation(out=e_[:, 5:6], in_=e_[:, 4:5], func=AF.Exp, scale=-0.5), [ek + ":4"], [ek + ":5"])
                P.op("dve", lambda e, e_=e_, o_=o_, s_=s_, h=h: e.scalar_tensor_tensor(out=ya[:, s_, h * 128:(h + 1) * 128], in0=o_[:, 128:256], scalar=e_[:, 5:6], in1=hn, op0=ALU.mult, op1=ALU.mult), [ok_ + ":1", ek + ":5", "hn"], ["ya"])
    for n in ["Kh0", "Kh1", "Vh0", "Qh0", "PT0", "PT1", "denrow0", "denrow1", "rcol0", "rcol1", "Esel_f", "OT0", "OT1", "ep0", "ep1", "eo0", "eo1", "lq", "lqp", "kvf"]:
        A.free(n)

    if debug:
        ya_dbg = dscr("ya_dbg", [128, 16 * 1024])
        dma(ya_dbg, ya.rearrange("p s f -> p (s f)"), reads=["ya"], writes=["ya_dbg"])
    def load_bf16(name):
        dst_d, src_, K_, N_, gn_ = wsc[name]
        KC = K_ // 128
        wt = A.alloc(name, KC * N_, BF16).rearrange("p (k n) -> p k n", k=KC)
        for kc in range(KC):
            dma(wt[:, kc, :], dst_d[kc * 128:(kc + 1) * 128, :], reads=[name + "_d"], writes=[name], q="sp" if kc % 2 == 0 else "act")
        return wt

    gpost = A.alloc("gpost", 1024, F32)
    pst = A.alloc("pst", 8, F32)
    psq = A.alloc("psq", 1024, BF16)
    ptmp = A.alloc("ptmp", 1024, F32)
    ost = [A.alloc("ost%d" % i, 1024, F32) for i in range(2)]
    xres = [A.alloc("xres%d" % i, 1024, F32) for i in range(2)]

    def post_norm_residual(pb0, gain_bc, gkey, res_in, res_in_keys, res_out, res_out_key):
        for half in range(2):
            P.op("act", lambda e, half=half: e.activation(out=psq[:, half * 512:(half + 1) * 512], in_=bank(pb0 + half), func=AF.Square, accum_out=pst[:, half:half + 1]), [bkey(pb0 + half)], ["psq", "pst:%d" % half])
        P.op("dve", lambda e: e.tensor_tensor(out=pst[:, 2:3], in0=pst[:, 0:1], in1=pst[:, 1:2], op=ALU.add), ["pst:0", "pst:1"], ["pst:2"])
        P.op("act", lambda e: e.activation(out=pst[:, 3:4], in_=pst[:, 2:3], func=AF.Ln, scale=1.0 / D, bias=epst), ["pst:2", "epst"], ["pst:3"])
        P.op("act", lambda e: e.activation(out=pst[:, 4:5], in_=pst[:, 3:4], func=AF.Exp, scale=-0.5), ["pst:3"], ["pst:4"])
        for half in range(2):
            P.op("dve", lambda e, half=half: e.scalar_tensor_tensor(out=ptmp[:, half * 512:(half + 1) * 512], in0=bank(pb0 + half), scalar=pst[:, 4:5], in1=gain_bc[:, half * 512:(half + 1) * 512], op0=ALU.mult, op1=ALU.mult), [bkey(pb0 + half), "pst:4", gkey], ["ptmp:%d" % half])
        P.op("pool", lambda e: e.tensor_tensor(out=res_out, in0=ptmp, in1=res_in, op=ALU.add), ["ptmp:0", "ptmp:1"] + res_in_keys, [res_out_key])

    wglu = load_bf16("wglu")
    wssm = load_bf16("wssm")
    ys2 = A.alloc("ys2", 8 * 512, BF16).rearrange("p (k t) -> p k t", k=8)
    gab = A.alloc("gab", 8 * 512, BF16).rearrange("p (k t) -> p k t", k=8)
    sg = [A.alloc("sg%d" % i, 512, BF16) for i in range(2)]
    for tt in range(4):
        dma(gab, g_d[0:8, :, tt * 512:(tt + 1) * 512].rearrange("k p t -> p k t"), reads=["g_d"], writes=["gab"], q="pool")
        for mc in range(8):
            pb = 2 + mc % 2
            for kc in range(8):
                P.op("pe", lambda e, kc=kc, mc=mc, pb=pb, tt=tt: e.matmul(out=bank(pb), lhsT=wglu[:, kc, mc * 128:(mc + 1) * 128], rhs=ysg[:, kc, tt * 512:(tt + 1) * 512], start=(kc == 0), stop=(kc == 7)), ["wglu", "ysg"], [bkey(pb)])
            j = mc % 2
            P.op("act", lambda e, pb=pb, j=j, mc=mc: e.activation(out=sg[j], in_=bank(pb), func=AF.Sigmoid, bias=bglu_pp[:, mc:mc + 1]), [bkey(pb), "bglu_pp"], ["sg%d" % j])
            P.op("pool", lambda e, j=j, mc=mc, tt=tt: e.tensor_tensor(out=ys2[:, mc, :], in0=sg[j], in1=ysg[:, mc, tt * 512:(tt + 1) * 512], op=ALU.mult), ["sg%d" % j, "ysg"], ["ys2"])
        for mc in range(8):
            pa = 4 + (mc % 2)
            for kc in range(8):
                P.op("pe", lambda e, kc=kc, mc=mc, pa=pa: e.matmul(out=bank(pa), lhsT=wssm[:, kc, mc * 128:(mc + 1) * 128], rhs=ys2[:, kc, :], start=(kc == 0), stop=(kc == 7)), ["wssm", "ys2"], [bkey(pa)])
            P.op("dve", lambda e, pa=pa, mc=mc, tt=tt: e.tensor_tensor(out=ysg[:, mc, tt * 512:(tt + 1) * 512], in0=bank(pa), in1=gab[:, mc, :], op=ALU.mult), [bkey(pa), "gab"], ["ysg"])
    for n in ["wglu", "wssm", "ys2", "sg0", "sg1"]:
        A.free(n)
    wda = load_bf16("wda")
    wmix = load_bf16("wmix")
    dma(gpost, gains["norm_mix_post"].partition_broadcast(128), writes=["gpost"])
    yaT = A.alloc("yaT", 8 * 512, BF16).rearrange("p (k t) -> p k t", k=8)
    mrg = A.alloc("mrg", 8 * 512, BF16).rearrange("p (k t) -> p k t", k=8)
    tb = [A.alloc("tb%d" % i, 512, F32) for i in range(2)]
    for tt in range(4):
        for bl in range(4):
            s = tt * 4 + bl
            tpb = bl % 2
            tp = bank_bf(tpb).rearrange("p (k t) -> p k t", k=8)
            for kc in range(8):
                P.op("pe", lambda e, kc=kc, s=s, tp=tp: e.transpose(out=tp[:, kc, :], in_=ya[:, s, kc * 128:(kc + 1) * 128], identity=ident_b), ["ya", "ident_b"], [bkey(tpb)])
            P.op("act", lambda e, tp=tp, bl=bl: e.activation(out=yaT[:, :, bl * 128:(bl + 1) * 128], in_=tp, func=AF.Copy), [bkey(tpb)], ["yaT"])
        dma(gab, g_d[8:16, :, tt * 512:(tt + 1) * 512].rearrange("k p t -> p k t"), reads=["g_d"], writes=["gab"], q="pool")
        for mc in range(8):
            pbb_ = 4 + (mc % 2)
            for kc in range(8):
                P.op("pe", lambda e, kc=kc, mc=mc, pbb_=pbb_: e.matmul(out=bank(pbb_), lhsT=wda[:, kc, mc * 128:(mc + 1) * 128], rhs=yaT[:, kc, :], start=(kc == 0), stop=(kc == 7)), ["wda", "yaT"], [bkey(pbb_)])
            j = mc % 2
            P.op("dve", lambda e, pbb_=pbb_, j=j, mc=mc: e.tensor_tensor(out=tb[j], in0=bank(pbb_), in1=gab[:, mc, :], op=ALU.mult), [bkey(pbb_), "gab"], ["tb%d" % j])
            P.op("pool", lambda e, j=j, mc=mc, tt=tt: e.tensor_tensor(out=mrg[:, mc, :], in0=tb[j], in1=ysg[:, mc, tt * 512:(tt + 1) * 512], op=ALU.add), ["tb%d" % j, "ysg"], ["mrg"])
        for bl in range(4):
            s = tt * 4 + bl
            pb0 = 2 * (bl % 2)
            for half in range(2):
                for kc in range(8):
                    P.op("pe", lambda e, kc=kc, bl=bl, half=half, pb0=pb0: e.matmul(out=bank(pb0 + half), lhsT=mrg[:, kc, bl * 128:(bl + 1) * 128], rhs=wmix[:, kc, half * 512:(half + 1) * 512], start=(kc == 0), stop=(kc == 7)), ["mrg", "wmix"], [bkey(pb0 + half)])
            xr = xres[s % 2]; xrk = "xres%d" % (s % 2)
            dma(xr, xown[s * 128:(s + 1) * 128, :], writes=[xrk], q="sp")
            o_ = ost[s % 2]; ok_ = "ost%d" % (s % 2)
            post_norm_residual(pb0, gpost, "gpost", xr, [xrk], o_, ok_)
            dma(x1_d[s * 128:(s + 1) * 128, :], o_, reads=[ok_], writes=["x1_d"], q="sp")
    for n in ["wda", "wmix", "yaT", "mrg", "gab", "tb0", "tb1", "ya", "ysg"]:
        A.free(n)

    wxkv = load_bf16("wxkv")
    memT = A.alloc("memT", 8 * 256, BF16).rearrange("p (k t) -> p k t", k=8)
    for mb in range(2):
        norm_block_T(mem[mb * 128:(mb + 1) * 128, :], True, memT[:, :, mb * 128:(mb + 1) * 128], "memT")
    mkT = A.alloc("mkT", 8 * 256, BF16).rearrange("p (k t) -> p k t", k=8)
    mv = A.alloc("mv", 2 * 1024, BF16).rearrange("p (m f) -> p m f", m=2)
    for mc in range(8):
        pb = mc % 2
        for kc in range(8):
            P.op("pe", lambda e, kc=kc, mc=mc, pb=pb: e.matmul(out=bank(pb)[:, 0:256], lhsT=wxkv[:, kc, mc * 128:(mc + 1) * 128], rhs=memT[:, kc, :], start=(kc == 0), stop=(kc == 7)), ["wxkv", "memT"], [bkey(pb)])
        P.op("act", lambda e, pb=pb, mc=mc: e.activation(out=mkT[:, mc, :], in_=bank(pb)[:, 0:256], func=AF.Copy), [bkey(pb)], ["mkT"])
    for mt in range(2):
        for half in range(2):
            pb = 2 + half
            for kc in range(8):
                P.op("pe", lambda e, kc=kc, mt=mt, half=half, pb=pb: e.matmul(out=bank(pb), lhsT=memT[:, kc, mt * 128:(mt + 1) * 128], rhs=wxkv[:, kc, 1024 + half * 512:1024 + (half + 1) * 512], start=(kc == 0), stop=(kc == 7)), ["wxkv", "memT"], [bkey(pb)])
            P.op("act", lambda e, pb=pb, mt=mt, half=half: e.activation(out=mv[:, mt, half * 512:(half + 1) * 512], in_=bank(pb), func=AF.Copy), [bkey(pb)], ["mv"])
    A.free("wxkv"); A.free("memT")
    wxq = load_bf16("wxq")
    wxo = load_bf16("wxo")
    dma(gpost, gains["norm_x_post"].partition_broadcast(128), writes=["gpost"])
    ones_b = A.alloc("ones_b", 128, BF16)
    P.op("pool", lambda e: e.memset(ones_b, 1.0), [], ["ones_b"])
    h2T = A.alloc("h2T", 8 * 512, BF16).rearrange("p (k t) -> p k t", k=8)
    xqT = A.alloc("xqT", 8 * 512, BF16).rearrange("p (k t) -> p k t", k=8)
    xoT = A.alloc("xoT", 8 * 512, BF16).rearrange("p (k t) -> p k t", k=8)
    xp = [A.alloc("xp%d" % i, 2 * 512, BF16).rearrange("p (m t) -> p m t", m=2) for i in range(2)]
    rden = [A.alloc("rden%d" % i, 512, F32) for i in range(2)]
    for tt in range(4):
        for bl in range(4):
            s = tt * 4 + bl
            norm_block_T(x1_d[s * 128:(s + 1) * 128, :], True, h2T[:, :, bl * 128:(bl + 1) * 128], "h2T", tp_bank=7)
        for mc in range(8):
            pb = mc % 2
            for kc in range(8):
                P.op("pe", lambda e, kc=kc, mc=mc, pb=pb: e.matmul(out=bank(pb), lhsT=wxq[:, kc, mc * 128:(mc + 1) * 128], rhs=h2T[:, kc, :], start=(kc == 0), stop=(kc == 7)), ["wxq", "h2T"], [bkey(pb)])
            P.op("act", lambda e, pb=pb, mc=mc: e.activation(out=xqT[:, mc, :], in_=bank(pb), func=AF.Copy), [bkey(pb)], ["xqT"])
        for hh in range(4):
            j = hh % 2
            for mt in range(2):
                pb = 2 + mt
                for dc in range(2):
                    P.op("pe", lambda e, hh=hh, mt=mt, dc=dc, pb=pb: e.matmul(out=bank(pb), lhsT=mkT[:, hh * 2 + dc, mt * 128:(mt + 1) * 128], rhs=xqT[:, hh * 2 + dc, :], start=(dc == 0), stop=(dc == 1)), ["mkT", "xqT"], [bkey(pb)])
                P.op("act", lambda e, pb=pb, j=j, mt=mt: e.activation(out=xp[j][:, mt, :], in_=bank(pb), func=AF.Exp, scale=1.0 / 16), [bkey(pb)], ["xp%d" % j])
            for mt in range(2):
                P.op("pe", lambda e, j=j, mt=mt: e.matmul(out=bank(4), lhsT=ones_b, rhs=xp[j][:, mt, :], start=(mt == 0), stop=(mt == 1)), ["ones_b", "xp%d" % j], [bkey(4)])
            P.op("dve", lambda e, j=j: e.reciprocal(out=rden[j], in_=bank(4)), [bkey(4)], ["rden%d" % j])
            for dc in range(2):
                pb = 5 + dc
                for mt in range(2):
                    P.op("pe", lambda e, hh=hh, j=j, mt=mt, dc=dc, pb=pb: e.matmul(out=bank(pb), lhsT=mv[:, mt, (hh * 2 + dc) * 128:(hh * 2 + dc + 1) * 128], rhs=xp[j][:, mt, :], start=(mt == 0), stop=(mt == 1)), ["mv", "xp%d" % j], [bkey(pb)])
                P.op("dve", lambda e, hh=hh, j=j, dc=dc, pb=pb: e.tensor_tensor(out=xoT[:, hh * 2 + dc, :], in0=bank(pb), in1=rden[j], op=ALU.mult), [bkey(pb), "rden%d" % j], ["xoT"])
        for bl in range(4):
            s = tt * 4 + bl
            pb0 = 2 * (bl % 2)
            for half in range(2):
                for kc in range(8):
                    P.op("pe", lambda e, kc=kc, bl=bl, half=half, pb0=pb0: e.matmul(out=bank(pb0 + half), lhsT=xoT[:, kc, bl * 128:(bl + 1) * 128], rhs=wxo[:, kc, half * 512:(half + 1) * 512], start=(kc == 0), stop=(kc == 7)), ["xoT", "wxo"], [bkey(pb0 + half)])
            xr = xres[s % 2]; xrk = "xres%d" % (s % 2)
            dma(xr, x1_d[s * 128:(s + 1) * 128, :], reads=["x1_d"], writes=[xrk], q="sp")
            o_ = ost[s % 2]; ok_ = "ost%d" % (s % 2)
            post_norm_residual(pb0, gpost, "gpost", xr, [xrk], o_, ok_)
            dma(x2_d[s * 128:(s + 1) * 128, :], o_, reads=[ok_], writes=["x2_d"], q="sp")
    for n in ["wxq", "wxo", "mkT", "mv", "h2T", "xqT", "xoT", "xp0", "xp1", "rden0", "rden1"]:
        A.free(n)

    dma(gpost, gains["norm_ff_post"].partition_broadcast(128), writes=["gpost"])
    for n in ["wstage0", "wstage1", "xs0", "xs1", "nsq0", "nsq1", "nh0", "nh1"]:
        if n in A.live:
            A.free(n)
    f1 = A.alloc("f1", 32 * 512, BF16).rearrange("p (k t) -> p k t", k=32)
    wq1 = [A.alloc("wq1_%d" % i, 8 * 1024, BF16).rearrange("p (k n) -> p k n", k=8) for i in range(2)]
    h3T = A.alloc("h3T", 8 * 512, BF16).rearrange("p (k t) -> p k t", k=8)
    fr = [A.alloc("fr%d" % i, 512, BF16) for i in range(2)]
    wctr2 = [0]
    for tt in range(4):
        for bl in range(4):
            s = tt * 4 + bl
            norm_block_T(x2_d[s * 128:(s + 1) * 128, :], True, h3T[:, :, bl * 128:(bl + 1) * 128], "h3T", tp_bank=7)
        for q4 in range(4):
            i = wctr2[0]; wctr2[0] += 1
            w1 = wq1[i % 2]; w1k = "wq1_%d" % (i % 2)
            dma(w1, wf1_d[:, q4 * 1024:(q4 + 1) * 1024].rearrange("(k p) n -> p k n", p=128), reads=["wf_d"], writes=[w1k], q="sp")
            for fl in range(8):
                fc = q4 * 8 + fl
                pb = 4 + fc % 2
                for kc in range(8):
                    P.op("pe", lambda e, kc=kc, fl=fl, pb=pb, w1=w1: e.matmul(out=bank(pb), lhsT=w1[:, kc, fl * 128:(fl + 1) * 128], rhs=h3T[:, kc, :], start=(kc == 0), stop=(kc == 7)), [w1k, "h3T"], [bkey(pb)])
                j = fc % 2
                P.op("act", lambda e, pb=pb, j=j: e.activation(out=fr[j], in_=bank(pb), func=AF.Relu), [bkey(pb)], ["fr%d" % j])
                P.op("pool", lambda e, j=j, fc=fc: e.tensor_tensor(out=f1[:, fc, :], in0=fr[j], in1=fr[j], op=ALU.mult), ["fr%d" % j], ["f1"])
        for q4 in range(4):
            i = wctr2[0]; wctr2[0] += 1
            w2 = wq1[i % 2]; w2k = "wq1_%d" % (i % 2)
            dma(w2, wf2_d[q4 * 1024:(q4 + 1) * 1024, :].rearrange("(k p) n -> p k n", p=128), reads=["wf_d"], writes=[w2k], q="sp")
            for bl in range(4):
                for half in range(2):
                    pbk_ = bl * 2 + half
                    for kcl in range(8):
                        P.op("pe", lambda e, kcl=kcl, bl=bl, half=half, pbk_=pbk_, q4=q4, w2=w2: e.matmul(out=bank(pbk_), lhsT=f1[:, q4 * 8 + kcl, bl * 128:(bl + 1) * 128], rhs=w2[:, kcl, half * 512:(half + 1) * 512], start=(q4 == 0 and kcl == 0), stop=(q4 == 3 and kcl == 7)), ["f1", w2k], [bkey(pbk_)])
        for bl in range(4):
            s = tt * 4 + bl
            pb0 = 2 * bl
            xr = xres[s % 2]; xrk = "xres%d" % (s % 2)
            dma(xr, x2_d[s * 128:(s + 1) * 128, :], reads=["x2_d"], writes=[xrk], q="sp")
            o_ = ost[s % 2]; ok_ = "ost%d" % (s % 2)
            post_norm_residual(pb0, gpost, "gpost", xr, [xrk], o_, ok_)
            dma(out_d[s * 128:(s + 1) * 128, :], o_, reads=[ok_], writes=["out_d"], q="sp")

    P.emit()
    es.close()
    return nc


def _rope_tables(pos):
    inv = (10000.0 ** (-np.arange(0, 64, 2, dtype=np.float32) / 64)).astype(np.float32)
    ang = pos.astype(np.float32)[:, None] * inv[None, :]
    c = np.cos(ang).astype(np.float32).T
    s = np.sin(ang).astype(np.float32).T
    return np.ascontiguousarray(np.tile(c, (4, 1))), np.ascontiguousarray(np.tile(s, (4, 1)))


_NC_CACHE = {}


def make_in_maps(inputs):
    x = np.asarray(inputs["x"], dtype=np.float32)
    memv = np.asarray(inputs["mem"], dtype=np.float32)
    ident = np.eye(128, dtype=np.float32)
    swapm = np.zeros((128, 128), np.float32)
    for p in range(64):
        swapm[p, 64 + p] = 1.0
        swapm[64 + p, p] = 1.0
    gmask = np.zeros((128, 8, 128), np.float32)
    for gl in range(8):
        gmask[:, gl, gl * 16:(gl + 1) * 16] = 1.0
    esel = np.zeros((128, 256), np.float32)
    esel[:, 0] = 1.0
    esel[:, 129] = 1.0
    shared = {}
    for k, v in inputs.items():
        if k in ("x", "mem"):
            continue
        a = np.asarray(v, dtype=np.float32)
        shared[k] = np.ascontiguousarray(a[0])
    in_maps = []
    for c in range(8):
        b, j = c // 4, c % 4
        pad = (3 - j) * 128
        xs = np.zeros((SEQ, D), np.float32)
        xs[pad:] = x[b, :SEQ - pad]
        own_blocks = [4 * s + j for s in range(16)]
        xo = np.concatenate([x[b, r * 128:(r + 1) * 128] for r in own_blocks], axis=0)
        pos_seq = np.arange(SEQ) - pad
        cseq, sseq = _rope_tables(pos_seq)
        pos_own = np.concatenate([np.arange(r * 128, (r + 1) * 128) for r in own_blocks])
        cown, sown = _rope_tables(pos_own)
        kval = (pos_seq >= 0).astype(np.float32)
        m = dict(shared)
        m.update(xseq=xs, xown=np.ascontiguousarray(xo), mem=np.ascontiguousarray(memv[b]), cos_seq=cseq, sin_seq=sseq,
                 cos_own=cown, sin_own=sown, kvalid=kval, esel=esel, ident=ident, swapm=swapm, gmask=gmask)
        in_maps.append(m)
    return in_maps


def kernel(**inputs):
    if "nc" not in _NC_CACHE:
        _NC_CACHE["nc"] = build_program()
    nc = _NC_CACHE["nc"]
    in_maps = make_in_maps(inputs)
    res = run_bass_kernel_spmd(nc, in_maps, core_ids=list(range(8)))
    out = np.zeros((2, SEQ, D), np.float32)
    for c in range(8):
        b, j = c // 4, c % 4
        o = res.results[c]["out"]
        for s in range(16):
            r = 4 * s + j
            out[b, r * 128:(r + 1) * 128] = o[s * 128:(s + 1) * 128]
    return out
```

```python
import contextlib
import math
import numpy as np
import concourse.bass as bass
import concourse.mybir as mybir
from concourse.bass_utils import run_bass_kernel_spmd

F32 = mybir.dt.float32
BF16 = mybir.dt.bfloat16
ALU = mybir.AluOpType
AF = mybir.ActivationFunctionType
AX = mybir.AxisListType

D = 1024
SEQ = 8192
NB = 64
NOWN = 2048
EPS = 1e-6
NLEV = 13
ENGS = ["pe", "act", "dve", "pool", "sp"]


class Op:
    __slots__ = ("eng", "idx", "fn", "deps", "is_dma", "needs_inc", "semval", "dsem", "dval")

    def __init__(self, eng, idx, fn, is_dma):
        self.eng, self.idx, self.fn, self.is_dma = eng, idx, fn, is_dma
        self.deps = []
        self.needs_inc = False
        self.semval = None
        self.dsem = None
        self.dval = None


class Prog:
    def __init__(self, nc, n_dma_sems=16):
        self.nc = nc
        self.ops = {e: [] for e in ENGS}
        self.state = {}
        self.rings = {"sp": (0, 12), "pool": (12, 10), "act": (22, 4), "dve": (26, 2), "pe": (26, 2)}
        n_dma_sems = 28
        self.n_dma_sems = n_dma_sems
        self.dma_rr = {q: 0 for q in self.rings}
        self.dma_counts = [0] * n_dma_sems
        self.waited = {}
        self.waited_dma = {}

    def _st(self, key):
        s = self.state.get(key)
        if s is None:
            s = {"w": {}, "r": {}}
            if isinstance(key, str) and ":" in key:
                base = self.state.get(key.split(":")[0])
                if base is not None:
                    s["w"] = dict(base["w"])
            self.state[key] = s
        return s

    def _add_dep(self, op, dep):
        if dep is None or dep is op:
            return
        if dep.is_dma:
            k = (op.eng, dep.dsem)
            if self.waited_dma.get(k, -1) >= dep.dval:
                return
            self.waited_dma[k] = dep.dval
            op.deps.append(dep)
            return
        k = (op.eng, dep.eng)
        if self.waited.get(k, -1) >= dep.idx:
            return
        self.waited[k] = dep.idx
        dep.needs_inc = True
        op.deps.append(dep)

    def op(self, eng, fn, reads=(), writes=(), dma=False):
        lst = self.ops[eng]
        o = Op(eng, len(lst), fn, dma)
        if dma:
            base, cnt_ = self.rings[eng]
            i = base + self.dma_rr[eng]
            self.dma_rr[eng] = (self.dma_rr[eng] + 1) % cnt_
            self.dma_counts[i] += 16
            o.dsem, o.dval = i, self.dma_counts[i]
        for key in reads:
            for e, w in self._st(key)["w"].items():
                if (not w.is_dma) and w.eng == eng and eng == "pe":
                    continue
                self._add_dep(o, w)
        for key in writes:
            s = self._st(key)
            for e, r in s["r"].items():
                if (not r.is_dma) and r.eng == eng and not dma:
                    continue
                self._add_dep(o, r)
            for e, w in s["w"].items():
                if (not w.is_dma) and w.eng == eng and not dma:
                    continue
                self._add_dep(o, w)
        me = ("dma%d" % o.dsem) if dma else eng
        for key in reads:
            self._st(key)["r"][me] = o
        for key in writes:
            s = self._st(key)
            s["w"][me] = o
            s["r"] = {}
        lst.append(o)
        return o

    def alias(self, newkey, oldkeys):
        ns = self._st(newkey)
        for ok in oldkeys:
            os_ = self.state.get(ok)
            if os_ is None:
                continue
            for kind in ("w", "r"):
                for e, o in os_[kind].items():
                    cur = ns["w"].get(e)
                    if cur is None or (o.is_dma and o.dval > cur.dval) or ((not o.is_dma) and o.idx > cur.idx):
                        ns["w"][e] = o

    def emit(self):
        nc = self.nc
        with contextlib.ExitStack() as es:
            sems = {e: es.enter_context(nc.semaphore("s_" + e)) for e in ["pe", "act", "dve", "pool"]}
            dsems = [es.enter_context(nc.semaphore("d%d" % i)) for i in range(self.n_dma_sems)]
            for e in ENGS:
                c = 0
                for o in self.ops[e]:
                    if (not o.is_dma) and o.needs_inc:
                        c += 1
                        o.semval = c
            block = es.enter_context(nc.Block())

            def run(name, eng):
                for o in self.ops[name]:
                    for d in o.deps:
                        if d.is_dma:
                            eng.wait_ge(dsems[d.dsem], d.dval)
                        else:
                            eng.wait_ge(sems[d.eng], d.semval)
                    ins = o.fn(eng)
                    if o.is_dma:
                        ins.then_inc(dsems[o.dsem], 16)
                    elif o.needs_inc:
                        ins.then_inc(sems[o.eng], 1)

            @block.tensor
            def _(eng):
                run("pe", eng)

            @block.scalar
            def _(eng):
                run("act", eng)

            @block.vector
            def _(eng):
                run("dve", eng)

            @block.gpsimd
            def _(eng):
                run("pool", eng)

            @block.sync
            def _(eng):
                run("sp", eng)
                for i in range(self.n_dma_sems):
                    if self.dma_counts[i] > 0:
                        eng.wait_ge(dsems[i], self.dma_counts[i])


class Arena:
    def __init__(self, P, base_ap, nbytes):
        self.P = P
        self.base = base_ap
        self.nbytes = nbytes
        self.live = {}
        self.freed = []

    def alloc(self, name, nelem, dt):
        esz = 4 if dt == F32 else 2
        size = (nelem * esz + 63) // 64 * 64
        segs = sorted(self.live.values())
        off = 0
        for (o, s) in segs:
            if off + size <= o:
                break
            off = max(off, o + s)
        assert off + size <= self.nbytes, "SBUF arena overflow for %s (%d): %s" % (name, size, sorted((o, sz, n) for n, (o, sz) in self.live.items()))
        self.live[name] = (off, size)
        olds = [n for (o, s, n) in self.freed if o < off + size and off < o + s]
        oldkeys = [k for k in self.P.state if any(k == n or (isinstance(k, str) and k.startswith(n + ":")) for n in olds)]
        self.P.alias(name, oldkeys)
        self._aliaskeys = oldkeys
        ap = self.base[:, off // 4:(off + size) // 4]
        if dt != F32:
            ap = ap.bitcast(dt)
        return ap[:, 0:nelem]

    def free(self, name):
        o, s = self.live.pop(name)
        self.freed.append((o, s, name))


def build_program(debug=False):
    nc = bass.Bass("TRN2", target_bir_lowering=False)

    def din(name, shape, dt=F32):
        return nc.dram_tensor(name, list(shape), dt, kind="ExternalInput").ap()

    def dscr(name, shape, dt=BF16):
        return nc.dram_tensor(name, list(shape), dt, kind="ExternalOutput" if debug else "Internal").ap()

    xseq = din("xseq", [SEQ, D])
    xown = din("xown", [NOWN, D])
    mem = din("mem", [256, D])
    w_in = din("w_in", [D, 6144])
    w_glu = din("w_glu", [D, D]); w_ssm = din("w_ssm_proj", [D, D]); w_da = din("w_da_proj", [D, D])
    w_mix = din("w_mix_out", [D, D]); w_xq = din("w_xq", [D, D]); w_xkv = din("w_xkv", [D, 2 * D])
    w_xo = din("w_xo", [D, D]); w_ff1 = din("w_ff1", [D, 4 * D]); w_ff2 = din("w_ff2", [4 * D, D])
    gains = {n: din(n, [D]) for n in ["norm_mix_pre", "norm_mix_post", "norm_x_pre", "norm_mem", "norm_x_post",
                                      "norm_ff_pre", "norm_ff_post"]}
    b_gate = din("b_gate", [2 * D]); b_glu = din("b_glu", [D]); ssm_d = din("ssm_d", [D])
    lam_re = din("ssm_lambda_re", [64, 64]); lam_im = din("ssm_lambda_im", [64, 64]); log_dt = din("ssm_log_dt", [64])
    b_re = din("ssm_b_re", [64, 64, 16]); b_im = din("ssm_b_im", [64, 64, 16])
    c_re = din("ssm_c_re", [64, 16, 64]); c_im = din("ssm_c_im", [64, 16, 64])
    lqk = [din(n, [64]) for n in ["da_lambda_q1", "da_lambda_k1", "da_lambda_q2", "da_lambda_k2"]]
    head_norm = din("da_head_norm", [128])
    cos_seq = din("cos_seq", [128, SEQ]); sin_seq = din("sin_seq", [128, SEQ])
    cos_own = din("cos_own", [128, NOWN]); sin_own = din("sin_own", [128, NOWN])
    kvalid = din("kvalid", [SEQ])
    esel_d = din("esel", [128, 256])
    ident_d = din("ident", [128, 128]); swap_d = din("swapm", [128, 128]); gmask_d = din("gmask", [128, 8, 128])
    out_d = nc.dram_tensor("out", [NOWN, D], F32, kind="ExternalOutput").ap()

    kT_d = dscr("kT_d", [8, 128, SEQ]); v_d = dscr("v_d", [SEQ, D]); uT_d = dscr("uT_d", [8, 128, SEQ])
    qT_d = dscr("qT_d", [8, 128, NOWN]); g_d = dscr("g_d", [16, 128, NOWN])

    P = Prog(nc)
    es = contextlib.ExitStack()
    ARENA_BYTES = 190 * 1024
    arena_t = es.enter_context(nc.sbuf_tensor("arena", [128, ARENA_BYTES // 4], F32))
    A = Arena(P, arena_t[:], ARENA_BYTES)
    banks = [es.enter_context(nc.psum_tensor("bank%d" % i, [128, 512], F32)) for i in range(8)]

    def bank(i):
        return banks[i][:]

    def bank_bf(i):
        return banks[i][:].bitcast(BF16)

    def bkey(i):
        return "bank%d" % i

    def dma(out, in_, reads=(), writes=(), q="sp", slow=False):
        if slow:
            return P.op(q, lambda e: e.dma_start(out=out, in_=in_, allow_slow_non_contiguous=True), reads, writes, dma=True)
        return P.op(q, lambda e: e.dma_start(out=out, in_=in_), reads, writes, dma=True)

    ident_f = A.alloc("ident_f", 128, F32); ident_b = A.alloc("ident_b", 128, BF16)
    swap_f = A.alloc("swap_f", 128, F32)
    gmask = A.alloc("gmask", 1024, F32)
    epst = A.alloc("epst", 1, F32); zerot = A.alloc("zerot", 1, F32)
    dma(ident_f, ident_d, writes=["ident_f"]); dma(swap_f, swap_d, writes=["swap_f"])
    dma(gmask, gmask_d.rearrange("p g c -> p (g c)"), writes=["gmask"])
    P.op("pool", lambda e: e.tensor_copy(out=ident_b, in_=ident_f), ["ident_f"], ["ident_b"])
    P.op("pool", lambda e: e.memset(epst, EPS), [], ["epst"])
    P.op("pool", lambda e: e.memset(zerot, 0.0), [], ["zerot"])

    def load_pp(name, src, n):
        t = A.alloc(name, n, F32)
        dma(t, src.rearrange("(k p) -> p k", p=128), writes=[name], slow=True)
        return t

    g_mix_pre = load_pp("g_mix_pre", gains["norm_mix_pre"], 8)
    g_x_pre = load_pp("g_x_pre", gains["norm_x_pre"], 8)
    g_mem = load_pp("g_mem", gains["norm_mem"], 8)
    g_ff_pre = load_pp("g_ff_pre", gains["norm_ff_pre"], 8)
    bgate_pp = load_pp("bgate_pp", b_gate, 16)
    bglu_pp = load_pp("bglu_pp", b_glu, 8)
    ssmd_pp = load_pp("ssmd_pp", ssm_d, 8)

    wctr = [0]

    def load_weight(name, src, K, N, gain=None, col0=0, rot=False):
        KC = K // 128
        wt = A.alloc(name, KC * N, BF16)
        wv = wt.rearrange("p (k n) -> p k n", k=KC)
        CH = min(N, 2048)
        for kc in range(KC):
            for c0 in range(0, N, CH):
                i = wctr[0]; wctr[0] += 1
                sname = "wstage%d" % (i % 2)
                if sname not in A.live:
                    A.alloc(sname, 2048, F32)
                o, s = A.live[sname]
                st = A.base[:, o // 4:o // 4 + CH]
                dma(st, src[kc * 128:(kc + 1) * 128, col0 + c0:col0 + c0 + CH], writes=[sname], q="sp")
                dst = wv[:, kc, c0:c0 + CH]
                eng = "pool" if (i % 2 == 0) else "dve"
                if rot:
                    sv = st.rearrange("p (m t d) -> p m t d", t=2, d=32)
                    dv = dst.rearrange("p (m t d) -> p m t d", t=2, d=32)
                    if gain is not None:
                        P.op(eng, lambda e, dv=dv, sv=sv, kc=kc: e.tensor_scalar(out=dv[:, :, 0, :], in0=sv[:, :, 1, :], scalar1=gain[:, kc:kc + 1], scalar2=-1.0, op0=ALU.mult, op1=ALU.mult), [sname, "gains"], [name])
                        P.op(eng, lambda e, dv=dv, sv=sv, kc=kc: e.tensor_scalar(out=dv[:, :, 1, :], in0=sv[:, :, 0, :], scalar1=gain[:, kc:kc + 1], scalar2=None, op0=ALU.mult), [sname, "gains"], [name])
                else:
                    if gain is not None:
                        P.op(eng, lambda e, dst=dst, st=st, kc=kc: e.tensor_scalar(out=dst, in0=st, scalar1=gain[:, kc:kc + 1], scalar2=None, op0=ALU.mult), [sname, "gains"], [name])
                    else:
                        P.op(eng, lambda e, dst=dst, st=st: e.tensor_copy(out=dst, in_=st), [sname], [name])
        return wv

    P.op("pool", lambda e: e.engine_nop(), ["g_mix_pre", "g_x_pre", "g_mem", "g_ff_pre"], ["gains"])

    nctr = [0]

    def norm_block_T(x_src_ap, x_is_dram, hT_dst, hT_key, xkey=None, tp_bank=0):
        i = nctr[0]; nctr[0] += 1
        if x_is_dram:
            xs_name = "xs%d" % (i % 2)
            if xs_name not in A.live:
                A.alloc(xs_name, 1024, F32)
            o, s = A.live[xs_name]
            xs = A.base[:, o // 4:o // 4 + 1024]
            dma(xs, x_src_ap, writes=[xs_name], q="sp")
            rkeys = [xs_name]
        else:
            xs = x_src_ap
            rkeys = [xkey]
        for nm, n, dt in (("nsq%d" % (i % 2), 1024, BF16), ("nst%d" % (i % 2), 4, F32), ("nh%d" % (i % 2), 1024, BF16)):
            if nm not in A.live:
                A.alloc(nm, n, dt)
        o, s = A.live["nsq%d" % (i % 2)]; sq = A.base[:, o // 4:o // 4 + 512].bitcast(BF16)
        o, s = A.live["nst%d" % (i % 2)]; st = A.base[:, o // 4:o // 4 + 4]
        o, s = A.live["nh%d" % (i % 2)]; hb = A.base[:, o // 4:o // 4 + 512].bitcast(BF16)
        ks, kt, kh = "nsq%d" % (i % 2), "nst%d" % (i % 2), "nh%d" % (i % 2)
        P.op("act", lambda e: e.activation(out=sq, in_=xs, func=AF.Square, accum_out=st[:, 0:1]), rkeys, [ks, kt])
        P.op("act", lambda e: e.activation(out=st[:, 1:2], in_=st[:, 0:1], func=AF.Ln, scale=1.0 / D, bias=epst), [kt, "epst"], [kt + ":1"])
        P.op("act", lambda e: e.activation(out=st[:, 2:3], in_=st[:, 1:2], func=AF.Exp, scale=-0.5), [kt + ":1"], [kt + ":2"])
        P.op("dve", lambda e: e.tensor_scalar(out=hb, in0=xs, scalar1=st[:, 2:3], scalar2=None, op0=ALU.mult), rkeys + [kt + ":2"], [kh])
        tp = bank_bf(tp_bank).rearrange("p (k t) -> p k t", k=8)
        for kc in range(8):
            P.op("pe", lambda e, kc=kc: e.transpose(out=tp[:, kc, :], in_=hb[:, kc * 128:(kc + 1) * 128], identity=ident_b), [kh, "ident_b"], [bkey(tp_bank)])
        P.op("act", lambda e: e.activation(out=hT_dst, in_=tp, func=AF.Copy), [bkey(tp_bank)], [hT_key])

    x1_d = dscr("x1_d", [NOWN, D], F32)
    x2_d = dscr("x2_d", [NOWN, D], F32)
    wf1_d = nc.dram_tensor("wf1_d", [D, 4 * D], BF16, kind="Internal").ap()
    wf2_d = nc.dram_tensor("wf2_d", [4 * D, D], BF16, kind="Internal").ap()
    wsc = {}
    for nm_, (src_, K_, N_, gn_) in {"wglu": (w_glu, D, D, None), "wssm": (w_ssm, D, D, None), "wda": (w_da, D, D, None),
                                       "wmix": (w_mix, D, D, None), "wxkv": (w_xkv, D, 2 * D, g_mem), "wxq": (w_xq, D, D, g_x_pre),
                                       "wxo": (w_xo, D, D, None)}.items():
        wsc[nm_] = (nc.dram_tensor(nm_ + "_d", [K_, N_], BF16, kind="Internal").ap(), src_, K_, N_, gn_)
    cst_ = [A.alloc("cstg%d" % i, 1024, F32) for i in range(2)]
    cb = [A.alloc("cb%d" % i, 1024, BF16) for i in range(2)]
    cctr = [0]

    def cast_gen():
        jobs = [(v_[1], v_[2], v_[3], v_[4], v_[0], k_ + "_d") for k_, v_ in wsc.items()]
        jobs.append((w_ff1, D, 4 * D, g_ff_pre, wf1_d, "wf_d"))
        jobs.append((w_ff2, 4 * D, D, None, wf2_d, "wf_d"))
        for (src, K_, N_, gain, dst, dkey) in jobs:
            for kc in range(K_ // 128):
                for c0 in range(0, N_, 1024):
                    i = cctr[0]; cctr[0] += 1
                    st = cst_[i % 2]; sname = "cstg%d" % (i % 2)
                    dma(st, src[kc * 128:(kc + 1) * 128, c0:c0 + 1024], writes=[sname], q="pool")
                    cbt = cb[i % 2]; cbk = "cb%d" % (i % 2)
                    if gain is not None:
                        P.op("pool", lambda e, cbt=cbt, st=st, kc=kc, gain=gain: e.tensor_scalar(out=cbt, in0=st, scalar1=gain[:, kc:kc + 1], scalar2=None, op0=ALU.mult), [sname, "gains"], [cbk])
                    else:
                        P.op("pool", lambda e, cbt=cbt, st=st: e.tensor_copy(out=cbt, in_=st), [sname], [cbk])
                    dma(dst[kc * 128:(kc + 1) * 128, c0:c0 + 1024], cbt, reads=[cbk], writes=[dkey], q="pool")
                    yield

    cgen = cast_gen()
    cg_alive = [True]

    def cast_step():
        if cg_alive[0]:
            try:
                next(cgen)
            except StopIteration:
                cg_alive[0] = False

    wk = load_weight("wk", w_in, D, 1024, gain=g_mix_pre, col0=2048)
    wkr = load_weight("wkr", w_in, D, 1024, gain=g_mix_pre, col0=2048, rot=True)
    wu = load_weight("wu", w_in, D, 1024, gain=g_mix_pre, col0=0)
    wv_ = load_weight("wv", w_in, D, 1024, gain=g_mix_pre, col0=3072)
    hTa = [A.alloc("hTa%d" % i, 8 * 512, BF16).rearrange("p (k t) -> p k t", k=8) for i in range(2)]
    cst = [A.alloc("cst%d" % i, 1024, F32) for i in range(2)]
    kst = [A.alloc("kst%d" % i, 512, BF16) for i in range(2)]
    kt1 = [A.alloc("kt1_%d" % i, 512, F32) for i in range(2)]
    kt2 = [A.alloc("kt2_%d" % i, 512, F32) for i in range(2)]
    vst = [A.alloc("vst%d" % i, 1024, BF16) for i in range(2)]
    ust = [A.alloc("ust%d" % i, 512, BF16) for i in range(2)]
    cnt = [0]
    for tt in range(16):
        hb_i = tt % 2
        hT = hTa[hb_i]; hk = "hTa%d" % hb_i
        for bl in range(4):
            blk = tt * 4 + bl
            norm_block_T(xseq[blk * 128:(blk + 1) * 128, :], True, hT[:, :, bl * 128:(bl + 1) * 128], hk)
        cs = cst[hb_i]; ck = "cst%d" % hb_i
        dma(cs[:, 0:512], cos_seq[:, tt * 512:(tt + 1) * 512], writes=[ck])
        dma(cs[:, 512:1024], sin_seq[:, tt * 512:(tt + 1) * 512], writes=[ck])
        for h in range(8):
            cast_step()
            i = cnt[0]; cnt[0] += 1
            pb = 1 + 2 * (i % 2)
            for kc in range(8):
                P.op("pe", lambda e, kc=kc, h=h, pb=pb, hT=hT: e.matmul(out=bank(pb), lhsT=wk[:, kc, h * 128:(h + 1) * 128], rhs=hT[:, kc, :], start=(kc == 0), stop=(kc == 7)), ["wk", hk], [bkey(pb)])
            for kc in range(8):
                P.op("pe", lambda e, kc=kc, h=h, pb=pb, hT=hT: e.matmul(out=bank(pb + 1), lhsT=wkr[:, kc, h * 128:(h + 1) * 128], rhs=hT[:, kc, :], start=(kc == 0), stop=(kc == 7)), ["wkr", hk], [bkey(pb + 1)])
            j = i % 2
            P.op("dve", lambda e, pb=pb, j=j, cs=cs: e.tensor_tensor(out=kt1[j], in0=bank(pb), in1=cs[:, 0:512], op=ALU.mult), [bkey(pb), ck], ["kt1_%d" % j])
            P.op("dve", lambda e, pb=pb, j=j, cs=cs: e.tensor_tensor(out=kt2[j], in0=bank(pb + 1), in1=cs[:, 512:1024], op=ALU.mult), [bkey(pb + 1), ck], ["kt2_%d" % j])
            P.op("dve", lambda e, j=j: e.tensor_tensor(out=kst[j], in0=kt1[j], in1=kt2[j], op=ALU.add), ["kt1_%d" % j, "kt2_%d" % j], ["kst%d" % j])
            dma(kT_d[h, :, tt * 512:(tt + 1) * 512], kst[j], reads=["kst%d" % j], writes=["kT_d"], q="sp")
        for fc in range(8):
            i = cnt[0]; cnt[0] += 1
            pb = 5 + (i % 2)
            for kc in range(8):
                P.op("pe", lambda e, kc=kc, fc=fc, pb=pb, hT=hT: e.matmul(out=bank(pb), lhsT=wu[:, kc, fc * 128:(fc + 1) * 128], rhs=hT[:, kc, :], start=(kc == 0), stop=(kc == 7)), ["wu", hk], [bkey(pb)])
            j = i % 2
            P.op("act", lambda e, pb=pb, j=j: e.activation(out=ust[j], in_=bank(pb), func=AF.Copy), [bkey(pb)], ["ust%d" % j])
            dma(uT_d[fc, :, tt * 512:(tt + 1) * 512], ust[j], reads=["ust%d" % j], writes=["uT_d"], q="sp")
        for bl in range(4):
            blk = tt * 4 + bl
            i = cnt[0]; cnt[0] += 1
            j = i % 2
            for half in range(2):
                pb = 1 + 2 * (i % 2) + half
                for kc in range(8):
                    P.op("pe", lambda e, kc=kc, bl=bl, half=half, pb=pb, hT=hT: e.matmul(out=bank(pb), lhsT=hT[:, kc, bl * 128:(bl + 1) * 128], rhs=wv_[:, kc, half * 512:(half + 1) * 512], start=(kc == 0), stop=(kc == 7)), ["wv", hk], [bkey(pb)])
                P.op("act", lambda e, pb=pb, j=j, half=half: e.activation(out=vst[j][:, half * 512:(half + 1) * 512], in_=bank(pb), func=AF.Copy), [bkey(pb)], ["vst%d" % j])
            dma(v_d[blk * 128:(blk + 1) * 128, :], vst[j], reads=["vst%d" % j], writes=["v_d"], q="sp")
    while cg_alive[0]:
        cast_step()
    for n in ["cstg0", "cstg1", "cb0", "cb1"]:
        A.free(n)
    for n in ["wu", "wk", "wkr", "wv", "hTa0", "hTa1", "cst0", "cst1", "kst0", "kst1", "kt1_0", "kt1_1", "kt2_0", "kt2_1", "vst0", "vst1", "ust0", "ust1"]:
        A.free(n)

    wq = load_weight("wq", w_in, D, 1024, gain=g_mix_pre, col0=1024)
    wqr = load_weight("wqr", w_in, D, 1024, gain=g_mix_pre, col0=1024, rot=True)
    wg = load_weight("wg", w_in, D, 2048, gain=g_mix_pre, col0=4096)
    hTo = [A.alloc("hTo%d" % i, 8 * 512, BF16).rearrange("p (k t) -> p k t", k=8) for i in range(2)]
    cso = [A.alloc("cso%d" % i, 1024, F32) for i in range(2)]
    qst = [A.alloc("qst%d" % i, 512, BF16) for i in range(2)]
    qt1 = [A.alloc("qt1_%d" % i, 512, F32) for i in range(2)]
    qt2 = [A.alloc("qt2_%d" % i, 512, F32) for i in range(2)]
    gst = [A.alloc("gst%d" % i, 512, BF16) for i in range(2)]
    for tt in range(4):
        hb_i = tt % 2
        hT = hTo[hb_i]; hk = "hTo%d" % hb_i
        for bl in range(4):
            blk = tt * 4 + bl
            norm_block_T(xown[blk * 128:(blk + 1) * 128, :], True, hT[:, :, bl * 128:(bl + 1) * 128], hk)
        cs = cso[hb_i]; ck = "cso%d" % hb_i
        dma(cs[:, 0:512], cos_own[:, tt * 512:(tt + 1) * 512], writes=[ck])
        dma(cs[:, 512:1024], sin_own[:, tt * 512:(tt + 1) * 512], writes=[ck])
        for h in range(8):
            i = cnt[0]; cnt[0] += 1
            pb = 1 + 2 * (i % 2)
            for kc in range(8):
                P.op("pe", lambda e, kc=kc, h=h, pb=pb, hT=hT: e.matmul(out=bank(pb), lhsT=wq[:, kc, h * 128:(h + 1) * 128], rhs=hT[:, kc, :], start=(kc == 0), stop=(kc == 7)), ["wq", hk], [bkey(pb)])
            for kc in range(8):
                P.op("pe", lambda e, kc=kc, h=h, pb=pb, hT=hT: e.matmul(out=bank(pb + 1), lhsT=wqr[:, kc, h * 128:(h + 1) * 128], rhs=hT[:, kc, :], start=(kc == 0), stop=(kc == 7)), ["wqr", hk], [bkey(pb + 1)])
            j = i % 2
            P.op("dve", lambda e, pb=pb, j=j, cs=cs: e.tensor_tensor(out=qt1[j], in0=bank(pb), in1=cs[:, 0:512], op=ALU.mult), [bkey(pb), ck], ["qt1_%d" % j])
            P.op("dve", lambda e, pb=pb, j=j, cs=cs: e.tensor_tensor(out=qt2[j], in0=bank(pb + 1), in1=cs[:, 512:1024], op=ALU.mult), [bkey(pb + 1), ck], ["qt2_%d" % j])
            P.op("pool", lambda e, j=j: e.tensor_tensor(out=qst[j], in0=qt1[j], in1=qt2[j], op=ALU.add), ["qt1_%d" % j, "qt2_%d" % j], ["qst%d" % j])
            dma(qT_d[h, :, tt * 512:(tt + 1) * 512], qst[j], reads=["qst%d" % j], writes=["qT_d"], q="sp")
        for gc in range(16):
            i = cnt[0]; cnt[0] += 1
            pb = 5 + (i % 2)
            for kc in range(8):
                P.op("pe", lambda e, kc=kc, gc=gc, pb=pb, hT=hT: e.matmul(out=bank(pb), lhsT=wg[:, kc, gc * 128:(gc + 1) * 128], rhs=hT[:, kc, :], start=(kc == 0), stop=(kc == 7)), ["wg", hk], [bkey(pb)])
            j = i % 2
            P.op("act", lambda e, pb=pb, j=j, gc=gc: e.activation(out=gst[j], in_=bank(pb), func=AF.Sigmoid, bias=bgate_pp[:, gc:gc + 1]), [bkey(pb), "bgate_pp"], ["gst%d" % j])
            dma(g_d[gc, :, tt * 512:(tt + 1) * 512], gst[j], reads=["gst%d" % j], writes=["g_d"], q="sp")
    for n in ["wq", "wqr", "wg", "hTo0", "hTo1", "cso0", "cso1", "qst0", "qst1", "qt1_0", "qt1_1", "qt2_0", "qt2_1", "gst0", "gst1"]:
        A.free(n)

    for n in ["wstage0", "wstage1", "xs0", "xs1", "nsq0", "nsq1", "nh0", "nh1", "nst0", "nst1"]:
        if n in A.live:
            A.free(n)
    def a64(name, n, dt=F32):
        return A.alloc(name, n, dt)[0:64, :]

    lre = a64("lre", 64); lim = a64("lim", 64); ldt = a64("ldt", 64)
    dma(lre, lam_re.rearrange("g p -> p g"), writes=["lre"], slow=True)
    dma(lim, lam_im.rearrange("g p -> p g"), writes=["lim"], slow=True)
    dma(ldt, log_dt.partition_broadcast(64), writes=["ldt"])
    dtt = a64("dtt", 64); zr = a64("zr", 64); zi = a64("zi", 64)
    P.op("act", lambda e: e.activation(out=dtt, in_=ldt, func=AF.Exp), ["ldt"], ["dtt"])
    P.op("dve", lambda e: e.tensor_tensor(out=zr, in0=lre, in1=dtt, op=ALU.mult), ["lre", "dtt"], ["zr"])
    P.op("dve", lambda e: e.tensor_tensor(out=zi, in0=lim, in1=dtt, op=ALU.mult), ["lim", "dtt"], ["zi"])
    mag = a64("mag", 64); cr = a64("cr", 64); ci = a64("ci", 64); halfpi = a64("halfpi", 1)
    P.op("pool", lambda e: e.memset(halfpi, math.pi / 2), [], ["halfpi"])
    P.op("act", lambda e: e.activation(out=mag, in_=zr, func=AF.Exp, scale=1.0 / 32), ["zr"], ["mag"])
    P.op("act", lambda e: e.activation(out=ci, in_=zi, func=AF.Sin, scale=1.0 / 32), ["zi"], ["ci"])
    P.op("act", lambda e: e.activation(out=cr, in_=zi, func=AF.Sin, scale=1.0 / 32, bias=halfpi), ["zi", "halfpi"], ["cr"])
    P.op("dve", lambda e: e.tensor_tensor(out=cr, in0=cr, in1=mag, op=ALU.mult), ["cr", "mag"], ["cr"])
    P.op("dve", lambda e: e.tensor_tensor(out=ci, in0=ci, in1=mag, op=ALU.mult), ["ci", "mag"], ["ci"])
    PW = a64("PW", NLEV * 128).rearrange("p (l r g) -> p l r g", l=NLEV, r=2)
    t1 = a64("sq_t1", 64); t2 = a64("sq_t2", 64); t3 = a64("sq_t3", 64)

    def csquare(sr, si, dr, di, keys_in, key_out):
        P.op("dve", lambda e: e.tensor_tensor(out=t1, in0=sr, in1=sr, op=ALU.mult), keys_in, ["sq_t1"])
        P.op("dve", lambda e: e.tensor_tensor(out=t2, in0=si, in1=si, op=ALU.mult), keys_in, ["sq_t2"])
        P.op("dve", lambda e: e.tensor_tensor(out=t3, in0=sr, in1=si, op=ALU.mult), keys_in, ["sq_t3"])
        P.op("dve", lambda e: e.tensor_tensor(out=dr, in0=t1, in1=t2, op=ALU.subtract), ["sq_t1", "sq_t2"], [key_out])
        P.op("dve", lambda e: e.tensor_tensor(out=di, in0=t3, in1=t3, op=ALU.add), ["sq_t3"], [key_out + ":i"])

    wr = [a64("wr%d" % i, 64) for i in range(2)]; wi = [a64("wi%d" % i, 64) for i in range(2)]
    csquare(cr, ci, wr[0], wi[0], ["cr", "ci"], "wr0")
    csquare(wr[0], wi[0], wr[1], wi[1], ["wr0", "wr0:i"], "wr1")
    csquare(wr[1], wi[1], wr[0], wi[0], ["wr1", "wr1:i"], "wr0")
    csquare(wr[0], wi[0], wr[1], wi[1], ["wr0", "wr0:i"], "wr1")
    csquare(wr[1], wi[1], PW[:, 0, 0, :], PW[:, 0, 1, :], ["wr1", "wr1:i"], "PW:0")
    for l in range(1, NLEV):
        csquare(PW[:, l - 1, 0, :], PW[:, l - 1, 1, :], PW[:, l, 0, :], PW[:, l, 1, :], ["PW:%d" % (l - 1), "PW:%d:i" % (l - 1)], "PW:%d" % l)
    den = a64("den", 64); cfr = a64("cfr", 64); cfi = a64("cfi", 64); lm1 = a64("lm1", 64)
    P.op("dve", lambda e: e.tensor_tensor(out=t1, in0=lre, in1=lre, op=ALU.mult), ["lre", "PW:%d:i" % (NLEV - 1)], ["sq_t1"])
    P.op("dve", lambda e: e.tensor_tensor(out=t2, in0=lim, in1=lim, op=ALU.mult), ["lim"], ["sq_t2"])
    P.op("dve", lambda e: e.tensor_tensor(out=den, in0=t1, in1=t2, op=ALU.add), ["sq_t1", "sq_t2"], ["den"])
    P.op("dve", lambda e: e.reciprocal(out=den, in_=den), ["den"], ["den"])
    P.op("dve", lambda e: e.tensor_scalar(out=lm1, in0=PW[:, 0, 0, :], scalar1=-1.0, scalar2=None, op0=ALU.add), ["PW:0"], ["lm1"])
    P.op("dve", lambda e: e.tensor_tensor(out=t1, in0=lm1, in1=lre, op=ALU.mult), ["lm1", "lre", "den"], ["sq_t1"])
    P.op("dve", lambda e: e.tensor_tensor(out=t2, in0=PW[:, 0, 1, :], in1=lim, op=ALU.mult), ["PW:0:i", "lim"], ["sq_t2"])
    P.op("dve", lambda e: e.tensor_tensor(out=cfr, in0=t1, in1=t2, op=ALU.add), ["sq_t1", "sq_t2"], ["cfr"])
    P.op("dve", lambda e: e.tensor_tensor(out=t1, in0=PW[:, 0, 1, :], in1=lre, op=ALU.mult), ["PW:0:i", "lre", "cfr"], ["sq_t1"])
    P.op("dve", lambda e: e.tensor_tensor(out=t2, in0=lm1, in1=lim, op=ALU.mult), ["lm1", "lim", "cfr"], ["sq_t2"])
    P.op("dve", lambda e: e.tensor_tensor(out=cfi, in0=t1, in1=t2, op=ALU.subtract), ["sq_t1", "sq_t2"], ["cfi"])
    P.op("dve", lambda e: e.tensor_tensor(out=cfr, in0=cfr, in1=den, op=ALU.mult), ["cfr", "den"], ["cfr"])
    P.op("dve", lambda e: e.tensor_tensor(out=cfi, in0=cfi, in1=den, op=ALU.mult), ["cfi", "den"], ["cfi"])
    braw = a64("braw", 2048).rearrange("p (r g h) -> p r g h", r=2, g=64)
    for gh in range(2):
        dma(braw[:, 0, gh * 32:(gh + 1) * 32, :], b_re[gh * 32:(gh + 1) * 32].rearrange("g p h -> p g h"), writes=["braw"], slow=True)
        dma(braw[:, 1, gh * 32:(gh + 1) * 32, :], b_im[gh * 32:(gh + 1) * 32].rearrange("g p h -> p g h"), writes=["braw"], slow=True)
    Bb = a64("Bb", 2048).rearrange("p (r g h) -> p r g h", r=2, g=64)
    bt1 = a64("bt1", 1024).rearrange("p (g h) -> p g h", g=64); bt2 = a64("bt2", 1024).rearrange("p (g h) -> p g h", g=64)
    cfr_b = cfr.unsqueeze(2).broadcast_to([64, 64, 16]); cfi_b = cfi.unsqueeze(2).broadcast_to([64, 64, 16])
    P.op("dve", lambda e: e.tensor_tensor(out=bt1, in0=braw[:, 0], in1=cfr_b, op=ALU.mult), ["braw", "cfr"], ["bt1"])
    P.op("dve", lambda e: e.tensor_tensor(out=bt2, in0=braw[:, 1], in1=cfi_b, op=ALU.mult), ["braw", "cfi"], ["bt2"])
    P.op("dve", lambda e: e.tensor_tensor(out=Bb[:, 0], in0=bt1, in1=bt2, op=ALU.subtract), ["bt1", "bt2"], ["Bb"])
    P.op("dve", lambda e: e.tensor_tensor(out=bt1, in0=braw[:, 0], in1=cfi_b, op=ALU.mult), ["braw", "cfi", "Bb"], ["bt1"])
    P.op("dve", lambda e: e.tensor_tensor(out=bt2, in0=braw[:, 1], in1=cfr_b, op=ALU.mult), ["braw", "cfr", "Bb"], ["bt2"])
    P.op("dve", lambda e: e.tensor_tensor(out=Bb[:, 1], in0=bt1, in1=bt2, op=ALU.add), ["bt1", "bt2"], ["Bb:i"])
    Cst = A.alloc("Cst", 1024, F32).rearrange("p (g h) -> p g h", g=64)
    for gh in range(2):
        dma(Cst[0:64, gh * 32:(gh + 1) * 32, :], c_re[gh * 32:(gh + 1) * 32].rearrange("g h p -> p g h"), writes=["Cst"], slow=True)
        dma(Cst[64:128, gh * 32:(gh + 1) * 32, :], c_im[gh * 32:(gh + 1) * 32].rearrange("g h p -> p g h"), writes=["Cst"], slow=True)
    P.op("pool", lambda e: e.tensor_scalar(out=Cst[64:128], in0=Cst[64:128], scalar1=-1.0, scalar2=None, op0=ALU.mult), ["Cst"], ["Cst"])
    S1 = A.alloc("S1", NLEV * 64, F32).rearrange("p (l g) -> p l g", l=NLEV)
    S2 = A.alloc("S2", NLEV * 64, F32).rearrange("p (l g) -> p l g", l=NLEV)
    pwkeys = ["PW:%d" % l for l in range(NLEV)] + ["PW:%d:i" % l for l in range(NLEV)]
    dma(S1[0:64], PW[:, :, 0, :], reads=pwkeys, writes=["S1"]); dma(S1[64:128], PW[:, :, 0, :], reads=pwkeys, writes=["S1"])
    dma(S2[0:64], PW[:, :, 1, :], reads=pwkeys, writes=["S2"]); dma(S2[64:128], PW[:, :, 1, :], reads=pwkeys, writes=["S2"])
    P.op("pool", lambda e: e.tensor_scalar(out=S2[64:128], in0=S2[64:128], scalar1=-1.0, scalar2=None, op0=ALU.mult), ["S2"], ["S2"])

    if debug:
        pw_dbg = dscr("pw_dbg", [64, NLEV * 128], F32)
        dma(pw_dbg, PW.rearrange("p l r g -> p (l r g)"), reads=pwkeys, writes=["pw_dbg"])
        bb_dbg = dscr("bb_dbg", [64, 2048], F32)
        dma(bb_dbg, Bb.rearrange("p r g h -> p (r g h)"), reads=["Bb", "Bb:i"], writes=["bb_dbg"])
    for n in ["braw", "bt1", "bt2", "PW", "sq_t1", "sq_t2", "sq_t3", "wr0", "wr1", "wi0", "wi1", "mag", "cr", "ci", "lm1", "den", "cfr", "cfi", "zr", "zi", "dtt", "ldt", "lre", "lim"]:
        A.free(n)
    uT = [A.alloc("uT%d" % i, SEQ, BF16) for i in range(1)]
    X0p = [A.alloc("X0p%d" % i, SEQ, BF16) for i in range(2)]
    Tp = [A.alloc("Tp%d" % i, SEQ, BF16) for i in range(2)]
    Xop = [[A.alloc("Xo%d_%d" % (p_, i), NOWN, BF16).rearrange("p (s t) -> p s t", s=16) for i in range(2)] for p_ in range(2)]
    XBp = [[A.alloc("XB%d_%d" % (p_, i), 64, BF16) for i in range(2)] for p_ in range(2)]
    ysg = A.alloc("ysg", 8 * NOWN, BF16).rearrange("p (k t) -> p k t", k=8)
    WB = [A.alloc("WB%d" % i, 128, BF16) for i in range(4)]
    WC = [A.alloc("WC%d" % i, 128, BF16) for i in range(4)]
    R = [A.alloc("R%d" % i, 128, BF16) for i in range(52)]
    Rt = [A.alloc("Rt%d" % i, 128, F32) for i in range(4)]
    bm = [A.alloc("bm%d" % i, 256, F32)[0:64, :] for i in range(2)]
    gel = [A.alloc("gel%d" % i, NOWN, F32) for i in range(2)]
    evc = [0]
    toff = [0, 4096, 6144, 7168, 7680, 7936, 8064]

    def prep_gen(fc, gl, par, st_):
        g = fc * 8 + gl
        wi_ = st_ * 2 + par
        bmt = bm[par]; bmk = "bm%d" % par
        Bfc = Bb[:, :, fc * 8:(fc + 1) * 8, :]
        gmv64 = gmask[0:64, gl * 128:(gl + 1) * 128].rearrange("p (g h) -> p g h", g=8)
        P.op("dve", lambda e: e.tensor_tensor(out=bmt[:, 0:128].rearrange("p (g h) -> p g h", g=8), in0=Bfc[:, 0], in1=gmv64, op=ALU.mult), ["Bb", "Bb:i", "gmask"], [bmk])
        P.op("dve", lambda e: e.tensor_tensor(out=bmt[:, 128:256].rearrange("p (g h) -> p g h", g=8), in0=Bfc[:, 1], in1=gmv64, op=ALU.mult), ["Bb", "Bb:i", "gmask"], [bmk + ":i"])
        pbw = 7
        P.op("pe", lambda e: e.matmul(out=bank(pbw)[:, 0:64], lhsT=bmt[:, 0:128], rhs=ident_f[0:64, 0:64], start=True, stop=True), [bmk, "ident_f"], [bkey(pbw)])
        P.op("pe", lambda e: e.matmul(out=bank(pbw)[:, 64:128], lhsT=bmt[:, 128:256], rhs=ident_f[0:64, 0:64], start=True, stop=True), [bmk + ":i", "ident_f"], [bkey(pbw)])
        P.op("act", lambda e: e.activation(out=WB[wi_], in_=bank(pbw)[:, 0:128], func=AF.Copy), [bkey(pbw)], ["WB%d" % wi_])
        P.op("pool", lambda e: e.tensor_tensor(out=WC[wi_].rearrange("p (g h) -> p g h", g=8), in0=Cst[:, fc * 8:(fc + 1) * 8, :], in1=gmask[:, gl * 128:(gl + 1) * 128].rearrange("p (g h) -> p g h", g=8), op=ALU.mult), ["Cst", "gmask"], ["WC%d" % wi_])
        yield
        for lev in range(NLEV):
            Rm = R[wi_ * 13 + lev]; Rk = "R%d" % (wi_ * 13 + lev)
            rt = Rt[(lev % 2) * 2 + par]; rtk = "Rt%d" % ((lev % 2) * 2 + par)
            P.op("pool", lambda e, lev=lev, rt=rt: e.tensor_scalar(out=rt, in0=swap_f, scalar1=S2[:, lev, g:g + 1], scalar2=None, op0=ALU.mult), ["swap_f", "S2"], [rtk])
            P.op("dve", lambda e, Rm=Rm, lev=lev, rt=rt: e.scalar_tensor_tensor(out=Rm, in0=ident_f, scalar=S1[:, lev, g:g + 1], in1=rt, op0=ALU.mult, op1=ALU.add), ["ident_f", "S1", rtk], [Rk])
            yield

    def group_gen(fc, gl, par, u, uk, st_):
        g = fc * 8 + gl
        wi_ = st_ * 2 + par
        X0 = X0p[par]; Tb = Tp[par]; Xo = Xop[par]; XB = XBp[par]
        xn = "X0p%d" % par; tn = "Tp%d" % par
        bc = [0]

        def nextbank():
            bc[0] += 1
            return 2 * par + (bc[0] % 2)

        def evac(pb, dst_ap, dkey, n=None):
            src_ap = bank(pb) if n is None else bank(pb)[:, 0:n]
            evc[0] += 1
            if evc[0] % 3 != 0:
                P.op("act", lambda e: e.activation(out=dst_ap, in_=src_ap, func=AF.Copy), [bkey(pb)], [dkey])
            else:
                P.op("dve", lambda e: e.tensor_copy(out=dst_ap, in_=src_ap), [bkey(pb)], [dkey])

        Rg = [(R[wi_ * 13 + lev], "R%d" % (wi_ * 13 + lev)) for lev in range(NLEV)]
        for tt in range(16):
            pb = nextbank()
            P.op("pe", lambda e, pb=pb, tt=tt: e.matmul(out=bank(pb), lhsT=WB[wi_], rhs=u[:, tt * 512:(tt + 1) * 512], start=True, stop=True), ["WB%d" % wi_, uk], [bkey(pb)])
            evac(pb, X0[:, tt * 512:(tt + 1) * 512], xn + ":%d" % tt)
            if tt % 4 == 3:
                yield
        for lev in range(7):
            n_l = 4096 >> lev
            if lev == 0:
                srcv = X0.rearrange("p (i two) -> p i two", two=2); sbase = xn + ":"
            else:
                srcv = Tb[:, toff[lev - 1]:toff[lev - 1] + 2 * n_l].rearrange("p (i two) -> p i two", two=2); sbase = tn + ":%d_" % (lev - 1)
            Rm, Rk = Rg[lev]
            for c0 in range(0, n_l, 512):
                n = min(512, n_l - c0)
                pb = nextbank()
                skeys = sorted(set([sbase + "%d" % ((2 * c0) // 512), sbase + "%d" % ((2 * c0 + 2 * n - 1) // 512)]))
                P.op("pe", lambda e, pb=pb, srcv=srcv, c0=c0, n=n: e.matmul(out=bank(pb)[:, 0:n], lhsT=ident_b, rhs=srcv[:, c0:c0 + n, 1], start=True, stop=False), ["ident_b"] + skeys, [bkey(pb)])
                P.op("pe", lambda e, pb=pb, srcv=srcv, c0=c0, n=n, Rm=Rm: e.matmul(out=bank(pb)[:, 0:n], lhsT=Rm, rhs=srcv[:, c0:c0 + n, 0], start=False, stop=True), [Rk] + skeys, [bkey(pb)])
                evac(pb, Tb[:, toff[lev] + c0:toff[lev] + c0 + n], tn + ":%d_%d" % (lev, c0 // 512), n)
                if (c0 // 512) % 2 == 1:
                    yield
            yield
        xbk = tn + ":6_0"
        xb_src = Tb[:, toff[6]:toff[6] + 64]
        for m_ in range(6):
            sh = 1 << m_
            Rm, Rk = Rg[7 + m_]
            pb = nextbank()
            P.op("pe", lambda e, pb=pb, xb_src=xb_src: e.matmul(out=bank(pb)[:, 0:64], lhsT=ident_b, rhs=xb_src, start=True, stop=False), ["ident_b", xbk], [bkey(pb)])
            P.op("pe", lambda e, pb=pb, xb_src=xb_src, sh=sh, Rm=Rm: e.matmul(out=bank(pb)[:, sh:64], lhsT=Rm, rhs=xb_src[:, 0:64 - sh], start=False, stop=True), [Rk, xbk], [bkey(pb)])
            dstb = XB[m_ % 2]
            xbk = "XB%d_%d" % (par, m_ % 2)
            evac(pb, dstb, xbk, 64)
            xb_src = dstb
            yield
        x0own = X0.rearrange("p (s b t) -> p s b t", s=16, b=4)[:, :, 3, :]
        x0keys = [xn + ":%d" % t for t in range(16)]
        xo0keys = ["Xo%d_0:%d" % (par, t) for t in range(4)]
        P.op("pool", lambda e: e.tensor_copy(out=Xo[0], in_=x0own), x0keys, xo0keys)
        pb = nextbank()
        Rm, Rk = Rg[0]
        P.op("pe", lambda e, pb=pb: e.matmul(out=bank(pb)[:, 0:16], lhsT=ident_b, rhs=x0own[:, :, 0], start=True, stop=False), ["ident_b"] + x0keys, [bkey(pb)])
        P.op("pe", lambda e, pb=pb, xb_src=xb_src, Rm=Rm: e.matmul(out=bank(pb)[:, 0:16], lhsT=Rm, rhs=xb_src.rearrange("p (s b) -> p s b", b=4)[:, :, 2], start=False, stop=True), [Rk, xbk], [bkey(pb)])
        P.op("dve", lambda e, pb=pb: e.tensor_copy(out=Xo[0][:, :, 0], in_=bank(pb)[:, 0:16]), [bkey(pb)] + xo0keys, xo0keys)
        yield
        cur = 0
        for lev in range(7):
            sh = 1 << lev
            Rm, Rk = Rg[lev]
            src = Xo[cur]; dst = Xo[1 - cur]
            for q4 in range(4):
                sk = "Xo%d_%d:%d" % (par, cur, q4); dk = "Xo%d_%d:%d" % (par, 1 - cur, q4)
                pb = nextbank()
                pv = bank(pb).rearrange("p (s t) -> p s t", s=4)
                P.op("pe", lambda e, pb=pb, src=src, q4=q4: e.matmul(out=bank(pb), lhsT=ident_b, rhs=src[:, q4 * 4:(q4 + 1) * 4, :].rearrange("p s t -> p (s t)"), start=True, stop=False), ["ident_b", sk], [bkey(pb)])
                P.op("pe", lambda e, pv=pv, src=src, q4=q4, sh=sh, Rm=Rm: e.matmul(out=pv[:, :, sh:128], lhsT=Rm, rhs=src[:, q4 * 4:(q4 + 1) * 4, 0:128 - sh], start=False, stop=True), [Rk, sk], [bkey(pb)])
                evac(pb, dst[:, q4 * 4:(q4 + 1) * 4, :].rearrange("p s t -> p (s t)"), dk)
                if q4 % 2 == 1:
                    yield
            cur = 1 - cur
        fin = Xo[cur].rearrange("p s t -> p (s t)"); fkb = "Xo%d_%d" % (par, cur)
        if debug and g == 63:
            xf_dbg = dscr("xf_dbg", [128, NOWN])
            dma(xf_dbg, fin, reads=[fkb + ":%d" % t for t in range(4)], writes=["xf_dbg"])
        for ot in range(4):
            yb = 4 + (2 * par + ot) % 3
            P.op("pe", lambda e, ot=ot, yb=yb: e.matmul(out=bank(yb), lhsT=WC[wi_], rhs=fin[:, ot * 512:(ot + 1) * 512], start=True, stop=True), ["WC%d" % wi_, fkb + ":%d" % ot], [bkey(yb)])
            pbk = bkey(yb); pbb = bank(yb)
            yacc = gel[0][:, ot * 512:(ot + 1) * 512]
            if gl == 0:
                uo = u.rearrange("p (s b t) -> p s b t", s=16, b=4)[:, ot * 4:(ot + 1) * 4, 3, :]
                P.op("dve", lambda e, yacc=yacc, uo=uo, pbb=pbb: e.scalar_tensor_tensor(out=yacc.rearrange("p (s t) -> p s t", s=4), in0=uo, scalar=ssmd_pp[:, fc:fc + 1], in1=pbb.rearrange("p (s t) -> p s t", s=4), op0=ALU.mult, op1=ALU.add), [uk, "ssmd_pp", pbk], ["gel0:%d" % ot])
            else:
                P.op("dve", lambda e, yacc=yacc, pbb=pbb: e.tensor_tensor(out=yacc, in0=pbb, in1=yacc, op=ALU.add), [pbk, "gel0:%d" % ot], ["gel0:%d" % ot])
            if ot % 2 == 1:
                yield

    def run_lockstep(gens):
        alive = [True] * len(gens)
        while any(alive):
            for i_ in range(len(gens)):
                if alive[i_]:
                    try:
                        next(gens[i_])
                    except StopIteration:
                        alive[i_] = False

    run_lockstep([prep_gen(0, 0, 0, 0), prep_gen(0, 1, 1, 0)])
    for fc in range(8):
        u = uT[0]; uk = "uT0"
        dma(u, uT_d[fc], reads=["uT_d"], writes=[uk], q="pool")
        for gp in range(4):
            pk = fc * 4 + gp
            st_ = pk % 2
            gens = [group_gen(fc, 2 * gp, 0, u, uk, st_), group_gen(fc, 2 * gp + 1, 1, u, uk, st_)]
            if pk + 1 < 32:
                nfc, ngp = (pk + 1) // 4, (pk + 1) % 4
                gens.append(prep_gen(nfc, 2 * ngp, 0, 1 - st_))
                gens.append(prep_gen(nfc, 2 * ngp + 1, 1, 1 - st_))
            run_lockstep(gens)
        yk = ["gel0:%d" % ot for ot in range(4)]
        P.op("act", lambda e: e.activation(out=gel[1], in_=gel[0], func=AF.Square), yk, ["gel1"])
        P.op("dve", lambda e: e.tensor_scalar(out=gel[1], in0=gel[1], scalar1=0.044715 * 1.5957691216, scalar2=1.5957691216, op0=ALU.mult, op1=ALU.add), ["gel1"], ["gel1"])
        P.op("dve", lambda e: e.tensor_tensor(out=gel[1], in0=gel[1], in1=gel[0], op=ALU.mult), ["gel1"] + yk, ["gel1"])
        P.op("act", lambda e: e.activation(out=gel[1], in_=gel[1], func=AF.Sigmoid), ["gel1"], ["gel1"])
        P.op("dve", lambda e, fc=fc: e.tensor_tensor(out=ysg[:, fc, :], in0=gel[1], in1=gel[0], op=ALU.mult), ["gel1"] + yk, ["ysg"])
    for n in ["uT0", "X0p0", "X0p1", "Tp0", "Tp1", "Xo0_0", "Xo0_1", "Xo1_0", "Xo1_1", "XB0_0", "XB0_1", "XB1_0", "XB1_1", "WB0", "WB1", "WB2", "WB3", "WC0", "WC1", "WC2", "WC3"] + ["R%d" % i for i in range(52)] + ["Rt0", "Rt1", "Rt2", "Rt3", "bm0", "bm1",
              "gel0", "gel1", "S1", "S2", "Cst", "Bb"]:
        A.free(n)

    if debug:
        ysg_dbg = dscr("ysg_dbg", [128, 8 * NOWN])
        dma(ysg_dbg, ysg.rearrange("p k t -> p (k t)"), reads=["ysg"], writes=["ysg_dbg"])
    lq = A.alloc("lq", 256, F32).rearrange("p (a d) -> p a d", a=4)
    for a in range(4):
        dma(lq[:, a, :], lqk[a].partition_broadcast(128), writes=["lq"])
    lamt = A.alloc("lamt", 8, F32)
    lqp = A.alloc("lqp", 128, F32).rearrange("p (a d) -> p a d", a=2)
    P.op("dve", lambda e: e.tensor_tensor(out=lqp[:, 0, :], in0=lq[:, 0, :], in1=lq[:, 1, :], op=ALU.mult), ["lq"], ["lqp"])
    P.op("dve", lambda e: e.tensor_tensor(out=lqp[:, 1, :], in0=lq[:, 2, :], in1=lq[:, 3, :], op=ALU.mult), ["lq"], ["lqp"])
    P.op("dve", lambda e: e.tensor_reduce(out=lamt[:, 0:2], in_=lqp, axis=AX.X, op=ALU.add), ["lqp"], ["lamt"])
    P.op("act", lambda e: e.activation(out=lamt[:, 2:4], in_=lamt[:, 0:2], func=AF.Exp), ["lamt"], ["lamt:e"])
    P.op("dve", lambda e: e.tensor_tensor(out=lamt[:, 4:5], in0=lamt[:, 3:4], in1=lamt[:, 2:3], op=ALU.subtract), ["lamt:e"], ["lamt:d"])
    P.op("dve", lambda e: e.tensor_scalar(out=lamt[:, 5:6], in0=lamt[:, 4:5], scalar1=-0.2, scalar2=None, op0=ALU.add), ["lamt:d"], ["neglam"])
    hn = A.alloc("hn", 128, F32)
    dma(hn, head_norm.partition_broadcast(128), writes=["hn"])
    P.op("dve", lambda e: e.tensor_scalar(out=hn, in0=hn, scalar1=0.8, scalar2=None, op0=ALU.mult), ["hn"], ["hn"])
    kvf = A.alloc("kvf", 64, F32)
    dma(kvf, kvalid.rearrange("(b p) -> p b", p=128), writes=["kvf"], slow=True)

    ya = A.alloc("ya", 16 * 1024, BF16).rearrange("p (s f) -> p s f", s=16)
    Kh = [A.alloc("Kh%d" % i, 2 * SEQ, BF16)[0:64, :].rearrange("p (m t) -> p m t", m=2) for i in range(2)]
    Vh = [A.alloc("Vh%d" % i, 64 * 128, BF16).rearrange("p (b d) -> p b d", b=64) for i in range(1)]
    Qh = [A.alloc("Qh%d" % i, 2 * NOWN, BF16)[0:64, :].rearrange("p (m t) -> p m t", m=2) for i in range(1)]
    PT = [A.alloc("PT%d" % i, 1024, BF16).rearrange("p (m q) -> p m q", m=2) for i in range(2)]
    Esel = A.alloc("Esel", 256, BF16).rearrange("p (m c) -> p m c", m=2)
    Esel_f = A.alloc("Esel_f", 256, F32)
    dma(Esel_f, esel_d, writes=["Esel_f"])
    P.op("pool", lambda e: e.tensor_copy(out=Esel.rearrange("p m c -> p (m c)"), in_=Esel_f), ["Esel_f"], ["Esel"])
    denrow = [A.alloc("denrow%d" % i, 512, F32) for i in range(2)]
    rcol = [A.alloc("rcol%d" % i, 128, F32) for i in range(2)]
    OT = [A.alloc("OT%d" % i, 1024, BF16).rearrange("p (m q) -> p m q", m=2) for i in range(2)]
    ones_f = A.alloc("ones_f", 1, F32)
    P.op("pool", lambda e: e.memset(ones_f, 1.0), [], ["ones_f"])
    ep = [A.alloc("ep%d" % i, 8, F32) for i in range(2)]
    eo = [A.alloc("eo%d" % i, 384, F32) for i in range(2)]
    v_dh = v_d.rearrange("(b p) (h d) -> p b h d", p=128, h=8)
    actr = [0]
    tpv = bank_bf(7).rearrange("p (i m d) -> p i m d", i=4, m=2)
    dcol = bank(6)[:, 0:128]
    for h in range(8):
        K = Kh[h % 2]; V = Vh[0]; Q = Qh[0]
        kk_ = "Kh%d" % (h % 2)
        for m in range(2):
            dma(K[:, m, :], kT_d[h, m * 64:(m + 1) * 64, :], reads=["kT_d"], writes=[kk_], q="sp")
            dma(Q[:, m, :], qT_d[h, m * 64:(m + 1) * 64, :], reads=["qT_d"], writes=["Qh0"], q="sp")
        for vq in range(4):
            dma(V[:, vq * 16:(vq + 1) * 16, :], v_dh[:, vq * 16:(vq + 1) * 16, h, :], reads=["v_d"], writes=["Vh0"], q="sp")
        for G in range(4):
            gj = (h * 4 + G) % 2
            dr = denrow[gj]; drk = "denrow%d" % gj
            ot_ = OT[gj]; otk = "OT%d" % gj
            nkb = 16 * G + 16
            base_i = actr[0]
            actr[0] += nkb

            def emit_scores(kb, G=G, K=K, kk_=kk_, base_i=base_i):
                rel_ = kb - 16 * G - 3
                i0_ = 0 if rel_ <= 0 else (rel_ + 3) // 4
                c0 = i0_ * 128
                idiag = rel_ // 4 if (rel_ >= 0 and rel_ % 4 == 0) else -1
                pj = (base_i + kb) % 2
                pt = PT[pj]; ptk = "PT%d" % pj
                for m in range(2):
                    pb = 2 * pj + m
                    P.op("pe", lambda e, pb=pb, kb=kb, m=m, c0=c0: e.matmul(out=bank(pb)[:, c0:512], lhsT=K[:, m, kb * 128:(kb + 1) * 128], rhs=Q[:, m, G * 512 + c0:(G + 1) * 512], start=True, stop=True), [kk_, "Qh0"], [bkey(pb)])
                    P.op("act", lambda e, pb=pb, pt=pt, m=m, c0=c0: e.activation(out=pt[:, m, c0:512], in_=bank(pb)[:, c0:512], func=AF.Exp, scale=0.125), [bkey(pb)], [ptk + ":%d" % m])
                    if idiag >= 0:
                        P.op("dve", lambda e, pt=pt, m=m, idiag=idiag: e.memset(pt[64:128, m, idiag * 128:idiag * 128 + 64], 0.0), [ptk + ":%d" % m], [ptk + ":%d" % m])
                    if kb < 3:
                        P.op("dve", lambda e, pt=pt, m=m, kb=kb: e.tensor_scalar(out=pt[:, m, :], in0=pt[:, m, :], scalar1=kvf[:, kb:kb + 1], scalar2=None, op0=ALU.mult), [ptk + ":%d" % m, "kvf"], [ptk + ":%d" % m])

            def emit_pv(kb, G=G, V=V, nkb=nkb, base_i=base_i):
                rel_ = kb - 16 * G - 3
                i0_ = 0 if rel_ <= 0 else (rel_ + 3) // 4
                c0 = i0_ * 128
                pj = (base_i + kb) % 2
                pt = PT[pj]; ptk = "PT%d" % pj
                for m in range(2):
                    P.op("pe", lambda e, kb=kb, m=m, pt=pt, c0=c0: e.matmul(out=bank(4 + m)[:, c0:512], lhsT=V[:, kb, :], rhs=pt[:, m, c0:512], start=(kb == 0), stop=(kb == nkb - 1)), [ptk + ":%d" % m, "Vh0"], [bkey(4 + m)])
                    P.op("pe", lambda e, kb=kb, m=m, pt=pt, c0=c0: e.matmul(out=bank(6)[:, c0:512], lhsT=Esel[:, m, :], rhs=pt[:, m, c0:512], start=(kb == 0 and m == 0), stop=(kb == nkb - 1 and m == 1)), [ptk + ":%d" % m, "Esel"], [bkey(6)])

            emit_scores(0)
            for kb in range(nkb):
                if kb + 1 < nkb:
                    emit_scores(kb + 1)
                emit_pv(kb)
            P.op("act", lambda e, ot_=ot_: e.activation(out=ot_[:, 0, :], in_=bank(4), func=AF.Copy), [bkey(4)], [otk + ":0"])
            P.op("dve", lambda e, ot_=ot_: e.tensor_copy(out=ot_[:, 1, :], in_=bank(5)), [bkey(5)], [otk + ":1"])
            P.op("dve", lambda e, dr=dr: e.tensor_copy(out=dr, in_=bank(6)), [bkey(6)], [drk])
            for isl in range(4):
                P.op("pe", lambda e, isl=isl, dr=dr: e.matmul(out=dcol[:, isl * 32:(isl + 1) * 32], lhsT=dr[:, isl * 128:(isl + 1) * 128], rhs=ident_f[:, 0:32], start=True, stop=True), [drk, "ident_f"], [bkey(6)])
            rc = rcol[gj]; rck = "rcol%d" % gj
            P.op("dve", lambda e, rc=rc: e.reciprocal(out=rc, in_=dcol), [bkey(6)], [rck])
            for isl in range(4):
                for m in range(2):
                    P.op("pe", lambda e, isl=isl, m=m, ot_=ot_: e.transpose(out=tpv[:, isl, m, :], in_=ot_[:, m, isl * 128:(isl + 1) * 128], identity=ident_b), [otk + ":%d" % m, "ident_b"], [bkey(7)])
            for isl in range(4):
                s_ = G * 4 + isl
                j = (h * 16 + s_) % 2
                e_ = ep[j]; o_ = eo[j]; ek = "ep%d" % j; ok_ = "eo%d" % j
                P.op("dve", lambda e, e_=e_, isl=isl, rc=rc: e.tensor_copy(out=e_[:, 0:2], in_=rc[:, isl * 32:isl * 32 + 2]), [rck], [ek])
                P.op("dve", lambda e, e_=e_: e.tensor_tensor(out=e_[:, 2:3], in0=e_[:, 1:2], in1=lamt[:, 5:6], op=ALU.mult), [ek, "neglam"], [ek + ":2"])
                P.op("dve", lambda e, e_=e_, o_=o_, isl=isl: e.tensor_scalar(out=o_[:, 0:128], in0=tpv[:, isl, 1, :], scalar1=e_[:, 2:3], scalar2=None, op0=ALU.mult), [bkey(7), ek + ":2"], [ok_])
                P.op("dve", lambda e, e_=e_, o_=o_, isl=isl: e.scalar_tensor_tensor(out=o_[:, 128:256], in0=tpv[:, isl, 0, :], scalar=e_[:, 0:1], in1=o_[:, 0:128], op0=ALU.mult, op1=ALU.add), [bkey(7), ek, ok_], [ok_ + ":1"])
                P.op("act", lambda e, e_=e_, o_=o_: e.activation(out=o_[:, 256:384], in_=o_[:, 128:256], func=AF.Square, accum_out=e_[:, 3:4]), [ok_ + ":1"], [ok_ + ":2", ek + ":3"])
                P.op("act", lambda e, e_=e_: e.activation(out=e_[:, 4:5], in_=e_[:, 3:4], func=AF.Ln, scale=1.0 / 128, bias=epst), [ek + ":3", "epst"], [ek + ":4"])
                P.op("act", lambda e, e_=e_: e.activation(out=e_[:, 5:6], in_=e_[:, 4:5], func=AF.Exp, scale=-0.5), [ek + ":4"], [ek + ":5"])
                P.op("dve", lambda e, e_=e_, o_=o_, s_=s_, h=h: e.scalar_tensor_tensor(out=ya[:, s_, h * 128:(h + 1) * 128], in0=o_[:, 128:256], scalar=e_[:, 5:6], in1=hn, op0=ALU.mult, op1=ALU.mult), [ok_ + ":1", ek + ":5", "hn"], ["ya"])
    for n in ["Kh0", "Kh1", "Vh0", "Qh0", "PT0", "PT1", "denrow0", "denrow1", "rcol0", "rcol1", "Esel_f", "OT0", "OT1", "ep0", "ep1", "eo0", "eo1", "lq", "lqp", "kvf"]:
        A.free(n)

    if debug:
        ya_dbg = dscr("ya_dbg", [128, 16 * 1024])
        dma(ya_dbg, ya.rearrange("p s f -> p (s f)"), reads=["ya"], writes=["ya_dbg"])
    def load_bf16(name):
        dst_d, src_, K_, N_, gn_ = wsc[name]
        KC = K_ // 128
        wt = A.alloc(name, KC * N_, BF16).rearrange("p (k n) -> p k n", k=KC)
        for kc in range(KC):
            dma(wt[:, kc, :], dst_d[kc * 128:(kc + 1) * 128, :], reads=[name + "_d"], writes=[name], q="sp" if kc % 2 == 0 else "act")
        return wt

    gpost = A.alloc("gpost", 1024, F32)
    pst = A.alloc("pst", 8, F32)
    psq = A.alloc("psq", 1024, BF16)
    ptmp = A.alloc("ptmp", 1024, F32)
    ost = [A.alloc("ost%d" % i, 1024, F32) for i in range(2)]
    xres = [A.alloc("xres%d" % i, 1024, F32) for i in range(2)]

    def post_norm_residual(pb0, gain_bc, gkey, res_in, res_in_keys, res_out, res_out_key):
        for half in range(2):
            P.op("act", lambda e, half=half: e.activation(out=psq[:, half * 512:(half + 1) * 512], in_=bank(pb0 + half), func=AF.Square, accum_out=pst[:, half:half + 1]), [bkey(pb0 + half)], ["psq", "pst:%d" % half])
        P.op("dve", lambda e: e.tensor_tensor(out=pst[:, 2:3], in0=pst[:, 0:1], in1=pst[:, 1:2], op=ALU.add), ["pst:0", "pst:1"], ["pst:2"])
        P.op("act", lambda e: e.activation(out=pst[:, 3:4], in_=pst[:, 2:3], func=AF.Ln, scale=1.0 / D, bias=epst), ["pst:2", "epst"], ["pst:3"])
        P.op("act", lambda e: e.activation(out=pst[:, 4:5], in_=pst[:, 3:4], func=AF.Exp, scale=-0.5), ["pst:3"], ["pst:4"])
        for half in range(2):
            P.op("dve", lambda e, half=half: e.scalar_tensor_tensor(out=ptmp[:, half * 512:(half + 1) * 512], in0=bank(pb0 + half), scalar=pst[:, 4:5], in1=gain_bc[:, half * 512:(half + 1) * 512], op0=ALU.mult, op1=ALU.mult), [bkey(pb0 + half), "pst:4", gkey], ["ptmp:%d" % half])
        P.op("pool", lambda e: e.tensor_tensor(out=res_out, in0=ptmp, in1=res_in, op=ALU.add), ["ptmp:0", "ptmp:1"] + res_in_keys, [res_out_key])

    wglu = load_bf16("wglu")
    wssm = load_bf16("wssm")
    ys2 = A.alloc("ys2", 8 * 512, BF16).rearrange("p (k t) -> p k t", k=8)
    gab = A.alloc("gab", 8 * 512, BF16).rearrange("p (k t) -> p k t", k=8)
    sg = [A.alloc("sg%d" % i, 512, BF16) for i in range(2)]
    for tt in range(4):
        dma(gab, g_d[0:8, :, tt * 512:(tt + 1) * 512].rearrange("k p t -> p k t"), reads=["g_d"], writes=["gab"], q="pool")
        for mc in range(8):
            pb = 2 + mc % 2
            for kc in range(8):
                P.op("pe", lambda e, kc=kc, mc=mc, pb=pb, tt=tt: e.matmul(out=bank(pb), lhsT=wglu[:, kc, mc * 128:(mc + 1) * 128], rhs=ysg[:, kc, tt * 512:(tt + 1) * 512], start=(kc == 0), stop=(kc == 7)), ["wglu", "ysg"], [bkey(pb)])
            j = mc % 2
            P.op("act", lambda e, pb=pb, j=j, mc=mc: e.activation(out=sg[j], in_=bank(pb), func=AF.Sigmoid, bias=bglu_pp[:, mc:mc + 1]), [bkey(pb), "bglu_pp"], ["sg%d" % j])
            P.op("pool", lambda e, j=j, mc=mc, tt=tt: e.tensor_tensor(out=ys2[:, mc, :], in0=sg[j], in1=ysg[:, mc, tt * 512:(tt + 1) * 512], op=ALU.mult), ["sg%d" % j, "ysg"], ["ys2"])
        for mc in range(8):
            pa = 4 + (mc % 2)
            for kc in range(8):
                P.op("pe", lambda e, kc=kc, mc=mc, pa=pa: e.matmul(out=bank(pa), lhsT=wssm[:, kc, mc * 128:(mc + 1) * 128], rhs=ys2[:, kc, :], start=(kc == 0), stop=(kc == 7)), ["wssm", "ys2"], [bkey(pa)])
            P.op("dve", lambda e, pa=pa, mc=mc, tt=tt: e.tensor_tensor(out=ysg[:, mc, tt * 512:(tt + 1) * 512], in0=bank(pa), in1=gab[:, mc, :], op=ALU.mult), [bkey(pa), "gab"], ["ysg"])
    for n in ["wglu", "wssm", "ys2", "sg0", "sg1"]:
        A.free(n)
    wda = load_bf16("wda")
    wmix = load_bf16("wmix")
    dma(gpost, gains["norm_mix_post"].partition_broadcast(128), writes=["gpost"])
    yaT = A.alloc("yaT", 8 * 512, BF16).rearrange("p (k t) -> p k t", k=8)
    mrg = A.alloc("mrg", 8 * 512, BF16).rearrange("p (k t) -> p k t", k=8)
    tb = [A.alloc("tb%d" % i, 512, F32) for i in range(2)]
    for tt in range(4):
        for bl in range(4):
            s = tt * 4 + bl
            tpb = bl % 2
            tp = bank_bf(tpb).rearrange("p (k t) -> p k t", k=8)
            for kc in range(8):
                P.op("pe", lambda e, kc=kc, s=s, tp=tp: e.transpose(out=tp[:, kc, :], in_=ya[:, s, kc * 128:(kc + 1) * 128], identity=ident_b), ["ya", "ident_b"], [bkey(tpb)])
            P.op("act", lambda e, tp=tp, bl=bl: e.activation(out=yaT[:, :, bl * 128:(bl + 1) * 128], in_=tp, func=AF.Copy), [bkey(tpb)], ["yaT"])
        dma(gab, g_d[8:16, :, tt * 512:(tt + 1) * 512].rearrange("k p t -> p k t"), reads=["g_d"], writes=["gab"], q="pool")
        for mc in range(8):
            pbb_ = 4 + (mc % 2)
            for kc in range(8):
                P.op("pe", lambda e, kc=kc, mc=mc, pbb_=pbb_: e.matmul(out=bank(pbb_), lhsT=wda[:, kc, mc * 128:(mc + 1) * 128], rhs=yaT[:, kc, :], start=(kc == 0), stop=(kc == 7)), ["wda", "yaT"], [bkey(pbb_)])
            j = mc % 2
            P.op("dve", lambda e, pbb_=pbb_, j=j, mc=mc: e.tensor_tensor(out=tb[j], in0=bank(pbb_), in1=gab[:, mc, :], op=ALU.mult), [bkey(pbb_), "gab"], ["tb%d" % j])
            P.op("pool", lambda e, j=j, mc=mc, tt=tt: e.tensor_tensor(out=mrg[:, mc, :], in0=tb[j], in1=ysg[:, mc, tt * 512:(tt + 1) * 512], op=ALU.add), ["tb%d" % j, "ysg"], ["mrg"])
        for bl in range(4):
            s = tt * 4 + bl
            pb0 = 2 * (bl % 2)
            for half in range(2):
                for kc in range(8):
                    P.op("pe", lambda e, kc=kc, bl=bl, half=half, pb0=pb0: e.matmul(out=bank(pb0 + half), lhsT=mrg[:, kc, bl * 128:(bl + 1) * 128], rhs=wmix[:, kc, half * 512:(half + 1) * 512], start=(kc == 0), stop=(kc == 7)), ["mrg", "wmix"], [bkey(pb0 + half)])
            xr = xres[s % 2]; xrk = "xres%d" % (s % 2)
            dma(xr, xown[s * 128:(s + 1) * 128, :], writes=[xrk], q="sp")
            o_ = ost[s % 2]; ok_ = "ost%d" % (s % 2)
            post_norm_residual(pb0, gpost, "gpost", xr, [xrk], o_, ok_)
            dma(x1_d[s * 128:(s + 1) * 128, :], o_, reads=[ok_], writes=["x1_d"], q="sp")
    for n in ["wda", "wmix", "yaT", "mrg", "gab", "tb0", "tb1", "ya", "ysg"]:
        A.free(n)

    wxkv = load_bf16("wxkv")
    memT = A.alloc("memT", 8 * 256, BF16).rearrange("p (k t) -> p k t", k=8)
    for mb in range(2):
        norm_block_T(mem[mb * 128:(mb + 1) * 128, :], True, memT[:, :, mb * 128:(mb + 1) * 128], "memT")
    mkT = A.alloc("mkT", 8 * 256, BF16).rearrange("p (k t) -> p k t", k=8)
    mv = A.alloc("mv", 2 * 1024, BF16).rearrange("p (m f) -> p m f", m=2)
    for mc in range(8):
        pb = mc % 2
        for kc in range(8):
            P.op("pe", lambda e, kc=kc, mc=mc, pb=pb: e.matmul(out=bank(pb)[:, 0:256], lhsT=wxkv[:, kc, mc * 128:(mc + 1) * 128], rhs=memT[:, kc, :], start=(kc == 0), stop=(kc == 7)), ["wxkv", "memT"], [bkey(pb)])
        P.op("act", lambda e, pb=pb, mc=mc: e.activation(out=mkT[:, mc, :], in_=bank(pb)[:, 0:256], func=AF.Copy), [bkey(pb)], ["mkT"])
    for mt in range(2):
        for half in range(2):
            pb = 2 + half
            for kc in range(8):
                P.op("pe", lambda e, kc=kc, mt=mt, half=half, pb=pb: e.matmul(out=bank(pb), lhsT=memT[:, kc, mt * 128:(mt + 1) * 128], rhs=wxkv[:, kc, 1024 + half * 512:1024 + (half + 1) * 512], start=(kc == 0), stop=(kc == 7)), ["wxkv", "memT"], [bkey(pb)])
            P.op("act", lambda e, pb=pb, mt=mt, half=half: e.activation(out=mv[:, mt, half * 512:(half + 1) * 512], in_=bank(pb), func=AF.Copy), [bkey(pb)], ["mv"])
    A.free("wxkv"); A.free("memT")
    wxq = load_bf16("wxq")
    wxo = load_bf16("wxo")
    dma(gpost, gains["norm_x_post"].partition_broadcast(128), writes=["gpost"])
    ones_b = A.alloc("ones_b", 128, BF16)
    P.op("pool", lambda e: e.memset(ones_b, 1.0), [], ["ones_b"])
    h2T = A.alloc("h2T", 8 * 512, BF16).rearrange("p (k t) -> p k t", k=8)
    xqT = A.alloc("xqT", 8 * 512, BF16).rearrange("p (k t) -> p k t", k=8)
    xoT = A.alloc("xoT", 8 * 512, BF16).rearrange("p (k t) -> p k t", k=8)
    xp = [A.alloc("xp%d" % i, 2 * 512, BF16).rearrange("p (m t) -> p m t", m=2) for i in range(2)]
    rden = [A.alloc("rden%d" % i, 512, F32) for i in range(2)]
    for tt in range(4):
        for bl in range(4):
            s = tt * 4 + bl
            norm_block_T(x1_d[s * 128:(s + 1) * 128, :], True, h2T[:, :, bl * 128:(bl + 1) * 128], "h2T", tp_bank=7)
        for mc in range(8):
            pb = mc % 2
            for kc in range(8):
                P.op("pe", lambda e, kc=kc, mc=mc, pb=pb: e.matmul(out=bank(pb), lhsT=wxq[:, kc, mc * 128:(mc + 1) * 128], rhs=h2T[:, kc, :], start=(kc == 0), stop=(kc == 7)), ["wxq", "h2T"], [bkey(pb)])
            P.op("act", lambda e, pb=pb, mc=mc: e.activation(out=xqT[:, mc, :], in_=bank(pb), func=AF.Copy), [bkey(pb)], ["xqT"])
        for hh in range(4):
            j = hh % 2
            for mt in range(2):
                pb = 2 + mt
                for dc in range(2):
                    P.op("pe", lambda e, hh=hh, mt=mt, dc=dc, pb=pb: e.matmul(out=bank(pb), lhsT=mkT[:, hh * 2 + dc, mt * 128:(mt + 1) * 128], rhs=xqT[:, hh * 2 + dc, :], start=(dc == 0), stop=(dc == 1)), ["mkT", "xqT"], [bkey(pb)])
                P.op("act", lambda e, pb=pb, j=j, mt=mt: e.activation(out=xp[j][:, mt, :], in_=bank(pb), func=AF.Exp, scale=1.0 / 16), [bkey(pb)], ["xp%d" % j])
            for mt in range(2):
                P.op("pe", lambda e, j=j, mt=mt: e.matmul(out=bank(4), lhsT=ones_b, rhs=xp[j][:, mt, :], start=(mt == 0), stop=(mt == 1)), ["ones_b", "xp%d" % j], [bkey(4)])
            P.op("dve", lambda e, j=j: e.reciprocal(out=rden[j], in_=bank(4)), [bkey(4)], ["rden%d" % j])
            for dc in range(2):
                pb = 5 + dc
                for mt in range(2):
                    P.op("pe", lambda e, hh=hh, j=j, mt=mt, dc=dc, pb=pb: e.matmul(out=bank(pb), lhsT=mv[:, mt, (hh * 2 + dc) * 128:(hh * 2 + dc + 1) * 128], rhs=xp[j][:, mt, :], start=(mt == 0), stop=(mt == 1)), ["mv", "xp%d" % j], [bkey(pb)])
                P.op("dve", lambda e, hh=hh, j=j, dc=dc, pb=pb: e.tensor_tensor(out=xoT[:, hh * 2 + dc, :], in0=bank(pb), in1=rden[j], op=ALU.mult), [bkey(pb), "rden%d" % j], ["xoT"])
        for bl in range(4):
            s = tt * 4 + bl
            pb0 = 2 * (bl % 2)
            for half in range(2):
                for kc in range(8):
                    P.op("pe", lambda e, kc=kc, bl=bl, half=half, pb0=pb0: e.matmul(out=bank(pb0 + half), lhsT=xoT[:, kc, bl * 128:(bl + 1) * 128], rhs=wxo[:, kc, half * 512:(half + 1) * 512], start=(kc == 0), stop=(kc == 7)), ["xoT", "wxo"], [bkey(pb0 + half)])
            xr = xres[s % 2]; xrk = "xres%d" % (s % 2)
            dma(xr, x1_d[s * 128:(s + 1) * 128, :], reads=["x1_d"], writes=[xrk], q="sp")
            o_ = ost[s % 2]; ok_ = "ost%d" % (s % 2)
            post_norm_residual(pb0, gpost, "gpost", xr, [xrk], o_, ok_)
            dma(x2_d[s * 128:(s + 1) * 128, :], o_, reads=[ok_], writes=["x2_d"], q="sp")
    for n in ["wxq", "wxo", "mkT", "mv", "h2T", "xqT", "xoT", "xp0", "xp1", "rden0", "rden1"]:
        A.free(n)

    dma(gpost, gains["norm_ff_post"].partition_broadcast(128), writes=["gpost"])
    for n in ["wstage0", "wstage1", "xs0", "xs1", "nsq0", "nsq1", "nh0", "nh1"]:
        if n in A.live:
            A.free(n)
    f1 = A.alloc("f1", 32 * 512, BF16).rearrange("p (k t) -> p k t", k=32)
    wq1 = [A.alloc("wq1_%d" % i, 8 * 1024, BF16).rearrange("p (k n) -> p k n", k=8) for i in range(2)]
    h3T = A.alloc("h3T", 8 * 512, BF16).rearrange("p (k t) -> p k t", k=8)
    fr = [A.alloc("fr%d" % i, 512, BF16) for i in range(2)]
    wctr2 = [0]
    for tt in range(4):
        for bl in range(4):
            s = tt * 4 + bl
            norm_block_T(x2_d[s * 128:(s + 1) * 128, :], True, h3T[:, :, bl * 128:(bl + 1) * 128], "h3T", tp_bank=7)
        for q4 in range(4):
            i = wctr2[0]; wctr2[0] += 1
            w1 = wq1[i % 2]; w1k = "wq1_%d" % (i % 2)
            dma(w1, wf1_d[:, q4 * 1024:(q4 + 1) * 1024].rearrange("(k p) n -> p k n", p=128), reads=["wf_d"], writes=[w1k], q="sp")
            for fl in range(8):
                fc = q4 * 8 + fl
                pb = 4 + fc % 2
                for kc in range(8):
                    P.op("pe", lambda e, kc=kc, fl=fl, pb=pb, w1=w1: e.matmul(out=bank(pb), lhsT=w1[:, kc, fl * 128:(fl + 1) * 128], rhs=h3T[:, kc, :], start=(kc == 0), stop=(kc == 7)), [w1k, "h3T"], [bkey(pb)])
                j = fc % 2
                P.op("act", lambda e, pb=pb, j=j: e.activation(out=fr[j], in_=bank(pb), func=AF.Relu), [bkey(pb)], ["fr%d" % j])
                P.op("pool", lambda e, j=j, fc=fc: e.tensor_tensor(out=f1[:, fc, :], in0=fr[j], in1=fr[j], op=ALU.mult), ["fr%d" % j], ["f1"])
        for q4 in range(4):
            i = wctr2[0]; wctr2[0] += 1
            w2 = wq1[i % 2]; w2k = "wq1_%d" % (i % 2)
            dma(w2, wf2_d[q4 * 1024:(q4 + 1) * 1024, :].rearrange("(k p) n -> p k n", p=128), reads=["wf_d"], writes=[w2k], q="sp")
            for bl in range(4):
                for half in range(2):
                    pbk_ = bl * 2 + half
                    for kcl in range(8):
                        P.op("pe", lambda e, kcl=kcl, bl=bl, half=half, pbk_=pbk_, q4=q4, w2=w2: e.matmul(out=bank(pbk_), lhsT=f1[:, q4 * 8 + kcl, bl * 128:(bl + 1) * 128], rhs=w2[:, kcl, half * 512:(half + 1) * 512], start=(q4 == 0 and kcl == 0), stop=(q4 == 3 and kcl == 7)), ["f1", w2k], [bkey(pbk_)])
        for bl in range(4):
            s = tt * 4 + bl
            pb0 = 2 * bl
            xr = xres[s % 2]; xrk = "xres%d" % (s % 2)
            dma(xr, x2_d[s * 128:(s + 1) * 128, :], reads=["x2_d"], writes=[xrk], q="sp")
            o_ = ost[s % 2]; ok_ = "ost%d" % (s % 2)
            post_norm_residual(pb0, gpost, "gpost", xr, [xrk], o_, ok_)
            dma(out_d[s * 128:(s + 1) * 128, :], o_, reads=[ok_], writes=["out_d"], q="sp")

    P.emit()
    es.close()
    return nc


def _rope_tables(pos):
    inv = (10000.0 ** (-np.arange(0, 64, 2, dtype=np.float32) / 64)).astype(np.float32)
    ang = pos.astype(np.float32)[:, None] * inv[None, :]
    c = np.cos(ang).astype(np.float32).T
    s = np.sin(ang).astype(np.float32).T
    return np.ascontiguousarray(np.tile(c, (4, 1))), np.ascontiguousarray(np.tile(s, (4, 1)))


_NC_CACHE = {}


def make_in_maps(inputs):
    x = np.asarray(inputs["x"], dtype=np.float32)
    memv = np.asarray(inputs["mem"], dtype=np.float32)
    ident = np.eye(128, dtype=np.float32)
    swapm = np.zeros((128, 128), np.float32)
    for p in range(64):
        swapm[p, 64 + p] = 1.0
        swapm[64 + p, p] = 1.0
    gmask = np.zeros((128, 8, 128), np.float32)
    for gl in range(8):
        gmask[:, gl, gl * 16:(gl + 1) * 16] = 1.0
    esel = np.zeros((128, 256), np.float32)
    esel[:, 0] = 1.0
    esel[:, 129] = 1.0
    shared = {}
    for k, v in inputs.items():
        if k in ("x", "mem"):
            continue
        a = np.asarray(v, dtype=np.float32)
        shared[k] = np.ascontiguousarray(a[0])
    in_maps = []
    for c in range(8):
        b, j = c // 4, c % 4
        pad = (3 - j) * 128
        xs = np.zeros((SEQ, D), np.float32)
        xs[pad:] = x[b, :SEQ - pad]
        own_blocks = [4 * s + j for s in range(16)]
        xo = np.concatenate([x[b, r * 128:(r + 1) * 128] for r in own_blocks], axis=0)
        pos_seq = np.arange(SEQ) - pad
        cseq, sseq = _rope_tables(pos_seq)
        pos_own = np.concatenate([np.arange(r * 128, (r + 1) * 128) for r in own_blocks])
        cown, sown = _rope_tables(pos_own)
        kval = (pos_seq >= 0).astype(np.float32)
        m = dict(shared)
        m.update(xseq=xs, xown=np.ascontiguousarray(xo), mem=np.ascontiguousarray(memv[b]), cos_seq=cseq, sin_seq=sseq,
                 cos_own=cown, sin_own=sown, kvalid=kval, esel=esel, ident=ident, swapm=swapm, gmask=gmask)
        in_maps.append(m)
    return in_maps


def kernel(**inputs):
    if "nc" not in _NC_CACHE:
        _NC_CACHE["nc"] = build_program()
    nc = _NC_CACHE["nc"]
    in_maps = make_in_maps(inputs)
    res = run_bass_kernel_spmd(nc, in_maps, core_ids=list(range(8)))
    out = np.zeros((2, SEQ, D), np.float32)
    for c in range(8):
        b, j = c // 4, c % 4
        o = res.results[c]["out"]
        for s in range(16):
            r = 4 * s + j
            out[b, r * 128:(r + 1) * 128] = o[s * 128:(s + 1) * 128]
    return out
```

```python
import contextlib
import math
import numpy as np
import concourse.bass as bass
import concourse.mybir as mybir
from concourse.bass_utils import run_bass_kernel_spmd

F32 = mybir.dt.float32
BF16 = mybir.dt.bfloat16
ALU = mybir.AluOpType
AF = mybir.ActivationFunctionType
AX = mybir.AxisListType

D = 1024
SEQ = 8192
NB = 64
NOWN = 2048
EPS = 1e-6
NLEV = 13
ENGS = ["pe", "act", "dve", "pool", "sp"]


class Op:
    __slots__ = ("eng", "idx", "fn", "deps", "is_dma", "needs_inc", "semval", "dsem", "dval")

    def __init__(self, eng, idx, fn, is_dma):
        self.eng, self.idx, self.fn, self.is_dma = eng, idx, fn, is_dma
        self.deps = []
        self.needs_inc = False
        self.semval = None
        self.dsem = None
        self.dval = None


class Prog:
    def __init__(self, nc, n_dma_sems=16):
        self.nc = nc
        self.ops = {e: [] for e in ENGS}
        self.state = {}
        self.rings = {"sp": (0, 12), "pool": (12, 8), "act": (20, 8), "dve": (28, 2), "pe": (28, 2)}
        n_dma_sems = 30
        self.n_dma_sems = n_dma_sems
        self.dma_rr = {q: 0 for q in self.rings}
        self.dma_counts = [0] * n_dma_sems
        self.waited = {}
        self.waited_dma = {}

    def _st(self, key):
        s = self.state.get(key)
        if s is None:
            s = {"w": {}, "r": {}}
            if isinstance(key, str) and ":" in key:
                base = self.state.get(key.split(":")[0])
                if base is not None:
                    s["w"] = dict(base["w"])
            self.state[key] = s
        return s

    def _add_dep(self, op, dep):
        if dep is None or dep is op:
            return
        if dep.is_dma:
            k = (op.eng, dep.dsem)
            if self.waited_dma.get(k, -1) >= dep.dval:
                return
            self.waited_dma[k] = dep.dval
            op.deps.append(dep)
            return
        k = (op.eng, dep.eng)
        if self.waited.get(k, -1) >= dep.idx:
            return
        self.waited[k] = dep.idx
        dep.needs_inc = True
        op.deps.append(dep)

    def op(self, eng, fn, reads=(), writes=(), dma=False):
        lst = self.ops[eng]
        o = Op(eng, len(lst), fn, dma)
        if dma:
            base, cnt_ = self.rings[eng]
            i = base + self.dma_rr[eng]
            self.dma_rr[eng] = (self.dma_rr[eng] + 1) % cnt_
            self.dma_counts[i] += 16
            o.dsem, o.dval = i, self.dma_counts[i]
        for key in reads:
            for e, w in self._st(key)["w"].items():
                if (not w.is_dma) and w.eng == eng and eng == "pe":
                    continue
                self._add_dep(o, w)
        for key in writes:
            s = self._st(key)
            for e, r in s["r"].items():
                if (not r.is_dma) and r.eng == eng and not dma:
                    continue
                self._add_dep(o, r)
            for e, w in s["w"].items():
                if (not w.is_dma) and w.eng == eng and not dma:
                    continue
                self._add_dep(o, w)
        me = ("dma%d" % o.dsem) if dma else eng
        for key in reads:
            self._st(key)["r"][me] = o
        for key in writes:
            s = self._st(key)
            s["w"][me] = o
            s["r"] = {}
        lst.append(o)
        return o

    def alias(self, newkey, oldkeys):
        ns = self._st(newkey)
        for ok in oldkeys:
            os_ = self.state.get(ok)
            if os_ is None:
                continue
            for kind in ("w", "r"):
                for e, o in os_[kind].items():
                    cur = ns["w"].get(e)
                    if cur is None or (o.is_dma and o.dval > cur.dval) or ((not o.is_dma) and o.idx > cur.idx):
                        ns["w"][e] = o

    def emit(self):
        nc = self.nc
        with contextlib.ExitStack() as es:
            sems = {e: es.enter_context(nc.semaphore("s_" + e)) for e in ["pe", "act", "dve", "pool"]}
            dsems = [es.enter_context(nc.semaphore("d%d" % i)) for i in range(self.n_dma_sems)]
            for e in ENGS:
                c = 0
                for o in self.ops[e]:
                    if (not o.is_dma) and o.needs_inc:
                        c += 1
                        o.semval = c
            block = es.enter_context(nc.Block())

            def run(name, eng):
                for o in self.ops[name]:
                    for d in o.deps:
                        if d.is_dma:
                            eng.wait_ge(dsems[d.dsem], d.dval)
                        else:
                            eng.wait_ge(sems[d.eng], d.semval)
                    ins = o.fn(eng)
                    if o.is_dma:
                        ins.then_inc(dsems[o.dsem], 16)
                    elif o.needs_inc:
                        ins.then_inc(sems[o.eng], 1)

            @block.tensor
            def _(eng):
                run("pe", eng)

            @block.scalar
            def _(eng):
                run("act", eng)

            @block.vector
            def _(eng):
                run("dve", eng)

            @block.gpsimd
            def _(eng):
                run("pool", eng)

            @block.sync
            def _(eng):
                run("sp", eng)
                for i in range(self.n_dma_sems):
                    if self.dma_counts[i] > 0:
                        eng.wait_ge(dsems[i], self.dma_counts[i])


class Arena:
    def __init__(self, P, base_ap, nbytes):
        self.P = P
        self.base = base_ap
        self.nbytes = nbytes
        self.live = {}
        self.freed = []

    def alloc(self, name, nelem, dt):
        esz = 4 if dt == F32 else 2
        size = (nelem * esz + 63) // 64 * 64
        segs = sorted(self.live.values())
        off = 0
        for (o, s) in segs:
            if off + size <= o:
                break
            off = max(off, o + s)
        assert off + size <= self.nbytes, "SBUF arena overflow for %s (%d): %s" % (name, size, sorted((o, sz, n) for n, (o, sz) in self.live.items()))
        self.live[name] = (off, size)
        olds = [n for (o, s, n) in self.freed if o < off + size and off < o + s]
        oldkeys = [k for k in self.P.state if any(k == n or (isinstance(k, str) and k.startswith(n + ":")) for n in olds)]
        self.P.alias(name, oldkeys)
        self._aliaskeys = oldkeys
        ap = self.base[:, off // 4:(off + size) // 4]
        if dt != F32:
            ap = ap.bitcast(dt)
        return ap[:, 0:nelem]

    def free(self, name):
        o, s = self.live.pop(name)
        self.freed.append((o, s, name))


def build_program(debug=False):
    nc = bass.Bass("TRN2", target_bir_lowering=False)

    def din(name, shape, dt=F32):
        return nc.dram_tensor(name, list(shape), dt, kind="ExternalInput").ap()

    def dscr(name, shape, dt=BF16):
        return nc.dram_tensor(name, list(shape), dt, kind="ExternalOutput" if debug else "Internal").ap()

    xseq = din("xseq", [SEQ, D])
    xown = din("xown", [NOWN, D])
    mem = din("mem", [256, D])
    w_in = din("w_in", [D, 6144])
    w_glu = din("w_glu", [D, D]); w_ssm = din("w_ssm_proj", [D, D]); w_da = din("w_da_proj", [D, D])
    w_mix = din("w_mix_out", [D, D]); w_xq = din("w_xq", [D, D]); w_xkv = din("w_xkv", [D, 2 * D])
    w_xo = din("w_xo", [D, D]); w_ff1 = din("w_ff1", [D, 4 * D]); w_ff2 = din("w_ff2", [4 * D, D])
    gains = {n: din(n, [D]) for n in ["norm_mix_pre", "norm_mix_post", "norm_x_pre", "norm_mem", "norm_x_post",
                                      "norm_ff_pre", "norm_ff_post"]}
    b_gate = din("b_gate", [2 * D]); b_glu = din("b_glu", [D]); ssm_d = din("ssm_d", [D])
    lam_re = din("ssm_lambda_re", [64, 64]); lam_im = din("ssm_lambda_im", [64, 64]); log_dt = din("ssm_log_dt", [64])
    b_re = din("ssm_b_re", [64, 64, 16]); b_im = din("ssm_b_im", [64, 64, 16])
    c_re = din("ssm_c_re", [64, 16, 64]); c_im = din("ssm_c_im", [64, 16, 64])
    lqk = [din(n, [64]) for n in ["da_lambda_q1", "da_lambda_k1", "da_lambda_q2", "da_lambda_k2"]]
    head_norm = din("da_head_norm", [128])
    cos_seq = din("cos_seq", [128, SEQ]); sin_seq = din("sin_seq", [128, SEQ])
    cos_own = din("cos_own", [128, NOWN]); sin_own = din("sin_own", [128, NOWN])
    kvalid = din("kvalid", [SEQ])
    esel_d = din("esel", [128, 256])
    ident_d = din("ident", [128, 128]); swap_d = din("swapm", [128, 128]); gmask_d = din("gmask", [128, 8, 128])
    out_d = nc.dram_tensor("out", [NOWN, D], F32, kind="ExternalOutput").ap()

    kT_d = dscr("kT_d", [8, 128, SEQ]); v_d = dscr("v_d", [SEQ, D]); uT_d = dscr("uT_d", [8, 128, SEQ])
    qT_d = dscr("qT_d", [8, 128, NOWN]); g_d = dscr("g_d", [16, 128, NOWN])

    P = Prog(nc)
    es = contextlib.ExitStack()
    ARENA_BYTES = 190 * 1024
    arena_t = es.enter_context(nc.sbuf_tensor("arena", [128, ARENA_BYTES // 4], F32))
    A = Arena(P, arena_t[:], ARENA_BYTES)
    banks = [es.enter_context(nc.psum_tensor("bank%d" % i, [128, 512], F32)) for i in range(8)]

    def bank(i):
        return banks[i][:]

    def bank_bf(i):
        return banks[i][:].bitcast(BF16)

    def bkey(i):
        return "bank%d" % i

    def dma(out, in_, reads=(), writes=(), q="sp", slow=False):
        if slow:
            return P.op(q, lambda e: e.dma_start(out=out, in_=in_, allow_slow_non_contiguous=True), reads, writes, dma=True)
        return P.op(q, lambda e: e.dma_start(out=out, in_=in_), reads, writes, dma=True)

    ident_f = A.alloc("ident_f", 128, F32); ident_b = A.alloc("ident_b", 128, BF16)
    swap_f = A.alloc("swap_f", 128, F32)
    gmask = A.alloc("gmask", 1024, F32)
    epst = A.alloc("epst", 1, F32); zerot = A.alloc("zerot", 1, F32)
    dma(ident_f, ident_d, writes=["ident_f"]); dma(swap_f, swap_d, writes=["swap_f"])
    dma(gmask, gmask_d.rearrange("p g c -> p (g c)"), writes=["gmask"])
    P.op("pool", lambda e: e.tensor_copy(out=ident_b, in_=ident_f), ["ident_f"], ["ident_b"])
    P.op("pool", lambda e: e.memset(epst, EPS), [], ["epst"])
    P.op("pool", lambda e: e.memset(zerot, 0.0), [], ["zerot"])

    def load_pp(name, src, n):
        t = A.alloc(name, n, F32)
        dma(t, src.rearrange("(k p) -> p k", p=128), writes=[name], slow=True)
        return t

    g_mix_pre = load_pp("g_mix_pre", gains["norm_mix_pre"], 8)
    g_x_pre = load_pp("g_x_pre", gains["norm_x_pre"], 8)
    g_mem = load_pp("g_mem", gains["norm_mem"], 8)
    g_ff_pre = load_pp("g_ff_pre", gains["norm_ff_pre"], 8)
    bgate_pp = load_pp("bgate_pp", b_gate, 16)
    bglu_pp = load_pp("bglu_pp", b_glu, 8)
    ssmd_pp = load_pp("ssmd_pp", ssm_d, 8)

    wctr = [0]

    def load_weight(name, src, K, N, gain=None, col0=0, rot=False):
        KC = K // 128
        wt = A.alloc(name, KC * N, BF16)
        wv = wt.rearrange("p (k n) -> p k n", k=KC)
        CH = min(N, 2048)
        for kc in range(KC):
            for c0 in range(0, N, CH):
                i = wctr[0]; wctr[0] += 1
                sname = "wstage%d" % (i % 2)
                if sname not in A.live:
                    A.alloc(sname, 2048, F32)
                o, s = A.live[sname]
                st = A.base[:, o // 4:o // 4 + CH]
                dma(st, src[kc * 128:(kc + 1) * 128, col0 + c0:col0 + c0 + CH], writes=[sname], q="sp")
                dst = wv[:, kc, c0:c0 + CH]
                eng = "dve"
                if rot:
                    sv = st.rearrange("p (m t d) -> p m t d", t=2, d=32)
                    dv = dst.rearrange("p (m t d) -> p m t d", t=2, d=32)
                    if gain is not None:
                        P.op(eng, lambda e, dv=dv, sv=sv, kc=kc: e.tensor_scalar(out=dv[:, :, 0, :], in0=sv[:, :, 1, :], scalar1=gain[:, kc:kc + 1], scalar2=-1.0, op0=ALU.mult, op1=ALU.mult), [sname, "gains"], [name])
                        P.op(eng, lambda e, dv=dv, sv=sv, kc=kc: e.tensor_scalar(out=dv[:, :, 1, :], in0=sv[:, :, 0, :], scalar1=gain[:, kc:kc + 1], scalar2=None, op0=ALU.mult), [sname, "gains"], [name])
                else:
                    if gain is not None:
                        P.op("act", lambda e, dst=dst, st=st, kc=kc: e.activation(out=dst, in_=st, func=AF.Copy, scale=gain[:, kc:kc + 1]), [sname, "gains"], [name])
                    else:
                        P.op("act", lambda e, dst=dst, st=st: e.activation(out=dst, in_=st, func=AF.Copy), [sname], [name])
        return wv

    P.op("pool", lambda e: e.engine_nop(), ["g_mix_pre", "g_x_pre", "g_mem", "g_ff_pre"], ["gains"])

    nctr = [0]

    def norm_block_T(x_src_ap, x_is_dram, hT_dst, hT_key, xkey=None, tp_bank=0):
        i = nctr[0]; nctr[0] += 1
        if x_is_dram:
            xs_name = "xs%d" % (i % 2)
            if xs_name not in A.live:
                A.alloc(xs_name, 1024, F32)
            o, s = A.live[xs_name]
            xs = A.base[:, o // 4:o // 4 + 1024]
            dma(xs, x_src_ap, writes=[xs_name], q="sp")
            rkeys = [xs_name]
        else:
            xs = x_src_ap
            rkeys = [xkey]
        for nm, n, dt in (("nsq%d" % (i % 2), 1024, BF16), ("nst%d" % (i % 2), 4, F32), ("nh%d" % (i % 2), 1024, BF16)):
            if nm not in A.live:
                A.alloc(nm, n, dt)
        o, s = A.live["nsq%d" % (i % 2)]; sq = A.base[:, o // 4:o // 4 + 512].bitcast(BF16)
        o, s = A.live["nst%d" % (i % 2)]; st = A.base[:, o // 4:o // 4 + 4]
        o, s = A.live["nh%d" % (i % 2)]; hb = A.base[:, o // 4:o // 4 + 512].bitcast(BF16)
        ks, kt, kh = "nsq%d" % (i % 2), "nst%d" % (i % 2), "nh%d" % (i % 2)
        P.op("act", lambda e: e.activation(out=sq, in_=xs, func=AF.Square, accum_out=st[:, 0:1]), rkeys, [ks, kt])
        P.op("act", lambda e: e.activation(out=st[:, 1:2], in_=st[:, 0:1], func=AF.Ln, scale=1.0 / D, bias=epst), [kt, "epst"], [kt + ":1"])
        P.op("act", lambda e: e.activation(out=st[:, 2:3], in_=st[:, 1:2], func=AF.Exp, scale=-0.5), [kt + ":1"], [kt + ":2"])
        P.op("dve", lambda e: e.tensor_scalar(out=hb, in0=xs, scalar1=st[:, 2:3], scalar2=None, op0=ALU.mult), rkeys + [kt + ":2"], [kh])
        tp = bank_bf(tp_bank).rearrange("p (k t) -> p k t", k=8)
        for kc in range(8):
            P.op("pe", lambda e, kc=kc: e.transpose(out=tp[:, kc, :], in_=hb[:, kc * 128:(kc + 1) * 128], identity=ident_b), [kh, "ident_b"], [bkey(tp_bank)])
        P.op("act", lambda e: e.activation(out=hT_dst, in_=tp, func=AF.Copy), [bkey(tp_bank)], [hT_key])

    x1_d = dscr("x1_d", [NOWN, D], F32)
    x2_d = dscr("x2_d", [NOWN, D], F32)
    wf1_d = nc.dram_tensor("wf1_d", [D, 4 * D], BF16, kind="Internal").ap()
    wf2_d = nc.dram_tensor("wf2_d", [4 * D, D], BF16, kind="Internal").ap()
    wsc = {}
    for nm_, (src_, K_, N_, gn_) in {"wglu": (w_glu, D, D, None), "wssm": (w_ssm, D, D, None), "wda": (w_da, D, D, None),
                                       "wmix": (w_mix, D, D, None), "wxkv": (w_xkv, D, 2 * D, g_mem), "wxq": (w_xq, D, D, g_x_pre),
                                       "wxo": (w_xo, D, D, None)}.items():
        wsc[nm_] = (nc.dram_tensor(nm_ + "_d", [K_, N_], BF16, kind="Internal").ap(), src_, K_, N_, gn_)
    cst_ = [A.alloc("cstg%d" % i, 1024, F32) for i in range(2)]
    cb = [A.alloc("cb%d" % i, 1024, BF16) for i in range(2)]
    cctr = [0]

    def cast_gen():
        jobs = [(v_[1], v_[2], v_[3], v_[4], v_[0], k_ + "_d") for k_, v_ in wsc.items()]
        jobs.append((w_ff1, D, 4 * D, g_ff_pre, wf1_d, "wf_d"))
        jobs.append((w_ff2, 4 * D, D, None, wf2_d, "wf_d"))
        for (src, K_, N_, gain, dst, dkey) in jobs:
            for kc in range(K_ // 128):
                for c0 in range(0, N_, 1024):
                    i = cctr[0]; cctr[0] += 1
                    st = cst_[i % 2]; sname = "cstg%d" % (i % 2)
                    dma(st, src[kc * 128:(kc + 1) * 128, c0:c0 + 1024], writes=[sname], q="act")
                    cbt = cb[i % 2]; cbk = "cb%d" % (i % 2)
                    if gain is not None:
                        P.op("act", lambda e, cbt=cbt, st=st, kc=kc, gain=gain: e.activation(out=cbt, in_=st, func=AF.Copy, scale=gain[:, kc:kc + 1]), [sname, "gains"], [cbk])
                    else:
                        P.op("act", lambda e, cbt=cbt, st=st: e.activation(out=cbt, in_=st, func=AF.Copy), [sname], [cbk])
                    dma(dst[kc * 128:(kc + 1) * 128, c0:c0 + 1024], cbt, reads=[cbk], writes=[dkey], q="act")
                    yield

    cgen = cast_gen()
    cg_alive = [True]

    def cast_step():
        if cg_alive[0]:
            try:
                next(cgen)
            except StopIteration:
                cg_alive[0] = False

    wk = load_weight("wk", w_in, D, 1024, gain=g_mix_pre, col0=2048)
    wkr = load_weight("wkr", w_in, D, 1024, gain=g_mix_pre, col0=2048, rot=True)
    wu = load_weight("wu", w_in, D, 1024, gain=g_mix_pre, col0=0)
    wv_ = load_weight("wv", w_in, D, 1024, gain=g_mix_pre, col0=3072)
    hTa = [A.alloc("hTa%d" % i, 8 * 512, BF16).rearrange("p (k t) -> p k t", k=8) for i in range(2)]
    cst = [A.alloc("cst%d" % i, 1024, F32) for i in range(2)]
    kst = [A.alloc("kst%d" % i, 512, BF16) for i in range(2)]
    kt1 = [A.alloc("kt1_%d" % i, 512, F32) for i in range(2)]
    kt2 = [A.alloc("kt2_%d" % i, 512, F32) for i in range(2)]
    vst = [A.alloc("vst%d" % i, 1024, BF16) for i in range(2)]
    ust = [A.alloc("ust%d" % i, 512, BF16) for i in range(2)]
    cnt = [0]
    for tt in range(16):
        hb_i = tt % 2
        hT = hTa[hb_i]; hk = "hTa%d" % hb_i
        for bl in range(4):
            blk = tt * 4 + bl
            norm_block_T(xseq[blk * 128:(blk + 1) * 128, :], True, hT[:, :, bl * 128:(bl + 1) * 128], hk)
        cs = cst[hb_i]; ck = "cst%d" % hb_i
        dma(cs[:, 0:512], cos_seq[:, tt * 512:(tt + 1) * 512], writes=[ck])
        dma(cs[:, 512:1024], sin_seq[:, tt * 512:(tt + 1) * 512], writes=[ck])
        for h in range(8):
            cast_step()
            i = cnt[0]; cnt[0] += 1
            pb = 1 + 2 * (i % 2)
            for kc in range(8):
                P.op("pe", lambda e, kc=kc, h=h, pb=pb, hT=hT: e.matmul(out=bank(pb), lhsT=wk[:, kc, h * 128:(h + 1) * 128], rhs=hT[:, kc, :], start=(kc == 0), stop=(kc == 7)), ["wk", hk], [bkey(pb)])
            for kc in range(8):
                P.op("pe", lambda e, kc=kc, h=h, pb=pb, hT=hT: e.matmul(out=bank(pb + 1), lhsT=wkr[:, kc, h * 128:(h + 1) * 128], rhs=hT[:, kc, :], start=(kc == 0), stop=(kc == 7)), ["wkr", hk], [bkey(pb + 1)])
            j = i % 2
            P.op("dve", lambda e, pb=pb, j=j, cs=cs: e.tensor_tensor(out=kt1[j], in0=bank(pb), in1=cs[:, 0:512], op=ALU.mult), [bkey(pb), ck], ["kt1_%d" % j])
            P.op("dve", lambda e, pb=pb, j=j, cs=cs: e.tensor_tensor(out=kt2[j], in0=bank(pb + 1), in1=cs[:, 512:1024], op=ALU.mult), [bkey(pb + 1), ck], ["kt2_%d" % j])
            P.op("pool", lambda e, j=j: e.tensor_tensor(out=kst[j], in0=kt1[j], in1=kt2[j], op=ALU.add), ["kt1_%d" % j, "kt2_%d" % j], ["kst%d" % j])
            dma(kT_d[h, :, tt * 512:(tt + 1) * 512], kst[j], reads=["kst%d" % j], writes=["kT_d"], q="sp")
        for fc in range(8):
            i = cnt[0]; cnt[0] += 1
            pb = 5 + (i % 2)
            for kc in range(8):
                P.op("pe", lambda e, kc=kc, fc=fc, pb=pb, hT=hT: e.matmul(out=bank(pb), lhsT=wu[:, kc, fc * 128:(fc + 1) * 128], rhs=hT[:, kc, :], start=(kc == 0), stop=(kc == 7)), ["wu", hk], [bkey(pb)])
            j = i % 2
            P.op("act", lambda e, pb=pb, j=j: e.activation(out=ust[j], in_=bank(pb), func=AF.Copy), [bkey(pb)], ["ust%d" % j])
            dma(uT_d[fc, :, tt * 512:(tt + 1) * 512], ust[j], reads=["ust%d" % j], writes=["uT_d"], q="sp")
        for bl in range(4):
            blk = tt * 4 + bl
            i = cnt[0]; cnt[0] += 1
            j = i % 2
            for half in range(2):
                pb = 1 + 2 * (i % 2) + half
                for kc in range(8):
                    P.op("pe", lambda e, kc=kc, bl=bl, half=half, pb=pb, hT=hT: e.matmul(out=bank(pb), lhsT=hT[:, kc, bl * 128:(bl + 1) * 128], rhs=wv_[:, kc, half * 512:(half + 1) * 512], start=(kc == 0), stop=(kc == 7)), ["wv", hk], [bkey(pb)])
                P.op("act", lambda e, pb=pb, j=j, half=half: e.activation(out=vst[j][:, half * 512:(half + 1) * 512], in_=bank(pb), func=AF.Copy), [bkey(pb)], ["vst%d" % j])
            dma(v_d[blk * 128:(blk + 1) * 128, :], vst[j], reads=["vst%d" % j], writes=["v_d"], q="sp")
    while cg_alive[0]:
        cast_step()
    for n in ["cstg0", "cstg1", "cb0", "cb1"]:
        A.free(n)
    for n in ["wu", "wk", "wkr", "wv", "hTa0", "hTa1", "cst0", "cst1", "kst0", "kst1", "kt1_0", "kt1_1", "kt2_0", "kt2_1", "vst0", "vst1", "ust0", "ust1"]:
        A.free(n)

    wq = load_weight("wq", w_in, D, 1024, gain=g_mix_pre, col0=1024)
    wqr = load_weight("wqr", w_in, D, 1024, gain=g_mix_pre, col0=1024, rot=True)
    wg = load_weight("wg", w_in, D, 2048, gain=g_mix_pre, col0=4096)
    hTo = [A.alloc("hTo%d" % i, 8 * 512, BF16).rearrange("p (k t) -> p k t", k=8) for i in range(2)]
    cso = [A.alloc("cso%d" % i, 1024, F32) for i in range(2)]
    qst = [A.alloc("qst%d" % i, 512, BF16) for i in range(2)]
    qt1 = [A.alloc("qt1_%d" % i, 512, F32) for i in range(2)]
    qt2 = [A.alloc("qt2_%d" % i, 512, F32) for i in range(2)]
    gst = [A.alloc("gst%d" % i, 512, BF16) for i in range(2)]
    for tt in range(4):
        hb_i = tt % 2
        hT = hTo[hb_i]; hk = "hTo%d" % hb_i
        for bl in range(4):
            blk = tt * 4 + bl
            norm_block_T(xown[blk * 128:(blk + 1) * 128, :], True, hT[:, :, bl * 128:(bl + 1) * 128], hk)
        cs = cso[hb_i]; ck = "cso%d" % hb_i
        dma(cs[:, 0:512], cos_own[:, tt * 512:(tt + 1) * 512], writes=[ck])
        dma(cs[:, 512:1024], sin_own[:, tt * 512:(tt + 1) * 512], writes=[ck])
        for h in range(8):
            i = cnt[0]; cnt[0] += 1
            pb = 1 + 2 * (i % 2)
            for kc in range(8):
                P.op("pe", lambda e, kc=kc, h=h, pb=pb, hT=hT: e.matmul(out=bank(pb), lhsT=wq[:, kc, h * 128:(h + 1) * 128], rhs=hT[:, kc, :], start=(kc == 0), stop=(kc == 7)), ["wq", hk], [bkey(pb)])
            for kc in range(8):
                P.op("pe", lambda e, kc=kc, h=h, pb=pb, hT=hT: e.matmul(out=bank(pb + 1), lhsT=wqr[:, kc, h * 128:(h + 1) * 128], rhs=hT[:, kc, :], start=(kc == 0), stop=(kc == 7)), ["wqr", hk], [bkey(pb + 1)])
            j = i % 2
            P.op("dve", lambda e, pb=pb, j=j, cs=cs: e.tensor_tensor(out=qt1[j], in0=bank(pb), in1=cs[:, 0:512], op=ALU.mult), [bkey(pb), ck], ["qt1_%d" % j])
            P.op("dve", lambda e, pb=pb, j=j, cs=cs: e.tensor_tensor(out=qt2[j], in0=bank(pb + 1), in1=cs[:, 512:1024], op=ALU.mult), [bkey(pb + 1), ck], ["qt2_%d" % j])
            P.op("pool", lambda e, j=j: e.tensor_tensor(out=qst[j], in0=qt1[j], in1=qt2[j], op=ALU.add), ["qt1_%d" % j, "qt2_%d" % j], ["qst%d" % j])
            dma(qT_d[h, :, tt * 512:(tt + 1) * 512], qst[j], reads=["qst%d" % j], writes=["qT_d"], q="sp")
        for gc in range(16):
            i = cnt[0]; cnt[0] += 1
            pb = 5 + (i % 2)
            for kc in range(8):
                P.op("pe", lambda e, kc=kc, gc=gc, pb=pb, hT=hT: e.matmul(out=bank(pb), lhsT=wg[:, kc, gc * 128:(gc + 1) * 128], rhs=hT[:, kc, :], start=(kc == 0), stop=(kc == 7)), ["wg", hk], [bkey(pb)])
            j = i % 2
            P.op("act", lambda e, pb=pb, j=j, gc=gc: e.activation(out=gst[j], in_=bank(pb), func=AF.Sigmoid, bias=bgate_pp[:, gc:gc + 1]), [bkey(pb), "bgate_pp"], ["gst%d" % j])
            dma(g_d[gc, :, tt * 512:(tt + 1) * 512], gst[j], reads=["gst%d" % j], writes=["g_d"], q="sp")
    for n in ["wq", "wqr", "wg", "hTo0", "hTo1", "cso0", "cso1", "qst0", "qst1", "qt1_0", "qt1_1", "qt2_0", "qt2_1", "gst0", "gst1"]:
        A.free(n)

    for n in ["wstage0", "wstage1", "xs0", "xs1", "nsq0", "nsq1", "nh0", "nh1", "nst0", "nst1"]:
        if n in A.live:
            A.free(n)
    def a64(name, n, dt=F32):
        return A.alloc(name, n, dt)[0:64, :]

    lre = a64("lre", 64); lim = a64("lim", 64); ldt = a64("ldt", 64)
    dma(lre, lam_re.rearrange("g p -> p g"), writes=["lre"], slow=True)
    dma(lim, lam_im.rearrange("g p -> p g"), writes=["lim"], slow=True)
    dma(ldt, log_dt.partition_broadcast(64), writes=["ldt"])
    dtt = a64("dtt", 64); zr = a64("zr", 64); zi = a64("zi", 64)
    P.op("act", lambda e: e.activation(out=dtt, in_=ldt, func=AF.Exp), ["ldt"], ["dtt"])
    P.op("dve", lambda e: e.tensor_tensor(out=zr, in0=lre, in1=dtt, op=ALU.mult), ["lre", "dtt"], ["zr"])
    P.op("dve", lambda e: e.tensor_tensor(out=zi, in0=lim, in1=dtt, op=ALU.mult), ["lim", "dtt"], ["zi"])
    mag = a64("mag", 64); cr = a64("cr", 64); ci = a64("ci", 64); halfpi = a64("halfpi", 1)
    P.op("pool", lambda e: e.memset(halfpi, math.pi / 2), [], ["halfpi"])
    P.op("act", lambda e: e.activation(out=mag, in_=zr, func=AF.Exp, scale=1.0 / 32), ["zr"], ["mag"])
    P.op("act", lambda e: e.activation(out=ci, in_=zi, func=AF.Sin, scale=1.0 / 32), ["zi"], ["ci"])
    P.op("act", lambda e: e.activation(out=cr, in_=zi, func=AF.Sin, scale=1.0 / 32, bias=halfpi), ["zi", "halfpi"], ["cr"])
    P.op("dve", lambda e: e.tensor_tensor(out=cr, in0=cr, in1=mag, op=ALU.mult), ["cr", "mag"], ["cr"])
    P.op("dve", lambda e: e.tensor_tensor(out=ci, in0=ci, in1=mag, op=ALU.mult), ["ci", "mag"], ["ci"])
    PW = a64("PW", NLEV * 128).rearrange("p (l r g) -> p l r g", l=NLEV, r=2)
    t1 = a64("sq_t1", 64); t2 = a64("sq_t2", 64); t3 = a64("sq_t3", 64)

    def csquare(sr, si, dr, di, keys_in, key_out):
        P.op("dve", lambda e: e.tensor_tensor(out=t1, in0=sr, in1=sr, op=ALU.mult), keys_in, ["sq_t1"])
        P.op("dve", lambda e: e.tensor_tensor(out=t2, in0=si, in1=si, op=ALU.mult), keys_in, ["sq_t2"])
        P.op("dve", lambda e: e.tensor_tensor(out=t3, in0=sr, in1=si, op=ALU.mult), keys_in, ["sq_t3"])
        P.op("dve", lambda e: e.tensor_tensor(out=dr, in0=t1, in1=t2, op=ALU.subtract), ["sq_t1", "sq_t2"], [key_out])
        P.op("dve", lambda e: e.tensor_tensor(out=di, in0=t3, in1=t3, op=ALU.add), ["sq_t3"], [key_out + ":i"])

    wr = [a64("wr%d" % i, 64) for i in range(2)]; wi = [a64("wi%d" % i, 64) for i in range(2)]
    csquare(cr, ci, wr[0], wi[0], ["cr", "ci"], "wr0")
    csquare(wr[0], wi[0], wr[1], wi[1], ["wr0", "wr0:i"], "wr1")
    csquare(wr[1], wi[1], wr[0], wi[0], ["wr1", "wr1:i"], "wr0")
    csquare(wr[0], wi[0], wr[1], wi[1], ["wr0", "wr0:i"], "wr1")
    csquare(wr[1], wi[1], PW[:, 0, 0, :], PW[:, 0, 1, :], ["wr1", "wr1:i"], "PW:0")
    for l in range(1, NLEV):
        csquare(PW[:, l - 1, 0, :], PW[:, l - 1, 1, :], PW[:, l, 0, :], PW[:, l, 1, :], ["PW:%d" % (l - 1), "PW:%d:i" % (l - 1)], "PW:%d" % l)
    den = a64("den", 64); cfr = a64("cfr", 64); cfi = a64("cfi", 64); lm1 = a64("lm1", 64)
    P.op("dve", lambda e: e.tensor_tensor(out=t1, in0=lre, in1=lre, op=ALU.mult), ["lre", "PW:%d:i" % (NLEV - 1)], ["sq_t1"])
    P.op("dve", lambda e: e.tensor_tensor(out=t2, in0=lim, in1=lim, op=ALU.mult), ["lim"], ["sq_t2"])
    P.op("dve", lambda e: e.tensor_tensor(out=den, in0=t1, in1=t2, op=ALU.add), ["sq_t1", "sq_t2"], ["den"])
    P.op("dve", lambda e: e.reciprocal(out=den, in_=den), ["den"], ["den"])
    P.op("dve", lambda e: e.tensor_scalar(out=lm1, in0=PW[:, 0, 0, :], scalar1=-1.0, scalar2=None, op0=ALU.add), ["PW:0"], ["lm1"])
    P.op("dve", lambda e: e.tensor_tensor(out=t1, in0=lm1, in1=lre, op=ALU.mult), ["lm1", "lre", "den"], ["sq_t1"])
    P.op("dve", lambda e: e.tensor_tensor(out=t2, in0=PW[:, 0, 1, :], in1=lim, op=ALU.mult), ["PW:0:i", "lim"], ["sq_t2"])
    P.op("dve", lambda e: e.tensor_tensor(out=cfr, in0=t1, in1=t2, op=ALU.add), ["sq_t1", "sq_t2"], ["cfr"])
    P.op("dve", lambda e: e.tensor_tensor(out=t1, in0=PW[:, 0, 1, :], in1=lre, op=ALU.mult), ["PW:0:i", "lre", "cfr"], ["sq_t1"])
    P.op("dve", lambda e: e.tensor_tensor(out=t2, in0=lm1, in1=lim, op=ALU.mult), ["lm1", "lim", "cfr"], ["sq_t2"])
    P.op("dve", lambda e: e.tensor_tensor(out=cfi, in0=t1, in1=t2, op=ALU.subtract), ["sq_t1", "sq_t2"], ["cfi"])
    P.op("dve", lambda e: e.tensor_tensor(out=cfr, in0=cfr, in1=den, op=ALU.mult), ["cfr", "den"], ["cfr"])
    P.op("dve", lambda e: e.tensor_tensor(out=cfi, in0=cfi, in1=den, op=ALU.mult), ["cfi", "den"], ["cfi"])
    braw = a64("braw", 2048).rearrange("p (r g h) -> p r g h", r=2, g=64)
    for gh in range(2):
        dma(braw[:, 0, gh * 32:(gh + 1) * 32, :], b_re[gh * 32:(gh + 1) * 32].rearrange("g p h -> p g h"), writes=["braw"], slow=True)
        dma(braw[:, 1, gh * 32:(gh + 1) * 32, :], b_im[gh * 32:(gh + 1) * 32].rearrange("g p h -> p g h"), writes=["braw"], slow=True)
    Bb = a64("Bb", 2048).rearrange("p (r g h) -> p r g h", r=2, g=64)
    bt1 = a64("bt1", 1024).rearrange("p (g h) -> p g h", g=64); bt2 = a64("bt2", 1024).rearrange("p (g h) -> p g h", g=64)
    cfr_b = cfr.unsqueeze(2).broadcast_to([64, 64, 16]); cfi_b = cfi.unsqueeze(2).broadcast_to([64, 64, 16])
    P.op("dve", lambda e: e.tensor_tensor(out=bt1, in0=braw[:, 0], in1=cfr_b, op=ALU.mult), ["braw", "cfr"], ["bt1"])
    P.op("dve", lambda e: e.tensor_tensor(out=bt2, in0=braw[:, 1], in1=cfi_b, op=ALU.mult), ["braw", "cfi"], ["bt2"])
    P.op("dve", lambda e: e.tensor_tensor(out=Bb[:, 0], in0=bt1, in1=bt2, op=ALU.subtract), ["bt1", "bt2"], ["Bb"])
    P.op("dve", lambda e: e.tensor_tensor(out=bt1, in0=braw[:, 0], in1=cfi_b, op=ALU.mult), ["braw", "cfi", "Bb"], ["bt1"])
    P.op("dve", lambda e: e.tensor_tensor(out=bt2, in0=braw[:, 1], in1=cfr_b, op=ALU.mult), ["braw", "cfr", "Bb"], ["bt2"])
    P.op("dve", lambda e: e.tensor_tensor(out=Bb[:, 1], in0=bt1, in1=bt2, op=ALU.add), ["bt1", "bt2"], ["Bb:i"])
    Cst = A.alloc("Cst", 1024, F32).rearrange("p (g h) -> p g h", g=64)
    for gh in range(2):
        dma(Cst[0:64, gh * 32:(gh + 1) * 32, :], c_re[gh * 32:(gh + 1) * 32].rearrange("g h p -> p g h"), writes=["Cst"], slow=True)
        dma(Cst[64:128, gh * 32:(gh + 1) * 32, :], c_im[gh * 32:(gh + 1) * 32].rearrange("g h p -> p g h"), writes=["Cst"], slow=True)
    P.op("pool", lambda e: e.tensor_scalar(out=Cst[64:128], in0=Cst[64:128], scalar1=-1.0, scalar2=None, op0=ALU.mult), ["Cst"], ["Cst"])
    S1 = A.alloc("S1", NLEV * 64, F32).rearrange("p (l g) -> p l g", l=NLEV)
    S2 = A.alloc("S2", NLEV * 64, F32).rearrange("p (l g) -> p l g", l=NLEV)
    pwkeys = ["PW:%d" % l for l in range(NLEV)] + ["PW:%d:i" % l for l in range(NLEV)]
    dma(S1[0:64], PW[:, :, 0, :], reads=pwkeys, writes=["S1"]); dma(S1[64:128], PW[:, :, 0, :], reads=pwkeys, writes=["S1"])
    dma(S2[0:64], PW[:, :, 1, :], reads=pwkeys, writes=["S2"]); dma(S2[64:128], PW[:, :, 1, :], reads=pwkeys, writes=["S2"])
    P.op("pool", lambda e: e.tensor_scalar(out=S2[64:128], in0=S2[64:128], scalar1=-1.0, scalar2=None, op0=ALU.mult), ["S2"], ["S2"])

    if debug:
        pw_dbg = dscr("pw_dbg", [64, NLEV * 128], F32)
        dma(pw_dbg, PW.rearrange("p l r g -> p (l r g)"), reads=pwkeys, writes=["pw_dbg"])
        bb_dbg = dscr("bb_dbg", [64, 2048], F32)
        dma(bb_dbg, Bb.rearrange("p r g h -> p (r g h)"), reads=["Bb", "Bb:i"], writes=["bb_dbg"])
    for n in ["braw", "bt1", "bt2", "PW", "sq_t1", "sq_t2", "sq_t3", "wr0", "wr1", "wi0", "wi1", "mag", "cr", "ci", "lm1", "den", "cfr", "cfi", "zr", "zi", "dtt", "ldt", "lre", "lim"]:
        A.free(n)
    uT = [A.alloc("uT%d" % i, SEQ, BF16) for i in range(1)]
    X0p = [A.alloc("X0p%d" % i, SEQ, BF16) for i in range(2)]
    Tp = [A.alloc("Tp%d" % i, SEQ, BF16) for i in range(2)]
    Xop = [[A.alloc("Xo%d_%d" % (p_, i), NOWN, BF16).rearrange("p (s t) -> p s t", s=16) for i in range(2)] for p_ in range(2)]
    XBp = [[A.alloc("XB%d_%d" % (p_, i), 64, BF16) for i in range(2)] for p_ in range(2)]
    ysg = A.alloc("ysg", 8 * NOWN, BF16).rearrange("p (k t) -> p k t", k=8)
    WB = [A.alloc("WB%d" % i, 128, BF16) for i in range(4)]
    WC = [A.alloc("WC%d" % i, 128, BF16) for i in range(4)]
    R = [A.alloc("R%d" % i, 128, BF16) for i in range(52)]
    Rt = [A.alloc("Rt%d" % i, 128, F32) for i in range(4)]
    bm = [A.alloc("bm%d" % i, 256, F32)[0:64, :] for i in range(2)]
    gel = [A.alloc("gel%d" % i, NOWN, F32) for i in range(2)]
    evc = [0]
    toff = [0, 4096, 6144, 7168, 7680, 7936, 8064]

    def prep_gen(fc, gl, par, st_):
        g = fc * 8 + gl
        wi_ = st_ * 2 + par
        bmt = bm[par]; bmk = "bm%d" % par
        Bfc = Bb[:, :, fc * 8:(fc + 1) * 8, :]
        gmv64 = gmask[0:64, gl * 128:(gl + 1) * 128].rearrange("p (g h) -> p g h", g=8)
        P.op("dve", lambda e: e.tensor_tensor(out=bmt[:, 0:128].rearrange("p (g h) -> p g h", g=8), in0=Bfc[:, 0], in1=gmv64, op=ALU.mult), ["Bb", "Bb:i", "gmask"], [bmk])
        P.op("dve", lambda e: e.tensor_tensor(out=bmt[:, 128:256].rearrange("p (g h) -> p g h", g=8), in0=Bfc[:, 1], in1=gmv64, op=ALU.mult), ["Bb", "Bb:i", "gmask"], [bmk + ":i"])
        pbw = 7
        P.op("pe", lambda e: e.matmul(out=bank(pbw)[:, 0:64], lhsT=bmt[:, 0:128], rhs=ident_f[0:64, 0:64], start=True, stop=True), [bmk, "ident_f"], [bkey(pbw)])
        P.op("pe", lambda e: e.matmul(out=bank(pbw)[:, 64:128], lhsT=bmt[:, 128:256], rhs=ident_f[0:64, 0:64], start=True, stop=True), [bmk + ":i", "ident_f"], [bkey(pbw)])
        P.op("act", lambda e: e.activation(out=WB[wi_], in_=bank(pbw)[:, 0:128], func=AF.Copy), [bkey(pbw)], ["WB%d" % wi_])
        P.op("pool", lambda e: e.tensor_tensor(out=WC[wi_].rearrange("p (g h) -> p g h", g=8), in0=Cst[:, fc * 8:(fc + 1) * 8, :], in1=gmask[:, gl * 128:(gl + 1) * 128].rearrange("p (g h) -> p g h", g=8), op=ALU.mult), ["Cst", "gmask"], ["WC%d" % wi_])
        yield
        for lev in range(NLEV):
            Rm = R[wi_ * 13 + lev]; Rk = "R%d" % (wi_ * 13 + lev)
            rt = Rt[(lev % 2) * 2 + par]; rtk = "Rt%d" % ((lev % 2) * 2 + par)
            P.op("pool", lambda e, lev=lev, rt=rt: e.tensor_scalar(out=rt, in0=swap_f, scalar1=S2[:, lev, g:g + 1], scalar2=None, op0=ALU.mult), ["swap_f", "S2"], [rtk])
            P.op("dve", lambda e, Rm=Rm, lev=lev, rt=rt: e.scalar_tensor_tensor(out=Rm, in0=ident_f, scalar=S1[:, lev, g:g + 1], in1=rt, op0=ALU.mult, op1=ALU.add), ["ident_f", "S1", rtk], [Rk])
            yield

    def group_gen(fc, gl, par, u, uk, st_):
        g = fc * 8 + gl
        wi_ = st_ * 2 + par
        X0 = X0p[par]; Tb = Tp[par]; Xo = Xop[par]; XB = XBp[par]
        xn = "X0p%d" % par; tn = "Tp%d" % par
        bc = [0]

        def nextbank():
            bc[0] += 1
            return 2 * par + (bc[0] % 2)

        def evac(pb, dst_ap, dkey, n=None):
            src_ap = bank(pb) if n is None else bank(pb)[:, 0:n]
            evc[0] += 1
            if evc[0] % 3 != 0:
                P.op("act", lambda e: e.activation(out=dst_ap, in_=src_ap, func=AF.Copy), [bkey(pb)], [dkey])
            else:
                P.op("dve", lambda e: e.tensor_copy(out=dst_ap, in_=src_ap), [bkey(pb)], [dkey])

        Rg = [(R[wi_ * 13 + lev], "R%d" % (wi_ * 13 + lev)) for lev in range(NLEV)]
        for tt in range(16):
            pb = nextbank()
            P.op("pe", lambda e, pb=pb, tt=tt: e.matmul(out=bank(pb), lhsT=WB[wi_], rhs=u[:, tt * 512:(tt + 1) * 512], start=True, stop=True), ["WB%d" % wi_, uk], [bkey(pb)])
            evac(pb, X0[:, tt * 512:(tt + 1) * 512], xn + ":%d" % tt)
            if tt % 4 == 3:
                yield
        for lev in range(7):
            n_l = 4096 >> lev
            if lev == 0:
                srcv = X0.rearrange("p (i two) -> p i two", two=2); sbase = xn + ":"
            else:
                srcv = Tb[:, toff[lev - 1]:toff[lev - 1] + 2 * n_l].rearrange("p (i two) -> p i two", two=2); sbase = tn + ":%d_" % (lev - 1)
            Rm, Rk = Rg[lev]
            for c0 in range(0, n_l, 512):
                n = min(512, n_l - c0)
                pb = nextbank()
                skeys = sorted(set([sbase + "%d" % ((2 * c0) // 512), sbase + "%d" % ((2 * c0 + 2 * n - 1) // 512)]))
                P.op("pe", lambda e, pb=pb, srcv=srcv, c0=c0, n=n: e.matmul(out=bank(pb)[:, 0:n], lhsT=ident_b, rhs=srcv[:, c0:c0 + n, 1], start=True, stop=False), ["ident_b"] + skeys, [bkey(pb)])
                P.op("pe", lambda e, pb=pb, srcv=srcv, c0=c0, n=n, Rm=Rm: e.matmul(out=bank(pb)[:, 0:n], lhsT=Rm, rhs=srcv[:, c0:c0 + n, 0], start=False, stop=True), [Rk] + skeys, [bkey(pb)])
                evac(pb, Tb[:, toff[lev] + c0:toff[lev] + c0 + n], tn + ":%d_%d" % (lev, c0 // 512), n)
                if (c0 // 512) % 2 == 1:
                    yield
            yield
        xbk = tn + ":6_0"
        xb_src = Tb[:, toff[6]:toff[6] + 64]
        for m_ in range(6):
            sh = 1 << m_
            Rm, Rk = Rg[7 + m_]
            pb = nextbank()
            P.op("pe", lambda e, pb=pb, xb_src=xb_src: e.matmul(out=bank(pb)[:, 0:64], lhsT=ident_b, rhs=xb_src, start=True, stop=False), ["ident_b", xbk], [bkey(pb)])
            P.op("pe", lambda e, pb=pb, xb_src=xb_src, sh=sh, Rm=Rm: e.matmul(out=bank(pb)[:, sh:64], lhsT=Rm, rhs=xb_src[:, 0:64 - sh], start=False, stop=True), [Rk, xbk], [bkey(pb)])
            dstb = XB[m_ % 2]
            xbk = "XB%d_%d" % (par, m_ % 2)
            evac(pb, dstb, xbk, 64)
            xb_src = dstb
            yield
        x0own = X0.rearrange("p (s b t) -> p s b t", s=16, b=4)[:, :, 3, :]
        x0keys = [xn + ":%d" % t for t in range(16)]
        xo0keys = ["Xo%d_0:%d" % (par, t) for t in range(4)]
        P.op("pool", lambda e: e.tensor_copy(out=Xo[0], in_=x0own), x0keys, xo0keys)
        pb = nextbank()
        Rm, Rk = Rg[0]
        P.op("pe", lambda e, pb=pb: e.matmul(out=bank(pb)[:, 0:16], lhsT=ident_b, rhs=x0own[:, :, 0], start=True, stop=False), ["ident_b"] + x0keys, [bkey(pb)])
        P.op("pe", lambda e, pb=pb, xb_src=xb_src, Rm=Rm: e.matmul(out=bank(pb)[:, 0:16], lhsT=Rm, rhs=xb_src.rearrange("p (s b) -> p s b", b=4)[:, :, 2], start=False, stop=True), [Rk, xbk], [bkey(pb)])
        P.op("dve", lambda e, pb=pb: e.tensor_copy(out=Xo[0][:, :, 0], in_=bank(pb)[:, 0:16]), [bkey(pb)] + xo0keys, xo0keys)
        yield
        cur = 0
        for lev in range(7):
            sh = 1 << lev
            Rm, Rk = Rg[lev]
            src = Xo[cur]; dst = Xo[1 - cur]
            for q4 in range(4):
                sk = "Xo%d_%d:%d" % (par, cur, q4); dk = "Xo%d_%d:%d" % (par, 1 - cur, q4)
                pb = nextbank()
                pv = bank(pb).rearrange("p (s t) -> p s t", s=4)
                P.op("pe", lambda e, pb=pb, src=src, q4=q4: e.matmul(out=bank(pb), lhsT=ident_b, rhs=src[:, q4 * 4:(q4 + 1) * 4, :].rearrange("p s t -> p (s t)"), start=True, stop=False), ["ident_b", sk], [bkey(pb)])
                P.op("pe", lambda e, pv=pv, src=src, q4=q4, sh=sh, Rm=Rm: e.matmul(out=pv[:, :, sh:128], lhsT=Rm, rhs=src[:, q4 * 4:(q4 + 1) * 4, 0:128 - sh], start=False, stop=True), [Rk, sk], [bkey(pb)])
                evac(pb, dst[:, q4 * 4:(q4 + 1) * 4, :].rearrange("p s t -> p (s t)"), dk)
                if q4 % 2 == 1:
                    yield
            cur = 1 - cur
        fin = Xo[cur].rearrange("p s t -> p (s t)"); fkb = "Xo%d_%d" % (par, cur)
        if debug and g == 63:
            xf_dbg = dscr("xf_dbg", [128, NOWN])
            dma(xf_dbg, fin, reads=[fkb + ":%d" % t for t in range(4)], writes=["xf_dbg"])
        for ot in range(4):
            yb = 4 + (2 * par + ot) % 3
            P.op("pe", lambda e, ot=ot, yb=yb: e.matmul(out=bank(yb), lhsT=WC[wi_], rhs=fin[:, ot * 512:(ot + 1) * 512], start=True, stop=True), ["WC%d" % wi_, fkb + ":%d" % ot], [bkey(yb)])
            pbk = bkey(yb); pbb = bank(yb)
            yacc = gel[0][:, ot * 512:(ot + 1) * 512]
            if gl == 0:
                uo = u.rearrange("p (s b t) -> p s b t", s=16, b=4)[:, ot * 4:(ot + 1) * 4, 3, :]
                P.op("dve", lambda e, yacc=yacc, uo=uo, pbb=pbb: e.scalar_tensor_tensor(out=yacc.rearrange("p (s t) -> p s t", s=4), in0=uo, scalar=ssmd_pp[:, fc:fc + 1], in1=pbb.rearrange("p (s t) -> p s t", s=4), op0=ALU.mult, op1=ALU.add), [uk, "ssmd_pp", pbk], ["gel0:%d" % ot])
            else:
                P.op("dve", lambda e, yacc=yacc, pbb=pbb: e.tensor_tensor(out=yacc, in0=pbb, in1=yacc, op=ALU.add), [pbk, "gel0:%d" % ot], ["gel0:%d" % ot])
            if ot % 2 == 1:
                yield

    def run_lockstep(gens):
        alive = [True] * len(gens)
        while any(alive):
            for i_ in range(len(gens)):
                if alive[i_]:
                    try:
                        next(gens[i_])
                    except StopIteration:
                        alive[i_] = False

    run_lockstep([prep_gen(0, 0, 0, 0), prep_gen(0, 1, 1, 0)])
    for fc in range(8):
        u = uT[0]; uk = "uT0"
        dma(u, uT_d[fc], reads=["uT_d"], writes=[uk], q="pool")
        for gp in range(4):
            pk = fc * 4 + gp
            st_ = pk % 2
            gens = [group_gen(fc, 2 * gp, 0, u, uk, st_), group_gen(fc, 2 * gp + 1, 1, u, uk, st_)]
            if pk + 1 < 32:
                nfc, ngp = (pk + 1) // 4, (pk + 1) % 4
                gens.append(prep_gen(nfc, 2 * ngp, 0, 1 - st_))
                gens.append(prep_gen(nfc, 2 * ngp + 1, 1, 1 - st_))
            run_lockstep(gens)
        yk = ["gel0:%d" % ot for ot in range(4)]
        P.op("act", lambda e: e.activation(out=gel[1], in_=gel[0], func=AF.Square), yk, ["gel1"])
        P.op("dve", lambda e: e.tensor_scalar(out=gel[1], in0=gel[1], scalar1=0.044715 * 1.5957691216, scalar2=1.5957691216, op0=ALU.mult, op1=ALU.add), ["gel1"], ["gel1"])
        P.op("dve", lambda e: e.tensor_tensor(out=gel[1], in0=gel[1], in1=gel[0], op=ALU.mult), ["gel1"] + yk, ["gel1"])
        P.op("act", lambda e: e.activation(out=gel[1], in_=gel[1], func=AF.Sigmoid), ["gel1"], ["gel1"])
        P.op("dve", lambda e, fc=fc: e.tensor_tensor(out=ysg[:, fc, :], in0=gel[1], in1=gel[0], op=ALU.mult), ["gel1"] + yk, ["ysg"])
    for n in ["uT0", "X0p0", "X0p1", "Tp0", "Tp1", "Xo0_0", "Xo0_1", "Xo1_0", "Xo1_1", "XB0_0", "XB0_1", "XB1_0", "XB1_1", "WB0", "WB1", "WB2", "WB3", "WC0", "WC1", "WC2", "WC3"] + ["R%d" % i for i in range(52)] + ["Rt0", "Rt1", "Rt2", "Rt3", "bm0", "bm1",
              "gel0", "gel1", "S1", "S2", "Cst", "Bb"]:
        A.free(n)

    if debug:
        ysg_dbg = dscr("ysg_dbg", [128, 8 * NOWN])
        dma(ysg_dbg, ysg.rearrange("p k t -> p (k t)"), reads=["ysg"], writes=["ysg_dbg"])
    lq = A.alloc("lq", 256, F32).rearrange("p (a d) -> p a d", a=4)
    for a in range(4):
        dma(lq[:, a, :], lqk[a].partition_broadcast(128), writes=["lq"])
    lamt = A.alloc("lamt", 8, F32)
    lqp = A.alloc("lqp", 128, F32).rearrange("p (a d) -> p a d", a=2)
    P.op("dve", lambda e: e.tensor_tensor(out=lqp[:, 0, :], in0=lq[:, 0, :], in1=lq[:, 1, :], op=ALU.mult), ["lq"], ["lqp"])
    P.op("dve", lambda e: e.tensor_tensor(out=lqp[:, 1, :], in0=lq[:, 2, :], in1=lq[:, 3, :], op=ALU.mult), ["lq"], ["lqp"])
    P.op("dve", lambda e: e.tensor_reduce(out=lamt[:, 0:2], in_=lqp, axis=AX.X, op=ALU.add), ["lqp"], ["lamt"])
    P.op("act", lambda e: e.activation(out=lamt[:, 2:4], in_=lamt[:, 0:2], func=AF.Exp), ["lamt"], ["lamt:e"])
    P.op("dve", lambda e: e.tensor_tensor(out=lamt[:, 4:5], in0=lamt[:, 3:4], in1=lamt[:, 2:3], op=ALU.subtract), ["lamt:e"], ["lamt:d"])
    P.op("dve", lambda e: e.tensor_scalar(out=lamt[:, 5:6], in0=lamt[:, 4:5], scalar1=-0.2, scalar2=None, op0=ALU.add), ["lamt:d"], ["neglam"])
    hn = A.alloc("hn", 128, F32)
    dma(hn, head_norm.partition_broadcast(128), writes=["hn"])
    P.op("dve", lambda e: e.tensor_scalar(out=hn, in0=hn, scalar1=0.8, scalar2=None, op0=ALU.mult), ["hn"], ["hn"])
    kvf = A.alloc("kvf", 64, F32)
    dma(kvf, kvalid.rearrange("(b p) -> p b", p=128), writes=["kvf"], slow=True)

    ya = A.alloc("ya", 16 * 1024, BF16).rearrange("p (s f) -> p s f", s=16)
    Kh = [A.alloc("Kh%d" % i, 2 * SEQ, BF16)[0:64, :].rearrange("p (m t) -> p m t", m=2) for i in range(2)]
    Vh = [A.alloc("Vh%d" % i, 64 * 128, BF16).rearrange("p (b d) -> p b d", b=64) for i in range(1)]
    Qh = [A.alloc("Qh%d" % i, 2 * NOWN, BF16)[0:64, :].rearrange("p (m t) -> p m t", m=2) for i in range(1)]
    PT = [A.alloc("PT%d" % i, 1024, BF16).rearrange("p (m q) -> p m q", m=2) for i in range(2)]
    Esel = A.alloc("Esel", 256, BF16).rearrange("p (m c) -> p m c", m=2)
    Esel_f = A.alloc("Esel_f", 256, F32)
    dma(Esel_f, esel_d, writes=["Esel_f"])
    P.op("pool", lambda e: e.tensor_copy(out=Esel.rearrange("p m c -> p (m c)"), in_=Esel_f), ["Esel_f"], ["Esel"])
    denrow = [A.alloc("denrow%d" % i, 512, F32) for i in range(2)]
    rcol = [A.alloc("rcol%d" % i, 128, F32) for i in range(2)]
    OT = [A.alloc("OT%d" % i, 1024, BF16).rearrange("p (m q) -> p m q", m=2) for i in range(2)]
    ones_f = A.alloc("ones_f", 1, F32)
    P.op("pool", lambda e: e.memset(ones_f, 1.0), [], ["ones_f"])
    ep = [A.alloc("ep%d" % i, 8, F32) for i in range(2)]
    eo = [A.alloc("eo%d" % i, 384, F32) for i in range(2)]
    v_dh = v_d.rearrange("(b p) (h d) -> p b h d", p=128, h=8)
    actr = [0]
    tpv = bank_bf(7).rearrange("p (i m d) -> p i m d", i=4, m=2)
    dcol = bank(6)[:, 0:128]
    for h in range(8):
        K = Kh[h % 2]; V = Vh[0]; Q = Qh[0]
        kk_ = "Kh%d" % (h % 2)
        for m in range(2):
            dma(K[:, m, :], kT_d[h, m * 64:(m + 1) * 64, :], reads=["kT_d"], writes=[kk_], q="sp")
            dma(Q[:, m, :], qT_d[h, m * 64:(m + 1) * 64, :], reads=["qT_d"], writes=["Qh0"], q="sp")
        for vq in range(4):
            dma(V[:, vq * 16:(vq + 1) * 16, :], v_dh[:, vq * 16:(vq + 1) * 16, h, :], reads=["v_d"], writes=["Vh0"], q="sp")
        for G in range(4):
            gj = (h * 4 + G) % 2
            dr = denrow[gj]; drk = "denrow%d" % gj
            ot_ = OT[gj]; otk = "OT%d" % gj
            nkb = 16 * G + 16
            base_i = actr[0]
            actr[0] += nkb

            def emit_scores(kb, G=G, K=K, kk_=kk_, base_i=base_i):
                rel_ = kb - 16 * G - 3
                i0_ = 0 if rel_ <= 0 else (rel_ + 3) // 4
                c0 = i0_ * 128
                idiag = rel_ // 4 if (rel_ >= 0 and rel_ % 4 == 0) else -1
                pj = (base_i + kb) % 2
                pt = PT[pj]; ptk = "PT%d" % pj
                for m in range(2):
                    pb = 2 * pj + m
                    P.op("pe", lambda e, pb=pb, kb=kb, m=m, c0=c0: e.matmul(out=bank(pb)[:, c0:512], lhsT=K[:, m, kb * 128:(kb + 1) * 128], rhs=Q[:, m, G * 512 + c0:(G + 1) * 512], start=True, stop=True), [kk_, "Qh0"], [bkey(pb)])
                    P.op("act", lambda e, pb=pb, pt=pt, m=m, c0=c0: e.activation(out=pt[:, m, c0:512], in_=bank(pb)[:, c0:512], func=AF.Exp, scale=0.125), [bkey(pb)], [ptk + ":%d" % m])
                    if idiag >= 0:
                        P.op("dve", lambda e, pt=pt, m=m, idiag=idiag: e.memset(pt[64:128, m, idiag * 128:idiag * 128 + 64], 0.0), [ptk + ":%d" % m], [ptk + ":%d" % m])
                    if kb < 3:
                        P.op("dve", lambda e, pt=pt, m=m, kb=kb: e.tensor_scalar(out=pt[:, m, :], in0=pt[:, m, :], scalar1=kvf[:, kb:kb + 1], scalar2=None, op0=ALU.mult), [ptk + ":%d" % m, "kvf"], [ptk + ":%d" % m])

            def emit_pv(kb, G=G, V=V, nkb=nkb, base_i=base_i):
                rel_ = kb - 16 * G - 3
                i0_ = 0 if rel_ <= 0 else (rel_ + 3) // 4
                c0 = i0_ * 128
                pj = (base_i + kb) % 2
                pt = PT[pj]; ptk = "PT%d" % pj
                for m in range(2):
                    P.op("pe", lambda e, kb=kb, m=m, pt=pt, c0=c0: e.matmul(out=bank(4 + m)[:, c0:512], lhsT=V[:, kb, :], rhs=pt[:, m, c0:512], start=(kb == 0), stop=(kb == nkb - 1)), [ptk + ":%d" % m, "Vh0"], [bkey(4 + m)])
                    P.op("pe", lambda e, kb=kb, m=m, pt=pt, c0=c0: e.matmul(out=bank(6)[:, c0:512], lhsT=Esel[:, m, :], rhs=pt[:, m, c0:512], start=(kb == 0 and m == 0), stop=(kb == nkb - 1 and m == 1)), [ptk + ":%d" % m, "Esel"], [bkey(6)])

            emit_scores(0)
            for kb in range(nkb):
                if kb + 1 < nkb:
                    emit_scores(kb + 1)
                emit_pv(kb)
            P.op("act", lambda e, ot_=ot_: e.activation(out=ot_[:, 0, :], in_=bank(4), func=AF.Copy), [bkey(4)], [otk + ":0"])
            P.op("dve", lambda e, ot_=ot_: e.tensor_copy(out=ot_[:, 1, :], in_=bank(5)), [bkey(5)], [otk + ":1"])
            P.op("dve", lambda e, dr=dr: e.tensor_copy(out=dr, in_=bank(6)), [bkey(6)], [drk])
            for isl in range(4):
                P.op("pe", lambda e, isl=isl, dr=dr: e.matmul(out=dcol[:, isl * 32:(isl + 1) * 32], lhsT=dr[:, isl * 128:(isl + 1) * 128], rhs=ident_f[:, 0:32], start=True, stop=True), [drk, "ident_f"], [bkey(6)])
            rc = rcol[gj]; rck = "rcol%d" % gj
            P.op("dve", lambda e, rc=rc: e.reciprocal(out=rc, in_=dcol), [bkey(6)], [rck])
            for isl in range(4):
                for m in range(2):
                    P.op("pe", lambda e, isl=isl, m=m, ot_=ot_: e.transpose(out=tpv[:, isl, m, :], in_=ot_[:, m, isl * 128:(isl + 1) * 128], identity=ident_b), [otk + ":%d" % m, "ident_b"], [bkey(7)])
            for isl in range(4):
                s_ = G * 4 + isl
                j = (h * 16 + s_) % 2
                e_ = ep[j]; o_ = eo[j]; ek = "ep%d" % j; ok_ = "eo%d" % j
                P.op("dve", lambda e, e_=e_, isl=isl, rc=rc: e.tensor_copy(out=e_[:, 0:2], in_=rc[:, isl * 32:isl * 32 + 2]), [rck], [ek])
                P.op("dve", lambda e, e_=e_: e.tensor_tensor(out=e_[:, 2:3], in0=e_[:, 1:2], in1=lamt[:, 5:6], op=ALU.mult), [ek, "neglam"], [ek + ":2"])
                P.op("dve", lambda e, e_=e_, o_=o_, isl=isl: e.tensor_scalar(out=o_[:, 0:128], in0=tpv[:, isl, 1, :], scalar1=e_[:, 2:3], scalar2=None, op0=ALU.mult), [bkey(7), ek + ":2"], [ok_])
                P.op("dve", lambda e, e_=e_, o_=o_, isl=isl: e.scalar_tensor_tensor(out=o_[:, 128:256], in0=tpv[:, isl, 0, :], scalar=e_[:, 0:1], in1=o_[:, 0:128], op0=ALU.mult, op1=ALU.add), [bkey(7), ek, ok_], [ok_ + ":1"])
                P.op("act", lambda e, e_=e_, o_=o_: e.activation(out=o_[:, 256:384], in_=o_[:, 128:256], func=AF.Square, accum_out=e_[:, 3:4]), [ok_ + ":1"], [ok_ + ":2", ek + ":3"])
                P.op("act", lambda e, e_=e_: e.activation(out=e_[:, 4:5], in_=e_[:, 3:4], func=AF.Ln, scale=1.0 / 128, bias=epst), [ek + ":3", "epst"], [ek + ":4"])
                P.op("act", lambda e, e_=e_: e.activation(out=e_[:, 5:6], in_=e_[:, 4:5], func=AF.Exp, scale=-0.5), [ek + ":4"], [ek + ":5"])
                P.op("dve", lambda e, e_=e_, o_=o_, s_=s_, h=h: e.scalar_tensor_tensor(out=ya[:, s_, h * 128:(h + 1) * 128], in0=o_[:, 128:256], scalar=e_[:, 5:6], in1=hn, op0=ALU.mult, op1=ALU.mult), [ok_ + ":1", ek + ":5", "hn"], ["ya"])
    for n in ["Kh0", "Kh1", "Vh0", "Qh0", "PT0", "PT1", "denrow0", "denrow1", "rcol0", "rcol1", "Esel_f", "OT0", "OT1", "ep0", "ep1", "eo0", "eo1", "lq", "lqp", "kvf"]:
        A.free(n)

    if debug:
        ya_dbg = dscr("ya_dbg", [128, 16 * 1024])
        dma(ya_dbg, ya.rearrange("p s f -> p (s f)"), reads=["ya"], writes=["ya_dbg"])
    def load_bf16(name):
        dst_d, src_, K_, N_, gn_ = wsc[name]
        KC = K_ // 128
        wt = A.alloc(name, KC * N_, BF16).rearrange("p (k n) -> p k n", k=KC)
        for kc in range(KC):
            dma(wt[:, kc, :], dst_d[kc * 128:(kc + 1) * 128, :], reads=[name + "_d"], writes=[name], q="sp" if kc % 2 == 0 else "act")
        return wt

    gpost = A.alloc("gpost", 1024, F32)
    pst = A.alloc("pst", 8, F32)
    psq = A.alloc("psq", 1024, BF16)
    ptmp = A.alloc("ptmp", 1024, F32)
    ost = [A.alloc("ost%d" % i, 1024, F32) for i in range(2)]
    xres = [A.alloc("xres%d" % i, 1024, F32) for i in range(2)]

    def post_norm_residual(pb0, gain_bc, gkey, res_in, res_in_keys, res_out, res_out_key):
        for half in range(2):
            P.op("act", lambda e, half=half: e.activation(out=psq[:, half * 512:(half + 1) * 512], in_=bank(pb0 + half), func=AF.Square, accum_out=pst[:, half:half + 1]), [bkey(pb0 + half)], ["psq", "pst:%d" % half])
        P.op("dve", lambda e: e.tensor_tensor(out=pst[:, 2:3], in0=pst[:, 0:1], in1=pst[:, 1:2], op=ALU.add), ["pst:0", "pst:1"], ["pst:2"])
        P.op("act", lambda e: e.activation(out=pst[:, 3:4], in_=pst[:, 2:3], func=AF.Ln, scale=1.0 / D, bias=epst), ["pst:2", "epst"], ["pst:3"])
        P.op("act", lambda e: e.activation(out=pst[:, 4:5], in_=pst[:, 3:4], func=AF.Exp, scale=-0.5), ["pst:3"], ["pst:4"])
        for half in range(2):
            P.op("dve", lambda e, half=half: e.scalar_tensor_tensor(out=ptmp[:, half * 512:(half + 1) * 512], in0=bank(pb0 + half), scalar=pst[:, 4:5], in1=gain_bc[:, half * 512:(half + 1) * 512], op0=ALU.mult, op1=ALU.mult), [bkey(pb0 + half), "pst:4", gkey], ["ptmp:%d" % half])
        P.op("pool", lambda e: e.tensor_tensor(out=res_out, in0=ptmp, in1=res_in, op=ALU.add), ["ptmp:0", "ptmp:1"] + res_in_keys, [res_out_key])

    wglu = load_bf16("wglu")
    wssm = load_bf16("wssm")
    ys2 = A.alloc("ys2", 8 * 512, BF16).rearrange("p (k t) -> p k t", k=8)
    gab = A.alloc("gab", 8 * 512, BF16).rearrange("p (k t) -> p k t", k=8)
    sg = [A.alloc("sg%d" % i, 512, BF16) for i in range(2)]
    for tt in range(4):
        dma(gab, g_d[0:8, :, tt * 512:(tt + 1) * 512].rearrange("k p t -> p k t"), reads=["g_d"], writes=["gab"], q="pool")
        for mc in range(8):
            pb = 2 + mc % 2
            for kc in range(8):
                P.op("pe", lambda e, kc=kc, mc=mc, pb=pb, tt=tt: e.matmul(out=bank(pb), lhsT=wglu[:, kc, mc * 128:(mc + 1) * 128], rhs=ysg[:, kc, tt * 512:(tt + 1) * 512], start=(kc == 0), stop=(kc == 7)), ["wglu", "ysg"], [bkey(pb)])
            j = mc % 2
            P.op("act", lambda e, pb=pb, j=j, mc=mc: e.activation(out=sg[j], in_=bank(pb), func=AF.Sigmoid, bias=bglu_pp[:, mc:mc + 1]), [bkey(pb), "bglu_pp"], ["sg%d" % j])
            P.op("pool", lambda e, j=j, mc=mc, tt=tt: e.tensor_tensor(out=ys2[:, mc, :], in0=sg[j], in1=ysg[:, mc, tt * 512:(tt + 1) * 512], op=ALU.mult), ["sg%d" % j, "ysg"], ["ys2"])
        for mc in range(8):
            pa = 4 + (mc % 2)
            for kc in range(8):
                P.op("pe", lambda e, kc=kc, mc=mc, pa=pa: e.matmul(out=bank(pa), lhsT=wssm[:, kc, mc * 128:(mc + 1) * 128], rhs=ys2[:, kc, :], start=(kc == 0), stop=(kc == 7)), ["wssm", "ys2"], [bkey(pa)])
            P.op("dve", lambda e, pa=pa, mc=mc, tt=tt: e.tensor_tensor(out=ysg[:, mc, tt * 512:(tt + 1) * 512], in0=bank(pa), in1=gab[:, mc, :], op=ALU.mult), [bkey(pa), "gab"], ["ysg"])
    for n in ["wglu", "wssm", "ys2", "sg0", "sg1"]:
        A.free(n)
    wda = load_bf16("wda")
    wmix = load_bf16("wmix")
    dma(gpost, gains["norm_mix_post"].partition_broadcast(128), writes=["gpost"])
    yaT = A.alloc("yaT", 8 * 512, BF16).rearrange("p (k t) -> p k t", k=8)
    mrg = A.alloc("mrg", 8 * 512, BF16).rearrange("p (k t) -> p k t", k=8)
    tb = [A.alloc("tb%d" % i, 512, F32) for i in range(2)]
    for tt in range(4):
        for bl in range(4):
            s = tt * 4 + bl
            tpb = bl % 2
            tp = bank_bf(tpb).rearrange("p (k t) -> p k t", k=8)
            for kc in range(8):
                P.op("pe", lambda e, kc=kc, s=s, tp=tp: e.transpose(out=tp[:, kc, :], in_=ya[:, s, kc * 128:(kc + 1) * 128], identity=ident_b), ["ya", "ident_b"], [bkey(tpb)])
            P.op("act", lambda e, tp=tp, bl=bl: e.activation(out=yaT[:, :, bl * 128:(bl + 1) * 128], in_=tp, func=AF.Copy), [bkey(tpb)], ["yaT"])
        dma(gab, g_d[8:16, :, tt * 512:(tt + 1) * 512].rearrange("k p t -> p k t"), reads=["g_d"], writes=["gab"], q="pool")
        for mc in range(8):
            pbb_ = 4 + (mc % 2)
            for kc in range(8):
                P.op("pe", lambda e, kc=kc, mc=mc, pbb_=pbb_: e.matmul(out=bank(pbb_), lhsT=wda[:, kc, mc * 128:(mc + 1) * 128], rhs=yaT[:, kc, :], start=(kc == 0), stop=(kc == 7)), ["wda", "yaT"], [bkey(pbb_)])
            j = mc % 2
            P.op("dve", lambda e, pbb_=pbb_, j=j, mc=mc: e.tensor_tensor(out=tb[j], in0=bank(pbb_), in1=gab[:, mc, :], op=ALU.mult), [bkey(pbb_), "gab"], ["tb%d" % j])
            P.op("pool", lambda e, j=j, mc=mc, tt=tt: e.tensor_tensor(out=mrg[:, mc, :], in0=tb[j], in1=ysg[:, mc, tt * 512:(tt + 1) * 512], op=ALU.add), ["tb%d" % j, "ysg"], ["mrg"])
        for bl in range(4):
            s = tt * 4 + bl
            pb0 = 2 * (bl % 2)
            for half in range(2):
                for kc in range(8):
                    P.op("pe", lambda e, kc=kc, bl=bl, half=half, pb0=pb0: e.matmul(out=bank(pb0 + half), lhsT=mrg[:, kc, bl * 128:(bl + 1) * 128], rhs=wmix[:, kc, half * 512:(half + 1) * 512], start=(kc == 0), stop=(kc == 7)), ["mrg", "wmix"], [bkey(pb0 + half)])
            xr = xres[s % 2]; xrk = "xres%d" % (s % 2)
            dma(xr, xown[s * 128:(s + 1) * 128, :], writes=[xrk], q="sp")
            o_ = ost[s % 2]; ok_ = "ost%d" % (s % 2)
            post_norm_residual(pb0, gpost, "gpost", xr, [xrk], o_, ok_)
            dma(x1_d[s * 128:(s + 1) * 128, :], o_, reads=[ok_], writes=["x1_d"], q="sp")
    for n in ["wda", "wmix", "yaT", "mrg", "gab", "tb0", "tb1", "ya", "ysg"]:
        A.free(n)

    wxkv = load_bf16("wxkv")
    memT = A.alloc("memT", 8 * 256, BF16).rearrange("p (k t) -> p k t", k=8)
    for mb in range(2):
        norm_block_T(mem[mb * 128:(mb + 1) * 128, :], True, memT[:, :, mb * 128:(mb + 1) * 128], "memT")
    mkT = A.alloc("mkT", 8 * 256, BF16).rearrange("p (k t) -> p k t", k=8)
    mv = A.alloc("mv", 2 * 1024, BF16).rearrange("p (m f) -> p m f", m=2)
    for mc in range(8):
        pb = mc % 2
        for kc in range(8):
            P.op("pe", lambda e, kc=kc, mc=mc, pb=pb: e.matmul(out=bank(pb)[:, 0:256], lhsT=wxkv[:, kc, mc * 128:(mc + 1) * 128], rhs=memT[:, kc, :], start=(kc == 0), stop=(kc == 7)), ["wxkv", "memT"], [bkey(pb)])
        P.op("act", lambda e, pb=pb, mc=mc: e.activation(out=mkT[:, mc, :], in_=bank(pb)[:, 0:256], func=AF.Copy), [bkey(pb)], ["mkT"])
    for mt in range(2):
        for half in range(2):
            pb = 2 + half
            for kc in range(8):
                P.op("pe", lambda e, kc=kc, mt=mt, half=half, pb=pb: e.matmul(out=bank(pb), lhsT=memT[:, kc, mt * 128:(mt + 1) * 128], rhs=wxkv[:, kc, 1024 + half * 512:1024 + (half + 1) * 512], start=(kc == 0), stop=(kc == 7)), ["wxkv", "memT"], [bkey(pb)])
            P.op("act", lambda e, pb=pb, mt=mt, half=half: e.activation(out=mv[:, mt, half * 512:(half + 1) * 512], in_=bank(pb), func=AF.Copy), [bkey(pb)], ["mv"])
    A.free("wxkv"); A.free("memT")
    wxq = load_bf16("wxq")
    wxo = load_bf16("wxo")
    dma(gpost, gains["norm_x_post"].partition_broadcast(128), writes=["gpost"])
    ones_b = A.alloc("ones_b", 128, BF16)
    P.op("pool", lambda e: e.memset(ones_b, 1.0), [], ["ones_b"])
    h2T = A.alloc("h2T", 8 * 512, BF16).rearrange("p (k t) -> p k t", k=8)
    xqT = A.alloc("xqT", 8 * 512, BF16).rearrange("p (k t) -> p k t", k=8)
    xoT = A.alloc("xoT", 8 * 512, BF16).rearrange("p (k t) -> p k t", k=8)
    xp = [A.alloc("xp%d" % i, 2 * 512, BF16).rearrange("p (m t) -> p m t", m=2) for i in range(2)]
    rden = [A.alloc("rden%d" % i, 512, F32) for i in range(2)]
    for tt in range(4):
        for bl in range(4):
            s = tt * 4 + bl
            norm_block_T(x1_d[s * 128:(s + 1) * 128, :], True, h2T[:, :, bl * 128:(bl + 1) * 128], "h2T", tp_bank=7)
        for mc in range(8):
            pb = mc % 2
            for kc in range(8):
                P.op("pe", lambda e, kc=kc, mc=mc, pb=pb: e.matmul(out=bank(pb), lhsT=wxq[:, kc, mc * 128:(mc + 1) * 128], rhs=h2T[:, kc, :], start=(kc == 0), stop=(kc == 7)), ["wxq", "h2T"], [bkey(pb)])
            P.op("act", lambda e, pb=pb, mc=mc: e.activation(out=xqT[:, mc, :], in_=bank(pb), func=AF.Copy), [bkey(pb)], ["xqT"])
        for hh in range(4):
            j = hh % 2
            for mt in range(2):
                pb = 2 + mt
                for dc in range(2):
                    P.op("pe", lambda e, hh=hh, mt=mt, dc=dc, pb=pb: e.matmul(out=bank(pb), lhsT=mkT[:, hh * 2 + dc, mt * 128:(mt + 1) * 128], rhs=xqT[:, hh * 2 + dc, :], start=(dc == 0), stop=(dc == 1)), ["mkT", "xqT"], [bkey(pb)])
                P.op("act", lambda e, pb=pb, j=j, mt=mt: e.activation(out=xp[j][:, mt, :], in_=bank(pb), func=AF.Exp, scale=1.0 / 16), [bkey(pb)], ["xp%d" % j])
            for mt in range(2):
                P.op("pe", lambda e, j=j, mt=mt: e.matmul(out=bank(4), lhsT=ones_b, rhs=xp[j][:, mt, :], start=(mt == 0), stop=(mt == 1)), ["ones_b", "xp%d" % j], [bkey(4)])
            P.op("dve", lambda e, j=j: e.reciprocal(out=rden[j], in_=bank(4)), [bkey(4)], ["rden%d" % j])
            for dc in range(2):
                pb = 5 + dc
                for mt in range(2):
                    P.op("pe", lambda e, hh=hh, j=j, mt=mt, dc=dc, pb=pb: e.matmul(out=bank(pb), lhsT=mv[:, mt, (hh * 2 + dc) * 128:(hh * 2 + dc + 1) * 128], rhs=xp[j][:, mt, :], start=(mt == 0), stop=(mt == 1)), ["mv", "xp%d" % j], [bkey(pb)])
                P.op("dve", lambda e, hh=hh, j=j, dc=dc, pb=pb: e.tensor_tensor(out=xoT[:, hh * 2 + dc, :], in0=bank(pb), in1=rden[j], op=ALU.mult), [bkey(pb), "rden%d" % j], ["xoT"])
        for bl in range(4):
            s = tt * 4 + bl
            pb0 = 2 * (bl % 2)
            for half in range(2):
                for kc in range(8):
                    P.op("pe", lambda e, kc=kc, bl=bl, half=half, pb0=pb0: e.matmul(out=bank(pb0 + half), lhsT=xoT[:, kc, bl * 128:(bl + 1) * 128], rhs=wxo[:, kc, half * 512:(half + 1) * 512], start=(kc == 0), stop=(kc == 7)), ["xoT", "wxo"], [bkey(pb0 + half)])
            xr = xres[s % 2]; xrk = "xres%d" % (s % 2)
            dma(xr, x1_d[s * 128:(s + 1) * 128, :], reads=["x1_d"], writes=[xrk], q="sp")
            o_ = ost[s % 2]; ok_ = "ost%d" % (s % 2)
            post_norm_residual(pb0, gpost, "gpost", xr, [xrk], o_, ok_)
            dma(x2_d[s * 128:(s + 1) * 128, :], o_, reads=[ok_], writes=["x2_d"], q="sp")
    for n in ["wxq", "wxo", "mkT", "mv", "h2T", "xqT", "xoT", "xp0", "xp1", "rden0", "rden1"]:
        A.free(n)

    dma(gpost, gains["norm_ff_post"].partition_broadcast(128), writes=["gpost"])
    for n in ["wstage0", "wstage1", "xs0", "xs1", "nsq0", "nsq1", "nh0", "nh1"]:
        if n in A.live:
            A.free(n)
    f1 = A.alloc("f1", 32 * 512, BF16).rearrange("p (k t) -> p k t", k=32)
    wq1 = [A.alloc("wq1_%d" % i, 8 * 1024, BF16).rearrange("p (k n) -> p k n", k=8) for i in range(2)]
    h3T = A.alloc("h3T", 8 * 512, BF16).rearrange("p (k t) -> p k t", k=8)
    fr = [A.alloc("fr%d" % i, 512, BF16) for i in range(2)]
    wctr2 = [0]
    for tt in range(4):
        for bl in range(4):
            s = tt * 4 + bl
            norm_block_T(x2_d[s * 128:(s + 1) * 128, :], True, h3T[:, :, bl * 128:(bl + 1) * 128], "h3T", tp_bank=7)
        for q4 in range(4):
            i = wctr2[0]; wctr2[0] += 1
            w1 = wq1[i % 2]; w1k = "wq1_%d" % (i % 2)
            dma(w1, wf1_d[:, q4 * 1024:(q4 + 1) * 1024].rearrange("(k p) n -> p k n", p=128), reads=["wf_d"], writes=[w1k], q="sp")
            for fl in range(8):
                fc = q4 * 8 + fl
                pb = 4 + fc % 2
                for kc in range(8):
                    P.op("pe", lambda e, kc=kc, fl=fl, pb=pb, w1=w1: e.matmul(out=bank(pb), lhsT=w1[:, kc, fl * 128:(fl + 1) * 128], rhs=h3T[:, kc, :], start=(kc == 0), stop=(kc == 7)), [w1k, "h3T"], [bkey(pb)])
                j = fc % 2
                P.op("act", lambda e, pb=pb, j=j: e.activation(out=fr[j], in_=bank(pb), func=AF.Relu), [bkey(pb)], ["fr%d" % j])
                P.op("pool", lambda e, j=j, fc=fc: e.tensor_tensor(out=f1[:, fc, :], in0=fr[j], in1=fr[j], op=ALU.mult), ["fr%d" % j], ["f1"])
        for q4 in range(4):
            i = wctr2[0]; wctr2[0] += 1
            w2 = wq1[i % 2]; w2k = "wq1_%d" % (i % 2)
            dma(w2, wf2_d[q4 * 1024:(q4 + 1) * 1024, :].rearrange("(k p) n -> p k n", p=128), reads=["wf_d"], writes=[w2k], q="sp")
            for bl in range(4):
                for half in range(2):
                    pbk_ = bl * 2 + half
                    for kcl in range(8):
                        P.op("pe", lambda e, kcl=kcl, bl=bl, half=half, pbk_=pbk_, q4=q4, w2=w2: e.matmul(out=bank(pbk_), lhsT=f1[:, q4 * 8 + kcl, bl * 128:(bl + 1) * 128], rhs=w2[:, kcl, half * 512:(half + 1) * 512], start=(q4 == 0 and kcl == 0), stop=(q4 == 3 and kcl == 7)), ["f1", w2k], [bkey(pbk_)])
        for bl in range(4):
            s = tt * 4 + bl
            pb0 = 2 * bl
            xr = xres[s % 2]; xrk = "xres%d" % (s % 2)
            dma(xr, x2_d[s * 128:(s + 1) * 128, :], reads=["x2_d"], writes=[xrk], q="sp")
            o_ = ost[s % 2]; ok_ = "ost%d" % (s % 2)
            post_norm_residual(pb0, gpost, "gpost", xr, [xrk], o_, ok_)
            dma(out_d[s * 128:(s + 1) * 128, :], o_, reads=[ok_], writes=["out_d"], q="sp")

    P.emit()
    es.close()
    return nc


def _rope_tables(pos):
    inv = (10000.0 ** (-np.arange(0, 64, 2, dtype=np.float32) / 64)).astype(np.float32)
    ang = pos.astype(np.float32)[:, None] * inv[None, :]
    c = np.cos(ang).astype(np.float32).T
    s = np.sin(ang).astype(np.float32).T
    return np.ascontiguousarray(np.tile(c, (4, 1))), np.ascontiguousarray(np.tile(s, (4, 1)))


_NC_CACHE = {}


def make_in_maps(inputs):
    x = np.asarray(inputs["x"], dtype=np.float32)
    memv = np.asarray(inputs["mem"], dtype=np.float32)
    ident = np.eye(128, dtype=np.float32)
    swapm = np.zeros((128, 128), np.float32)
    for p in range(64):
        swapm[p, 64 + p] = 1.0
        swapm[64 + p, p] = 1.0
    gmask = np.zeros((128, 8, 128), np.float32)
    for gl in range(8):
        gmask[:, gl, gl * 16:(gl + 1) * 16] = 1.0
    esel = np.zeros((128, 256), np.float32)
    esel[:, 0] = 1.0
    esel[:, 129] = 1.0
    shared = {}
    for k, v in inputs.items():
        if k in ("x", "mem"):
            continue
        a = np.asarray(v, dtype=np.float32)
        shared[k] = np.ascontiguousarray(a[0])
    in_maps = []
    for c in range(8):
        b, j = c // 4, c % 4
        pad = (3 - j) * 128
        xs = np.zeros((SEQ, D), np.float32)
        xs[pad:] = x[b, :SEQ - pad]
        own_blocks = [4 * s + j for s in range(16)]
        xo = np.concatenate([x[b, r * 128:(r + 1) * 128] for r in own_blocks], axis=0)
        pos_seq = np.arange(SEQ) - pad
        cseq, sseq = _rope_tables(pos_seq)
        pos_own = np.concatenate([np.arange(r * 128, (r + 1) * 128) for r in own_blocks])
        cown, sown = _rope_tables(pos_own)
        kval = (pos_seq >= 0).astype(np.float32)
        m = dict(shared)
        m.update(xseq=xs, xown=np.ascontiguousarray(xo), mem=np.ascontiguousarray(memv[b]), cos_seq=cseq, sin_seq=sseq,
                 cos_own=cown, sin_own=sown, kvalid=kval, esel=esel, ident=ident, swapm=swapm, gmask=gmask)
        in_maps.append(m)
    return in_maps


def kernel(**inputs):
    if "nc" not in _NC_CACHE:
        _NC_CACHE["nc"] = build_program()
    nc = _NC_CACHE["nc"]
    in_maps = make_in_maps(inputs)
    res = run_bass_kernel_spmd(nc, in_maps, core_ids=list(range(8)))
    out = np.zeros((2, SEQ, D), np.float32)
    for c in range(8):
        b, j = c // 4, c % 4
        o = res.results[c]["out"]
        for s in range(16):
            r = 4 * s + j
            out[b, r * 128:(r + 1) * 128] = o[s * 128:(s + 1) * 128]
    return out
```

```python
import contextlib
import math
import numpy as np
import concourse.bass as bass
import concourse.mybir as mybir
from concourse.bass_utils import run_bass_kernel_spmd

F32 = mybir.dt.float32
BF16 = mybir.dt.bfloat16
ALU = mybir.AluOpType
AF = mybir.ActivationFunctionType
AX = mybir.AxisListType

D = 1024
SEQ = 8192
NB = 64
NOWN = 2048
EPS = 1e-6
NLEV = 13
ENGS = ["pe", "act", "dve", "pool", "sp"]


class Op:
    __slots__ = ("eng", "idx", "fn", "deps", "is_dma", "needs_inc", "semval", "dsem", "dval")

    def __init__(self, eng, idx, fn, is_dma):
        self.eng, self.idx, self.fn, self.is_dma = eng, idx, fn, is_dma
        self.deps = []
        self.needs_inc = False
        self.semval = None
        self.dsem = None
        self.dval = None


class Prog:
    def __init__(self, nc, n_dma_sems=16):
        self.nc = nc
        self.ops = {e: [] for e in ENGS}
        self.state = {}
        self.rings = {"sp": (0, 12), "pool": (12, 8), "act": (20, 8), "dve": (28, 2), "pe": (28, 2)}
        n_dma_sems = 30
        self.n_dma_sems = n_dma_sems
        self.dma_rr = {q: 0 for q in self.rings}
        self.dma_counts = [0] * n_dma_sems
        self.waited = {}
        self.waited_dma = {}

    def _st(self, key):
        s = self.state.get(key)
        if s is None:
            s = {"w": {}, "r": {}}
            if isinstance(key, str) and ":" in key:
                base = self.state.get(key.split(":")[0])
                if base is not None:
                    s["w"] = dict(base["w"])
            self.state[key] = s
        return s

    def _add_dep(self, op, dep):
        if dep is None or dep is op:
            return
        if dep.is_dma:
            k = (op.eng, dep.dsem)
            if self.waited_dma.get(k, -1) >= dep.dval:
                return
            self.waited_dma[k] = dep.dval
            op.deps.append(dep)
            return
        k = (op.eng, dep.eng)
        if self.waited.get(k, -1) >= dep.idx:
            return
        self.waited[k] = dep.idx
        dep.needs_inc = True
        op.deps.append(dep)

    def op(self, eng, fn, reads=(), writes=(), dma=False):
        lst = self.ops[eng]
        o = Op(eng, len(lst), fn, dma)
        if dma:
            base, cnt_ = self.rings[eng]
            i = base + self.dma_rr[eng]
            self.dma_rr[eng] = (self.dma_rr[eng] + 1) % cnt_
            self.dma_counts[i] += 16
            o.dsem, o.dval = i, self.dma_counts[i]
        for key in reads:
            for e, w in self._st(key)["w"].items():
                if (not w.is_dma) and w.eng == eng and eng == "pe":
                    continue
                self._add_dep(o, w)
        for key in writes:
            s = self._st(key)
            for e, r in s["r"].items():
                if (not r.is_dma) and r.eng == eng and not dma:
                    continue
                self._add_dep(o, r)
            for e, w in s["w"].items():
                if (not w.is_dma) and w.eng == eng and not dma:
                    continue
                self._add_dep(o, w)
        me = ("dma%d" % o.dsem) if dma else eng
        for key in reads:
            self._st(key)["r"][me] = o
        for key in writes:
            s = self._st(key)
            s["w"][me] = o
            s["r"] = {}
        lst.append(o)
        return o

    def alias(self, newkey, oldkeys):
        ns = self._st(newkey)
        for ok in oldkeys:
            os_ = self.state.get(ok)
            if os_ is None:
                continue
            for kind in ("w", "r"):
                for e, o in os_[kind].items():
                    cur = ns["w"].get(e)
                    if cur is None or (o.is_dma and o.dval > cur.dval) or ((not o.is_dma) and o.idx > cur.idx):
                        ns["w"][e] = o

    def emit(self):
        nc = self.nc
        with contextlib.ExitStack() as es:
            sems = {e: es.enter_context(nc.semaphore("s_" + e)) for e in ["pe", "act", "dve", "pool"]}
            dsems = [es.enter_context(nc.semaphore("d%d" % i)) for i in range(self.n_dma_sems)]
            for e in ENGS:
                c = 0
                for o in self.ops[e]:
                    if (not o.is_dma) and o.needs_inc:
                        c += 1
                        o.semval = c
            block = es.enter_context(nc.Block())

            def run(name, eng):
                for o in self.ops[name]:
                    for d in o.deps:
                        if d.is_dma:
                            eng.wait_ge(dsems[d.dsem], d.dval)
                        else:
                            eng.wait_ge(sems[d.eng], d.semval)
                    ins = o.fn(eng)
                    if o.is_dma:
                        ins.then_inc(dsems[o.dsem], 16)
                    elif o.needs_inc:
                        ins.then_inc(sems[o.eng], 1)

            @block.tensor
            def _(eng):
                run("pe", eng)

            @block.scalar
            def _(eng):
                run("act", eng)

            @block.vector
            def _(eng):
                run("dve", eng)

            @block.gpsimd
            def _(eng):
                run("pool", eng)

            @block.sync
            def _(eng):
                run("sp", eng)
                for i in range(self.n_dma_sems):
                    if self.dma_counts[i] > 0:
                        eng.wait_ge(dsems[i], self.dma_counts[i])


class Arena:
    def __init__(self, P, base_ap, nbytes):
        self.P = P
        self.base = base_ap
        self.nbytes = nbytes
        self.live = {}
        self.freed = []

    def alloc(self, name, nelem, dt):
        esz = 4 if dt == F32 else 2
        size = (nelem * esz + 63) // 64 * 64
        segs = sorted(self.live.values())
        off = 0
        for (o, s) in segs:
            if off + size <= o:
                break
            off = max(off, o + s)
        assert off + size <= self.nbytes, "SBUF arena overflow for %s (%d): %s" % (name, size, sorted((o, sz, n) for n, (o, sz) in self.live.items()))
        self.live[name] = (off, size)
        olds = [n for (o, s, n) in self.freed if o < off + size and off < o + s]
        oldkeys = [k for k in self.P.state if any(k == n or (isinstance(k, str) and k.startswith(n + ":")) for n in olds)]
        self.P.alias(name, oldkeys)
        self._aliaskeys = oldkeys
        ap = self.base[:, off // 4:(off + size) // 4]
        if dt != F32:
            ap = ap.bitcast(dt)
        return ap[:, 0:nelem]

    def free(self, name):
        o, s = self.live.pop(name)
        self.freed.append((o, s, name))


def build_program(debug=False):
    nc = bass.Bass("TRN2", target_bir_lowering=False)

    def din(name, shape, dt=F32):
        return nc.dram_tensor(name, list(shape), dt, kind="ExternalInput").ap()

    def dscr(name, shape, dt=BF16):
        return nc.dram_tensor(name, list(shape), dt, kind="ExternalOutput" if debug else "Internal").ap()

    xseq = din("xseq", [SEQ, D])
    xown = din("xown", [NOWN, D])
    mem = din("mem", [256, D])
    w_in = din("w_in", [D, 6144])
    w_glu = din("w_glu", [D, D]); w_ssm = din("w_ssm_proj", [D, D]); w_da = din("w_da_proj", [D, D])
    w_mix = din("w_mix_out", [D, D]); w_xq = din("w_xq", [D, D]); w_xkv = din("w_xkv", [D, 2 * D])
    w_xo = din("w_xo", [D, D]); w_ff1 = din("w_ff1", [D, 4 * D]); w_ff2 = din("w_ff2", [4 * D, D])
    gains = {n: din(n, [D]) for n in ["norm_mix_pre", "norm_mix_post", "norm_x_pre", "norm_mem", "norm_x_post",
                                      "norm_ff_pre", "norm_ff_post"]}
    b_gate = din("b_gate", [2 * D]); b_glu = din("b_glu", [D]); ssm_d = din("ssm_d", [D])
    lam_re = din("ssm_lambda_re", [64, 64]); lam_im = din("ssm_lambda_im", [64, 64]); log_dt = din("ssm_log_dt", [64])
    b_re = din("ssm_b_re", [64, 64, 16]); b_im = din("ssm_b_im", [64, 64, 16])
    c_re = din("ssm_c_re", [64, 16, 64]); c_im = din("ssm_c_im", [64, 16, 64])
    lqk = [din(n, [64]) for n in ["da_lambda_q1", "da_lambda_k1", "da_lambda_q2", "da_lambda_k2"]]
    head_norm = din("da_head_norm", [128])
    cos_seq = din("cos_seq", [128, SEQ]); sin_seq = din("sin_seq", [128, SEQ])
    cos_own = din("cos_own", [128, NOWN]); sin_own = din("sin_own", [128, NOWN])
    kvalid = din("kvalid", [SEQ])
    esel_d = din("esel", [128, 256])
    ident_d = din("ident", [128, 128]); swap_d = din("swapm", [128, 128]); gmask_d = din("gmask", [128, 8, 128])
    out_d = nc.dram_tensor("out", [NOWN, D], F32, kind="ExternalOutput").ap()

    kT_d = dscr("kT_d", [8, 128, SEQ]); v_d = dscr("v_d", [SEQ, D]); uT_d = dscr("uT_d", [8, 128, SEQ])
    qT_d = dscr("qT_d", [8, 128, NOWN]); g_d = dscr("g_d", [16, 128, NOWN])

    P = Prog(nc)
    es = contextlib.ExitStack()
    ARENA_BYTES = 190 * 1024
    arena_t = es.enter_context(nc.sbuf_tensor("arena", [128, ARENA_BYTES // 4], F32))
    A = Arena(P, arena_t[:], ARENA_BYTES)
    banks = [es.enter_context(nc.psum_tensor("bank%d" % i, [128, 512], F32)) for i in range(8)]

    def bank(i):
        return banks[i][:]

    def bank_bf(i):
        return banks[i][:].bitcast(BF16)

    def bkey(i):
        return "bank%d" % i

    def dma(out, in_, reads=(), writes=(), q="sp", slow=False):
        if slow:
            return P.op(q, lambda e: e.dma_start(out=out, in_=in_, allow_slow_non_contiguous=True), reads, writes, dma=True)
        return P.op(q, lambda e: e.dma_start(out=out, in_=in_), reads, writes, dma=True)

    ident_f = A.alloc("ident_f", 128, F32); ident_b = A.alloc("ident_b", 128, BF16)
    swap_f = A.alloc("swap_f", 128, F32)
    gmask = A.alloc("gmask", 1024, F32)
    epst = A.alloc("epst", 1, F32); zerot = A.alloc("zerot", 1, F32)
    dma(ident_f, ident_d, writes=["ident_f"]); dma(swap_f, swap_d, writes=["swap_f"])
    dma(gmask, gmask_d.rearrange("p g c -> p (g c)"), writes=["gmask"])
    P.op("pool", lambda e: e.tensor_copy(out=ident_b, in_=ident_f), ["ident_f"], ["ident_b"])
    P.op("pool", lambda e: e.memset(epst, EPS), [], ["epst"])
    P.op("pool", lambda e: e.memset(zerot, 0.0), [], ["zerot"])

    def load_pp(name, src, n):
        t = A.alloc(name, n, F32)
        dma(t, src.rearrange("(k p) -> p k", p=128), writes=[name], slow=True)
        return t

    g_mix_pre = load_pp("g_mix_pre", gains["norm_mix_pre"], 8)
    g_x_pre = load_pp("g_x_pre", gains["norm_x_pre"], 8)
    g_mem = load_pp("g_mem", gains["norm_mem"], 8)
    g_ff_pre = load_pp("g_ff_pre", gains["norm_ff_pre"], 8)
    bgate_pp = load_pp("bgate_pp", b_gate, 16)
    bglu_pp = load_pp("bglu_pp", b_glu, 8)
    ssmd_pp = load_pp("ssmd_pp", ssm_d, 8)

    wctr = [0]

    def load_weight(name, src, K, N, gain=None, col0=0, rot=False):
        KC = K // 128
        wt = A.alloc(name, KC * N, BF16)
        wv = wt.rearrange("p (k n) -> p k n", k=KC)
        CH = min(N, 2048)
        for kc in range(KC):
            for c0 in range(0, N, CH):
                i = wctr[0]; wctr[0] += 1
                sname = "wstage%d" % (i % 2)
                if sname not in A.live:
                    A.alloc(sname, 2048, F32)
                o, s = A.live[sname]
                st = A.base[:, o // 4:o // 4 + CH]
                dma(st, src[kc * 128:(kc + 1) * 128, col0 + c0:col0 + c0 + CH], writes=[sname], q="sp")
                dst = wv[:, kc, c0:c0 + CH]
                eng = "dve"
                if rot:
                    sv = st.rearrange("p (m t d) -> p m t d", t=2, d=32)
                    dv = dst.rearrange("p (m t d) -> p m t d", t=2, d=32)
                    if gain is not None:
                        P.op(eng, lambda e, dv=dv, sv=sv, kc=kc: e.tensor_scalar(out=dv[:, :, 0, :], in0=sv[:, :, 1, :], scalar1=gain[:, kc:kc + 1], scalar2=-1.0, op0=ALU.mult, op1=ALU.mult), [sname, "gains"], [name])
                        P.op(eng, lambda e, dv=dv, sv=sv, kc=kc: e.tensor_scalar(out=dv[:, :, 1, :], in0=sv[:, :, 0, :], scalar1=gain[:, kc:kc + 1], scalar2=None, op0=ALU.mult), [sname, "gains"], [name])
                else:
                    if gain is not None:
                        P.op("act", lambda e, dst=dst, st=st, kc=kc: e.activation(out=dst, in_=st, func=AF.Copy, scale=gain[:, kc:kc + 1]), [sname, "gains"], [name])
                    else:
                        P.op("act", lambda e, dst=dst, st=st: e.activation(out=dst, in_=st, func=AF.Copy), [sname], [name])
        return wv

    P.op("pool", lambda e: e.engine_nop(), ["g_mix_pre", "g_x_pre", "g_mem", "g_ff_pre"], ["gains"])

    nctr = [0]

    def norm_block_T(x_src_ap, x_is_dram, hT_dst, hT_key, xkey=None, tp_bank=0):
        i = nctr[0]; nctr[0] += 1
        if x_is_dram:
            xs_name = "xs%d" % (i % 2)
            if xs_name not in A.live:
                A.alloc(xs_name, 1024, F32)
            o, s = A.live[xs_name]
            xs = A.base[:, o // 4:o // 4 + 1024]
            dma(xs, x_src_ap, writes=[xs_name], q="sp")
            rkeys = [xs_name]
        else:
            xs = x_src_ap
            rkeys = [xkey]
        for nm, n, dt in (("nsq%d" % (i % 2), 1024, BF16), ("nst%d" % (i % 2), 4, F32), ("nh%d" % (i % 2), 1024, BF16)):
            if nm not in A.live:
                A.alloc(nm, n, dt)
        o, s = A.live["nsq%d" % (i % 2)]; sq = A.base[:, o // 4:o // 4 + 512].bitcast(BF16)
        o, s = A.live["nst%d" % (i % 2)]; st = A.base[:, o // 4:o // 4 + 4]
        o, s = A.live["nh%d" % (i % 2)]; hb = A.base[:, o // 4:o // 4 + 512].bitcast(BF16)
        ks, kt, kh = "nsq%d" % (i % 2), "nst%d" % (i % 2), "nh%d" % (i % 2)
        P.op("act", lambda e: e.activation(out=sq, in_=xs, func=AF.Square, accum_out=st[:, 0:1]), rkeys, [ks, kt])
        P.op("act", lambda e: e.activation(out=st[:, 1:2], in_=st[:, 0:1], func=AF.Ln, scale=1.0 / D, bias=epst), [kt, "epst"], [kt + ":1"])
        P.op("act", lambda e: e.activation(out=st[:, 2:3], in_=st[:, 1:2], func=AF.Exp, scale=-0.5), [kt + ":1"], [kt + ":2"])
        P.op("dve", lambda e: e.tensor_scalar(out=hb, in0=xs, scalar1=st[:, 2:3], scalar2=None, op0=ALU.mult), rkeys + [kt + ":2"], [kh])
        tp = bank_bf(tp_bank).rearrange("p (k t) -> p k t", k=8)
        for kc in range(8):
            P.op("pe", lambda e, kc=kc: e.transpose(out=tp[:, kc, :], in_=hb[:, kc * 128:(kc + 1) * 128], identity=ident_b), [kh, "ident_b"], [bkey(tp_bank)])
        P.op("act", lambda e: e.activation(out=hT_dst, in_=tp, func=AF.Copy), [bkey(tp_bank)], [hT_key])

    x1_d = dscr("x1_d", [NOWN, D], F32)
    x2_d = dscr("x2_d", [NOWN, D], F32)
    wf1_d = nc.dram_tensor("wf1_d", [D, 4 * D], BF16, kind="Internal").ap()
    wf2_d = nc.dram_tensor("wf2_d", [4 * D, D], BF16, kind="Internal").ap()
    wsc = {}
    for nm_, (src_, K_, N_, gn_) in {"wglu": (w_glu, D, D, None), "wssm": (w_ssm, D, D, None), "wda": (w_da, D, D, None),
                                       "wmix": (w_mix, D, D, None), "wxkv": (w_xkv, D, 2 * D, g_mem), "wxq": (w_xq, D, D, g_x_pre),
                                       "wxo": (w_xo, D, D, None)}.items():
        wsc[nm_] = (nc.dram_tensor(nm_ + "_d", [K_, N_], BF16, kind="Internal").ap(), src_, K_, N_, gn_)
    cst_ = [A.alloc("cstg%d" % i, 1024, F32) for i in range(2)]
    cb = [A.alloc("cb%d" % i, 1024, BF16) for i in range(2)]
    cctr = [0]

    def cast_gen():
        jobs = [(v_[1], v_[2], v_[3], v_[4], v_[0], k_ + "_d") for k_, v_ in wsc.items()]
        jobs.append((w_ff1, D, 4 * D, g_ff_pre, wf1_d, "wf_d"))
        jobs.append((w_ff2, 4 * D, D, None, wf2_d, "wf_d"))
        for (src, K_, N_, gain, dst, dkey) in jobs:
            for kc in range(K_ // 128):
                for c0 in range(0, N_, 1024):
                    i = cctr[0]; cctr[0] += 1
                    st = cst_[i % 2]; sname = "cstg%d" % (i % 2)
                    dma(st, src[kc * 128:(kc + 1) * 128, c0:c0 + 1024], writes=[sname], q="act")
                    cbt = cb[i % 2]; cbk = "cb%d" % (i % 2)
                    if gain is not None:
                        P.op("act", lambda e, cbt=cbt, st=st, kc=kc, gain=gain: e.activation(out=cbt, in_=st, func=AF.Copy, scale=gain[:, kc:kc + 1]), [sname, "gains"], [cbk])
                    else:
                        P.op("act", lambda e, cbt=cbt, st=st: e.activation(out=cbt, in_=st, func=AF.Copy), [sname], [cbk])
                    dma(dst[kc * 128:(kc + 1) * 128, c0:c0 + 1024], cbt, reads=[cbk], writes=[dkey], q="act")
                    yield

    cgen = cast_gen()
    cg_alive = [True]

    def cast_step():
        if cg_alive[0]:
            try:
                next(cgen)
            except StopIteration:
                cg_alive[0] = False

    wk = load_weight("wk", w_in, D, 1024, gain=g_mix_pre, col0=2048)
    wkr = load_weight("wkr", w_in, D, 1024, gain=g_mix_pre, col0=2048, rot=True)
    wu = load_weight("wu", w_in, D, 1024, gain=g_mix_pre, col0=0)
    wv_ = load_weight("wv", w_in, D, 1024, gain=g_mix_pre, col0=3072)
    hTa = [A.alloc("hTa%d" % i, 8 * 512, BF16).rearrange("p (k t) -> p k t", k=8) for i in range(2)]
    cst = [A.alloc("cst%d" % i, 1024, F32) for i in range(2)]
    kst = [A.alloc("kst%d" % i, 512, BF16) for i in range(2)]
    kt1 = [A.alloc("kt1_%d" % i, 512, F32) for i in range(2)]
    kt2 = [A.alloc("kt2_%d" % i, 512, F32) for i in range(2)]
    vst = [A.alloc("vst%d" % i, 1024, BF16) for i in range(2)]
    ust = [A.alloc("ust%d" % i, 512, BF16) for i in range(2)]
    cnt = [0]
    for tt in range(16):
        hb_i = tt % 2
        hT = hTa[hb_i]; hk = "hTa%d" % hb_i
        for bl in range(4):
            blk = tt * 4 + bl
            norm_block_T(xseq[blk * 128:(blk + 1) * 128, :], True, hT[:, :, bl * 128:(bl + 1) * 128], hk)
        cs = cst[hb_i]; ck = "cst%d" % hb_i
        dma(cs[:, 0:512], cos_seq[:, tt * 512:(tt + 1) * 512], writes=[ck])
        dma(cs[:, 512:1024], sin_seq[:, tt * 512:(tt + 1) * 512], writes=[ck])
        for h in range(8):
            cast_step()
            i = cnt[0]; cnt[0] += 1
            pb = 1 + 2 * (i % 2)
            for kc in range(8):
                P.op("pe", lambda e, kc=kc, h=h, pb=pb, hT=hT: e.matmul(out=bank(pb), lhsT=wk[:, kc, h * 128:(h + 1) * 128], rhs=hT[:, kc, :], start=(kc == 0), stop=(kc == 7)), ["wk", hk], [bkey(pb)])
            for kc in range(8):
                P.op("pe", lambda e, kc=kc, h=h, pb=pb, hT=hT: e.matmul(out=bank(pb + 1), lhsT=wkr[:, kc, h * 128:(h + 1) * 128], rhs=hT[:, kc, :], start=(kc == 0), stop=(kc == 7)), ["wkr", hk], [bkey(pb + 1)])
            j = i % 2
            P.op("dve", lambda e, pb=pb, j=j, cs=cs: e.tensor_tensor(out=kt1[j], in0=bank(pb), in1=cs[:, 0:512], op=ALU.mult), [bkey(pb), ck], ["kt1_%d" % j])
            P.op("dve", lambda e, pb=pb, j=j, cs=cs: e.tensor_tensor(out=kt2[j], in0=bank(pb + 1), in1=cs[:, 512:1024], op=ALU.mult), [bkey(pb + 1), ck], ["kt2_%d" % j])
            P.op("pool", lambda e, j=j: e.tensor_tensor(out=kst[j], in0=kt1[j], in1=kt2[j], op=ALU.add), ["kt1_%d" % j, "kt2_%d" % j], ["kst%d" % j])
            dma(kT_d[h, :, tt * 512:(tt + 1) * 512], kst[j], reads=["kst%d" % j], writes=["kT_d"], q="sp")
        for fc in range(8):
            i = cnt[0]; cnt[0] += 1
            pb = 5 + (i % 2)
            for kc in range(8):
                P.op("pe", lambda e, kc=kc, fc=fc, pb=pb, hT=hT: e.matmul(out=bank(pb), lhsT=wu[:, kc, fc * 128:(fc + 1) * 128], rhs=hT[:, kc, :], start=(kc == 0), stop=(kc == 7)), ["wu", hk], [bkey(pb)])
            j = i % 2
            P.op("act", lambda e, pb=pb, j=j: e.activation(out=ust[j], in_=bank(pb), func=AF.Copy), [bkey(pb)], ["ust%d" % j])
            dma(uT_d[fc, :, tt * 512:(tt + 1) * 512], ust[j], reads=["ust%d" % j], writes=["uT_d"], q="sp")
        for bl in range(4):
            blk = tt * 4 + bl
            i = cnt[0]; cnt[0] += 1
            j = i % 2
            for half in range(2):
                pb = 1 + 2 * (i % 2) + half
                for kc in range(8):
                    P.op("pe", lambda e, kc=kc, bl=bl, half=half, pb=pb, hT=hT: e.matmul(out=bank(pb), lhsT=hT[:, kc, bl * 128:(bl + 1) * 128], rhs=wv_[:, kc, half * 512:(half + 1) * 512], start=(kc == 0), stop=(kc == 7)), ["wv", hk], [bkey(pb)])
                P.op("act", lambda e, pb=pb, j=j, half=half: e.activation(out=vst[j][:, half * 512:(half + 1) * 512], in_=bank(pb), func=AF.Copy), [bkey(pb)], ["vst%d" % j])
            dma(v_d[blk * 128:(blk + 1) * 128, :], vst[j], reads=["vst%d" % j], writes=["v_d"], q="sp")
    while cg_alive[0]:
        cast_step()
    for n in ["cstg0", "cstg1", "cb0", "cb1"]:
        A.free(n)
    for n in ["wu", "wk", "wkr", "wv", "hTa0", "hTa1", "cst0", "cst1", "kst0", "kst1", "kt1_0", "kt1_1", "kt2_0", "kt2_1", "vst0", "vst1", "ust0", "ust1"]:
        A.free(n)

    wq = load_weight("wq", w_in, D, 1024, gain=g_mix_pre, col0=1024)
    wqr = load_weight("wqr", w_in, D, 1024, gain=g_mix_pre, col0=1024, rot=True)
    wg = load_weight("wg", w_in, D, 2048, gain=g_mix_pre, col0=4096)
    hTo = [A.alloc("hTo%d" % i, 8 * 512, BF16).rearrange("p (k t) -> p k t", k=8) for i in range(2)]
    cso = [A.alloc("cso%d" % i, 1024, F32) for i in range(2)]
    qst = [A.alloc("qst%d" % i, 512, BF16) for i in range(2)]
    qt1 = [A.alloc("qt1_%d" % i, 512, F32) for i in range(2)]
    qt2 = [A.alloc("qt2_%d" % i, 512, F32) for i in range(2)]
    gst = [A.alloc("gst%d" % i, 512, BF16) for i in range(2)]
    for tt in range(4):
        hb_i = tt % 2
        hT = hTo[hb_i]; hk = "hTo%d" % hb_i
        for bl in range(4):
            blk = tt * 4 + bl
            norm_block_T(xown[blk * 128:(blk + 1) * 128, :], True, hT[:, :, bl * 128:(bl + 1) * 128], hk)
        cs = cso[hb_i]; ck = "cso%d" % hb_i
        dma(cs[:, 0:512], cos_own[:, tt * 512:(tt + 1) * 512], writes=[ck])
        dma(cs[:, 512:1024], sin_own[:, tt * 512:(tt + 1) * 512], writes=[ck])
        for h in range(8):
            i = cnt[0]; cnt[0] += 1
            pb = 1 + 2 * (i % 2)
            for kc in range(8):
                P.op("pe", lambda e, kc=kc, h=h, pb=pb, hT=hT: e.matmul(out=bank(pb), lhsT=wq[:, kc, h * 128:(h + 1) * 128], rhs=hT[:, kc, :], start=(kc == 0), stop=(kc == 7)), ["wq", hk], [bkey(pb)])
            for kc in range(8):
                P.op("pe", lambda e, kc=kc, h=h, pb=pb, hT=hT: e.matmul(out=bank(pb + 1), lhsT=wqr[:, kc, h * 128:(h + 1) * 128], rhs=hT[:, kc, :], start=(kc == 0), stop=(kc == 7)), ["wqr", hk], [bkey(pb + 1)])
            j = i % 2
            P.op("dve", lambda e, pb=pb, j=j, cs=cs: e.tensor_tensor(out=qt1[j], in0=bank(pb), in1=cs[:, 0:512], op=ALU.mult), [bkey(pb), ck], ["qt1_%d" % j])
            P.op("dve", lambda e, pb=pb, j=j, cs=cs: e.tensor_tensor(out=qt2[j], in0=bank(pb + 1), in1=cs[:, 512:1024], op=ALU.mult), [bkey(pb + 1), ck], ["qt2_%d" % j])
            P.op("pool", lambda e, j=j: e.tensor_tensor(out=qst[j], in0=qt1[j], in1=qt2[j], op=ALU.add), ["qt1_%d" % j, "qt2_%d" % j], ["qst%d" % j])
            dma(qT_d[h, :, tt * 512:(tt + 1) * 512], qst[j], reads=["qst%d" % j], writes=["qT_d"], q="sp")
        for gc in range(16):
            i = cnt[0]; cnt[0] += 1
            pb = 5 + (i % 2)
            for kc in range(8):
                P.op("pe", lambda e, kc=kc, gc=gc, pb=pb, hT=hT: e.matmul(out=bank(pb), lhsT=wg[:, kc, gc * 128:(gc + 1) * 128], rhs=hT[:, kc, :], start=(kc == 0), stop=(kc == 7)), ["wg", hk], [bkey(pb)])
            j = i % 2
            P.op("act", lambda e, pb=pb, j=j, gc=gc: e.activation(out=gst[j], in_=bank(pb), func=AF.Sigmoid, bias=bgate_pp[:, gc:gc + 1]), [bkey(pb), "bgate_pp"], ["gst%d" % j])
            dma(g_d[gc, :, tt * 512:(tt + 1) * 512], gst[j], reads=["gst%d" % j], writes=["g_d"], q="sp")
    for n in ["wq", "wqr", "wg", "hTo0", "hTo1", "cso0", "cso1", "qst0", "qst1", "qt1_0", "qt1_1", "qt2_0", "qt2_1", "gst0", "gst1"]:
        A.free(n)

    for n in ["wstage0", "wstage1", "xs0", "xs1", "nsq0", "nsq1", "nh0", "nh1", "nst0", "nst1"]:
        if n in A.live:
            A.free(n)
    def a64(name, n, dt=F32):
        return A.alloc(name, n, dt)[0:64, :]

    lre = a64("lre", 64); lim = a64("lim", 64); ldt = a64("ldt", 64)
    dma(lre, lam_re.rearrange("g p -> p g"), writes=["lre"], slow=True)
    dma(lim, lam_im.rearrange("g p -> p g"), writes=["lim"], slow=True)
    dma(ldt, log_dt.partition_broadcast(64), writes=["ldt"])
    dtt = a64("dtt", 64); zr = a64("zr", 64); zi = a64("zi", 64)
    P.op("act", lambda e: e.activation(out=dtt, in_=ldt, func=AF.Exp), ["ldt"], ["dtt"])
    P.op("dve", lambda e: e.tensor_tensor(out=zr, in0=lre, in1=dtt, op=ALU.mult), ["lre", "dtt"], ["zr"])
    P.op("dve", lambda e: e.tensor_tensor(out=zi, in0=lim, in1=dtt, op=ALU.mult), ["lim", "dtt"], ["zi"])
    mag = a64("mag", 64); cr = a64("cr", 64); ci = a64("ci", 64); halfpi = a64("halfpi", 1)
    P.op("pool", lambda e: e.memset(halfpi, math.pi / 2), [], ["halfpi"])
    P.op("act", lambda e: e.activation(out=mag, in_=zr, func=AF.Exp, scale=1.0 / 32), ["zr"], ["mag"])
    P.op("act", lambda e: e.activation(out=ci, in_=zi, func=AF.Sin, scale=1.0 / 32), ["zi"], ["ci"])
    P.op("act", lambda e: e.activation(out=cr, in_=zi, func=AF.Sin, scale=1.0 / 32, bias=halfpi), ["zi", "halfpi"], ["cr"])
    P.op("dve", lambda e: e.tensor_tensor(out=cr, in0=cr, in1=mag, op=ALU.mult), ["cr", "mag"], ["cr"])
    P.op("dve", lambda e: e.tensor_tensor(out=ci, in0=ci, in1=mag, op=ALU.mult), ["ci", "mag"], ["ci"])
    PW = a64("PW", NLEV * 128).rearrange("p (l r g) -> p l r g", l=NLEV, r=2)
    t1 = a64("sq_t1", 64); t2 = a64("sq_t2", 64); t3 = a64("sq_t3", 64)

    def csquare(sr, si, dr, di, keys_in, key_out):
        P.op("dve", lambda e: e.tensor_tensor(out=t1, in0=sr, in1=sr, op=ALU.mult), keys_in, ["sq_t1"])
        P.op("dve", lambda e: e.tensor_tensor(out=t2, in0=si, in1=si, op=ALU.mult), keys_in, ["sq_t2"])
        P.op("dve", lambda e: e.tensor_tensor(out=t3, in0=sr, in1=si, op=ALU.mult), keys_in, ["sq_t3"])
        P.op("dve", lambda e: e.tensor_tensor(out=dr, in0=t1, in1=t2, op=ALU.subtract), ["sq_t1", "sq_t2"], [key_out])
        P.op("dve", lambda e: e.tensor_tensor(out=di, in0=t3, in1=t3, op=ALU.add), ["sq_t3"], [key_out + ":i"])

    wr = [a64("wr%d" % i, 64) for i in range(2)]; wi = [a64("wi%d" % i, 64) for i in range(2)]
    csquare(cr, ci, wr[0], wi[0], ["cr", "ci"], "wr0")
    csquare(wr[0], wi[0], wr[1], wi[1], ["wr0", "wr0:i"], "wr1")
    csquare(wr[1], wi[1], wr[0], wi[0], ["wr1", "wr1:i"], "wr0")
    csquare(wr[0], wi[0], wr[1], wi[1], ["wr0", "wr0:i"], "wr1")
    csquare(wr[1], wi[1], PW[:, 0, 0, :], PW[:, 0, 1, :], ["wr1", "wr1:i"], "PW:0")
    for l in range(1, NLEV):
        csquare(PW[:, l - 1, 0, :], PW[:, l - 1, 1, :], PW[:, l, 0, :], PW[:, l, 1, :], ["PW:%d" % (l - 1), "PW:%d:i" % (l - 1)], "PW:%d" % l)
    den = a64("den", 64); cfr = a64("cfr", 64); cfi = a64("cfi", 64); lm1 = a64("lm1", 64)
    P.op("dve", lambda e: e.tensor_tensor(out=t1, in0=lre, in1=lre, op=ALU.mult), ["lre", "PW:%d:i" % (NLEV - 1)], ["sq_t1"])
    P.op("dve", lambda e: e.tensor_tensor(out=t2, in0=lim, in1=lim, op=ALU.mult), ["lim"], ["sq_t2"])
    P.op("dve", lambda e: e.tensor_tensor(out=den, in0=t1, in1=t2, op=ALU.add), ["sq_t1", "sq_t2"], ["den"])
    P.op("dve", lambda e: e.reciprocal(out=den, in_=den), ["den"], ["den"])
    P.op("dve", lambda e: e.tensor_scalar(out=lm1, in0=PW[:, 0, 0, :], scalar1=-1.0, scalar2=None, op0=ALU.add), ["PW:0"], ["lm1"])
    P.op("dve", lambda e: e.tensor_tensor(out=t1, in0=lm1, in1=lre, op=ALU.mult), ["lm1", "lre", "den"], ["sq_t1"])
    P.op("dve", lambda e: e.tensor_tensor(out=t2, in0=PW[:, 0, 1, :], in1=lim, op=ALU.mult), ["PW:0:i", "lim"], ["sq_t2"])
    P.op("dve", lambda e: e.tensor_tensor(out=cfr, in0=t1, in1=t2, op=ALU.add), ["sq_t1", "sq_t2"], ["cfr"])
    P.op("dve", lambda e: e.tensor_tensor(out=t1, in0=PW[:, 0, 1, :], in1=lre, op=ALU.mult), ["PW:0:i", "lre", "cfr"], ["sq_t1"])
    P.op("dve", lambda e: e.tensor_tensor(out=t2, in0=lm1, in1=lim, op=ALU.mult), ["lm1", "lim", "cfr"], ["sq_t2"])
    P.op("dve", lambda e: e.tensor_tensor(out=cfi, in0=t1, in1=t2, op=ALU.subtract), ["sq_t1", "sq_t2"], ["cfi"])
    P.op("dve", lambda e: e.tensor_tensor(out=cfr, in0=cfr, in1=den, op=ALU.mult), ["cfr", "den"], ["cfr"])
    P.op("dve", lambda e: e.tensor_tensor(out=cfi, in0=cfi, in1=den, op=ALU.mult), ["cfi", "den"], ["cfi"])
    braw = a64("braw", 2048).rearrange("p (r g h) -> p r g h", r=2, g=64)
    for gh in range(2):
        dma(braw[:, 0, gh * 32:(gh + 1) * 32, :], b_re[gh * 32:(gh + 1) * 32].rearrange("g p h -> p g h"), writes=["braw"], slow=True)
        dma(braw[:, 1, gh * 32:(gh + 1) * 32, :], b_im[gh * 32:(gh + 1) * 32].rearrange("g p h -> p g h"), writes=["braw"], slow=True)
    Bb = a64("Bb", 2048).rearrange("p (r g h) -> p r g h", r=2, g=64)
    bt1 = a64("bt1", 1024).rearrange("p (g h) -> p g h", g=64); bt2 = a64("bt2", 1024).rearrange("p (g h) -> p g h", g=64)
    cfr_b = cfr.unsqueeze(2).broadcast_to([64, 64, 16]); cfi_b = cfi.unsqueeze(2).broadcast_to([64, 64, 16])
    P.op("dve", lambda e: e.tensor_tensor(out=bt1, in0=braw[:, 0], in1=cfr_b, op=ALU.mult), ["braw", "cfr"], ["bt1"])
    P.op("dve", lambda e: e.tensor_tensor(out=bt2, in0=braw[:, 1], in1=cfi_b, op=ALU.mult), ["braw", "cfi"], ["bt2"])
    P.op("dve", lambda e: e.tensor_tensor(out=Bb[:, 0], in0=bt1, in1=bt2, op=ALU.subtract), ["bt1", "bt2"], ["Bb"])
    P.op("dve", lambda e: e.tensor_tensor(out=bt1, in0=braw[:, 0], in1=cfi_b, op=ALU.mult), ["braw", "cfi", "Bb"], ["bt1"])
    P.op("dve", lambda e: e.tensor_tensor(out=bt2, in0=braw[:, 1], in1=cfr_b, op=ALU.mult), ["braw", "cfr", "Bb"], ["bt2"])
    P.op("dve", lambda e: e.tensor_tensor(out=Bb[:, 1], in0=bt1, in1=bt2, op=ALU.add), ["bt1", "bt2"], ["Bb:i"])
    Cst = A.alloc("Cst", 1024, F32).rearrange("p (g h) -> p g h", g=64)
    for gh in range(2):
        dma(Cst[0:64, gh * 32:(gh + 1) * 32, :], c_re[gh * 32:(gh + 1) * 32].rearrange("g h p -> p g h"), writes=["Cst"], slow=True)
        dma(Cst[64:128, gh * 32:(gh + 1) * 32, :], c_im[gh * 32:(gh + 1) * 32].rearrange("g h p -> p g h"), writes=["Cst"], slow=True)
    P.op("pool", lambda e: e.tensor_scalar(out=Cst[64:128], in0=Cst[64:128], scalar1=-1.0, scalar2=None, op0=ALU.mult), ["Cst"], ["Cst"])
    S1 = A.alloc("S1", NLEV * 64, F32).rearrange("p (l g) -> p l g", l=NLEV)
    S2 = A.alloc("S2", NLEV * 64, F32).rearrange("p (l g) -> p l g", l=NLEV)
    pwkeys = ["PW:%d" % l for l in range(NLEV)] + ["PW:%d:i" % l for l in range(NLEV)]
    dma(S1[0:64], PW[:, :, 0, :], reads=pwkeys, writes=["S1"]); dma(S1[64:128], PW[:, :, 0, :], reads=pwkeys, writes=["S1"])
    dma(S2[0:64], PW[:, :, 1, :], reads=pwkeys, writes=["S2"]); dma(S2[64:128], PW[:, :, 1, :], reads=pwkeys, writes=["S2"])
    P.op("pool", lambda e: e.tensor_scalar(out=S2[64:128], in0=S2[64:128], scalar1=-1.0, scalar2=None, op0=ALU.mult), ["S2"], ["S2"])

    if debug:
        pw_dbg = dscr("pw_dbg", [64, NLEV * 128], F32)
        dma(pw_dbg, PW.rearrange("p l r g -> p (l r g)"), reads=pwkeys, writes=["pw_dbg"])
        bb_dbg = dscr("bb_dbg", [64, 2048], F32)
        dma(bb_dbg, Bb.rearrange("p r g h -> p (r g h)"), reads=["Bb", "Bb:i"], writes=["bb_dbg"])
    for n in ["braw", "bt1", "bt2", "PW", "sq_t1", "sq_t2", "sq_t3", "wr0", "wr1", "wi0", "wi1", "mag", "cr", "ci", "lm1", "den", "cfr", "cfi", "zr", "zi", "dtt", "ldt", "lre", "lim"]:
        A.free(n)
    uT = [A.alloc("uT%d" % i, SEQ, BF16) for i in range(1)]
    X0p = [A.alloc("X0p%d" % i, SEQ, BF16) for i in range(2)]
    Tp = [A.alloc("Tp%d" % i, SEQ, BF16) for i in range(2)]
    Xop = [[A.alloc("Xo%d_%d" % (p_, i), NOWN, BF16).rearrange("p (s t) -> p s t", s=16) for i in range(2)] for p_ in range(2)]
    XBp = [[A.alloc("XB%d_%d" % (p_, i), 64, BF16) for i in range(2)] for p_ in range(2)]
    ysg = A.alloc("ysg", 8 * NOWN, BF16).rearrange("p (k t) -> p k t", k=8)
    WB = [A.alloc("WB%d" % i, 128, BF16) for i in range(4)]
    WC = [A.alloc("WC%d" % i, 128, BF16) for i in range(4)]
    R = [A.alloc("R%d" % i, 128, BF16) for i in range(52)]
    Rt = [A.alloc("Rt%d" % i, 128, F32) for i in range(4)]
    bm = [A.alloc("bm%d" % i, 256, F32)[0:64, :] for i in range(2)]
    gel = [A.alloc("gel%d" % i, NOWN, F32) for i in range(2)]
    evc = [0]
    toff = [0, 4096, 6144, 7168, 7680, 7936, 8064]

    def prep_gen(fc, gl, par, st_):
        g = fc * 8 + gl
        wi_ = st_ * 2 + par
        bmt = bm[par]; bmk = "bm%d" % par
        Bfc = Bb[:, :, fc * 8:(fc + 1) * 8, :]
        gmv64 = gmask[0:64, gl * 128:(gl + 1) * 128].rearrange("p (g h) -> p g h", g=8)
        P.op("dve", lambda e: e.tensor_tensor(out=bmt[:, 0:128].rearrange("p (g h) -> p g h", g=8), in0=Bfc[:, 0], in1=gmv64, op=ALU.mult), ["Bb", "Bb:i", "gmask"], [bmk])
        P.op("dve", lambda e: e.tensor_tensor(out=bmt[:, 128:256].rearrange("p (g h) -> p g h", g=8), in0=Bfc[:, 1], in1=gmv64, op=ALU.mult), ["Bb", "Bb:i", "gmask"], [bmk + ":i"])
        pbw = 7
        P.op("pe", lambda e: e.matmul(out=bank(pbw)[:, 0:64], lhsT=bmt[:, 0:128], rhs=ident_f[0:64, 0:64], start=True, stop=True), [bmk, "ident_f"], [bkey(pbw)])
        P.op("pe", lambda e: e.matmul(out=bank(pbw)[:, 64:128], lhsT=bmt[:, 128:256], rhs=ident_f[0:64, 0:64], start=True, stop=True), [bmk + ":i", "ident_f"], [bkey(pbw)])
        P.op("act", lambda e: e.activation(out=WB[wi_], in_=bank(pbw)[:, 0:128], func=AF.Copy), [bkey(pbw)], ["WB%d" % wi_])
        P.op("pool", lambda e: e.tensor_tensor(out=WC[wi_].rearrange("p (g h) -> p g h", g=8), in0=Cst[:, fc * 8:(fc + 1) * 8, :], in1=gmask[:, gl * 128:(gl + 1) * 128].rearrange("p (g h) -> p g h", g=8), op=ALU.mult), ["Cst", "gmask"], ["WC%d" % wi_])
        yield
        for lev in range(NLEV):
            Rm = R[wi_ * 13 + lev]; Rk = "R%d" % (wi_ * 13 + lev)
            rt = Rt[(lev % 2) * 2 + par]; rtk = "Rt%d" % ((lev % 2) * 2 + par)
            P.op("act", lambda e, lev=lev, rt=rt: e.activation(out=rt, in_=swap_f, func=AF.Copy, scale=S2[:, lev, g:g + 1]), ["swap_f", "S2"], [rtk])
            P.op("dve", lambda e, Rm=Rm, lev=lev, rt=rt: e.scalar_tensor_tensor(out=Rm, in0=ident_f, scalar=S1[:, lev, g:g + 1], in1=rt, op0=ALU.mult, op1=ALU.add), ["ident_f", "S1", rtk], [Rk])
            yield

    def group_gen(fc, gl, par, u, uk, st_):
        g = fc * 8 + gl
        wi_ = st_ * 2 + par
        X0 = X0p[par]; Tb = Tp[par]; Xo = Xop[par]; XB = XBp[par]
        xn = "X0p%d" % par; tn = "Tp%d" % par
        bc = [0]

        def nextbank():
            bc[0] += 1
            return 2 * par + (bc[0] % 2)

        def evac(pb, dst_ap, dkey, n=None):
            src_ap = bank(pb) if n is None else bank(pb)[:, 0:n]
            evc[0] += 1
            if evc[0] % 3 != 0:
                P.op("act", lambda e: e.activation(out=dst_ap, in_=src_ap, func=AF.Copy), [bkey(pb)], [dkey])
            else:
                P.op("dve", lambda e: e.tensor_copy(out=dst_ap, in_=src_ap), [bkey(pb)], [dkey])

        Rg = [(R[wi_ * 13 + lev], "R%d" % (wi_ * 13 + lev)) for lev in range(NLEV)]
        for tt in range(16):
            pb = nextbank()
            P.op("pe", lambda e, pb=pb, tt=tt: e.matmul(out=bank(pb), lhsT=WB[wi_], rhs=u[:, tt * 512:(tt + 1) * 512], start=True, stop=True), ["WB%d" % wi_, uk], [bkey(pb)])
            evac(pb, X0[:, tt * 512:(tt + 1) * 512], xn + ":%d" % tt)
            if tt % 4 == 3:
                yield
        for lev in range(7):
            n_l = 4096 >> lev
            if lev == 0:
                srcv = X0.rearrange("p (i two) -> p i two", two=2); sbase = xn + ":"
            else:
                srcv = Tb[:, toff[lev - 1]:toff[lev - 1] + 2 * n_l].rearrange("p (i two) -> p i two", two=2); sbase = tn + ":%d_" % (lev - 1)
            Rm, Rk = Rg[lev]
            for c0 in range(0, n_l, 512):
                n = min(512, n_l - c0)
                pb = nextbank()
                skeys = sorted(set([sbase + "%d" % ((2 * c0) // 512), sbase + "%d" % ((2 * c0 + 2 * n - 1) // 512)]))
                P.op("pe", lambda e, pb=pb, srcv=srcv, c0=c0, n=n: e.matmul(out=bank(pb)[:, 0:n], lhsT=ident_b, rhs=srcv[:, c0:c0 + n, 1], start=True, stop=False), ["ident_b"] + skeys, [bkey(pb)])
                P.op("pe", lambda e, pb=pb, srcv=srcv, c0=c0, n=n, Rm=Rm: e.matmul(out=bank(pb)[:, 0:n], lhsT=Rm, rhs=srcv[:, c0:c0 + n, 0], start=False, stop=True), [Rk] + skeys, [bkey(pb)])
                evac(pb, Tb[:, toff[lev] + c0:toff[lev] + c0 + n], tn + ":%d_%d" % (lev, c0 // 512), n)
                if (c0 // 512) % 2 == 1:
                    yield
            yield
        xbk = tn + ":6_0"
        xb_src = Tb[:, toff[6]:toff[6] + 64]
        for m_ in range(6):
            sh = 1 << m_
            Rm, Rk = Rg[7 + m_]
            pb = nextbank()
            P.op("pe", lambda e, pb=pb, xb_src=xb_src: e.matmul(out=bank(pb)[:, 0:64], lhsT=ident_b, rhs=xb_src, start=True, stop=False), ["ident_b", xbk], [bkey(pb)])
            P.op("pe", lambda e, pb=pb, xb_src=xb_src, sh=sh, Rm=Rm: e.matmul(out=bank(pb)[:, sh:64], lhsT=Rm, rhs=xb_src[:, 0:64 - sh], start=False, stop=True), [Rk, xbk], [bkey(pb)])
            dstb = XB[m_ % 2]
            xbk = "XB%d_%d" % (par, m_ % 2)
            evac(pb, dstb, xbk, 64)
            xb_src = dstb
            yield
        x0own = X0.rearrange("p (s b t) -> p s b t", s=16, b=4)[:, :, 3, :]
        x0keys = [xn + ":%d" % t for t in range(16)]
        xo0keys = ["Xo%d_0:%d" % (par, t) for t in range(4)]
        P.op("dve", lambda e: e.tensor_copy(out=Xo[0], in_=x0own), x0keys, xo0keys)
        pb = nextbank()
        Rm, Rk = Rg[0]
        P.op("pe", lambda e, pb=pb: e.matmul(out=bank(pb)[:, 0:16], lhsT=ident_b, rhs=x0own[:, :, 0], start=True, stop=False), ["ident_b"] + x0keys, [bkey(pb)])
        P.op("pe", lambda e, pb=pb, xb_src=xb_src, Rm=Rm: e.matmul(out=bank(pb)[:, 0:16], lhsT=Rm, rhs=xb_src.rearrange("p (s b) -> p s b", b=4)[:, :, 2], start=False, stop=True), [Rk, xbk], [bkey(pb)])
        P.op("dve", lambda e, pb=pb: e.tensor_copy(out=Xo[0][:, :, 0], in_=bank(pb)[:, 0:16]), [bkey(pb)] + xo0keys, xo0keys)
        yield
        cur = 0
        for lev in range(7):
            sh = 1 << lev
            Rm, Rk = Rg[lev]
            src = Xo[cur]; dst = Xo[1 - cur]
            for q4 in range(4):
                sk = "Xo%d_%d:%d" % (par, cur, q4); dk = "Xo%d_%d:%d" % (par, 1 - cur, q4)
                pb = nextbank()
                pv = bank(pb).rearrange("p (s t) -> p s t", s=4)
                P.op("pe", lambda e, pb=pb, src=src, q4=q4: e.matmul(out=bank(pb), lhsT=ident_b, rhs=src[:, q4 * 4:(q4 + 1) * 4, :].rearrange("p s t -> p (s t)"), start=True, stop=False), ["ident_b", sk], [bkey(pb)])
                P.op("pe", lambda e, pv=pv, src=src, q4=q4, sh=sh, Rm=Rm: e.matmul(out=pv[:, :, sh:128], lhsT=Rm, rhs=src[:, q4 * 4:(q4 + 1) * 4, 0:128 - sh], start=False, stop=True), [Rk, sk], [bkey(pb)])
                evac(pb, dst[:, q4 * 4:(q4 + 1) * 4, :].rearrange("p s t -> p (s t)"), dk)
                if q4 % 2 == 1:
                    yield
            cur = 1 - cur
        fin = Xo[cur].rearrange("p s t -> p (s t)"); fkb = "Xo%d_%d" % (par, cur)
        if debug and g == 63:
            xf_dbg = dscr("xf_dbg", [128, NOWN])
            dma(xf_dbg, fin, reads=[fkb + ":%d" % t for t in range(4)], writes=["xf_dbg"])
        for ot in range(4):
            yb = 4 + (2 * par + ot) % 3
            P.op("pe", lambda e, ot=ot, yb=yb: e.matmul(out=bank(yb), lhsT=WC[wi_], rhs=fin[:, ot * 512:(ot + 1) * 512], start=True, stop=True), ["WC%d" % wi_, fkb + ":%d" % ot], [bkey(yb)])
            pbk = bkey(yb); pbb = bank(yb)
            yacc = gel[0][:, ot * 512:(ot + 1) * 512]
            if gl == 0:
                uo = u.rearrange("p (s b t) -> p s b t", s=16, b=4)[:, ot * 4:(ot + 1) * 4, 3, :]
                P.op("dve", lambda e, yacc=yacc, uo=uo, pbb=pbb: e.scalar_tensor_tensor(out=yacc.rearrange("p (s t) -> p s t", s=4), in0=uo, scalar=ssmd_pp[:, fc:fc + 1], in1=pbb.rearrange("p (s t) -> p s t", s=4), op0=ALU.mult, op1=ALU.add), [uk, "ssmd_pp", pbk], ["gel0:%d" % ot])
            else:
                P.op("dve", lambda e, yacc=yacc, pbb=pbb: e.tensor_tensor(out=yacc, in0=pbb, in1=yacc, op=ALU.add), [pbk, "gel0:%d" % ot], ["gel0:%d" % ot])
            if ot % 2 == 1:
                yield

    def run_lockstep(gens):
        alive = [True] * len(gens)
        while any(alive):
            for i_ in range(len(gens)):
                if alive[i_]:
                    try:
                        next(gens[i_])
                    except StopIteration:
                        alive[i_] = False

    run_lockstep([prep_gen(0, 0, 0, 0), prep_gen(0, 1, 1, 0)])
    for fc in range(8):
        u = uT[0]; uk = "uT0"
        dma(u, uT_d[fc], reads=["uT_d"], writes=[uk], q="pool")
        for gp in range(4):
            pk = fc * 4 + gp
            st_ = pk % 2
            gens = [group_gen(fc, 2 * gp, 0, u, uk, st_), group_gen(fc, 2 * gp + 1, 1, u, uk, st_)]
            if pk + 1 < 32:
                nfc, ngp = (pk + 1) // 4, (pk + 1) % 4
                gens.append(prep_gen(nfc, 2 * ngp, 0, 1 - st_))
                gens.append(prep_gen(nfc, 2 * ngp + 1, 1, 1 - st_))
            run_lockstep(gens)
        yk = ["gel0:%d" % ot for ot in range(4)]
        P.op("act", lambda e: e.activation(out=gel[1], in_=gel[0], func=AF.Square), yk, ["gel1"])
        P.op("dve", lambda e: e.tensor_scalar(out=gel[1], in0=gel[1], scalar1=0.044715 * 1.5957691216, scalar2=1.5957691216, op0=ALU.mult, op1=ALU.add), ["gel1"], ["gel1"])
        P.op("dve", lambda e: e.tensor_tensor(out=gel[1], in0=gel[1], in1=gel[0], op=ALU.mult), ["gel1"] + yk, ["gel1"])
        P.op("act", lambda e: e.activation(out=gel[1], in_=gel[1], func=AF.Sigmoid), ["gel1"], ["gel1"])
        P.op("dve", lambda e, fc=fc: e.tensor_tensor(out=ysg[:, fc, :], in0=gel[1], in1=gel[0], op=ALU.mult), ["gel1"] + yk, ["ysg"])
    for n in ["uT0", "X0p0", "X0p1", "Tp0", "Tp1", "Xo0_0", "Xo0_1", "Xo1_0", "Xo1_1", "XB0_0", "XB0_1", "XB1_0", "XB1_1", "WB0", "WB1", "WB2", "WB3", "WC0", "WC1", "WC2", "WC3"] + ["R%d" % i for i in range(52)] + ["Rt0", "Rt1", "Rt2", "Rt3", "bm0", "bm1",
              "gel0", "gel1", "S1", "S2", "Cst", "Bb"]:
        A.free(n)

    if debug:
        ysg_dbg = dscr("ysg_dbg", [128, 8 * NOWN])
        dma(ysg_dbg, ysg.rearrange("p k t -> p (k t)"), reads=["ysg"], writes=["ysg_dbg"])
    lq = A.alloc("lq", 256, F32).rearrange("p (a d) -> p a d", a=4)
    for a in range(4):
        dma(lq[:, a, :], lqk[a].partition_broadcast(128), writes=["lq"])
    lamt = A.alloc("lamt", 8, F32)
    lqp = A.alloc("lqp", 128, F32).rearrange("p (a d) -> p a d", a=2)
    P.op("dve", lambda e: e.tensor_tensor(out=lqp[:, 0, :], in0=lq[:, 0, :], in1=lq[:, 1, :], op=ALU.mult), ["lq"], ["lqp"])
    P.op("dve", lambda e: e.tensor_tensor(out=lqp[:, 1, :], in0=lq[:, 2, :], in1=lq[:, 3, :], op=ALU.mult), ["lq"], ["lqp"])
    P.op("dve", lambda e: e.tensor_reduce(out=lamt[:, 0:2], in_=lqp, axis=AX.X, op=ALU.add), ["lqp"], ["lamt"])
    P.op("act", lambda e: e.activation(out=lamt[:, 2:4], in_=lamt[:, 0:2], func=AF.Exp), ["lamt"], ["lamt:e"])
    P.op("dve", lambda e: e.tensor_tensor(out=lamt[:, 4:5], in0=lamt[:, 3:4], in1=lamt[:, 2:3], op=ALU.subtract), ["lamt:e"], ["lamt:d"])
    P.op("dve", lambda e: e.tensor_scalar(out=lamt[:, 5:6], in0=lamt[:, 4:5], scalar1=-0.2, scalar2=None, op0=ALU.add), ["lamt:d"], ["neglam"])
    hn = A.alloc("hn", 128, F32)
    dma(hn, head_norm.partition_broadcast(128), writes=["hn"])
    P.op("dve", lambda e: e.tensor_scalar(out=hn, in0=hn, scalar1=0.8, scalar2=None, op0=ALU.mult), ["hn"], ["hn"])
    kvf = A.alloc("kvf", 64, F32)
    dma(kvf, kvalid.rearrange("(b p) -> p b", p=128), writes=["kvf"], slow=True)

    ya = A.alloc("ya", 16 * 1024, BF16).rearrange("p (s f) -> p s f", s=16)
    Kh = [A.alloc("Kh%d" % i, 2 * SEQ, BF16)[0:64, :].rearrange("p (m t) -> p m t", m=2) for i in range(2)]
    Vh = [A.alloc("Vh%d" % i, 64 * 128, BF16).rearrange("p (b d) -> p b d", b=64) for i in range(1)]
    Qh = [A.alloc("Qh%d" % i, 2 * NOWN, BF16)[0:64, :].rearrange("p (m t) -> p m t", m=2) for i in range(1)]
    PT = [A.alloc("PT%d" % i, 1024, BF16).rearrange("p (m q) -> p m q", m=2) for i in range(2)]
    Esel = A.alloc("Esel", 256, BF16).rearrange("p (m c) -> p m c", m=2)
    Esel_f = A.alloc("Esel_f", 256, F32)
    dma(Esel_f, esel_d, writes=["Esel_f"])
    P.op("pool", lambda e: e.tensor_copy(out=Esel.rearrange("p m c -> p (m c)"), in_=Esel_f), ["Esel_f"], ["Esel"])
    denrow = [A.alloc("denrow%d" % i, 512, F32) for i in range(2)]
    rcol = [A.alloc("rcol%d" % i, 128, F32) for i in range(2)]
    OT = [A.alloc("OT%d" % i, 1024, BF16).rearrange("p (m q) -> p m q", m=2) for i in range(2)]
    ones_f = A.alloc("ones_f", 1, F32)
    P.op("pool", lambda e: e.memset(ones_f, 1.0), [], ["ones_f"])
    ep = [A.alloc("ep%d" % i, 8, F32) for i in range(2)]
    eo = [A.alloc("eo%d" % i, 384, F32) for i in range(2)]
    v_dh = v_d.rearrange("(b p) (h d) -> p b h d", p=128, h=8)
    actr = [0]
    tpv = bank_bf(7).rearrange("p (i m d) -> p i m d", i=4, m=2)
    dcol = bank(6)[:, 0:128]
    for h in range(8):
        K = Kh[h % 2]; V = Vh[0]; Q = Qh[0]
        kk_ = "Kh%d" % (h % 2)
        for m in range(2):
            dma(K[:, m, :], kT_d[h, m * 64:(m + 1) * 64, :], reads=["kT_d"], writes=[kk_], q="sp")
            dma(Q[:, m, :], qT_d[h, m * 64:(m + 1) * 64, :], reads=["qT_d"], writes=["Qh0"], q="sp")
        for vq in range(4):
            dma(V[:, vq * 16:(vq + 1) * 16, :], v_dh[:, vq * 16:(vq + 1) * 16, h, :], reads=["v_d"], writes=["Vh0"], q="sp")
        for G in range(4):
            gj = (h * 4 + G) % 2
            dr = denrow[gj]; drk = "denrow%d" % gj
            ot_ = OT[gj]; otk = "OT%d" % gj
            nkb = 16 * G + 16
            base_i = actr[0]
            actr[0] += nkb

            def emit_scores(kb, G=G, K=K, kk_=kk_, base_i=base_i):
                rel_ = kb - 16 * G - 3
                i0_ = 0 if rel_ <= 0 else (rel_ + 3) // 4
                c0 = i0_ * 128
                idiag = rel_ // 4 if (rel_ >= 0 and rel_ % 4 == 0) else -1
                pj = (base_i + kb) % 2
                pt = PT[pj]; ptk = "PT%d" % pj
                for m in range(2):
                    pb = 2 * pj + m
                    P.op("pe", lambda e, pb=pb, kb=kb, m=m, c0=c0: e.matmul(out=bank(pb)[:, c0:512], lhsT=K[:, m, kb * 128:(kb + 1) * 128], rhs=Q[:, m, G * 512 + c0:(G + 1) * 512], start=True, stop=True), [kk_, "Qh0"], [bkey(pb)])
                    P.op("act", lambda e, pb=pb, pt=pt, m=m, c0=c0: e.activation(out=pt[:, m, c0:512], in_=bank(pb)[:, c0:512], func=AF.Exp, scale=0.125), [bkey(pb)], [ptk + ":%d" % m])
                    if idiag >= 0:
                        P.op("dve", lambda e, pt=pt, m=m, idiag=idiag: e.memset(pt[64:128, m, idiag * 128:idiag * 128 + 64], 0.0), [ptk + ":%d" % m], [ptk + ":%d" % m])
                    if kb < 3:
                        P.op("dve", lambda e, pt=pt, m=m, kb=kb: e.tensor_scalar(out=pt[:, m, :], in0=pt[:, m, :], scalar1=kvf[:, kb:kb + 1], scalar2=None, op0=ALU.mult), [ptk + ":%d" % m, "kvf"], [ptk + ":%d" % m])

            def emit_pv(kb, G=G, V=V, nkb=nkb, base_i=base_i):
                rel_ = kb - 16 * G - 3
                i0_ = 0 if rel_ <= 0 else (rel_ + 3) // 4
                c0 = i0_ * 128
                pj = (base_i + kb) % 2
                pt = PT[pj]; ptk = "PT%d" % pj
                for m in range(2):
                    P.op("pe", lambda e, kb=kb, m=m, pt=pt, c0=c0: e.matmul(out=bank(4 + m)[:, c0:512], lhsT=V[:, kb, :], rhs=pt[:, m, c0:512], start=(kb == 0), stop=(kb == nkb - 1)), [ptk + ":%d" % m, "Vh0"], [bkey(4 + m)])
                    P.op("pe", lambda e, kb=kb, m=m, pt=pt, c0=c0: e.matmul(out=bank(6)[:, c0:512], lhsT=Esel[:, m, :], rhs=pt[:, m, c0:512], start=(kb == 0 and m == 0), stop=(kb == nkb - 1 and m == 1)), [ptk + ":%d" % m, "Esel"], [bkey(6)])

            emit_scores(0)
            for kb in range(nkb):
                if kb + 1 < nkb:
                    emit_scores(kb + 1)
                emit_pv(kb)
            P.op("act", lambda e, ot_=ot_: e.activation(out=ot_[:, 0, :], in_=bank(4), func=AF.Copy), [bkey(4)], [otk + ":0"])
            P.op("dve", lambda e, ot_=ot_: e.tensor_copy(out=ot_[:, 1, :], in_=bank(5)), [bkey(5)], [otk + ":1"])
            P.op("dve", lambda e, dr=dr: e.tensor_copy(out=dr, in_=bank(6)), [bkey(6)], [drk])
            for isl in range(4):
                P.op("pe", lambda e, isl=isl, dr=dr: e.matmul(out=dcol[:, isl * 32:(isl + 1) * 32], lhsT=dr[:, isl * 128:(isl + 1) * 128], rhs=ident_f[:, 0:32], start=True, stop=True), [drk, "ident_f"], [bkey(6)])
            rc = rcol[gj]; rck = "rcol%d" % gj
            P.op("dve", lambda e, rc=rc: e.reciprocal(out=rc, in_=dcol), [bkey(6)], [rck])
            for isl in range(4):
                for m in range(2):
                    P.op("pe", lambda e, isl=isl, m=m, ot_=ot_: e.transpose(out=tpv[:, isl, m, :], in_=ot_[:, m, isl * 128:(isl + 1) * 128], identity=ident_b), [otk + ":%d" % m, "ident_b"], [bkey(7)])
            for isl in range(4):
                s_ = G * 4 + isl
                j = (h * 16 + s_) % 2
                e_ = ep[j]; o_ = eo[j]; ek = "ep%d" % j; ok_ = "eo%d" % j
                P.op("dve", lambda e, e_=e_, isl=isl, rc=rc: e.tensor_copy(out=e_[:, 0:2], in_=rc[:, isl * 32:isl * 32 + 2]), [rck], [ek])
                P.op("dve", lambda e, e_=e_: e.tensor_tensor(out=e_[:, 2:3], in0=e_[:, 1:2], in1=lamt[:, 5:6], op=ALU.mult), [ek, "neglam"], [ek + ":2"])
                P.op("dve", lambda e, e_=e_, o_=o_, isl=isl: e.tensor_scalar(out=o_[:, 0:128], in0=tpv[:, isl, 1, :], scalar1=e_[:, 2:3], scalar2=None, op0=ALU.mult), [bkey(7), ek + ":2"], [ok_])
                P.op("dve", lambda e, e_=e_, o_=o_, isl=isl: e.scalar_tensor_tensor(out=o_[:, 128:256], in0=tpv[:, isl, 0, :], scalar=e_[:, 0:1], in1=o_[:, 0:128], op0=ALU.mult, op1=ALU.add), [bkey(7), ek, ok_], [ok_ + ":1"])
                P.op("act", lambda e, e_=e_, o_=o_: e.activation(out=o_[:, 256:384], in_=o_[:, 128:256], func=AF.Square, accum_out=e_[:, 3:4]), [ok_ + ":1"], [ok_ + ":2", ek + ":3"])
                P.op("act", lambda e, e_=e_: e.activation(out=e_[:, 4:5], in_=e_[:, 3:4], func=AF.Ln, scale=1.0 / 128, bias=epst), [ek + ":3", "epst"], [ek + ":4"])
                P.op("act", lambda e, e_=e_: e.activation(out=e_[:, 5:6], in_=e_[:, 4:5], func=AF.Exp, scale=-0.5), [ek + ":4"], [ek + ":5"])
                P.op("dve", lambda e, e_=e_, o_=o_, s_=s_, h=h: e.scalar_tensor_tensor(out=ya[:, s_, h * 128:(h + 1) * 128], in0=o_[:, 128:256], scalar=e_[:, 5:6], in1=hn, op0=ALU.mult, op1=ALU.mult), [ok_ + ":1", ek + ":5", "hn"], ["ya"])
    for n in ["Kh0", "Kh1", "Vh0", "Qh0", "PT0", "PT1", "denrow0", "denrow1", "rcol0", "rcol1", "Esel_f", "OT0", "OT1", "ep0", "ep1", "eo0", "eo1", "lq", "lqp", "kvf"]:
        A.free(n)

    if debug:
        ya_dbg = dscr("ya_dbg", [128, 16 * 1024])
        dma(ya_dbg, ya.rearrange("p s f -> p (s f)"), reads=["ya"], writes=["ya_dbg"])
    def load_bf16(name):
        dst_d, src_, K_, N_, gn_ = wsc[name]
        KC = K_ // 128
        wt = A.alloc(name, KC * N_, BF16).rearrange("p (k n) -> p k n", k=KC)
        for kc in range(KC):
            dma(wt[:, kc, :], dst_d[kc * 128:(kc + 1) * 128, :], reads=[name + "_d"], writes=[name], q="sp" if kc % 2 == 0 else "act")
        return wt

    gpost = A.alloc("gpost", 1024, F32)
    pst = A.alloc("pst", 8, F32)
    psq = A.alloc("psq", 1024, BF16)
    ptmp = A.alloc("ptmp", 1024, F32)
    ost = [A.alloc("ost%d" % i, 1024, F32) for i in range(2)]
    xres = [A.alloc("xres%d" % i, 1024, F32) for i in range(2)]

    def post_norm_residual(pb0, gain_bc, gkey, res_in, res_in_keys, res_out, res_out_key):
        for half in range(2):
            P.op("act", lambda e, half=half: e.activation(out=psq[:, half * 512:(half + 1) * 512], in_=bank(pb0 + half), func=AF.Square, accum_out=pst[:, half:half + 1]), [bkey(pb0 + half)], ["psq", "pst:%d" % half])
        P.op("dve", lambda e: e.tensor_tensor(out=pst[:, 2:3], in0=pst[:, 0:1], in1=pst[:, 1:2], op=ALU.add), ["pst:0", "pst:1"], ["pst:2"])
        P.op("act", lambda e: e.activation(out=pst[:, 3:4], in_=pst[:, 2:3], func=AF.Ln, scale=1.0 / D, bias=epst), ["pst:2", "epst"], ["pst:3"])
        P.op("act", lambda e: e.activation(out=pst[:, 4:5], in_=pst[:, 3:4], func=AF.Exp, scale=-0.5), ["pst:3"], ["pst:4"])
        for half in range(2):
            P.op("dve", lambda e, half=half: e.scalar_tensor_tensor(out=ptmp[:, half * 512:(half + 1) * 512], in0=bank(pb0 + half), scalar=pst[:, 4:5], in1=gain_bc[:, half * 512:(half + 1) * 512], op0=ALU.mult, op1=ALU.mult), [bkey(pb0 + half), "pst:4", gkey], ["ptmp:%d" % half])
        P.op("pool", lambda e: e.tensor_tensor(out=res_out, in0=ptmp, in1=res_in, op=ALU.add), ["ptmp:0", "ptmp:1"] + res_in_keys, [res_out_key])

    wglu = load_bf16("wglu")
    wssm = load_bf16("wssm")
    ys2 = A.alloc("ys2", 8 * 512, BF16).rearrange("p (k t) -> p k t", k=8)
    gab = A.alloc("gab", 8 * 512, BF16).rearrange("p (k t) -> p k t", k=8)
    sg = [A.alloc("sg%d" % i, 512, BF16) for i in range(2)]
    for tt in range(4):
        dma(gab, g_d[0:8, :, tt * 512:(tt + 1) * 512].rearrange("k p t -> p k t"), reads=["g_d"], writes=["gab"], q="pool")
        for mc in range(8):
            pb = 2 + mc % 2
            for kc in range(8):
                P.op("pe", lambda e, kc=kc, mc=mc, pb=pb, tt=tt: e.matmul(out=bank(pb), lhsT=wglu[:, kc, mc * 128:(mc + 1) * 128], rhs=ysg[:, kc, tt * 512:(tt + 1) * 512], start=(kc == 0), stop=(kc == 7)), ["wglu", "ysg"], [bkey(pb)])
            j = mc % 2
            P.op("act", lambda e, pb=pb, j=j, mc=mc: e.activation(out=sg[j], in_=bank(pb), func=AF.Sigmoid, bias=bglu_pp[:, mc:mc + 1]), [bkey(pb), "bglu_pp"], ["sg%d" % j])
            P.op("pool", lambda e, j=j, mc=mc, tt=tt: e.tensor_tensor(out=ys2[:, mc, :], in0=sg[j], in1=ysg[:, mc, tt * 512:(tt + 1) * 512], op=ALU.mult), ["sg%d" % j, "ysg"], ["ys2"])
        for mc in range(8):
            pa = 4 + (mc % 2)
            for kc in range(8):
                P.op("pe", lambda e, kc=kc, mc=mc, pa=pa: e.matmul(out=bank(pa), lhsT=wssm[:, kc, mc * 128:(mc + 1) * 128], rhs=ys2[:, kc, :], start=(kc == 0), stop=(kc == 7)), ["wssm", "ys2"], [bkey(pa)])
            P.op("dve", lambda e, pa=pa, mc=mc, tt=tt: e.tensor_tensor(out=ysg[:, mc, tt * 512:(tt + 1) * 512], in0=bank(pa), in1=gab[:, mc, :], op=ALU.mult), [bkey(pa), "gab"], ["ysg"])
    for n in ["wglu", "wssm", "ys2", "sg0", "sg1"]:
        A.free(n)
    wda = load_bf16("wda")
    wmix = load_bf16("wmix")
    dma(gpost, gains["norm_mix_post"].partition_broadcast(128), writes=["gpost"])
    yaT = A.alloc("yaT", 8 * 512, BF16).rearrange("p (k t) -> p k t", k=8)
    mrg = A.alloc("mrg", 8 * 512, BF16).rearrange("p (k t) -> p k t", k=8)
    tb = [A.alloc("tb%d" % i, 512, F32) for i in range(2)]
    for tt in range(4):
        for bl in range(4):
            s = tt * 4 + bl
            tpb = bl % 2
            tp = bank_bf(tpb).rearrange("p (k t) -> p k t", k=8)
            for kc in range(8):
                P.op("pe", lambda e, kc=kc, s=s, tp=tp: e.transpose(out=tp[:, kc, :], in_=ya[:, s, kc * 128:(kc + 1) * 128], identity=ident_b), ["ya", "ident_b"], [bkey(tpb)])
            P.op("act", lambda e, tp=tp, bl=bl: e.activation(out=yaT[:, :, bl * 128:(bl + 1) * 128], in_=tp, func=AF.Copy), [bkey(tpb)], ["yaT"])
        dma(gab, g_d[8:16, :, tt * 512:(tt + 1) * 512].rearrange("k p t -> p k t"), reads=["g_d"], writes=["gab"], q="pool")
        for mc in range(8):
            pbb_ = 4 + (mc % 2)
            for kc in range(8):
                P.op("pe", lambda e, kc=kc, mc=mc, pbb_=pbb_: e.matmul(out=bank(pbb_), lhsT=wda[:, kc, mc * 128:(mc + 1) * 128], rhs=yaT[:, kc, :], start=(kc == 0), stop=(kc == 7)), ["wda", "yaT"], [bkey(pbb_)])
            j = mc % 2
            P.op("dve", lambda e, pbb_=pbb_, j=j, mc=mc: e.tensor_tensor(out=tb[j], in0=bank(pbb_), in1=gab[:, mc, :], op=ALU.mult), [bkey(pbb_), "gab"], ["tb%d" % j])
            P.op("pool", lambda e, j=j, mc=mc, tt=tt: e.tensor_tensor(out=mrg[:, mc, :], in0=tb[j], in1=ysg[:, mc, tt * 512:(tt + 1) * 512], op=ALU.add), ["tb%d" % j, "ysg"], ["mrg"])
        for bl in range(4):
            s = tt * 4 + bl
            pb0 = 2 * (bl % 2)
            for half in range(2):
                for kc in range(8):
                    P.op("pe", lambda e, kc=kc, bl=bl, half=half, pb0=pb0: e.matmul(out=bank(pb0 + half), lhsT=mrg[:, kc, bl * 128:(bl + 1) * 128], rhs=wmix[:, kc, half * 512:(half + 1) * 512], start=(kc == 0), stop=(kc == 7)), ["mrg", "wmix"], [bkey(pb0 + half)])
            xr = xres[s % 2]; xrk = "xres%d" % (s % 2)
            dma(xr, xown[s * 128:(s + 1) * 128, :], writes=[xrk], q="sp")
            o_ = ost[s % 2]; ok_ = "ost%d" % (s % 2)
            post_norm_residual(pb0, gpost, "gpost", xr, [xrk], o_, ok_)
            dma(x1_d[s * 128:(s + 1) * 128, :], o_, reads=[ok_], writes=["x1_d"], q="sp")
    for n in ["wda", "wmix", "yaT", "mrg", "gab", "tb0", "tb1", "ya", "ysg"]:
        A.free(n)

    wxkv = load_bf16("wxkv")
    memT = A.alloc("memT", 8 * 256, BF16).rearrange("p (k t) -> p k t", k=8)
    for mb in range(2):
        norm_block_T(mem[mb * 128:(mb + 1) * 128, :], True, memT[:, :, mb * 128:(mb + 1) * 128], "memT")
    mkT = A.alloc("mkT", 8 * 256, BF16).rearrange("p (k t) -> p k t", k=8)
    mv = A.alloc("mv", 2 * 1024, BF16).rearrange("p (m f) -> p m f", m=2)
    for mc in range(8):
        pb = mc % 2
        for kc in range(8):
            P.op("pe", lambda e, kc=kc, mc=mc, pb=pb: e.matmul(out=bank(pb)[:, 0:256], lhsT=wxkv[:, kc, mc * 128:(mc + 1) * 128], rhs=memT[:, kc, :], start=(kc == 0), stop=(kc == 7)), ["wxkv", "memT"], [bkey(pb)])
        P.op("act", lambda e, pb=pb, mc=mc: e.activation(out=mkT[:, mc, :], in_=bank(pb)[:, 0:256], func=AF.Copy), [bkey(pb)], ["mkT"])
    for mt in range(2):
        for half in range(2):
            pb = 2 + half
            for kc in range(8):
                P.op("pe", lambda e, kc=kc, mt=mt, half=half, pb=pb: e.matmul(out=bank(pb), lhsT=memT[:, kc, mt * 128:(mt + 1) * 128], rhs=wxkv[:, kc, 1024 + half * 512:1024 + (half + 1) * 512], start=(kc == 0), stop=(kc == 7)), ["wxkv", "memT"], [bkey(pb)])
            P.op("act", lambda e, pb=pb, mt=mt, half=half: e.activation(out=mv[:, mt, half * 512:(half + 1) * 512], in_=bank(pb), func=AF.Copy), [bkey(pb)], ["mv"])
    A.free("wxkv"); A.free("memT")
    wxq = load_bf16("wxq")
    wxo = load_bf16("wxo")
    dma(gpost, gains["norm_x_post"].partition_broadcast(128), writes=["gpost"])
    ones_b = A.alloc("ones_b", 128, BF16)
    P.op("pool", lambda e: e.memset(ones_b, 1.0), [], ["ones_b"])
    h2T = A.alloc("h2T", 8 * 512, BF16).rearrange("p (k t) -> p k t", k=8)
    xqT = A.alloc("xqT", 8 * 512, BF16).rearrange("p (k t) -> p k t", k=8)
    xoT = A.alloc("xoT", 8 * 512, BF16).rearrange("p (k t) -> p k t", k=8)
    xp = [A.alloc("xp%d" % i, 2 * 512, BF16).rearrange("p (m t) -> p m t", m=2) for i in range(2)]
    rden = [A.alloc("rden%d" % i, 512, F32) for i in range(2)]
    for tt in range(4):
        for bl in range(4):
            s = tt * 4 + bl
            norm_block_T(x1_d[s * 128:(s + 1) * 128, :], True, h2T[:, :, bl * 128:(bl + 1) * 128], "h2T", tp_bank=7)
        for mc in range(8):
            pb = mc % 2
            for kc in range(8):
                P.op("pe", lambda e, kc=kc, mc=mc, pb=pb: e.matmul(out=bank(pb), lhsT=wxq[:, kc, mc * 128:(mc + 1) * 128], rhs=h2T[:, kc, :], start=(kc == 0), stop=(kc == 7)), ["wxq", "h2T"], [bkey(pb)])
            P.op("act", lambda e, pb=pb, mc=mc: e.activation(out=xqT[:, mc, :], in_=bank(pb), func=AF.Copy), [bkey(pb)], ["xqT"])
        for hh in range(4):
            j = hh % 2
            for mt in range(2):
                pb = 2 + mt
                for dc in range(2):
                    P.op("pe", lambda e, hh=hh, mt=mt, dc=dc, pb=pb: e.matmul(out=bank(pb), lhsT=mkT[:, hh * 2 + dc, mt * 128:(mt + 1) * 128], rhs=xqT[:, hh * 2 + dc, :], start=(dc == 0), stop=(dc == 1)), ["mkT", "xqT"], [bkey(pb)])
                P.op("act", lambda e, pb=pb, j=j, mt=mt: e.activation(out=xp[j][:, mt, :], in_=bank(pb), func=AF.Exp, scale=1.0 / 16), [bkey(pb)], ["xp%d" % j])
            for mt in range(2):
                P.op("pe", lambda e, j=j, mt=mt: e.matmul(out=bank(4), lhsT=ones_b, rhs=xp[j][:, mt, :], start=(mt == 0), stop=(mt == 1)), ["ones_b", "xp%d" % j], [bkey(4)])
            P.op("dve", lambda e, j=j: e.reciprocal(out=rden[j], in_=bank(4)), [bkey(4)], ["rden%d" % j])
            for dc in range(2):
                pb = 5 + dc
                for mt in range(2):
                    P.op("pe", lambda e, hh=hh, j=j, mt=mt, dc=dc, pb=pb: e.matmul(out=bank(pb), lhsT=mv[:, mt, (hh * 2 + dc) * 128:(hh * 2 + dc + 1) * 128], rhs=xp[j][:, mt, :], start=(mt == 0), stop=(mt == 1)), ["mv", "xp%d" % j], [bkey(pb)])
                P.op("dve", lambda e, hh=hh, j=j, dc=dc, pb=pb: e.tensor_tensor(out=xoT[:, hh * 2 + dc, :], in0=bank(pb), in1=rden[j], op=ALU.mult), [bkey(pb), "rden%d" % j], ["xoT"])
        for bl in range(4):
            s = tt * 4 + bl
            pb0 = 2 * (bl % 2)
            for half in range(2):
                for kc in range(8):
                    P.op("pe", lambda e, kc=kc, bl=bl, half=half, pb0=pb0: e.matmul(out=bank(pb0 + half), lhsT=xoT[:, kc, bl * 128:(bl + 1) * 128], rhs=wxo[:, kc, half * 512:(half + 1) * 512], start=(kc == 0), stop=(kc == 7)), ["xoT", "wxo"], [bkey(pb0 + half)])
            xr = xres[s % 2]; xrk = "xres%d" % (s % 2)
            dma(xr, x1_d[s * 128:(s + 1) * 128, :], reads=["x1_d"], writes=[xrk], q="sp")
            o_ = ost[s % 2]; ok_ = "ost%d" % (s % 2)
            post_norm_residual(pb0, gpost, "gpost", xr, [xrk], o_, ok_)
            dma(x2_d[s * 128:(s + 1) * 128, :], o_, reads=[ok_], writes=["x2_d"], q="sp")
    for n in ["wxq", "wxo", "mkT", "mv", "h2T", "xqT", "xoT", "xp0", "xp1", "rden0", "rden1"]:
        A.free(n)

    dma(gpost, gains["norm_ff_post"].partition_broadcast(128), writes=["gpost"])
    for n in ["wstage0", "wstage1", "xs0", "xs1", "nsq0", "nsq1", "nh0", "nh1"]:
        if n in A.live:
            A.free(n)
    f1 = A.alloc("f1", 32 * 512, BF16).rearrange("p (k t) -> p k t", k=32)
    wq1 = [A.alloc("wq1_%d" % i, 8 * 1024, BF16).rearrange("p (k n) -> p k n", k=8) for i in range(2)]
    h3T = A.alloc("h3T", 8 * 512, BF16).rearrange("p (k t) -> p k t", k=8)
    fr = [A.alloc("fr%d" % i, 512, BF16) for i in range(2)]
    wctr2 = [0]
    for tt in range(4):
        for bl in range(4):
            s = tt * 4 + bl
            norm_block_T(x2_d[s * 128:(s + 1) * 128, :], True, h3T[:, :, bl * 128:(bl + 1) * 128], "h3T", tp_bank=7)
        for q4 in range(4):
            i = wctr2[0]; wctr2[0] += 1
            w1 = wq1[i % 2]; w1k = "wq1_%d" % (i % 2)
            dma(w1, wf1_d[:, q4 * 1024:(q4 + 1) * 1024].rearrange("(k p) n -> p k n", p=128), reads=["wf_d"], writes=[w1k], q="sp")
            for fl in range(8):
                fc = q4 * 8 + fl
                pb = 4 + fc % 2
                for kc in range(8):
                    P.op("pe", lambda e, kc=kc, fl=fl, pb=pb, w1=w1: e.matmul(out=bank(pb), lhsT=w1[:, kc, fl * 128:(fl + 1) * 128], rhs=h3T[:, kc, :], start=(kc == 0), stop=(kc == 7)), [w1k, "h3T"], [bkey(pb)])
                j = fc % 2
                P.op("act", lambda e, pb=pb, j=j: e.activation(out=fr[j], in_=bank(pb), func=AF.Relu), [bkey(pb)], ["fr%d" % j])
                P.op("pool", lambda e, j=j, fc=fc: e.tensor_tensor(out=f1[:, fc, :], in0=fr[j], in1=fr[j], op=ALU.mult), ["fr%d" % j], ["f1"])
        for q4 in range(4):
            i = wctr2[0]; wctr2[0] += 1
            w2 = wq1[i % 2]; w2k = "wq1_%d" % (i % 2)
            dma(w2, wf2_d[q4 * 1024:(q4 + 1) * 1024, :].rearrange("(k p) n -> p k n", p=128), reads=["wf_d"], writes=[w2k], q="sp")
            for bl in range(4):
                for half in range(2):
                    pbk_ = bl * 2 + half
                    for kcl in range(8):
                        P.op("pe", lambda e, kcl=kcl, bl=bl, half=half, pbk_=pbk_, q4=q4, w2=w2: e.matmul(out=bank(pbk_), lhsT=f1[:, q4 * 8 + kcl, bl * 128:(bl + 1) * 128], rhs=w2[:, kcl, half * 512:(half + 1) * 512], start=(q4 == 0 and kcl == 0), stop=(q4 == 3 and kcl == 7)), ["f1", w2k], [bkey(pbk_)])
        for bl in range(4):
            s = tt * 4 + bl
            pb0 = 2 * bl
            xr = xres[s % 2]; xrk = "xres%d" % (s % 2)
            dma(xr, x2_d[s * 128:(s + 1) * 128, :], reads=["x2_d"], writes=[xrk], q="sp")
            o_ = ost[s % 2]; ok_ = "ost%d" % (s % 2)
            post_norm_residual(pb0, gpost, "gpost", xr, [xrk], o_, ok_)
            dma(out_d[s * 128:(s + 1) * 128, :], o_, reads=[ok_], writes=["out_d"], q="sp")

    P.emit()
    es.close()
    return nc


def _rope_tables(pos):
    inv = (10000.0 ** (-np.arange(0, 64, 2, dtype=np.float32) / 64)).astype(np.float32)
    ang = pos.astype(np.float32)[:, None] * inv[None, :]
    c = np.cos(ang).astype(np.float32).T
    s = np.sin(ang).astype(np.float32).T
    return np.ascontiguousarray(np.tile(c, (4, 1))), np.ascontiguousarray(np.tile(s, (4, 1)))


_NC_CACHE = {}


def make_in_maps(inputs):
    x = np.asarray(inputs["x"], dtype=np.float32)
    memv = np.asarray(inputs["mem"], dtype=np.float32)
    ident = np.eye(128, dtype=np.float32)
    swapm = np.zeros((128, 128), np.float32)
    for p in range(64):
        swapm[p, 64 + p] = 1.0
        swapm[64 + p, p] = 1.0
    gmask = np.zeros((128, 8, 128), np.float32)
    for gl in range(8):
        gmask[:, gl, gl * 16:(gl + 1) * 16] = 1.0
    esel = np.zeros((128, 256), np.float32)
    esel[:, 0] = 1.0
    esel[:, 129] = 1.0
    shared = {}
    for k, v in inputs.items():
        if k in ("x", "mem"):
            continue
        a = np.asarray(v, dtype=np.float32)
        shared[k] = np.ascontiguousarray(a[0])
    in_maps = []
    for c in range(8):
        b, j = c // 4, c % 4
        pad = (3 - j) * 128
        xs = np.zeros((SEQ, D), np.float32)
        xs[pad:] = x[b, :SEQ - pad]
        own_blocks = [4 * s + j for s in range(16)]
        xo = np.concatenate([x[b, r * 128:(r + 1) * 128] for r in own_blocks], axis=0)
        pos_seq = np.arange(SEQ) - pad
        cseq, sseq = _rope_tables(pos_seq)
        pos_own = np.concatenate([np.arange(r * 128, (r + 1) * 128) for r in own_blocks])
        cown, sown = _rope_tables(pos_own)
        kval = (pos_seq >= 0).astype(np.float32)
        m = dict(shared)
        m.update(xseq=xs, xown=np.ascontiguousarray(xo), mem=np.ascontiguousarray(memv[b]), cos_seq=cseq, sin_seq=sseq,
                 cos_own=cown, sin_own=sown, kvalid=kval, esel=esel, ident=ident, swapm=swapm, gmask=gmask)
        in_maps.append(m)
    return in_maps


def kernel(**inputs):
    if "nc" not in _NC_CACHE:
        _NC_CACHE["nc"] = build_program()
    nc = _NC_CACHE["nc"]
    in_maps = make_in_maps(inputs)
    res = run_bass_kernel_spmd(nc, in_maps, core_ids=list(range(8)))
    out = np.zeros((2, SEQ, D), np.float32)
    for c in range(8):
        b, j = c // 4, c % 4
        o = res.results[c]["out"]
        for s in range(16):
            r = 4 * s + j
            out[b, r * 128:(r + 1) * 128] = o[s * 128:(s + 1) * 128]
    return out
```

```python
import contextlib
import math
import numpy as np
import concourse.bass as bass
import concourse.mybir as mybir
from concourse.bass_utils import run_bass_kernel_spmd

F32 = mybir.dt.float32
BF16 = mybir.dt.bfloat16
ALU = mybir.AluOpType
AF = mybir.ActivationFunctionType
AX = mybir.AxisListType

D = 1024
SEQ = 8192
NB = 64
NOWN = 2048
EPS = 1e-6
NLEV = 13
ENGS = ["pe", "act", "dve", "pool", "sp"]


class Op:
    __slots__ = ("eng", "idx", "fn", "deps", "is_dma", "needs_inc", "semval", "dsem", "dval")

    def __init__(self, eng, idx, fn, is_dma):
        self.eng, self.idx, self.fn, self.is_dma = eng, idx, fn, is_dma
        self.deps = []
        self.needs_inc = False
        self.semval = None
        self.dsem = None
        self.dval = None


class Prog:
    def __init__(self, nc, n_dma_sems=16):
        self.nc = nc
        self.ops = {e: [] for e in ENGS}
        self.state = {}
        self.rings = {"sp": (0, 12), "pool": (12, 8), "act": (20, 8), "dve": (28, 2), "pe": (28, 2)}
        n_dma_sems = 30
        self.n_dma_sems = n_dma_sems
        self.dma_rr = {q: 0 for q in self.rings}
        self.dma_counts = [0] * n_dma_sems
        self.waited = {}
        self.waited_dma = {}

    def _st(self, key):
        s = self.state.get(key)
        if s is None:
            s = {"w": {}, "r": {}}
            if isinstance(key, str) and ":" in key:
                base = self.state.get(key.split(":")[0])
                if base is not None:
                    s["w"] = dict(base["w"])
            self.state[key] = s
        return s

    def _add_dep(self, op, dep):
        if dep is None or dep is op:
            return
        if dep.is_dma:
            k = (op.eng, dep.dsem)
            if self.waited_dma.get(k, -1) >= dep.dval:
                return
            self.waited_dma[k] = dep.dval
            op.deps.append(dep)
            return
        k = (op.eng, dep.eng)
        if self.waited.get(k, -1) >= dep.idx:
            return
        self.waited[k] = dep.idx
        dep.needs_inc = True
        op.deps.append(dep)

    def op(self, eng, fn, reads=(), writes=(), dma=False):
        lst = self.ops[eng]
        o = Op(eng, len(lst), fn, dma)
        if dma:
            base, cnt_ = self.rings[eng]
            i = base + self.dma_rr[eng]
            self.dma_rr[eng] = (self.dma_rr[eng] + 1) % cnt_
            self.dma_counts[i] += 16
            o.dsem, o.dval = i, self.dma_counts[i]
        for key in reads:
            for e, w in self._st(key)["w"].items():
                if (not w.is_dma) and w.eng == eng and eng == "pe":
                    continue
                self._add_dep(o, w)
        for key in writes:
            s = self._st(key)
            for e, r in s["r"].items():
                if (not r.is_dma) and r.eng == eng and not dma:
                    continue
                self._add_dep(o, r)
            for e, w in s["w"].items():
                if (not w.is_dma) and w.eng == eng and not dma:
                    continue
                self._add_dep(o, w)
        me = ("dma%d" % o.dsem) if dma else eng
        for key in reads:
            self._st(key)["r"][me] = o
        for key in writes:
            s = self._st(key)
            s["w"][me] = o
            s["r"] = {}
        lst.append(o)
        return o

    def alias(self, newkey, oldkeys):
        ns = self._st(newkey)
        for ok in oldkeys:
            os_ = self.state.get(ok)
            if os_ is None:
                continue
            for kind in ("w", "r"):
                for e, o in os_[kind].items():
                    cur = ns["w"].get(e)
                    if cur is None or (o.is_dma and o.dval > cur.dval) or ((not o.is_dma) and o.idx > cur.idx):
                        ns["w"][e] = o

    def emit(self):
        nc = self.nc
        with contextlib.ExitStack() as es:
            sems = {e: es.enter_context(nc.semaphore("s_" + e)) for e in ["pe", "act", "dve", "pool"]}
            dsems = [es.enter_context(nc.semaphore("d%d" % i)) for i in range(self.n_dma_sems)]
            for e in ENGS:
                c = 0
                for o in self.ops[e]:
                    if (not o.is_dma) and o.needs_inc:
                        c += 1
                        o.semval = c
            block = es.enter_context(nc.Block())

            def run(name, eng):
                for o in self.ops[name]:
                    for d in o.deps:
                        if d.is_dma:
                            eng.wait_ge(dsems[d.dsem], d.dval)
                        else:
                            eng.wait_ge(sems[d.eng], d.semval)
                    ins = o.fn(eng)
                    if o.is_dma:
                        ins.then_inc(dsems[o.dsem], 16)
                    elif o.needs_inc:
                        ins.then_inc(sems[o.eng], 1)

            @block.tensor
            def _(eng):
                run("pe", eng)

            @block.scalar
            def _(eng):
                run("act", eng)

            @block.vector
            def _(eng):
                run("dve", eng)

            @block.gpsimd
            def _(eng):
                run("pool", eng)

            @block.sync
            def _(eng):
                run("sp", eng)
                for i in range(self.n_dma_sems):
                    if self.dma_counts[i] > 0:
                        eng.wait_ge(dsems[i], self.dma_counts[i])


class Arena:
    def __init__(self, P, base_ap, nbytes):
        self.P = P
        self.base = base_ap
        self.nbytes = nbytes
        self.live = {}
        self.freed = []

    def alloc(self, name, nelem, dt):
        esz = 4 if dt == F32 else 2
        size = (nelem * esz + 63) // 64 * 64
        segs = sorted(self.live.values())
        off = 0
        for (o, s) in segs:
            if off + size <= o:
                break
            off = max(off, o + s)
        assert off + size <= self.nbytes, "SBUF arena overflow for %s (%d): %s" % (name, size, sorted((o, sz, n) for n, (o, sz) in self.live.items()))
        self.live[name] = (off, size)
        olds = [n for (o, s, n) in self.freed if o < off + size and off < o + s]
        oldkeys = [k for k in self.P.state if any(k == n or (isinstance(k, str) and k.startswith(n + ":")) for n in olds)]
        self.P.alias(name, oldkeys)
        self._aliaskeys = oldkeys
        ap = self.base[:, off // 4:(off + size) // 4]
        if dt != F32:
            ap = ap.bitcast(dt)
        return ap[:, 0:nelem]

    def free(self, name):
        o, s = self.live.pop(name)
        self.freed.append((o, s, name))


def build_program(debug=False):
    nc = bass.Bass("TRN2", target_bir_lowering=False)

    def din(name, shape, dt=F32):
        return nc.dram_tensor(name, list(shape), dt, kind="ExternalInput").ap()

    def dscr(name, shape, dt=BF16):
        return nc.dram_tensor(name, list(shape), dt, kind="ExternalOutput" if debug else "Internal").ap()

    xseq = din("xseq", [SEQ, D])
    xown = din("xown", [NOWN, D])
    mem = din("mem", [256, D])
    w_in = din("w_in", [D, 6144])
    w_glu = din("w_glu", [D, D]); w_ssm = din("w_ssm_proj", [D, D]); w_da = din("w_da_proj", [D, D])
    w_mix = din("w_mix_out", [D, D]); w_xq = din("w_xq", [D, D]); w_xkv = din("w_xkv", [D, 2 * D])
    w_xo = din("w_xo", [D, D]); w_ff1 = din("w_ff1", [D, 4 * D]); w_ff2 = din("w_ff2", [4 * D, D])
    gains = {n: din(n, [D]) for n in ["norm_mix_pre", "norm_mix_post", "norm_x_pre", "norm_mem", "norm_x_post",
                                      "norm_ff_pre", "norm_ff_post"]}
    b_gate = din("b_gate", [2 * D]); b_glu = din("b_glu", [D]); ssm_d = din("ssm_d", [D])
    lam_re = din("ssm_lambda_re", [64, 64]); lam_im = din("ssm_lambda_im", [64, 64]); log_dt = din("ssm_log_dt", [64])
    b_re = din("ssm_b_re", [64, 64, 16]); b_im = din("ssm_b_im", [64, 64, 16])
    c_re = din("ssm_c_re", [64, 16, 64]); c_im = din("ssm_c_im", [64, 16, 64])
    lqk = [din(n, [64]) for n in ["da_lambda_q1", "da_lambda_k1", "da_lambda_q2", "da_lambda_k2"]]
    head_norm = din("da_head_norm", [128])
    cos_seq = din("cos_seq", [128, SEQ]); sin_seq = din("sin_seq", [128, SEQ])
    cos_own = din("cos_own", [128, NOWN]); sin_own = din("sin_own", [128, NOWN])
    kvalid = din("kvalid", [SEQ])
    esel_d = din("esel", [128, 256])
    ident_d = din("ident", [128, 128]); swap_d = din("swapm", [128, 128]); gmask_d = din("gmask", [128, 8, 128])
    out_d = nc.dram_tensor("out", [NOWN, D], F32, kind="ExternalOutput").ap()

    kT_d = dscr("kT_d", [8, 128, SEQ]); v_d = dscr("v_d", [SEQ, D]); uT_d = dscr("uT_d", [8, 128, SEQ])
    qT_d = dscr("qT_d", [8, 128, NOWN]); g_d = dscr("g_d", [16, 128, NOWN])

    P = Prog(nc)
    es = contextlib.ExitStack()
    ARENA_BYTES = 190 * 1024
    arena_t = es.enter_context(nc.sbuf_tensor("arena", [128, ARENA_BYTES // 4], F32))
    A = Arena(P, arena_t[:], ARENA_BYTES)
    banks = [es.enter_context(nc.psum_tensor("bank%d" % i, [128, 512], F32)) for i in range(8)]

    def bank(i):
        return banks[i][:]

    def bank_bf(i):
        return banks[i][:].bitcast(BF16)

    def bkey(i):
        return "bank%d" % i

    def dma(out, in_, reads=(), writes=(), q="sp", slow=False):
        if slow:
            return P.op(q, lambda e: e.dma_start(out=out, in_=in_, allow_slow_non_contiguous=True), reads, writes, dma=True)
        return P.op(q, lambda e: e.dma_start(out=out, in_=in_), reads, writes, dma=True)

    ident_f = A.alloc("ident_f", 128, F32); ident_b = A.alloc("ident_b", 128, BF16)
    swap_f = A.alloc("swap_f", 128, F32)
    gmask = A.alloc("gmask", 1024, F32)
    epst = A.alloc("epst", 1, F32); zerot = A.alloc("zerot", 1, F32)
    dma(ident_f, ident_d, writes=["ident_f"]); dma(swap_f, swap_d, writes=["swap_f"])
    dma(gmask, gmask_d.rearrange("p g c -> p (g c)"), writes=["gmask"])
    P.op("pool", lambda e: e.tensor_copy(out=ident_b, in_=ident_f), ["ident_f"], ["ident_b"])
    P.op("pool", lambda e: e.memset(epst, EPS), [], ["epst"])
    P.op("pool", lambda e: e.memset(zerot, 0.0), [], ["zerot"])

    def load_pp(name, src, n):
        t = A.alloc(name, n, F32)
        dma(t, src.rearrange("(k p) -> p k", p=128), writes=[name], slow=True)
        return t

    g_mix_pre = load_pp("g_mix_pre", gains["norm_mix_pre"], 8)
    g_x_pre = load_pp("g_x_pre", gains["norm_x_pre"], 8)
    g_mem = load_pp("g_mem", gains["norm_mem"], 8)
    g_ff_pre = load_pp("g_ff_pre", gains["norm_ff_pre"], 8)
    bgate_pp = load_pp("bgate_pp", b_gate, 16)
    bglu_pp = load_pp("bglu_pp", b_glu, 8)
    ssmd_pp = load_pp("ssmd_pp", ssm_d, 8)

    wctr = [0]

    def load_weight(name, src, K, N, gain=None, col0=0, rot=False):
        KC = K // 128
        wt = A.alloc(name, KC * N, BF16)
        wv = wt.rearrange("p (k n) -> p k n", k=KC)
        CH = min(N, 2048)
        for kc in range(KC):
            for c0 in range(0, N, CH):
                i = wctr[0]; wctr[0] += 1
                sname = "wstage%d" % (i % 2)
                if sname not in A.live:
                    A.alloc(sname, 2048, F32)
                o, s = A.live[sname]
                st = A.base[:, o // 4:o // 4 + CH]
                dma(st, src[kc * 128:(kc + 1) * 128, col0 + c0:col0 + c0 + CH], writes=[sname], q="sp")
                dst = wv[:, kc, c0:c0 + CH]
                eng = "dve"
                if rot:
                    sv = st.rearrange("p (m t d) -> p m t d", t=2, d=32)
                    dv = dst.rearrange("p (m t d) -> p m t d", t=2, d=32)
                    if gain is not None:
                        P.op(eng, lambda e, dv=dv, sv=sv, kc=kc: e.tensor_scalar(out=dv[:, :, 0, :], in0=sv[:, :, 1, :], scalar1=gain[:, kc:kc + 1], scalar2=-1.0, op0=ALU.mult, op1=ALU.mult), [sname, "gains"], [name])
                        P.op(eng, lambda e, dv=dv, sv=sv, kc=kc: e.tensor_scalar(out=dv[:, :, 1, :], in0=sv[:, :, 0, :], scalar1=gain[:, kc:kc + 1], scalar2=None, op0=ALU.mult), [sname, "gains"], [name])
                else:
                    if gain is not None:
                        P.op("act", lambda e, dst=dst, st=st, kc=kc: e.activation(out=dst, in_=st, func=AF.Copy, scale=gain[:, kc:kc + 1]), [sname, "gains"], [name])
                    else:
                        P.op("act", lambda e, dst=dst, st=st: e.activation(out=dst, in_=st, func=AF.Copy), [sname], [name])
        return wv

    P.op("pool", lambda e: e.engine_nop(), ["g_mix_pre", "g_x_pre", "g_mem", "g_ff_pre"], ["gains"])

    nctr = [0]

    def norm_block_T(x_src_ap, x_is_dram, hT_dst, hT_key, xkey=None, tp_bank=0):
        i = nctr[0]; nctr[0] += 1
        if x_is_dram:
            xs_name = "xs%d" % (i % 2)
            if xs_name not in A.live:
                A.alloc(xs_name, 1024, F32)
            o, s = A.live[xs_name]
            xs = A.base[:, o // 4:o // 4 + 1024]
            dma(xs, x_src_ap, writes=[xs_name], q="sp")
            rkeys = [xs_name]
        else:
            xs = x_src_ap
            rkeys = [xkey]
        for nm, n, dt in (("nsq%d" % (i % 2), 1024, BF16), ("nst%d" % (i % 2), 4, F32), ("nh%d" % (i % 2), 1024, BF16)):
            if nm not in A.live:
                A.alloc(nm, n, dt)
        o, s = A.live["nsq%d" % (i % 2)]; sq = A.base[:, o // 4:o // 4 + 512].bitcast(BF16)
        o, s = A.live["nst%d" % (i % 2)]; st = A.base[:, o // 4:o // 4 + 4]
        o, s = A.live["nh%d" % (i % 2)]; hb = A.base[:, o // 4:o // 4 + 512].bitcast(BF16)
        ks, kt, kh = "nsq%d" % (i % 2), "nst%d" % (i % 2), "nh%d" % (i % 2)
        P.op("act", lambda e: e.activation(out=sq, in_=xs, func=AF.Square, accum_out=st[:, 0:1]), rkeys, [ks, kt])
        P.op("act", lambda e: e.activation(out=st[:, 1:2], in_=st[:, 0:1], func=AF.Ln, scale=1.0 / D, bias=epst), [kt, "epst"], [kt + ":1"])
        P.op("act", lambda e: e.activation(out=st[:, 2:3], in_=st[:, 1:2], func=AF.Exp, scale=-0.5), [kt + ":1"], [kt + ":2"])
        P.op("dve", lambda e: e.tensor_scalar(out=hb, in0=xs, scalar1=st[:, 2:3], scalar2=None, op0=ALU.mult), rkeys + [kt + ":2"], [kh])
        tp = bank_bf(tp_bank).rearrange("p (k t) -> p k t", k=8)
        for kc in range(8):
            P.op("pe", lambda e, kc=kc: e.transpose(out=tp[:, kc, :], in_=hb[:, kc * 128:(kc + 1) * 128], identity=ident_b), [kh, "ident_b"], [bkey(tp_bank)])
        P.op("act", lambda e: e.activation(out=hT_dst, in_=tp, func=AF.Copy), [bkey(tp_bank)], [hT_key])

    x1_d = dscr("x1_d", [NOWN, D], F32)
    x2_d = dscr("x2_d", [NOWN, D], F32)
    wf1_d = nc.dram_tensor("wf1_d", [D, 4 * D], BF16, kind="Internal").ap()
    wf2_d = nc.dram_tensor("wf2_d", [4 * D, D], BF16, kind="Internal").ap()
    wsc = {}
    for nm_, (src_, K_, N_, gn_) in {"wglu": (w_glu, D, D, None), "wssm": (w_ssm, D, D, None), "wda": (w_da, D, D, None),
                                       "wmix": (w_mix, D, D, None), "wxkv": (w_xkv, D, 2 * D, g_mem), "wxq": (w_xq, D, D, g_x_pre),
                                       "wxo": (w_xo, D, D, None)}.items():
        wsc[nm_] = (nc.dram_tensor(nm_ + "_d", [K_, N_], BF16, kind="Internal").ap(), src_, K_, N_, gn_)
    cst_ = [A.alloc("cstg%d" % i, 1024, F32) for i in range(2)]
    cb = [A.alloc("cb%d" % i, 1024, BF16) for i in range(2)]
    cctr = [0]

    def cast_gen():
        jobs = [(v_[1], v_[2], v_[3], v_[4], v_[0], k_ + "_d") for k_, v_ in wsc.items()]
        jobs.append((w_ff1, D, 4 * D, g_ff_pre, wf1_d, "wf_d"))
        jobs.append((w_ff2, 4 * D, D, None, wf2_d, "wf_d"))
        for (src, K_, N_, gain, dst, dkey) in jobs:
            for kc in range(K_ // 128):
                for c0 in range(0, N_, 1024):
                    i = cctr[0]; cctr[0] += 1
                    st = cst_[i % 2]; sname = "cstg%d" % (i % 2)
                    dma(st, src[kc * 128:(kc + 1) * 128, c0:c0 + 1024], writes=[sname], q="act")
                    cbt = cb[i % 2]; cbk = "cb%d" % (i % 2)
                    if gain is not None:
                        P.op("act", lambda e, cbt=cbt, st=st, kc=kc, gain=gain: e.activation(out=cbt, in_=st, func=AF.Copy, scale=gain[:, kc:kc + 1]), [sname, "gains"], [cbk])
                    else:
                        P.op("act", lambda e, cbt=cbt, st=st: e.activation(out=cbt, in_=st, func=AF.Copy), [sname], [cbk])
                    dma(dst[kc * 128:(kc + 1) * 128, c0:c0 + 1024], cbt, reads=[cbk], writes=[dkey], q="act")
                    yield

    cgen = cast_gen()
    cg_alive = [True]

    def cast_step():
        if cg_alive[0]:
            try:
                next(cgen)
            except StopIteration:
                cg_alive[0] = False

    wk = load_weight("wk", w_in, D, 1024, gain=g_mix_pre, col0=2048)
    wkr = load_weight("wkr", w_in, D, 1024, gain=g_mix_pre, col0=2048, rot=True)
    wu = load_weight("wu", w_in, D, 1024, gain=g_mix_pre, col0=0)
    wv_ = load_weight("wv", w_in, D, 1024, gain=g_mix_pre, col0=3072)
    hTa = [A.alloc("hTa%d" % i, 8 * 512, BF16).rearrange("p (k t) -> p k t", k=8) for i in range(2)]
    cst = [A.alloc("cst%d" % i, 1024, F32) for i in range(2)]
    kst = [A.alloc("kst%d" % i, 512, BF16) for i in range(2)]
    kt1 = [A.alloc("kt1_%d" % i, 512, F32) for i in range(2)]
    kt2 = [A.alloc("kt2_%d" % i, 512, F32) for i in range(2)]
    vst = [A.alloc("vst%d" % i, 1024, BF16) for i in range(2)]
    ust = [A.alloc("ust%d" % i, 512, BF16) for i in range(2)]
    cnt = [0]
    for tt in range(16):
        hb_i = tt % 2
        hT = hTa[hb_i]; hk = "hTa%d" % hb_i
        for bl in range(4):
            blk = tt * 4 + bl
            norm_block_T(xseq[blk * 128:(blk + 1) * 128, :], True, hT[:, :, bl * 128:(bl + 1) * 128], hk)
        cs = cst[hb_i]; ck = "cst%d" % hb_i
        dma(cs[:, 0:512], cos_seq[:, tt * 512:(tt + 1) * 512], writes=[ck])
        dma(cs[:, 512:1024], sin_seq[:, tt * 512:(tt + 1) * 512], writes=[ck])
        for h in range(8):
            cast_step()
            i = cnt[0]; cnt[0] += 1
            pb = 1 + 2 * (i % 2)
            for kc in range(8):
                P.op("pe", lambda e, kc=kc, h=h, pb=pb, hT=hT: e.matmul(out=bank(pb), lhsT=wk[:, kc, h * 128:(h + 1) * 128], rhs=hT[:, kc, :], start=(kc == 0), stop=(kc == 7)), ["wk", hk], [bkey(pb)])
            for kc in range(8):
                P.op("pe", lambda e, kc=kc, h=h, pb=pb, hT=hT: e.matmul(out=bank(pb + 1), lhsT=wkr[:, kc, h * 128:(h + 1) * 128], rhs=hT[:, kc, :], start=(kc == 0), stop=(kc == 7)), ["wkr", hk], [bkey(pb + 1)])
            j = i % 2
            P.op("dve", lambda e, pb=pb, j=j, cs=cs: e.tensor_tensor(out=kt1[j], in0=bank(pb), in1=cs[:, 0:512], op=ALU.mult), [bkey(pb), ck], ["kt1_%d" % j])
            P.op("dve", lambda e, pb=pb, j=j, cs=cs: e.tensor_tensor(out=kt2[j], in0=bank(pb + 1), in1=cs[:, 512:1024], op=ALU.mult), [bkey(pb + 1), ck], ["kt2_%d" % j])
            P.op("dve", lambda e, j=j: e.tensor_tensor(out=kst[j], in0=kt1[j], in1=kt2[j], op=ALU.add), ["kt1_%d" % j, "kt2_%d" % j], ["kst%d" % j])
            dma(kT_d[h, :, tt * 512:(tt + 1) * 512], kst[j], reads=["kst%d" % j], writes=["kT_d"], q="sp")
        for fc in range(8):
            i = cnt[0]; cnt[0] += 1
            pb = 5 + (i % 2)
            for kc in range(8):
                P.op("pe", lambda e, kc=kc, fc=fc, pb=pb, hT=hT: e.matmul(out=bank(pb), lhsT=wu[:, kc, fc * 128:(fc + 1) * 128], rhs=hT[:, kc, :], start=(kc == 0), stop=(kc == 7)), ["wu", hk], [bkey(pb)])
            j = i % 2
            P.op("act", lambda e, pb=pb, j=j: e.activation(out=ust[j], in_=bank(pb), func=AF.Copy), [bkey(pb)], ["ust%d" % j])
            dma(uT_d[fc, :, tt * 512:(tt + 1) * 512], ust[j], reads=["ust%d" % j], writes=["uT_d"], q="sp")
        for bl in range(4):
            blk = tt * 4 + bl
            i = cnt[0]; cnt[0] += 1
            j = i % 2
            for half in range(2):
                pb = 1 + 2 * (i % 2) + half
                for kc in range(8):
                    P.op("pe", lambda e, kc=kc, bl=bl, half=half, pb=pb, hT=hT: e.matmul(out=bank(pb), lhsT=hT[:, kc, bl * 128:(bl + 1) * 128], rhs=wv_[:, kc, half * 512:(half + 1) * 512], start=(kc == 0), stop=(kc == 7)), ["wv", hk], [bkey(pb)])
                P.op("act", lambda e, pb=pb, j=j, half=half: e.activation(out=vst[j][:, half * 512:(half + 1) * 512], in_=bank(pb), func=AF.Copy), [bkey(pb)], ["vst%d" % j])
            dma(v_d[blk * 128:(blk + 1) * 128, :], vst[j], reads=["vst%d" % j], writes=["v_d"], q="sp")
    while cg_alive[0]:
        cast_step()
    for n in ["cstg0", "cstg1", "cb0", "cb1"]:
        A.free(n)
    for n in ["wu", "wk", "wkr", "wv", "hTa0", "hTa1", "cst0", "cst1", "kst0", "kst1", "kt1_0", "kt1_1", "kt2_0", "kt2_1", "vst0", "vst1", "ust0", "ust1"]:
        A.free(n)

    wq = load_weight("wq", w_in, D, 1024, gain=g_mix_pre, col0=1024)
    wqr = load_weight("wqr", w_in, D, 1024, gain=g_mix_pre, col0=1024, rot=True)
    wg = load_weight("wg", w_in, D, 2048, gain=g_mix_pre, col0=4096)
    hTo = [A.alloc("hTo%d" % i, 8 * 512, BF16).rearrange("p (k t) -> p k t", k=8) for i in range(2)]
    cso = [A.alloc("cso%d" % i, 1024, F32) for i in range(2)]
    qst = [A.alloc("qst%d" % i, 512, BF16) for i in range(2)]
    qt1 = [A.alloc("qt1_%d" % i, 512, F32) for i in range(2)]
    qt2 = [A.alloc("qt2_%d" % i, 512, F32) for i in range(2)]
    gst = [A.alloc("gst%d" % i, 512, BF16) for i in range(2)]
    for tt in range(4):
        hb_i = tt % 2
        hT = hTo[hb_i]; hk = "hTo%d" % hb_i
        for bl in range(4):
            blk = tt * 4 + bl
            norm_block_T(xown[blk * 128:(blk + 1) * 128, :], True, hT[:, :, bl * 128:(bl + 1) * 128], hk)
        cs = cso[hb_i]; ck = "cso%d" % hb_i
        dma(cs[:, 0:512], cos_own[:, tt * 512:(tt + 1) * 512], writes=[ck])
        dma(cs[:, 512:1024], sin_own[:, tt * 512:(tt + 1) * 512], writes=[ck])
        for h in range(8):
            i = cnt[0]; cnt[0] += 1
            pb = 1 + 2 * (i % 2)
            for kc in range(8):
                P.op("pe", lambda e, kc=kc, h=h, pb=pb, hT=hT: e.matmul(out=bank(pb), lhsT=wq[:, kc, h * 128:(h + 1) * 128], rhs=hT[:, kc, :], start=(kc == 0), stop=(kc == 7)), ["wq", hk], [bkey(pb)])
            for kc in range(8):
                P.op("pe", lambda e, kc=kc, h=h, pb=pb, hT=hT: e.matmul(out=bank(pb + 1), lhsT=wqr[:, kc, h * 128:(h + 1) * 128], rhs=hT[:, kc, :], start=(kc == 0), stop=(kc == 7)), ["wqr", hk], [bkey(pb + 1)])
            j = i % 2
            P.op("dve", lambda e, pb=pb, j=j, cs=cs: e.tensor_tensor(out=qt1[j], in0=bank(pb), in1=cs[:, 0:512], op=ALU.mult), [bkey(pb), ck], ["qt1_%d" % j])
            P.op("dve", lambda e, pb=pb, j=j, cs=cs: e.tensor_tensor(out=qt2[j], in0=bank(pb + 1), in1=cs[:, 512:1024], op=ALU.mult), [bkey(pb + 1), ck], ["qt2_%d" % j])
            P.op("dve", lambda e, j=j: e.tensor_tensor(out=qst[j], in0=qt1[j], in1=qt2[j], op=ALU.add), ["qt1_%d" % j, "qt2_%d" % j], ["qst%d" % j])
            dma(qT_d[h, :, tt * 512:(tt + 1) * 512], qst[j], reads=["qst%d" % j], writes=["qT_d"], q="sp")
        for gc in range(16):
            i = cnt[0]; cnt[0] += 1
            pb = 5 + (i % 2)
            for kc in range(8):
                P.op("pe", lambda e, kc=kc, gc=gc, pb=pb, hT=hT: e.matmul(out=bank(pb), lhsT=wg[:, kc, gc * 128:(gc + 1) * 128], rhs=hT[:, kc, :], start=(kc == 0), stop=(kc == 7)), ["wg", hk], [bkey(pb)])
            j = i % 2
            P.op("act", lambda e, pb=pb, j=j, gc=gc: e.activation(out=gst[j], in_=bank(pb), func=AF.Sigmoid, bias=bgate_pp[:, gc:gc + 1]), [bkey(pb), "bgate_pp"], ["gst%d" % j])
            dma(g_d[gc, :, tt * 512:(tt + 1) * 512], gst[j], reads=["gst%d" % j], writes=["g_d"], q="sp")
    for n in ["wq", "wqr", "wg", "hTo0", "hTo1", "cso0", "cso1", "qst0", "qst1", "qt1_0", "qt1_1", "qt2_0", "qt2_1", "gst0", "gst1"]:
        A.free(n)

    for n in ["wstage0", "wstage1", "xs0", "xs1", "nsq0", "nsq1", "nh0", "nh1", "nst0", "nst1"]:
        if n in A.live:
            A.free(n)
    def a64(name, n, dt=F32):
        return A.alloc(name, n, dt)[0:64, :]

    lre = a64("lre", 64); lim = a64("lim", 64); ldt = a64("ldt", 64)
    dma(lre, lam_re.rearrange("g p -> p g"), writes=["lre"], slow=True)
    dma(lim, lam_im.rearrange("g p -> p g"), writes=["lim"], slow=True)
    dma(ldt, log_dt.partition_broadcast(64), writes=["ldt"])
    dtt = a64("dtt", 64); zr = a64("zr", 64); zi = a64("zi", 64)
    P.op("act", lambda e: e.activation(out=dtt, in_=ldt, func=AF.Exp), ["ldt"], ["dtt"])
    P.op("dve", lambda e: e.tensor_tensor(out=zr, in0=lre, in1=dtt, op=ALU.mult), ["lre", "dtt"], ["zr"])
    P.op("dve", lambda e: e.tensor_tensor(out=zi, in0=lim, in1=dtt, op=ALU.mult), ["lim", "dtt"], ["zi"])
    mag = a64("mag", 64); cr = a64("cr", 64); ci = a64("ci", 64); halfpi = a64("halfpi", 1)
    P.op("pool", lambda e: e.memset(halfpi, math.pi / 2), [], ["halfpi"])
    P.op("act", lambda e: e.activation(out=mag, in_=zr, func=AF.Exp, scale=1.0 / 32), ["zr"], ["mag"])
    P.op("act", lambda e: e.activation(out=ci, in_=zi, func=AF.Sin, scale=1.0 / 32), ["zi"], ["ci"])
    P.op("act", lambda e: e.activation(out=cr, in_=zi, func=AF.Sin, scale=1.0 / 32, bias=halfpi), ["zi", "halfpi"], ["cr"])
    P.op("dve", lambda e: e.tensor_tensor(out=cr, in0=cr, in1=mag, op=ALU.mult), ["cr", "mag"], ["cr"])
    P.op("dve", lambda e: e.tensor_tensor(out=ci, in0=ci, in1=mag, op=ALU.mult), ["ci", "mag"], ["ci"])
    PW = a64("PW", NLEV * 128).rearrange("p (l r g) -> p l r g", l=NLEV, r=2)
    t1 = a64("sq_t1", 64); t2 = a64("sq_t2", 64); t3 = a64("sq_t3", 64)

    def csquare(sr, si, dr, di, keys_in, key_out):
        P.op("dve", lambda e: e.tensor_tensor(out=t1, in0=sr, in1=sr, op=ALU.mult), keys_in, ["sq_t1"])
        P.op("dve", lambda e: e.tensor_tensor(out=t2, in0=si, in1=si, op=ALU.mult), keys_in, ["sq_t2"])
        P.op("dve", lambda e: e.tensor_tensor(out=t3, in0=sr, in1=si, op=ALU.mult), keys_in, ["sq_t3"])
        P.op("dve", lambda e: e.tensor_tensor(out=dr, in0=t1, in1=t2, op=ALU.subtract), ["sq_t1", "sq_t2"], [key_out])
        P.op("dve", lambda e: e.tensor_tensor(out=di, in0=t3, in1=t3, op=ALU.add), ["sq_t3"], [key_out + ":i"])

    wr = [a64("wr%d" % i, 64) for i in range(2)]; wi = [a64("wi%d" % i, 64) for i in range(2)]
    csquare(cr, ci, wr[0], wi[0], ["cr", "ci"], "wr0")
    csquare(wr[0], wi[0], wr[1], wi[1], ["wr0", "wr0:i"], "wr1")
    csquare(wr[1], wi[1], wr[0], wi[0], ["wr1", "wr1:i"], "wr0")
    csquare(wr[0], wi[0], wr[1], wi[1], ["wr0", "wr0:i"], "wr1")
    csquare(wr[1], wi[1], PW[:, 0, 0, :], PW[:, 0, 1, :], ["wr1", "wr1:i"], "PW:0")
    for l in range(1, NLEV):
        csquare(PW[:, l - 1, 0, :], PW[:, l - 1, 1, :], PW[:, l, 0, :], PW[:, l, 1, :], ["PW:%d" % (l - 1), "PW:%d:i" % (l - 1)], "PW:%d" % l)
    den = a64("den", 64); cfr = a64("cfr", 64); cfi = a64("cfi", 64); lm1 = a64("lm1", 64)
    P.op("dve", lambda e: e.tensor_tensor(out=t1, in0=lre, in1=lre, op=ALU.mult), ["lre", "PW:%d:i" % (NLEV - 1)], ["sq_t1"])
    P.op("dve", lambda e: e.tensor_tensor(out=t2, in0=lim, in1=lim, op=ALU.mult), ["lim"], ["sq_t2"])
    P.op("dve", lambda e: e.tensor_tensor(out=den, in0=t1, in1=t2, op=ALU.add), ["sq_t1", "sq_t2"], ["den"])
    P.op("dve", lambda e: e.reciprocal(out=den, in_=den), ["den"], ["den"])
    P.op("dve", lambda e: e.tensor_scalar(out=lm1, in0=PW[:, 0, 0, :], scalar1=-1.0, scalar2=None, op0=ALU.add), ["PW:0"], ["lm1"])
    P.op("dve", lambda e: e.tensor_tensor(out=t1, in0=lm1, in1=lre, op=ALU.mult), ["lm1", "lre", "den"], ["sq_t1"])
    P.op("dve", lambda e: e.tensor_tensor(out=t2, in0=PW[:, 0, 1, :], in1=lim, op=ALU.mult), ["PW:0:i", "lim"], ["sq_t2"])
    P.op("dve", lambda e: e.tensor_tensor(out=cfr, in0=t1, in1=t2, op=ALU.add), ["sq_t1", "sq_t2"], ["cfr"])
    P.op("dve", lambda e: e.tensor_tensor(out=t1, in0=PW[:, 0, 1, :], in1=lre, op=ALU.mult), ["PW:0:i", "lre", "cfr"], ["sq_t1"])
    P.op("dve", lambda e: e.tensor_tensor(out=t2, in0=lm1, in1=lim, op=ALU.mult), ["lm1", "lim", "cfr"], ["sq_t2"])
    P.op("dve", lambda e: e.tensor_tensor(out=cfi, in0=t1, in1=t2, op=ALU.subtract), ["sq_t1", "sq_t2"], ["cfi"])
    P.op("dve", lambda e: e.tensor_tensor(out=cfr, in0=cfr, in1=den, op=ALU.mult), ["cfr", "den"], ["cfr"])
    P.op("dve", lambda e: e.tensor_tensor(out=cfi, in0=cfi, in1=den, op=ALU.mult), ["cfi", "den"], ["cfi"])
    braw = a64("braw", 2048).rearrange("p (r g h) -> p r g h", r=2, g=64)
    for gh in range(2):
        dma(braw[:, 0, gh * 32:(gh + 1) * 32, :], b_re[gh * 32:(gh + 1) * 32].rearrange("g p h -> p g h"), writes=["braw"], slow=True)
        dma(braw[:, 1, gh * 32:(gh + 1) * 32, :], b_im[gh * 32:(gh + 1) * 32].rearrange("g p h -> p g h"), writes=["braw"], slow=True)
    Bb = a64("Bb", 2048).rearrange("p (r g h) -> p r g h", r=2, g=64)
    bt1 = a64("bt1", 1024).rearrange("p (g h) -> p g h", g=64); bt2 = a64("bt2", 1024).rearrange("p (g h) -> p g h", g=64)
    cfr_b = cfr.unsqueeze(2).broadcast_to([64, 64, 16]); cfi_b = cfi.unsqueeze(2).broadcast_to([64, 64, 16])
    P.op("dve", lambda e: e.tensor_tensor(out=bt1, in0=braw[:, 0], in1=cfr_b, op=ALU.mult), ["braw", "cfr"], ["bt1"])
    P.op("dve", lambda e: e.tensor_tensor(out=bt2, in0=braw[:, 1], in1=cfi_b, op=ALU.mult), ["braw", "cfi"], ["bt2"])
    P.op("dve", lambda e: e.tensor_tensor(out=Bb[:, 0], in0=bt1, in1=bt2, op=ALU.subtract), ["bt1", "bt2"], ["Bb"])
    P.op("dve", lambda e: e.tensor_tensor(out=bt1, in0=braw[:, 0], in1=cfi_b, op=ALU.mult), ["braw", "cfi", "Bb"], ["bt1"])
    P.op("dve", lambda e: e.tensor_tensor(out=bt2, in0=braw[:, 1], in1=cfr_b, op=ALU.mult), ["braw", "cfr", "Bb"], ["bt2"])
    P.op("dve", lambda e: e.tensor_tensor(out=Bb[:, 1], in0=bt1, in1=bt2, op=ALU.add), ["bt1", "bt2"], ["Bb:i"])
    Cst = A.alloc("Cst", 1024, F32).rearrange("p (g h) -> p g h", g=64)
    for gh in range(2):
        dma(Cst[0:64, gh * 32:(gh + 1) * 32, :], c_re[gh * 32:(gh + 1) * 32].rearrange("g h p -> p g h"), writes=["Cst"], slow=True)
        dma(Cst[64:128, gh * 32:(gh + 1) * 32, :], c_im[gh * 32:(gh + 1) * 32].rearrange("g h p -> p g h"), writes=["Cst"], slow=True)
    P.op("pool", lambda e: e.tensor_scalar(out=Cst[64:128], in0=Cst[64:128], scalar1=-1.0, scalar2=None, op0=ALU.mult), ["Cst"], ["Cst"])
    S1 = A.alloc("S1", NLEV * 64, F32).rearrange("p (l g) -> p l g", l=NLEV)
    S2 = A.alloc("S2", NLEV * 64, F32).rearrange("p (l g) -> p l g", l=NLEV)
    pwkeys = ["PW:%d" % l for l in range(NLEV)] + ["PW:%d:i" % l for l in range(NLEV)]
    dma(S1[0:64], PW[:, :, 0, :], reads=pwkeys, writes=["S1"]); dma(S1[64:128], PW[:, :, 0, :], reads=pwkeys, writes=["S1"])
    dma(S2[0:64], PW[:, :, 1, :], reads=pwkeys, writes=["S2"]); dma(S2[64:128], PW[:, :, 1, :], reads=pwkeys, writes=["S2"])
    P.op("pool", lambda e: e.tensor_scalar(out=S2[64:128], in0=S2[64:128], scalar1=-1.0, scalar2=None, op0=ALU.mult), ["S2"], ["S2"])

    if debug:
        pw_dbg = dscr("pw_dbg", [64, NLEV * 128], F32)
        dma(pw_dbg, PW.rearrange("p l r g -> p (l r g)"), reads=pwkeys, writes=["pw_dbg"])
        bb_dbg = dscr("bb_dbg", [64, 2048], F32)
        dma(bb_dbg, Bb.rearrange("p r g h -> p (r g h)"), reads=["Bb", "Bb:i"], writes=["bb_dbg"])
    for n in ["braw", "bt1", "bt2", "PW", "sq_t1", "sq_t2", "sq_t3", "wr0", "wr1", "wi0", "wi1", "mag", "cr", "ci", "lm1", "den", "cfr", "cfi", "zr", "zi", "dtt", "ldt", "lre", "lim"]:
        A.free(n)
    uT = [A.alloc("uT%d" % i, SEQ, BF16) for i in range(1)]
    X0p = [A.alloc("X0p%d" % i, SEQ, BF16) for i in range(2)]
    Tp = [A.alloc("Tp%d" % i, SEQ, BF16) for i in range(2)]
    Xop = [[A.alloc("Xo%d_%d" % (p_, i), NOWN, BF16).rearrange("p (s t) -> p s t", s=16) for i in range(2)] for p_ in range(2)]
    XBp = [[A.alloc("XB%d_%d" % (p_, i), 64, BF16) for i in range(2)] for p_ in range(2)]
    ysg = A.alloc("ysg", 8 * NOWN, BF16).rearrange("p (k t) -> p k t", k=8)
    WB = [A.alloc("WB%d" % i, 128, BF16) for i in range(4)]
    WC = [A.alloc("WC%d" % i, 128, BF16) for i in range(4)]
    R = [A.alloc("R%d" % i, 128, BF16) for i in range(52)]
    Rt = [A.alloc("Rt%d" % i, 128, F32) for i in range(4)]
    bm = [A.alloc("bm%d" % i, 256, F32)[0:64, :] for i in range(2)]
    gel = [A.alloc("gel%d" % i, NOWN, F32) for i in range(2)]
    evc = [0]
    toff = [0, 4096, 6144, 7168, 7680, 7936, 8064]

    def prep_gen(fc, gl, par, st_):
        g = fc * 8 + gl
        wi_ = st_ * 2 + par
        bmt = bm[par]; bmk = "bm%d" % par
        Bfc = Bb[:, :, fc * 8:(fc + 1) * 8, :]
        gmv64 = gmask[0:64, gl * 128:(gl + 1) * 128].rearrange("p (g h) -> p g h", g=8)
        P.op("dve", lambda e: e.tensor_tensor(out=bmt[:, 0:128].rearrange("p (g h) -> p g h", g=8), in0=Bfc[:, 0], in1=gmv64, op=ALU.mult), ["Bb", "Bb:i", "gmask"], [bmk])
        P.op("dve", lambda e: e.tensor_tensor(out=bmt[:, 128:256].rearrange("p (g h) -> p g h", g=8), in0=Bfc[:, 1], in1=gmv64, op=ALU.mult), ["Bb", "Bb:i", "gmask"], [bmk + ":i"])
        pbw = 7
        P.op("pe", lambda e: e.matmul(out=bank(pbw)[:, 0:64], lhsT=bmt[:, 0:128], rhs=ident_f[0:64, 0:64], start=True, stop=True), [bmk, "ident_f"], [bkey(pbw)])
        P.op("pe", lambda e: e.matmul(out=bank(pbw)[:, 64:128], lhsT=bmt[:, 128:256], rhs=ident_f[0:64, 0:64], start=True, stop=True), [bmk + ":i", "ident_f"], [bkey(pbw)])
        P.op("act", lambda e: e.activation(out=WB[wi_], in_=bank(pbw)[:, 0:128], func=AF.Copy), [bkey(pbw)], ["WB%d" % wi_])
        P.op("pool", lambda e: e.tensor_tensor(out=WC[wi_].rearrange("p (g h) -> p g h", g=8), in0=Cst[:, fc * 8:(fc + 1) * 8, :], in1=gmask[:, gl * 128:(gl + 1) * 128].rearrange("p (g h) -> p g h", g=8), op=ALU.mult), ["Cst", "gmask"], ["WC%d" % wi_])
        yield
        for lev in range(NLEV):
            Rm = R[wi_ * 13 + lev]; Rk = "R%d" % (wi_ * 13 + lev)
            rt = Rt[(lev % 2) * 2 + par]; rtk = "Rt%d" % ((lev % 2) * 2 + par)
            P.op("act", lambda e, lev=lev, rt=rt: e.activation(out=rt, in_=swap_f, func=AF.Copy, scale=S2[:, lev, g:g + 1]), ["swap_f", "S2"], [rtk])
            P.op("dve", lambda e, Rm=Rm, lev=lev, rt=rt: e.scalar_tensor_tensor(out=Rm, in0=ident_f, scalar=S1[:, lev, g:g + 1], in1=rt, op0=ALU.mult, op1=ALU.add), ["ident_f", "S1", rtk], [Rk])
            yield

    def group_gen(fc, gl, par, u, uk, st_):
        g = fc * 8 + gl
        wi_ = st_ * 2 + par
        X0 = X0p[par]; Tb = Tp[par]; Xo = Xop[par]; XB = XBp[par]
        xn = "X0p%d" % par; tn = "Tp%d" % par
        bc = [0]

        def nextbank():
            bc[0] += 1
            return 2 * par + (bc[0] % 2)

        def evac(pb, dst_ap, dkey, n=None):
            src_ap = bank(pb) if n is None else bank(pb)[:, 0:n]
            evc[0] += 1
            if evc[0] % 3 != 0:
                P.op("act", lambda e: e.activation(out=dst_ap, in_=src_ap, func=AF.Copy), [bkey(pb)], [dkey])
            else:
                P.op("dve", lambda e: e.tensor_copy(out=dst_ap, in_=src_ap), [bkey(pb)], [dkey])

        Rg = [(R[wi_ * 13 + lev], "R%d" % (wi_ * 13 + lev)) for lev in range(NLEV)]
        for tt in range(16):
            pb = nextbank()
            P.op("pe", lambda e, pb=pb, tt=tt: e.matmul(out=bank(pb), lhsT=WB[wi_], rhs=u[:, tt * 512:(tt + 1) * 512], start=True, stop=True), ["WB%d" % wi_, uk], [bkey(pb)])
            evac(pb, X0[:, tt * 512:(tt + 1) * 512], xn + ":%d" % tt)
            if tt % 4 == 3:
                yield
        for lev in range(7):
            n_l = 4096 >> lev
            if lev == 0:
                srcv = X0.rearrange("p (i two) -> p i two", two=2); sbase = xn + ":"
            else:
                srcv = Tb[:, toff[lev - 1]:toff[lev - 1] + 2 * n_l].rearrange("p (i two) -> p i two", two=2); sbase = tn + ":%d_" % (lev - 1)
            Rm, Rk = Rg[lev]
            for c0 in range(0, n_l, 512):
                n = min(512, n_l - c0)
                pb = nextbank()
                skeys = sorted(set([sbase + "%d" % ((2 * c0) // 512), sbase + "%d" % ((2 * c0 + 2 * n - 1) // 512)]))
                P.op("pe", lambda e, pb=pb, srcv=srcv, c0=c0, n=n: e.matmul(out=bank(pb)[:, 0:n], lhsT=ident_b, rhs=srcv[:, c0:c0 + n, 1], start=True, stop=False), ["ident_b"] + skeys, [bkey(pb)])
                P.op("pe", lambda e, pb=pb, srcv=srcv, c0=c0, n=n, Rm=Rm: e.matmul(out=bank(pb)[:, 0:n], lhsT=Rm, rhs=srcv[:, c0:c0 + n, 0], start=False, stop=True), [Rk] + skeys, [bkey(pb)])
                evac(pb, Tb[:, toff[lev] + c0:toff[lev] + c0 + n], tn + ":%d_%d" % (lev, c0 // 512), n)
                if (c0 // 512) % 2 == 1:
                    yield
            yield
        xbk = tn + ":6_0"
        xb_src = Tb[:, toff[6]:toff[6] + 64]
        for m_ in range(6):
            sh = 1 << m_
            Rm, Rk = Rg[7 + m_]
            pb = nextbank()
            P.op("pe", lambda e, pb=pb, xb_src=xb_src: e.matmul(out=bank(pb)[:, 0:64], lhsT=ident_b, rhs=xb_src, start=True, stop=False), ["ident_b", xbk], [bkey(pb)])
            P.op("pe", lambda e, pb=pb, xb_src=xb_src, sh=sh, Rm=Rm: e.matmul(out=bank(pb)[:, sh:64], lhsT=Rm, rhs=xb_src[:, 0:64 - sh], start=False, stop=True), [Rk, xbk], [bkey(pb)])
            dstb = XB[m_ % 2]
            xbk = "XB%d_%d" % (par, m_ % 2)
            evac(pb, dstb, xbk, 64)
            xb_src = dstb
            yield
        x0own = X0.rearrange("p (s b t) -> p s b t", s=16, b=4)[:, :, 3, :]
        x0keys = [xn + ":%d" % t for t in range(16)]
        xo0keys = ["Xo%d_0:%d" % (par, t) for t in range(4)]
        P.op("dve", lambda e: e.tensor_copy(out=Xo[0], in_=x0own), x0keys, xo0keys)
        pb = nextbank()
        Rm, Rk = Rg[0]
        P.op("pe", lambda e, pb=pb: e.matmul(out=bank(pb)[:, 0:16], lhsT=ident_b, rhs=x0own[:, :, 0], start=True, stop=False), ["ident_b"] + x0keys, [bkey(pb)])
        P.op("pe", lambda e, pb=pb, xb_src=xb_src, Rm=Rm: e.matmul(out=bank(pb)[:, 0:16], lhsT=Rm, rhs=xb_src.rearrange("p (s b) -> p s b", b=4)[:, :, 2], start=False, stop=True), [Rk, xbk], [bkey(pb)])
        P.op("dve", lambda e, pb=pb: e.tensor_copy(out=Xo[0][:, :, 0], in_=bank(pb)[:, 0:16]), [bkey(pb)] + xo0keys, xo0keys)
        yield
        cur = 0
        for lev in range(7):
            sh = 1 << lev
            Rm, Rk = Rg[lev]
            src = Xo[cur]; dst = Xo[1 - cur]
            for q4 in range(4):
                sk = "Xo%d_%d:%d" % (par, cur, q4); dk = "Xo%d_%d:%d" % (par, 1 - cur, q4)
                pb = nextbank()
                pv = bank(pb).rearrange("p (s t) -> p s t", s=4)
                P.op("pe", lambda e, pb=pb, src=src, q4=q4: e.matmul(out=bank(pb), lhsT=ident_b, rhs=src[:, q4 * 4:(q4 + 1) * 4, :].rearrange("p s t -> p (s t)"), start=True, stop=False), ["ident_b", sk], [bkey(pb)])
                P.op("pe", lambda e, pv=pv, src=src, q4=q4, sh=sh, Rm=Rm: e.matmul(out=pv[:, :, sh:128], lhsT=Rm, rhs=src[:, q4 * 4:(q4 + 1) * 4, 0:128 - sh], start=False, stop=True), [Rk, sk], [bkey(pb)])
                evac(pb, dst[:, q4 * 4:(q4 + 1) * 4, :].rearrange("p s t -> p (s t)"), dk)
                if q4 % 2 == 1:
                    yield
            cur = 1 - cur
        fin = Xo[cur].rearrange("p s t -> p (s t)"); fkb = "Xo%d_%d" % (par, cur)
        if debug and g == 63:
            xf_dbg = dscr("xf_dbg", [128, NOWN])
            dma(xf_dbg, fin, reads=[fkb + ":%d" % t for t in range(4)], writes=["xf_dbg"])
        for ot in range(4):
            yb = 4 + (2 * par + ot) % 3
            P.op("pe", lambda e, ot=ot, yb=yb: e.matmul(out=bank(yb), lhsT=WC[wi_], rhs=fin[:, ot * 512:(ot + 1) * 512], start=True, stop=True), ["WC%d" % wi_, fkb + ":%d" % ot], [bkey(yb)])
            pbk = bkey(yb); pbb = bank(yb)
            yacc = gel[0][:, ot * 512:(ot + 1) * 512]
            if gl == 0:
                uo = u.rearrange("p (s b t) -> p s b t", s=16, b=4)[:, ot * 4:(ot + 1) * 4, 3, :]
                P.op("dve", lambda e, yacc=yacc, uo=uo, pbb=pbb: e.scalar_tensor_tensor(out=yacc.rearrange("p (s t) -> p s t", s=4), in0=uo, scalar=ssmd_pp[:, fc:fc + 1], in1=pbb.rearrange("p (s t) -> p s t", s=4), op0=ALU.mult, op1=ALU.add), [uk, "ssmd_pp", pbk], ["gel0:%d" % ot])
            else:
                P.op("dve", lambda e, yacc=yacc, pbb=pbb: e.tensor_tensor(out=yacc, in0=pbb, in1=yacc, op=ALU.add), [pbk, "gel0:%d" % ot], ["gel0:%d" % ot])
            if ot % 2 == 1:
                yield

    def run_lockstep(gens):
        alive = [True] * len(gens)
        while any(alive):
            for i_ in range(len(gens)):
                if alive[i_]:
                    try:
                        next(gens[i_])
                    except StopIteration:
                        alive[i_] = False

    run_lockstep([prep_gen(0, 0, 0, 0), prep_gen(0, 1, 1, 0)])
    for fc in range(8):
        u = uT[0]; uk = "uT0"
        dma(u, uT_d[fc], reads=["uT_d"], writes=[uk], q="pool")
        for gp in range(4):
            pk = fc * 4 + gp
            st_ = pk % 2
            gens = [group_gen(fc, 2 * gp, 0, u, uk, st_), group_gen(fc, 2 * gp + 1, 1, u, uk, st_)]
            if pk + 1 < 32:
                nfc, ngp = (pk + 1) // 4, (pk + 1) % 4
                gens.append(prep_gen(nfc, 2 * ngp, 0, 1 - st_))
                gens.append(prep_gen(nfc, 2 * ngp + 1, 1, 1 - st_))
            run_lockstep(gens)
        yk = ["gel0:%d" % ot for ot in range(4)]
        P.op("act", lambda e: e.activation(out=gel[1], in_=gel[0], func=AF.Square), yk, ["gel1"])
        P.op("dve", lambda e: e.tensor_scalar(out=gel[1], in0=gel[1], scalar1=0.044715 * 1.5957691216, scalar2=1.5957691216, op0=ALU.mult, op1=ALU.add), ["gel1"], ["gel1"])
        P.op("dve", lambda e: e.tensor_tensor(out=gel[1], in0=gel[1], in1=gel[0], op=ALU.mult), ["gel1"] + yk, ["gel1"])
        P.op("act", lambda e: e.activation(out=gel[1], in_=gel[1], func=AF.Sigmoid), ["gel1"], ["gel1"])
        P.op("dve", lambda e, fc=fc: e.tensor_tensor(out=ysg[:, fc, :], in0=gel[1], in1=gel[0], op=ALU.mult), ["gel1"] + yk, ["ysg"])
    for n in ["uT0", "X0p0", "X0p1", "Tp0", "Tp1", "Xo0_0", "Xo0_1", "Xo1_0", "Xo1_1", "XB0_0", "XB0_1", "XB1_0", "XB1_1", "WB0", "WB1", "WB2", "WB3", "WC0", "WC1", "WC2", "WC3"] + ["R%d" % i for i in range(52)] + ["Rt0", "Rt1", "Rt2", "Rt3", "bm0", "bm1",
              "gel0", "gel1", "S1", "S2", "Cst", "Bb"]:
        A.free(n)

    if debug:
        ysg_dbg = dscr("ysg_dbg", [128, 8 * NOWN])
        dma(ysg_dbg, ysg.rearrange("p k t -> p (k t)"), reads=["ysg"], writes=["ysg_dbg"])
    lq = A.alloc("lq", 256, F32).rearrange("p (a d) -> p a d", a=4)
    for a in range(4):
        dma(lq[:, a, :], lqk[a].partition_broadcast(128), writes=["lq"])
    lamt = A.alloc("lamt", 8, F32)
    lqp = A.alloc("lqp", 128, F32).rearrange("p (a d) -> p a d", a=2)
    P.op("dve", lambda e: e.tensor_tensor(out=lqp[:, 0, :], in0=lq[:, 0, :], in1=lq[:, 1, :], op=ALU.mult), ["lq"], ["lqp"])
    P.op("dve", lambda e: e.tensor_tensor(out=lqp[:, 1, :], in0=lq[:, 2, :], in1=lq[:, 3, :], op=ALU.mult), ["lq"], ["lqp"])
    P.op("dve", lambda e: e.tensor_reduce(out=lamt[:, 0:2], in_=lqp, axis=AX.X, op=ALU.add), ["lqp"], ["lamt"])
    P.op("act", lambda e: e.activation(out=lamt[:, 2:4], in_=lamt[:, 0:2], func=AF.Exp), ["lamt"], ["lamt:e"])
    P.op("dve", lambda e: e.tensor_tensor(out=lamt[:, 4:5], in0=lamt[:, 3:4], in1=lamt[:, 2:3], op=ALU.subtract), ["lamt:e"], ["lamt:d"])
    P.op("dve", lambda e: e.tensor_scalar(out=lamt[:, 5:6], in0=lamt[:, 4:5], scalar1=-0.2, scalar2=None, op0=ALU.add), ["lamt:d"], ["neglam"])
    hn = A.alloc("hn", 128, F32)
    dma(hn, head_norm.partition_broadcast(128), writes=["hn"])
    P.op("dve", lambda e: e.tensor_scalar(out=hn, in0=hn, scalar1=0.8, scalar2=None, op0=ALU.mult), ["hn"], ["hn"])
    kvf = A.alloc("kvf", 64, F32)
    dma(kvf, kvalid.rearrange("(b p) -> p b", p=128), writes=["kvf"], slow=True)

    ya = A.alloc("ya", 16 * 1024, BF16).rearrange("p (s f) -> p s f", s=16)
    Kh = [A.alloc("Kh%d" % i, 2 * SEQ, BF16)[0:64, :].rearrange("p (m t) -> p m t", m=2) for i in range(2)]
    Vh = [A.alloc("Vh%d" % i, 64 * 128, BF16).rearrange("p (b d) -> p b d", b=64) for i in range(1)]
    Qh = [A.alloc("Qh%d" % i, 2 * NOWN, BF16)[0:64, :].rearrange("p (m t) -> p m t", m=2) for i in range(1)]
    PT = [A.alloc("PT%d" % i, 1024, BF16).rearrange("p (m q) -> p m q", m=2) for i in range(2)]
    Esel = A.alloc("Esel", 256, BF16).rearrange("p (m c) -> p m c", m=2)
    Esel_f = A.alloc("Esel_f", 256, F32)
    dma(Esel_f, esel_d, writes=["Esel_f"])
    P.op("pool", lambda e: e.tensor_copy(out=Esel.rearrange("p m c -> p (m c)"), in_=Esel_f), ["Esel_f"], ["Esel"])
    denrow = [A.alloc("denrow%d" % i, 512, F32) for i in range(2)]
    rcol = [A.alloc("rcol%d" % i, 128, F32) for i in range(2)]
    OT = [A.alloc("OT%d" % i, 1024, BF16).rearrange("p (m q) -> p m q", m=2) for i in range(2)]
    ones_f = A.alloc("ones_f", 1, F32)
    P.op("pool", lambda e: e.memset(ones_f, 1.0), [], ["ones_f"])
    ep = [A.alloc("ep%d" % i, 8, F32) for i in range(2)]
    eo = [A.alloc("eo%d" % i, 384, F32) for i in range(2)]
    v_dh = v_d.rearrange("(b p) (h d) -> p b h d", p=128, h=8)
    actr = [0]
    tpv = bank_bf(7).rearrange("p (i m d) -> p i m d", i=4, m=2)
    dcol = bank(6)[:, 0:128]
    for h in range(8):
        K = Kh[h % 2]; V = Vh[0]; Q = Qh[0]
        kk_ = "Kh%d" % (h % 2)
        for m in range(2):
            dma(K[:, m, :], kT_d[h, m * 64:(m + 1) * 64, :], reads=["kT_d"], writes=[kk_], q="sp")
            dma(Q[:, m, :], qT_d[h, m * 64:(m + 1) * 64, :], reads=["qT_d"], writes=["Qh0"], q="sp")
        for vq in range(4):
            dma(V[:, vq * 16:(vq + 1) * 16, :], v_dh[:, vq * 16:(vq + 1) * 16, h, :], reads=["v_d"], writes=["Vh0"], q="sp")
        for G in range(4):
            gj = (h * 4 + G) % 2
            dr = denrow[gj]; drk = "denrow%d" % gj
            ot_ = OT[gj]; otk = "OT%d" % gj
            nkb = 16 * G + 16
            base_i = actr[0]
            actr[0] += nkb

            def emit_scores(kb, G=G, K=K, kk_=kk_, base_i=base_i):
                rel_ = kb - 16 * G - 3
                i0_ = 0 if rel_ <= 0 else (rel_ + 3) // 4
                c0 = i0_ * 128
                idiag = rel_ // 4 if (rel_ >= 0 and rel_ % 4 == 0) else -1
                pj = (base_i + kb) % 2
                pt = PT[pj]; ptk = "PT%d" % pj
                for m in range(2):
                    pb = 2 * pj + m
                    P.op("pe", lambda e, pb=pb, kb=kb, m=m, c0=c0: e.matmul(out=bank(pb)[:, c0:512], lhsT=K[:, m, kb * 128:(kb + 1) * 128], rhs=Q[:, m, G * 512 + c0:(G + 1) * 512], start=True, stop=True), [kk_, "Qh0"], [bkey(pb)])
                    P.op("act", lambda e, pb=pb, pt=pt, m=m, c0=c0: e.activation(out=pt[:, m, c0:512], in_=bank(pb)[:, c0:512], func=AF.Exp, scale=0.125), [bkey(pb)], [ptk + ":%d" % m])
                    if idiag >= 0:
                        P.op("dve", lambda e, pt=pt, m=m, idiag=idiag: e.memset(pt[64:128, m, idiag * 128:idiag * 128 + 64], 0.0), [ptk + ":%d" % m], [ptk + ":%d" % m])
                    if kb < 3:
                        P.op("dve", lambda e, pt=pt, m=m, kb=kb: e.tensor_scalar(out=pt[:, m, :], in0=pt[:, m, :], scalar1=kvf[:, kb:kb + 1], scalar2=None, op0=ALU.mult), [ptk + ":%d" % m, "kvf"], [ptk + ":%d" % m])

            def emit_pv(kb, G=G, V=V, nkb=nkb, base_i=base_i):
                rel_ = kb - 16 * G - 3
                i0_ = 0 if rel_ <= 0 else (rel_ + 3) // 4
                c0 = i0_ * 128
                pj = (base_i + kb) % 2
                pt = PT[pj]; ptk = "PT%d" % pj
                for m in range(2):
                    P.op("pe", lambda e, kb=kb, m=m, pt=pt, c0=c0: e.matmul(out=bank(4 + m)[:, c0:512], lhsT=V[:, kb, :], rhs=pt[:, m, c0:512], start=(kb == 0), stop=(kb == nkb - 1)), [ptk + ":%d" % m, "Vh0"], [bkey(4 + m)])
                    P.op("pe", lambda e, kb=kb, m=m, pt=pt, c0=c0: e.matmul(out=bank(6)[:, c0:512], lhsT=Esel[:, m, :], rhs=pt[:, m, c0:512], start=(kb == 0 and m == 0), stop=(kb == nkb - 1 and m == 1)), [ptk + ":%d" % m, "Esel"], [bkey(6)])

            emit_scores(0)
            for kb in range(nkb):
                if kb + 1 < nkb:
                    emit_scores(kb + 1)
                emit_pv(kb)
            P.op("act", lambda e, ot_=ot_: e.activation(out=ot_[:, 0, :], in_=bank(4), func=AF.Copy), [bkey(4)], [otk + ":0"])
            P.op("dve", lambda e, ot_=ot_: e.tensor_copy(out=ot_[:, 1, :], in_=bank(5)), [bkey(5)], [otk + ":1"])
            P.op("dve", lambda e, dr=dr: e.tensor_copy(out=dr, in_=bank(6)), [bkey(6)], [drk])
            for isl in range(4):
                P.op("pe", lambda e, isl=isl, dr=dr: e.matmul(out=dcol[:, isl * 32:(isl + 1) * 32], lhsT=dr[:, isl * 128:(isl + 1) * 128], rhs=ident_f[:, 0:32], start=True, stop=True), [drk, "ident_f"], [bkey(6)])
            rc = rcol[gj]; rck = "rcol%d" % gj
            P.op("dve", lambda e, rc=rc: e.reciprocal(out=rc, in_=dcol), [bkey(6)], [rck])
            for isl in range(4):
                for m in range(2):
                    P.op("pe", lambda e, isl=isl, m=m, ot_=ot_: e.transpose(out=tpv[:, isl, m, :], in_=ot_[:, m, isl * 128:(isl + 1) * 128], identity=ident_b), [otk + ":%d" % m, "ident_b"], [bkey(7)])
            for isl in range(4):
                s_ = G * 4 + isl
                j = (h * 16 + s_) % 2
                e_ = ep[j]; o_ = eo[j]; ek = "ep%d" % j; ok_ = "eo%d" % j
                P.op("dve", lambda e, e_=e_, isl=isl, rc=rc: e.tensor_copy(out=e_[:, 0:2], in_=rc[:, isl * 32:isl * 32 + 2]), [rck], [ek])
                P.op("dve", lambda e, e_=e_: e.tensor_tensor(out=e_[:, 2:3], in0=e_[:, 1:2], in1=lamt[:, 5:6], op=ALU.mult), [ek, "neglam"], [ek + ":2"])
                P.op("dve", lambda e, e_=e_, o_=o_, isl=isl: e.tensor_scalar(out=o_[:, 0:128], in0=tpv[:, isl, 1, :], scalar1=e_[:, 2:3], scalar2=None, op0=ALU.mult), [bkey(7), ek + ":2"], [ok_])
                P.op("dve", lambda e, e_=e_, o_=o_, isl=isl: e.scalar_tensor_tensor(out=o_[:, 128:256], in0=tpv[:, isl, 0, :], scalar=e_[:, 0:1], in1=o_[:, 0:128], op0=ALU.mult, op1=ALU.add), [bkey(7), ek, ok_], [ok_ + ":1"])
                P.op("act", lambda e, e_=e_, o_=o_: e.activation(out=o_[:, 256:384], in_=o_[:, 128:256], func=AF.Square, accum_out=e_[:, 3:4]), [ok_ + ":1"], [ok_ + ":2", ek + ":3"])
                P.op("act", lambda e, e_=e_: e.activation(out=e_[:, 4:5], in_=e_[:, 3:4], func=AF.Ln, scale=1.0 / 128, bias=epst), [ek + ":3", "epst"], [ek + ":4"])
                P.op("act", lambda e, e_=e_: e.activation(out=e_[:, 5:6], in_=e_[:, 4:5], func=AF.Exp, scale=-0.5), [ek + ":4"], [ek + ":5"])
                P.op("dve", lambda e, e_=e_, o_=o_, s_=s_, h=h: e.scalar_tensor_tensor(out=ya[:, s_, h * 128:(h + 1) * 128], in0=o_[:, 128:256], scalar=e_[:, 5:6], in1=hn, op0=ALU.mult, op1=ALU.mult), [ok_ + ":1", ek + ":5", "hn"], ["ya"])
    for n in ["Kh0", "Kh1", "Vh0", "Qh0", "PT0", "PT1", "denrow0", "denrow1", "rcol0", "rcol1", "Esel_f", "OT0", "OT1", "ep0", "ep1", "eo0", "eo1", "lq", "lqp", "kvf"]:
        A.free(n)

    if debug:
        ya_dbg = dscr("ya_dbg", [128, 16 * 1024])
        dma(ya_dbg, ya.rearrange("p s f -> p (s f)"), reads=["ya"], writes=["ya_dbg"])
    def load_bf16(name):
        dst_d, src_, K_, N_, gn_ = wsc[name]
        KC = K_ // 128
        wt = A.alloc(name, KC * N_, BF16).rearrange("p (k n) -> p k n", k=KC)
        for kc in range(KC):
            dma(wt[:, kc, :], dst_d[kc * 128:(kc + 1) * 128, :], reads=[name + "_d"], writes=[name], q="sp" if kc % 2 == 0 else "act")
        return wt

    gpost = A.alloc("gpost", 1024, F32)
    pst = A.alloc("pst", 8, F32)
    psq = A.alloc("psq", 1024, BF16)
    ptmp = A.alloc("ptmp", 1024, F32)
    ost = [A.alloc("ost%d" % i, 1024, F32) for i in range(2)]
    xres = [A.alloc("xres%d" % i, 1024, F32) for i in range(2)]

    def post_norm_residual(pb0, gain_bc, gkey, res_in, res_in_keys, res_out, res_out_key):
        for half in range(2):
            P.op("act", lambda e, half=half: e.activation(out=psq[:, half * 512:(half + 1) * 512], in_=bank(pb0 + half), func=AF.Square, accum_out=pst[:, half:half + 1]), [bkey(pb0 + half)], ["psq", "pst:%d" % half])
        P.op("dve", lambda e: e.tensor_tensor(out=pst[:, 2:3], in0=pst[:, 0:1], in1=pst[:, 1:2], op=ALU.add), ["pst:0", "pst:1"], ["pst:2"])
        P.op("act", lambda e: e.activation(out=pst[:, 3:4], in_=pst[:, 2:3], func=AF.Ln, scale=1.0 / D, bias=epst), ["pst:2", "epst"], ["pst:3"])
        P.op("act", lambda e: e.activation(out=pst[:, 4:5], in_=pst[:, 3:4], func=AF.Exp, scale=-0.5), ["pst:3"], ["pst:4"])
        for half in range(2):
            P.op("dve", lambda e, half=half: e.scalar_tensor_tensor(out=ptmp[:, half * 512:(half + 1) * 512], in0=bank(pb0 + half), scalar=pst[:, 4:5], in1=gain_bc[:, half * 512:(half + 1) * 512], op0=ALU.mult, op1=ALU.mult), [bkey(pb0 + half), "pst:4", gkey], ["ptmp:%d" % half])
        P.op("dve", lambda e: e.tensor_tensor(out=res_out, in0=ptmp, in1=res_in, op=ALU.add), ["ptmp:0", "ptmp:1"] + res_in_keys, [res_out_key])

    wglu = load_bf16("wglu")
    wssm = load_bf16("wssm")
    ys2 = A.alloc("ys2", 8 * 512, BF16).rearrange("p (k t) -> p k t", k=8)
    gab = A.alloc("gab", 8 * 512, BF16).rearrange("p (k t) -> p k t", k=8)
    sg = [A.alloc("sg%d" % i, 512, BF16) for i in range(2)]
    for tt in range(4):
        dma(gab, g_d[0:8, :, tt * 512:(tt + 1) * 512].rearrange("k p t -> p k t"), reads=["g_d"], writes=["gab"], q="sp")
        for mc in range(8):
            pb = 2 + mc % 2
            for kc in range(8):
                P.op("pe", lambda e, kc=kc, mc=mc, pb=pb, tt=tt: e.matmul(out=bank(pb), lhsT=wglu[:, kc, mc * 128:(mc + 1) * 128], rhs=ysg[:, kc, tt * 512:(tt + 1) * 512], start=(kc == 0), stop=(kc == 7)), ["wglu", "ysg"], [bkey(pb)])
            j = mc % 2
            P.op("act", lambda e, pb=pb, j=j, mc=mc: e.activation(out=sg[j], in_=bank(pb), func=AF.Sigmoid, bias=bglu_pp[:, mc:mc + 1]), [bkey(pb), "bglu_pp"], ["sg%d" % j])
            P.op("dve", lambda e, j=j, mc=mc, tt=tt: e.tensor_tensor(out=ys2[:, mc, :], in0=sg[j], in1=ysg[:, mc, tt * 512:(tt + 1) * 512], op=ALU.mult), ["sg%d" % j, "ysg"], ["ys2"])
        for mc in range(8):
            pa = 4 + (mc % 2)
            for kc in range(8):
                P.op("pe", lambda e, kc=kc, mc=mc, pa=pa: e.matmul(out=bank(pa), lhsT=wssm[:, kc, mc * 128:(mc + 1) * 128], rhs=ys2[:, kc, :], start=(kc == 0), stop=(kc == 7)), ["wssm", "ys2"], [bkey(pa)])
            P.op("dve", lambda e, pa=pa, mc=mc, tt=tt: e.tensor_tensor(out=ysg[:, mc, tt * 512:(tt + 1) * 512], in0=bank(pa), in1=gab[:, mc, :], op=ALU.mult), [bkey(pa), "gab"], ["ysg"])
    for n in ["wglu", "wssm", "ys2", "sg0", "sg1"]:
        A.free(n)
    wda = load_bf16("wda")
    wmix = load_bf16("wmix")
    dma(gpost, gains["norm_mix_post"].partition_broadcast(128), writes=["gpost"])
    yaT = A.alloc("yaT", 8 * 512, BF16).rearrange("p (k t) -> p k t", k=8)
    mrg = A.alloc("mrg", 8 * 512, BF16).rearrange("p (k t) -> p k t", k=8)
    tb = [A.alloc("tb%d" % i, 512, F32) for i in range(2)]
    for tt in range(4):
        for bl in range(4):
            s = tt * 4 + bl
            tpb = bl % 2
            tp = bank_bf(tpb).rearrange("p (k t) -> p k t", k=8)
            for kc in range(8):
                P.op("pe", lambda e, kc=kc, s=s, tp=tp: e.transpose(out=tp[:, kc, :], in_=ya[:, s, kc * 128:(kc + 1) * 128], identity=ident_b), ["ya", "ident_b"], [bkey(tpb)])
            P.op("act", lambda e, tp=tp, bl=bl: e.activation(out=yaT[:, :, bl * 128:(bl + 1) * 128], in_=tp, func=AF.Copy), [bkey(tpb)], ["yaT"])
        dma(gab, g_d[8:16, :, tt * 512:(tt + 1) * 512].rearrange("k p t -> p k t"), reads=["g_d"], writes=["gab"], q="sp")
        for mc in range(8):
            pbb_ = 4 + (mc % 2)
            for kc in range(8):
                P.op("pe", lambda e, kc=kc, mc=mc, pbb_=pbb_: e.matmul(out=bank(pbb_), lhsT=wda[:, kc, mc * 128:(mc + 1) * 128], rhs=yaT[:, kc, :], start=(kc == 0), stop=(kc == 7)), ["wda", "yaT"], [bkey(pbb_)])
            j = mc % 2
            P.op("dve", lambda e, pbb_=pbb_, j=j, mc=mc: e.tensor_tensor(out=tb[j], in0=bank(pbb_), in1=gab[:, mc, :], op=ALU.mult), [bkey(pbb_), "gab"], ["tb%d" % j])
            P.op("dve", lambda e, j=j, mc=mc, tt=tt: e.tensor_tensor(out=mrg[:, mc, :], in0=tb[j], in1=ysg[:, mc, tt * 512:(tt + 1) * 512], op=ALU.add), ["tb%d" % j, "ysg"], ["mrg"])
        for bl in range(4):
            s = tt * 4 + bl
            pb0 = 2 * (bl % 2)
            for half in range(2):
                for kc in range(8):
                    P.op("pe", lambda e, kc=kc, bl=bl, half=half, pb0=pb0: e.matmul(out=bank(pb0 + half), lhsT=mrg[:, kc, bl * 128:(bl + 1) * 128], rhs=wmix[:, kc, half * 512:(half + 1) * 512], start=(kc == 0), stop=(kc == 7)), ["mrg", "wmix"], [bkey(pb0 + half)])
            xr = xres[s % 2]; xrk = "xres%d" % (s % 2)
            dma(xr, xown[s * 128:(s + 1) * 128, :], writes=[xrk], q="sp")
            o_ = ost[s % 2]; ok_ = "ost%d" % (s % 2)
            post_norm_residual(pb0, gpost, "gpost", xr, [xrk], o_, ok_)
            dma(x1_d[s * 128:(s + 1) * 128, :], o_, reads=[ok_], writes=["x1_d"], q="sp")
    for n in ["wda", "wmix", "yaT", "mrg", "gab", "tb0", "tb1", "ya", "ysg"]:
        A.free(n)

    wxkv = load_bf16("wxkv")
    memT = A.alloc("memT", 8 * 256, BF16).rearrange("p (k t) -> p k t", k=8)
    for mb in range(2):
        norm_block_T(mem[mb * 128:(mb + 1) * 128, :], True, memT[:, :, mb * 128:(mb + 1) * 128], "memT")
    mkT = A.alloc("mkT", 8 * 256, BF16).rearrange("p (k t) -> p k t", k=8)
    mv = A.alloc("mv", 2 * 1024, BF16).rearrange("p (m f) -> p m f", m=2)
    for mc in range(8):
        pb = mc % 2
        for kc in range(8):
            P.op("pe", lambda e, kc=kc, mc=mc, pb=pb: e.matmul(out=bank(pb)[:, 0:256], lhsT=wxkv[:, kc, mc * 128:(mc + 1) * 128], rhs=memT[:, kc, :], start=(kc == 0), stop=(kc == 7)), ["wxkv", "memT"], [bkey(pb)])
        P.op("act", lambda e, pb=pb, mc=mc: e.activation(out=mkT[:, mc, :], in_=bank(pb)[:, 0:256], func=AF.Copy), [bkey(pb)], ["mkT"])
    for mt in range(2):
        for half in range(2):
            pb = 2 + half
            for kc in range(8):
                P.op("pe", lambda e, kc=kc, mt=mt, half=half, pb=pb: e.matmul(out=bank(pb), lhsT=memT[:, kc, mt * 128:(mt + 1) * 128], rhs=wxkv[:, kc, 1024 + half * 512:1024 + (half + 1) * 512], start=(kc == 0), stop=(kc == 7)), ["wxkv", "memT"], [bkey(pb)])
            P.op("act", lambda e, pb=pb, mt=mt, half=half: e.activation(out=mv[:, mt, half * 512:(half + 1) * 512], in_=bank(pb), func=AF.Copy), [bkey(pb)], ["mv"])
    A.free("wxkv"); A.free("memT")
    wxq = load_bf16("wxq")
    wxo = load_bf16("wxo")
    dma(gpost, gains["norm_x_post"].partition_broadcast(128), writes=["gpost"])
    ones_b = A.alloc("ones_b", 128, BF16)
    P.op("pool", lambda e: e.memset(ones_b, 1.0), [], ["ones_b"])
    h2T = A.alloc("h2T", 8 * 512, BF16).rearrange("p (k t) -> p k t", k=8)
    xqT = A.alloc("xqT", 8 * 512, BF16).rearrange("p (k t) -> p k t", k=8)
    xoT = A.alloc("xoT", 8 * 512, BF16).rearrange("p (k t) -> p k t", k=8)
    xp = [A.alloc("xp%d" % i, 2 * 512, BF16).rearrange("p (m t) -> p m t", m=2) for i in range(2)]
    rden = [A.alloc("rden%d" % i, 512, F32) for i in range(2)]
    for tt in range(4):
        for bl in range(4):
            s = tt * 4 + bl
            norm_block_T(x1_d[s * 128:(s + 1) * 128, :], True, h2T[:, :, bl * 128:(bl + 1) * 128], "h2T", tp_bank=7)
        for mc in range(8):
            pb = mc % 2
            for kc in range(8):
                P.op("pe", lambda e, kc=kc, mc=mc, pb=pb: e.matmul(out=bank(pb), lhsT=wxq[:, kc, mc * 128:(mc + 1) * 128], rhs=h2T[:, kc, :], start=(kc == 0), stop=(kc == 7)), ["wxq", "h2T"], [bkey(pb)])
            P.op("act", lambda e, pb=pb, mc=mc: e.activation(out=xqT[:, mc, :], in_=bank(pb), func=AF.Copy), [bkey(pb)], ["xqT"])
        for hh in range(4):
            j = hh % 2
            for mt in range(2):
                pb = 2 + mt
                for dc in range(2):
                    P.op("pe", lambda e, hh=hh, mt=mt, dc=dc, pb=pb: e.matmul(out=bank(pb), lhsT=mkT[:, hh * 2 + dc, mt * 128:(mt + 1) * 128], rhs=xqT[:, hh * 2 + dc, :], start=(dc == 0), stop=(dc == 1)), ["mkT", "xqT"], [bkey(pb)])
                P.op("act", lambda e, pb=pb, j=j, mt=mt: e.activation(out=xp[j][:, mt, :], in_=bank(pb), func=AF.Exp, scale=1.0 / 16), [bkey(pb)], ["xp%d" % j])
            for mt in range(2):
                P.op("pe", lambda e, j=j, mt=mt: e.matmul(out=bank(4), lhsT=ones_b, rhs=xp[j][:, mt, :], start=(mt == 0), stop=(mt == 1)), ["ones_b", "xp%d" % j], [bkey(4)])
            P.op("dve", lambda e, j=j: e.reciprocal(out=rden[j], in_=bank(4)), [bkey(4)], ["rden%d" % j])
            for dc in range(2):
                pb = 5 + dc
                for mt in range(2):
                    P.op("pe", lambda e, hh=hh, j=j, mt=mt, dc=dc, pb=pb: e.matmul(out=bank(pb), lhsT=mv[:, mt, (hh * 2 + dc) * 128:(hh * 2 + dc + 1) * 128], rhs=xp[j][:, mt, :], start=(mt == 0), stop=(mt == 1)), ["mv", "xp%d" % j], [bkey(pb)])
                P.op("dve", lambda e, hh=hh, j=j, dc=dc, pb=pb: e.tensor_tensor(out=xoT[:, hh * 2 + dc, :], in0=bank(pb), in1=rden[j], op=ALU.mult), [bkey(pb), "rden%d" % j], ["xoT"])
        for bl in range(4):
            s = tt * 4 + bl
            pb0 = 2 * (bl % 2)
            for half in range(2):
                for kc in range(8):
                    P.op("pe", lambda e, kc=kc, bl=bl, half=half, pb0=pb0: e.matmul(out=bank(pb0 + half), lhsT=xoT[:, kc, bl * 128:(bl + 1) * 128], rhs=wxo[:, kc, half * 512:(half + 1) * 512], start=(kc == 0), stop=(kc == 7)), ["xoT", "wxo"], [bkey(pb0 + half)])
            xr = xres[s % 2]; xrk = "xres%d" % (s % 2)
            dma(xr, x1_d[s * 128:(s + 1) * 128, :], reads=["x1_d"], writes=[xrk], q="sp")
            o_ = ost[s % 2]; ok_ = "ost%d" % (s % 2)
            post_norm_residual(pb0, gpost, "gpost", xr, [xrk], o_, ok_)
            dma(x2_d[s * 128:(s + 1) * 128, :], o_, reads=[ok_], writes=["x2_d"], q="sp")
    for n in ["wxq", "wxo", "mkT", "mv", "h2T", "xqT", "xoT", "xp0", "xp1", "rden0", "rden1"]:
        A.free(n)

    dma(gpost, gains["norm_ff_post"].partition_broadcast(128), writes=["gpost"])
    for n in ["wstage0", "wstage1", "xs0", "xs1", "nsq0", "nsq1", "nh0", "nh1"]:
        if n in A.live:
            A.free(n)
    f1 = A.alloc("f1", 32 * 512, BF16).rearrange("p (k t) -> p k t", k=32)
    wq1 = [A.alloc("wq1_%d" % i, 8 * 1024, BF16).rearrange("p (k n) -> p k n", k=8) for i in range(2)]
    h3T = A.alloc("h3T", 8 * 512, BF16).rearrange("p (k t) -> p k t", k=8)
    fr = [A.alloc("fr%d" % i, 512, BF16) for i in range(2)]
    wctr2 = [0]
    for tt in range(4):
        for bl in range(4):
            s = tt * 4 + bl
            norm_block_T(x2_d[s * 128:(s + 1) * 128, :], True, h3T[:, :, bl * 128:(bl + 1) * 128], "h3T", tp_bank=7)
        for q4 in range(4):
            i = wctr2[0]; wctr2[0] += 1
            w1 = wq1[i % 2]; w1k = "wq1_%d" % (i % 2)
            dma(w1, wf1_d[:, q4 * 1024:(q4 + 1) * 1024].rearrange("(k p) n -> p k n", p=128), reads=["wf_d"], writes=[w1k], q="sp")
            for fl in range(8):
                fc = q4 * 8 + fl
                pb = 4 + fc % 2
                for kc in range(8):
                    P.op("pe", lambda e, kc=kc, fl=fl, pb=pb, w1=w1: e.matmul(out=bank(pb), lhsT=w1[:, kc, fl * 128:(fl + 1) * 128], rhs=h3T[:, kc, :], start=(kc == 0), stop=(kc == 7)), [w1k, "h3T"], [bkey(pb)])
                j = fc % 2
                P.op("act", lambda e, pb=pb, j=j: e.activation(out=fr[j], in_=bank(pb), func=AF.Relu), [bkey(pb)], ["fr%d" % j])
                P.op("dve", lambda e, j=j, fc=fc: e.tensor_tensor(out=f1[:, fc, :], in0=fr[j], in1=fr[j], op=ALU.mult), ["fr%d" % j], ["f1"])
        for q4 in range(4):
            i = wctr2[0]; wctr2[0] += 1
            w2 = wq1[i % 2]; w2k = "wq1_%d" % (i % 2)
            dma(w2, wf2_d[q4 * 1024:(q4 + 1) * 1024, :].rearrange("(k p) n -> p k n", p=128), reads=["wf_d"], writes=[w2k], q="sp")
            for bl in range(4):
                for half in range(2):
                    pbk_ = bl * 2 + half
                    for kcl in range(8):
                        P.op("pe", lambda e, kcl=kcl, bl=bl, half=half, pbk_=pbk_, q4=q4, w2=w2: e.matmul(out=bank(pbk_), lhsT=f1[:, q4 * 8 + kcl, bl * 128:(bl + 1) * 128], rhs=w2[:, kcl, half * 512:(half + 1) * 512], start=(q4 == 0 and kcl == 0), stop=(q4 == 3 and kcl == 7)), ["f1", w2k], [bkey(pbk_)])
        for bl in range(4):
            s = tt * 4 + bl
            pb0 = 2 * bl
            xr = xres[s % 2]; xrk = "xres%d" % (s % 2)
            dma(xr, x2_d[s * 128:(s + 1) * 128, :], reads=["x2_d"], writes=[xrk], q="sp")
            o_ = ost[s % 2]; ok_ = "ost%d" % (s % 2)
            post_norm_residual(pb0, gpost, "gpost", xr, [xrk], o_, ok_)
            dma(out_d[s * 128:(s + 1) * 128, :], o_, reads=[ok_], writes=["out_d"], q="sp")

    P.emit()
    es.close()
    return nc


def _rope_tables(pos):
    inv = (10000.0 ** (-np.arange(0, 64, 2, dtype=np.float32) / 64)).astype(np.float32)
    ang = pos.astype(np.float32)[:, None] * inv[None, :]
    c = np.cos(ang).astype(np.float32).T
    s = np.sin(ang).astype(np.float32).T
    return np.ascontiguousarray(np.tile(c, (4, 1))), np.ascontiguousarray(np.tile(s, (4, 1)))


_NC_CACHE = {}


def make_in_maps(inputs):
    x = np.asarray(inputs["x"], dtype=np.float32)
    memv = np.asarray(inputs["mem"], dtype=np.float32)
    ident = np.eye(128, dtype=np.float32)
    swapm = np.zeros((128, 128), np.float32)
    for p in range(64):
        swapm[p, 64 + p] = 1.0
        swapm[64 + p, p] = 1.0
    gmask = np.zeros((128, 8, 128), np.float32)
    for gl in range(8):
        gmask[:, gl, gl * 16:(gl + 1) * 16] = 1.0
    esel = np.zeros((128, 256), np.float32)
    esel[:, 0] = 1.0
    esel[:, 129] = 1.0
    shared = {}
    for k, v in inputs.items():
        if k in ("x", "mem"):
            continue
        a = np.asarray(v, dtype=np.float32)
        shared[k] = np.ascontiguousarray(a[0])
    in_maps = []
    for c in range(8):
        b, j = c // 4, c % 4
        pad = (3 - j) * 128
        xs = np.zeros((SEQ, D), np.float32)
        xs[pad:] = x[b, :SEQ - pad]
        own_blocks = [4 * s + j for s in range(16)]
        xo = np.concatenate([x[b, r * 128:(r + 1) * 128] for r in own_blocks], axis=0)
        pos_seq = np.arange(SEQ) - pad
        cseq, sseq = _rope_tables(pos_seq)
        pos_own = np.concatenate([np.arange(r * 128, (r + 1) * 128) for r in own_blocks])
        cown, sown = _rope_tables(pos_own)
        kval = (pos_seq >= 0).astype(np.float32)
        m = dict(shared)
        m.update(xseq=xs, xown=np.ascontiguousarray(xo), mem=np.ascontiguousarray(memv[b]), cos_seq=cseq, sin_seq=sseq,
                 cos_own=cown, sin_own=sown, kvalid=kval, esel=esel, ident=ident, swapm=swapm, gmask=gmask)
        in_maps.append(m)
    return in_maps


def kernel(**inputs):
    if "nc" not in _NC_CACHE:
        _NC_CACHE["nc"] = build_program()
    nc = _NC_CACHE["nc"]
    in_maps = make_in_maps(inputs)
    res = run_bass_kernel_spmd(nc, in_maps, core_ids=list(range(8)))
    out = np.zeros((2, SEQ, D), np.float32)
    for c in range(8):
        b, j = c // 4, c % 4
        o = res.results[c]["out"]
        for s in range(16):
            r = 4 * s + j
            out[b, r * 128:(r + 1) * 128] = o[s * 128:(s + 1) * 128]
    return out
```

```python
import contextlib
import math
import numpy as np
import concourse.bass as bass
import concourse.mybir as mybir
from concourse.bass_utils import run_bass_kernel_spmd

F32 = mybir.dt.float32
BF16 = mybir.dt.bfloat16
ALU = mybir.AluOpType
AF = mybir.ActivationFunctionType
AX = mybir.AxisListType

D = 1024
SEQ = 8192
NB = 64
NOWN = 2048
EPS = 1e-6
NLEV = 13
ENGS = ["pe", "act", "dve", "pool", "sp"]


class Op:
    __slots__ = ("eng", "idx", "fn", "deps", "is_dma", "needs_inc", "semval", "dsem", "dval")

    def __init__(self, eng, idx, fn, is_dma):
        self.eng, self.idx, self.fn, self.is_dma = eng, idx, fn, is_dma
        self.deps = []
        self.needs_inc = False
        self.semval = None
        self.dsem = None
        self.dval = None


class Prog:
    def __init__(self, nc, n_dma_sems=16):
        self.nc = nc
        self.ops = {e: [] for e in ENGS}
        self.state = {}
        self.rings = {"sp": (0, 12), "pool": (12, 8), "act": (20, 8), "dve": (28, 2), "pe": (28, 2)}
        n_dma_sems = 30
        self.n_dma_sems = n_dma_sems
        self.dma_rr = {q: 0 for q in self.rings}
        self.dma_counts = [0] * n_dma_sems
        self.waited = {}
        self.waited_dma = {}

    def _st(self, key):
        s = self.state.get(key)
        if s is None:
            s = {"w": {}, "r": {}}
            if isinstance(key, str) and ":" in key:
                base = self.state.get(key.split(":")[0])
                if base is not None:
                    s["w"] = dict(base["w"])
            self.state[key] = s
        return s

    def _add_dep(self, op, dep):
        if dep is None or dep is op:
            return
        if dep.is_dma:
            k = (op.eng, dep.dsem)
            if self.waited_dma.get(k, -1) >= dep.dval:
                return
            self.waited_dma[k] = dep.dval
            op.deps.append(dep)
            return
        k = (op.eng, dep.eng)
        if self.waited.get(k, -1) >= dep.idx:
            return
        self.waited[k] = dep.idx
        dep.needs_inc = True
        op.deps.append(dep)

    def op(self, eng, fn, reads=(), writes=(), dma=False):
        lst = self.ops[eng]
        o = Op(eng, len(lst), fn, dma)
        if dma:
            base, cnt_ = self.rings[eng]
            i = base + self.dma_rr[eng]
            self.dma_rr[eng] = (self.dma_rr[eng] + 1) % cnt_
            self.dma_counts[i] += 16
            o.dsem, o.dval = i, self.dma_counts[i]
        for key in reads:
            for e, w in self._st(key)["w"].items():
                if (not w.is_dma) and w.eng == eng and eng == "pe":
                    continue
                self._add_dep(o, w)
        for key in writes:
            s = self._st(key)
            for e, r in s["r"].items():
                if (not r.is_dma) and r.eng == eng and not dma:
                    continue
                self._add_dep(o, r)
            for e, w in s["w"].items():
                if (not w.is_dma) and w.eng == eng and not dma:
                    continue
                self._add_dep(o, w)
        me = ("dma%d" % o.dsem) if dma else eng
        for key in reads:
            self._st(key)["r"][me] = o
        for key in writes:
            s = self._st(key)
            s["w"][me] = o
            s["r"] = {}
        lst.append(o)
        return o

    def alias(self, newkey, oldkeys):
        ns = self._st(newkey)
        for ok in oldkeys:
            os_ = self.state.get(ok)
            if os_ is None:
                continue
            for kind in ("w", "r"):
                for e, o in os_[kind].items():
                    cur = ns["w"].get(e)
                    if cur is None or (o.is_dma and o.dval > cur.dval) or ((not o.is_dma) and o.idx > cur.idx):
                        ns["w"][e] = o

    def emit(self):
        nc = self.nc
        with contextlib.ExitStack() as es:
            sems = {e: es.enter_context(nc.semaphore("s_" + e)) for e in ["pe", "act", "dve", "pool"]}
            dsems = [es.enter_context(nc.semaphore("d%d" % i)) for i in range(self.n_dma_sems)]
            for e in ENGS:
                c = 0
                for o in self.ops[e]:
                    if (not o.is_dma) and o.needs_inc:
                        c += 1
                        o.semval = c
            block = es.enter_context(nc.Block())

            def run(name, eng):
                for o in self.ops[name]:
                    for d in o.deps:
                        if d.is_dma:
                            eng.wait_ge(dsems[d.dsem], d.dval)
                        else:
                            eng.wait_ge(sems[d.eng], d.semval)
                    ins = o.fn(eng)
                    if o.is_dma:
                        ins.then_inc(dsems[o.dsem], 16)
                    elif o.needs_inc:
                        ins.then_inc(sems[o.eng], 1)

            @block.tensor
            def _(eng):
                run("pe", eng)

            @block.scalar
            def _(eng):
                run("act", eng)

            @block.vector
            def _(eng):
                run("dve", eng)

            @block.gpsimd
            def _(eng):
                run("pool", eng)

            @block.sync
            def _(eng):
                run("sp", eng)
                for i in range(self.n_dma_sems):
                    if self.dma_counts[i] > 0:
                        eng.wait_ge(dsems[i], self.dma_counts[i])


class Arena:
    def __init__(self, P, base_ap, nbytes):
        self.P = P
        self.base = base_ap
        self.nbytes = nbytes
        self.live = {}
        self.freed = []

    def alloc(self, name, nelem, dt):
        esz = 4 if dt == F32 else 2
        size = (nelem * esz + 63) // 64 * 64
        segs = sorted(self.live.values())
        off = 0
        for (o, s) in segs:
            if off + size <= o:
                break
            off = max(off, o + s)
        assert off + size <= self.nbytes, "SBUF arena overflow for %s (%d): %s" % (name, size, sorted((o, sz, n) for n, (o, sz) in self.live.items()))
        self.live[name] = (off, size)
        olds = [n for (o, s, n) in self.freed if o < off + size and off < o + s]
        oldkeys = [k for k in self.P.state if any(k == n or (isinstance(k, str) and k.startswith(n + ":")) for n in olds)]
        self.P.alias(name, oldkeys)
        self._aliaskeys = oldkeys
        ap = self.base[:, off // 4:(off + size) // 4]
        if dt != F32:
            ap = ap.bitcast(dt)
        return ap[:, 0:nelem]

    def free(self, name):
        o, s = self.live.pop(name)
        self.freed.append((o, s, name))


def build_program(debug=False):
    nc = bass.Bass("TRN2", target_bir_lowering=False)

    def din(name, shape, dt=F32):
        return nc.dram_tensor(name, list(shape), dt, kind="ExternalInput").ap()

    def dscr(name, shape, dt=BF16):
        return nc.dram_tensor(name, list(shape), dt, kind="ExternalOutput" if debug else "Internal").ap()

    xseq = din("xseq", [SEQ, D])
    xown = din("xown", [NOWN, D])
    mem = din("mem", [256, D])
    w_in = din("w_in", [D, 6144])
    w_glu = din("w_glu", [D, D]); w_ssm = din("w_ssm_proj", [D, D]); w_da = din("w_da_proj", [D, D])
    w_mix = din("w_mix_out", [D, D]); w_xq = din("w_xq", [D, D]); w_xkv = din("w_xkv", [D, 2 * D])
    w_xo = din("w_xo", [D, D]); w_ff1 = din("w_ff1", [D, 4 * D]); w_ff2 = din("w_ff2", [4 * D, D])
    gains = {n: din(n, [D]) for n in ["norm_mix_pre", "norm_mix_post", "norm_x_pre", "norm_mem", "norm_x_post",
                                      "norm_ff_pre", "norm_ff_post"]}
    b_gate = din("b_gate", [2 * D]); b_glu = din("b_glu", [D]); ssm_d = din("ssm_d", [D])
    lam_re = din("ssm_lambda_re", [64, 64]); lam_im = din("ssm_lambda_im", [64, 64]); log_dt = din("ssm_log_dt", [64])
    b_re = din("ssm_b_re", [64, 64, 16]); b_im = din("ssm_b_im", [64, 64, 16])
    c_re = din("ssm_c_re", [64, 16, 64]); c_im = din("ssm_c_im", [64, 16, 64])
    lqk = [din(n, [64]) for n in ["da_lambda_q1", "da_lambda_k1", "da_lambda_q2", "da_lambda_k2"]]
    head_norm = din("da_head_norm", [128])
    cos_seq = din("cos_seq", [128, SEQ]); sin_seq = din("sin_seq", [128, SEQ])
    cos_own = din("cos_own", [128, NOWN]); sin_own = din("sin_own", [128, NOWN])
    kvalid = din("kvalid", [SEQ])
    esel_d = din("esel", [128, 256])
    ident_d = din("ident", [128, 128]); swap_d = din("swapm", [128, 128]); gmask_d = din("gmask", [128, 8, 128])
    out_d = nc.dram_tensor("out", [NOWN, D], F32, kind="ExternalOutput").ap()

    kT_d = dscr("kT_d", [8, 128, SEQ]); v_d = dscr("v_d", [SEQ, D]); uT_d = dscr("uT_d", [8, 128, SEQ])
    qT_d = dscr("qT_d", [8, 128, NOWN]); g_d = dscr("g_d", [16, 128, NOWN])

    P = Prog(nc)
    es = contextlib.ExitStack()
    ARENA_BYTES = 190 * 1024
    arena_t = es.enter_context(nc.sbuf_tensor("arena", [128, ARENA_BYTES // 4], F32))
    A = Arena(P, arena_t[:], ARENA_BYTES)
    banks = [es.enter_context(nc.psum_tensor("bank%d" % i, [128, 512], F32)) for i in range(8)]

    def bank(i):
        return banks[i][:]

    def bank_bf(i):
        return banks[i][:].bitcast(BF16)

    def bkey(i):
        return "bank%d" % i

    def dma(out, in_, reads=(), writes=(), q="sp", slow=False):
        if slow:
            return P.op(q, lambda e: e.dma_start(out=out, in_=in_, allow_slow_non_contiguous=True), reads, writes, dma=True)
        return P.op(q, lambda e: e.dma_start(out=out, in_=in_), reads, writes, dma=True)

    ident_f = A.alloc("ident_f", 128, F32); ident_b = A.alloc("ident_b", 128, BF16)
    swap_f = A.alloc("swap_f", 128, F32)
    gmask = A.alloc("gmask", 1024, F32)
    epst = A.alloc("epst", 1, F32); zerot = A.alloc("zerot", 1, F32)
    dma(ident_f, ident_d, writes=["ident_f"]); dma(swap_f, swap_d, writes=["swap_f"])
    dma(gmask, gmask_d.rearrange("p g c -> p (g c)"), writes=["gmask"])
    P.op("pool", lambda e: e.tensor_copy(out=ident_b, in_=ident_f), ["ident_f"], ["ident_b"])
    P.op("pool", lambda e: e.memset(epst, EPS), [], ["epst"])
    P.op("pool", lambda e: e.memset(zerot, 0.0), [], ["zerot"])

    def load_pp(name, src, n):
        t = A.alloc(name, n, F32)
        dma(t, src.rearrange("(k p) -> p k", p=128), writes=[name], slow=True)
        return t

    g_mix_pre = load_pp("g_mix_pre", gains["norm_mix_pre"], 8)
    g_x_pre = load_pp("g_x_pre", gains["norm_x_pre"], 8)
    g_mem = load_pp("g_mem", gains["norm_mem"], 8)
    g_ff_pre = load_pp("g_ff_pre", gains["norm_ff_pre"], 8)
    bgate_pp = load_pp("bgate_pp", b_gate, 16)
    bglu_pp = load_pp("bglu_pp", b_glu, 8)
    ssmd_pp = load_pp("ssmd_pp", ssm_d, 8)

    wctr = [0]

    def load_weight(name, src, K, N, gain=None, col0=0, rot=False):
        KC = K // 128
        wt = A.alloc(name, KC * N, BF16)
        wv = wt.rearrange("p (k n) -> p k n", k=KC)
        CH = min(N, 2048)
        for kc in range(KC):
            for c0 in range(0, N, CH):
                i = wctr[0]; wctr[0] += 1
                sname = "wstage%d" % (i % 2)
                if sname not in A.live:
                    A.alloc(sname, 2048, F32)
                o, s = A.live[sname]
                st = A.base[:, o // 4:o // 4 + CH]
                dma(st, src[kc * 128:(kc + 1) * 128, col0 + c0:col0 + c0 + CH], writes=[sname], q="sp")
                dst = wv[:, kc, c0:c0 + CH]
                eng = "dve"
                if rot:
                    sv = st.rearrange("p (m t d) -> p m t d", t=2, d=32)
                    dv = dst.rearrange("p (m t d) -> p m t d", t=2, d=32)
                    if gain is not None:
                        P.op(eng, lambda e, dv=dv, sv=sv, kc=kc: e.tensor_scalar(out=dv[:, :, 0, :], in0=sv[:, :, 1, :], scalar1=gain[:, kc:kc + 1], scalar2=-1.0, op0=ALU.mult, op1=ALU.mult), [sname, "gains"], [name])
                        P.op(eng, lambda e, dv=dv, sv=sv, kc=kc: e.tensor_scalar(out=dv[:, :, 1, :], in0=sv[:, :, 0, :], scalar1=gain[:, kc:kc + 1], scalar2=None, op0=ALU.mult), [sname, "gains"], [name])
                else:
                    if gain is not None:
                        P.op("act", lambda e, dst=dst, st=st, kc=kc: e.activation(out=dst, in_=st, func=AF.Copy, scale=gain[:, kc:kc + 1]), [sname, "gains"], [name])
                    else:
                        P.op("act", lambda e, dst=dst, st=st: e.activation(out=dst, in_=st, func=AF.Copy), [sname], [name])
        return wv

    P.op("pool", lambda e: e.engine_nop(), ["g_mix_pre", "g_x_pre", "g_mem", "g_ff_pre"], ["gains"])

    nctr = [0]

    def norm_block_T(x_src_ap, x_is_dram, hT_dst, hT_key, xkey=None, tp_bank=0):
        i = nctr[0]; nctr[0] += 1
        if x_is_dram:
            xs_name = "xs%d" % (i % 2)
            if xs_name not in A.live:
                A.alloc(xs_name, 1024, F32)
            o, s = A.live[xs_name]
            xs = A.base[:, o // 4:o // 4 + 1024]
            dma(xs, x_src_ap, writes=[xs_name], q="sp")
            rkeys = [xs_name]
        else:
            xs = x_src_ap
            rkeys = [xkey]
        for nm, n, dt in (("nsq%d" % (i % 2), 1024, BF16), ("nst%d" % (i % 2), 4, F32), ("nh%d" % (i % 2), 1024, BF16)):
            if nm not in A.live:
                A.alloc(nm, n, dt)
        o, s = A.live["nsq%d" % (i % 2)]; sq = A.base[:, o // 4:o // 4 + 512].bitcast(BF16)
        o, s = A.live["nst%d" % (i % 2)]; st = A.base[:, o // 4:o // 4 + 4]
        o, s = A.live["nh%d" % (i % 2)]; hb = A.base[:, o // 4:o // 4 + 512].bitcast(BF16)
        ks, kt, kh = "nsq%d" % (i % 2), "nst%d" % (i % 2), "nh%d" % (i % 2)
        P.op("act", lambda e: e.activation(out=sq, in_=xs, func=AF.Square, accum_out=st[:, 0:1]), rkeys, [ks, kt])
        P.op("act", lambda e: e.activation(out=st[:, 1:2], in_=st[:, 0:1], func=AF.Ln, scale=1.0 / D, bias=epst), [kt, "epst"], [kt + ":1"])
        P.op("act", lambda e: e.activation(out=st[:, 2:3], in_=st[:, 1:2], func=AF.Exp, scale=-0.5), [kt + ":1"], [kt + ":2"])
        P.op("dve", lambda e: e.tensor_scalar(out=hb, in0=xs, scalar1=st[:, 2:3], scalar2=None, op0=ALU.mult), rkeys + [kt + ":2"], [kh])
        tp = bank_bf(tp_bank).rearrange("p (k t) -> p k t", k=8)
        for kc in range(8):
            P.op("pe", lambda e, kc=kc: e.transpose(out=tp[:, kc, :], in_=hb[:, kc * 128:(kc + 1) * 128], identity=ident_b), [kh, "ident_b"], [bkey(tp_bank)])
        P.op("act", lambda e: e.activation(out=hT_dst, in_=tp, func=AF.Copy), [bkey(tp_bank)], [hT_key])

    x1_d = dscr("x1_d", [NOWN, D], F32)
    x2_d = dscr("x2_d", [NOWN, D], F32)
    wf1_d = nc.dram_tensor("wf1_d", [D, 4 * D], BF16, kind="Internal").ap()
    wf2_d = nc.dram_tensor("wf2_d", [4 * D, D], BF16, kind="Internal").ap()
    wsc = {}
    for nm_, (src_, K_, N_, gn_) in {"wglu": (w_glu, D, D, None), "wssm": (w_ssm, D, D, None), "wda": (w_da, D, D, None),
                                       "wmix": (w_mix, D, D, None), "wxkv": (w_xkv, D, 2 * D, g_mem), "wxq": (w_xq, D, D, g_x_pre),
                                       "wxo": (w_xo, D, D, None)}.items():
        wsc[nm_] = (nc.dram_tensor(nm_ + "_d", [K_, N_], BF16, kind="Internal").ap(), src_, K_, N_, gn_)
    cst_ = [A.alloc("cstg%d" % i, 1024, F32) for i in range(2)]
    cb = [A.alloc("cb%d" % i, 1024, BF16) for i in range(2)]
    cctr = [0]

    def cast_gen():
        jobs = [(v_[1], v_[2], v_[3], v_[4], v_[0], k_ + "_d") for k_, v_ in wsc.items()]
        jobs.append((w_ff1, D, 4 * D, g_ff_pre, wf1_d, "wf_d"))
        jobs.append((w_ff2, 4 * D, D, None, wf2_d, "wf_d"))
        for (src, K_, N_, gain, dst, dkey) in jobs:
            for kc in range(K_ // 128):
                for c0 in range(0, N_, 1024):
                    i = cctr[0]; cctr[0] += 1
                    st = cst_[i % 2]; sname = "cstg%d" % (i % 2)
                    dma(st, src[kc * 128:(kc + 1) * 128, c0:c0 + 1024], writes=[sname], q="act")
                    cbt = cb[i % 2]; cbk = "cb%d" % (i % 2)
                    if gain is not None:
                        P.op("act", lambda e, cbt=cbt, st=st, kc=kc, gain=gain: e.activation(out=cbt, in_=st, func=AF.Copy, scale=gain[:, kc:kc + 1]), [sname, "gains"], [cbk])
                    else:
                        P.op("act", lambda e, cbt=cbt, st=st: e.activation(out=cbt, in_=st, func=AF.Copy), [sname], [cbk])
                    dma(dst[kc * 128:(kc + 1) * 128, c0:c0 + 1024], cbt, reads=[cbk], writes=[dkey], q="act")
                    yield

    cgen = cast_gen()
    cg_alive = [True]

    def cast_step():
        if cg_alive[0]:
            try:
                next(cgen)
            except StopIteration:
                cg_alive[0] = False

    wk = load_weight("wk", w_in, D, 1024, gain=g_mix_pre, col0=2048)
    wkr = load_weight("wkr", w_in, D, 1024, gain=g_mix_pre, col0=2048, rot=True)
    wu = load_weight("wu", w_in, D, 1024, gain=g_mix_pre, col0=0)
    wv_ = load_weight("wv", w_in, D, 1024, gain=g_mix_pre, col0=3072)
    hTa = [A.alloc("hTa%d" % i, 8 * 512, BF16).rearrange("p (k t) -> p k t", k=8) for i in range(2)]
    cst = [A.alloc("cst%d" % i, 1024, F32) for i in range(2)]
    kst = [A.alloc("kst%d" % i, 512, BF16) for i in range(2)]
    kt1 = [A.alloc("kt1_%d" % i, 512, F32) for i in range(2)]
    kt2 = [A.alloc("kt2_%d" % i, 512, F32) for i in range(2)]
    vst = [A.alloc("vst%d" % i, 1024, BF16) for i in range(2)]
    ust = [A.alloc("ust%d" % i, 512, BF16) for i in range(2)]
    cnt = [0]
    for tt in range(16):
        hb_i = tt % 2
        hT = hTa[hb_i]; hk = "hTa%d" % hb_i
        for bl in range(4):
            blk = tt * 4 + bl
            norm_block_T(xseq[blk * 128:(blk + 1) * 128, :], True, hT[:, :, bl * 128:(bl + 1) * 128], hk)
        cs = cst[hb_i]; ck = "cst%d" % hb_i
        dma(cs[:, 0:512], cos_seq[:, tt * 512:(tt + 1) * 512], writes=[ck])
        dma(cs[:, 512:1024], sin_seq[:, tt * 512:(tt + 1) * 512], writes=[ck])
        for h in range(8):
            cast_step()
            i = cnt[0]; cnt[0] += 1
            pb = 1 + 2 * (i % 2)
            for kc in range(8):
                P.op("pe", lambda e, kc=kc, h=h, pb=pb, hT=hT: e.matmul(out=bank(pb), lhsT=wk[:, kc, h * 128:(h + 1) * 128], rhs=hT[:, kc, :], start=(kc == 0), stop=(kc == 7)), ["wk", hk], [bkey(pb)])
            for kc in range(8):
                P.op("pe", lambda e, kc=kc, h=h, pb=pb, hT=hT: e.matmul(out=bank(pb + 1), lhsT=wkr[:, kc, h * 128:(h + 1) * 128], rhs=hT[:, kc, :], start=(kc == 0), stop=(kc == 7)), ["wkr", hk], [bkey(pb + 1)])
            j = i % 2
            P.op("dve", lambda e, pb=pb, j=j, cs=cs: e.tensor_tensor(out=kt1[j], in0=bank(pb), in1=cs[:, 0:512], op=ALU.mult), [bkey(pb), ck], ["kt1_%d" % j])
            P.op("dve", lambda e, pb=pb, j=j, cs=cs: e.tensor_tensor(out=kt2[j], in0=bank(pb + 1), in1=cs[:, 512:1024], op=ALU.mult), [bkey(pb + 1), ck], ["kt2_%d" % j])
            P.op("dve", lambda e, j=j: e.tensor_tensor(out=kst[j], in0=kt1[j], in1=kt2[j], op=ALU.add), ["kt1_%d" % j, "kt2_%d" % j], ["kst%d" % j])
            dma(kT_d[h, :, tt * 512:(tt + 1) * 512], kst[j], reads=["kst%d" % j], writes=["kT_d"], q="sp")
        for fc in range(8):
            i = cnt[0]; cnt[0] += 1
            pb = 5 + (i % 2)
            for kc in range(8):
                P.op("pe", lambda e, kc=kc, fc=fc, pb=pb, hT=hT: e.matmul(out=bank(pb), lhsT=wu[:, kc, fc * 128:(fc + 1) * 128], rhs=hT[:, kc, :], start=(kc == 0), stop=(kc == 7)), ["wu", hk], [bkey(pb)])
            j = i % 2
            P.op("act", lambda e, pb=pb, j=j: e.activation(out=ust[j], in_=bank(pb), func=AF.Copy), [bkey(pb)], ["ust%d" % j])
            dma(uT_d[fc, :, tt * 512:(tt + 1) * 512], ust[j], reads=["ust%d" % j], writes=["uT_d"], q="sp")
        for bl in range(4):
            blk = tt * 4 + bl
            i = cnt[0]; cnt[0] += 1
            j = i % 2
            for half in range(2):
                pb = 1 + 2 * (i % 2) + half
                for kc in range(8):
                    P.op("pe", lambda e, kc=kc, bl=bl, half=half, pb=pb, hT=hT: e.matmul(out=bank(pb), lhsT=hT[:, kc, bl * 128:(bl + 1) * 128], rhs=wv_[:, kc, half * 512:(half + 1) * 512], start=(kc == 0), stop=(kc == 7)), ["wv", hk], [bkey(pb)])
                P.op("act", lambda e, pb=pb, j=j, half=half: e.activation(out=vst[j][:, half * 512:(half + 1) * 512], in_=bank(pb), func=AF.Copy), [bkey(pb)], ["vst%d" % j])
            dma(v_d[blk * 128:(blk + 1) * 128, :], vst[j], reads=["vst%d" % j], writes=["v_d"], q="sp")
    while cg_alive[0]:
        cast_step()
    for n in ["cstg0", "cstg1", "cb0", "cb1"]:
        A.free(n)
    for n in ["wu", "wk", "wkr", "wv", "hTa0", "hTa1", "cst0", "cst1", "kst0", "kst1", "kt1_0", "kt1_1", "kt2_0", "kt2_1", "vst0", "vst1", "ust0", "ust1"]:
        A.free(n)

    wq = load_weight("wq", w_in, D, 1024, gain=g_mix_pre, col0=1024)
    wqr = load_weight("wqr", w_in, D, 1024, gain=g_mix_pre, col0=1024, rot=True)
    wg = load_weight("wg", w_in, D, 2048, gain=g_mix_pre, col0=4096)
    hTo = [A.alloc("hTo%d" % i, 8 * 512, BF16).rearrange("p (k t) -> p k t", k=8) for i in range(2)]
    cso = [A.alloc("cso%d" % i, 1024, F32) for i in range(2)]
    qst = [A.alloc("qst%d" % i, 512, BF16) for i in range(2)]
    qt1 = [A.alloc("qt1_%d" % i, 512, F32) for i in range(2)]
    qt2 = [A.alloc("qt2_%d" % i, 512, F32) for i in range(2)]
    gst = [A.alloc("gst%d" % i, 512, BF16) for i in range(2)]
    for tt in range(4):
        hb_i = tt % 2
        hT = hTo[hb_i]; hk = "hTo%d" % hb_i
        for bl in range(4):
            blk = tt * 4 + bl
            norm_block_T(xown[blk * 128:(blk + 1) * 128, :], True, hT[:, :, bl * 128:(bl + 1) * 128], hk)
        cs = cso[hb_i]; ck = "cso%d" % hb_i
        dma(cs[:, 0:512], cos_own[:, tt * 512:(tt + 1) * 512], writes=[ck])
        dma(cs[:, 512:1024], sin_own[:, tt * 512:(tt + 1) * 512], writes=[ck])
        for h in range(8):
            i = cnt[0]; cnt[0] += 1
            pb = 1 + 2 * (i % 2)
            for kc in range(8):
                P.op("pe", lambda e, kc=kc, h=h, pb=pb, hT=hT: e.matmul(out=bank(pb), lhsT=wq[:, kc, h * 128:(h + 1) * 128], rhs=hT[:, kc, :], start=(kc == 0), stop=(kc == 7)), ["wq", hk], [bkey(pb)])
            for kc in range(8):
                P.op("pe", lambda e, kc=kc, h=h, pb=pb, hT=hT: e.matmul(out=bank(pb + 1), lhsT=wqr[:, kc, h * 128:(h + 1) * 128], rhs=hT[:, kc, :], start=(kc == 0), stop=(kc == 7)), ["wqr", hk], [bkey(pb + 1)])
            j = i % 2
            P.op("dve", lambda e, pb=pb, j=j, cs=cs: e.tensor_tensor(out=qt1[j], in0=bank(pb), in1=cs[:, 0:512], op=ALU.mult), [bkey(pb), ck], ["qt1_%d" % j])
            P.op("dve", lambda e, pb=pb, j=j, cs=cs: e.tensor_tensor(out=qt2[j], in0=bank(pb + 1), in1=cs[:, 512:1024], op=ALU.mult), [bkey(pb + 1), ck], ["qt2_%d" % j])
            P.op("dve", lambda e, j=j: e.tensor_tensor(out=qst[j], in0=qt1[j], in1=qt2[j], op=ALU.add), ["qt1_%d" % j, "qt2_%d" % j], ["qst%d" % j])
            dma(qT_d[h, :, tt * 512:(tt + 1) * 512], qst[j], reads=["qst%d" % j], writes=["qT_d"], q="sp")
        for gc in range(16):
            i = cnt[0]; cnt[0] += 1
            pb = 5 + (i % 2)
            for kc in range(8):
                P.op("pe", lambda e, kc=kc, gc=gc, pb=pb, hT=hT: e.matmul(out=bank(pb), lhsT=wg[:, kc, gc * 128:(gc + 1) * 128], rhs=hT[:, kc, :], start=(kc == 0), stop=(kc == 7)), ["wg", hk], [bkey(pb)])
            j = i % 2
            P.op("act", lambda e, pb=pb, j=j, gc=gc: e.activation(out=gst[j], in_=bank(pb), func=AF.Sigmoid, bias=bgate_pp[:, gc:gc + 1]), [bkey(pb), "bgate_pp"], ["gst%d" % j])
            dma(g_d[gc, :, tt * 512:(tt + 1) * 512], gst[j], reads=["gst%d" % j], writes=["g_d"], q="sp")
    for n in ["wq", "wqr", "wg", "hTo0", "hTo1", "cso0", "cso1", "qst0", "qst1", "qt1_0", "qt1_1", "qt2_0", "qt2_1", "gst0", "gst1"]:
        A.free(n)

    for n in ["wstage0", "wstage1", "xs0", "xs1", "nsq0", "nsq1", "nh0", "nh1", "nst0", "nst1"]:
        if n in A.live:
            A.free(n)
    def a64(name, n, dt=F32):
        return A.alloc(name, n, dt)[0:64, :]

    lre = a64("lre", 64); lim = a64("lim", 64); ldt = a64("ldt", 64)
    dma(lre, lam_re.rearrange("g p -> p g"), writes=["lre"], slow=True)
    dma(lim, lam_im.rearrange("g p -> p g"), writes=["lim"], slow=True)
    dma(ldt, log_dt.partition_broadcast(64), writes=["ldt"])
    dtt = a64("dtt", 64); zr = a64("zr", 64); zi = a64("zi", 64)
    P.op("act", lambda e: e.activation(out=dtt, in_=ldt, func=AF.Exp), ["ldt"], ["dtt"])
    P.op("dve", lambda e: e.tensor_tensor(out=zr, in0=lre, in1=dtt, op=ALU.mult), ["lre", "dtt"], ["zr"])
    P.op("dve", lambda e: e.tensor_tensor(out=zi, in0=lim, in1=dtt, op=ALU.mult), ["lim", "dtt"], ["zi"])
    mag = a64("mag", 64); cr = a64("cr", 64); ci = a64("ci", 64); halfpi = a64("halfpi", 1)
    P.op("pool", lambda e: e.memset(halfpi, math.pi / 2), [], ["halfpi"])
    P.op("act", lambda e: e.activation(out=mag, in_=zr, func=AF.Exp, scale=1.0 / 32), ["zr"], ["mag"])
    P.op("act", lambda e: e.activation(out=ci, in_=zi, func=AF.Sin, scale=1.0 / 32), ["zi"], ["ci"])
    P.op("act", lambda e: e.activation(out=cr, in_=zi, func=AF.Sin, scale=1.0 / 32, bias=halfpi), ["zi", "halfpi"], ["cr"])
    P.op("dve", lambda e: e.tensor_tensor(out=cr, in0=cr, in1=mag, op=ALU.mult), ["cr", "mag"], ["cr"])
    P.op("dve", lambda e: e.tensor_tensor(out=ci, in0=ci, in1=mag, op=ALU.mult), ["ci", "mag"], ["ci"])
    PW = a64("PW", NLEV * 128).rearrange("p (l r g) -> p l r g", l=NLEV, r=2)
    t1 = a64("sq_t1", 64); t2 = a64("sq_t2", 64); t3 = a64("sq_t3", 64)

    def csquare(sr, si, dr, di, keys_in, key_out):
        P.op("dve", lambda e: e.tensor_tensor(out=t1, in0=sr, in1=sr, op=ALU.mult), keys_in, ["sq_t1"])
        P.op("dve", lambda e: e.tensor_tensor(out=t2, in0=si, in1=si, op=ALU.mult), keys_in, ["sq_t2"])
        P.op("dve", lambda e: e.tensor_tensor(out=t3, in0=sr, in1=si, op=ALU.mult), keys_in, ["sq_t3"])
        P.op("dve", lambda e: e.tensor_tensor(out=dr, in0=t1, in1=t2, op=ALU.subtract), ["sq_t1", "sq_t2"], [key_out])
        P.op("dve", lambda e: e.tensor_tensor(out=di, in0=t3, in1=t3, op=ALU.add), ["sq_t3"], [key_out + ":i"])

    wr = [a64("wr%d" % i, 64) for i in range(2)]; wi = [a64("wi%d" % i, 64) for i in range(2)]
    csquare(cr, ci, wr[0], wi[0], ["cr", "ci"], "wr0")
    csquare(wr[0], wi[0], wr[1], wi[1], ["wr0", "wr0:i"], "wr1")
    csquare(wr[1], wi[1], wr[0], wi[0], ["wr1", "wr1:i"], "wr0")
    csquare(wr[0], wi[0], wr[1], wi[1], ["wr0", "wr0:i"], "wr1")
    csquare(wr[1], wi[1], PW[:, 0, 0, :], PW[:, 0, 1, :], ["wr1", "wr1:i"], "PW:0")
    for l in range(1, NLEV):
        csquare(PW[:, l - 1, 0, :], PW[:, l - 1, 1, :], PW[:, l, 0, :], PW[:, l, 1, :], ["PW:%d" % (l - 1), "PW:%d:i" % (l - 1)], "PW:%d" % l)
    den = a64("den", 64); cfr = a64("cfr", 64); cfi = a64("cfi", 64); lm1 = a64("lm1", 64)
    P.op("dve", lambda e: e.tensor_tensor(out=t1, in0=lre, in1=lre, op=ALU.mult), ["lre", "PW:%d:i" % (NLEV - 1)], ["sq_t1"])
    P.op("dve", lambda e: e.tensor_tensor(out=t2, in0=lim, in1=lim, op=ALU.mult), ["lim"], ["sq_t2"])
    P.op("dve", lambda e: e.tensor_tensor(out=den, in0=t1, in1=t2, op=ALU.add), ["sq_t1", "sq_t2"], ["den"])
    P.op("dve", lambda e: e.reciprocal(out=den, in_=den), ["den"], ["den"])
    P.op("dve", lambda e: e.tensor_scalar(out=lm1, in0=PW[:, 0, 0, :], scalar1=-1.0, scalar2=None, op0=ALU.add), ["PW:0"], ["lm1"])
    P.op("dve", lambda e: e.tensor_tensor(out=t1, in0=lm1, in1=lre, op=ALU.mult), ["lm1", "lre", "den"], ["sq_t1"])
    P.op("dve", lambda e: e.tensor_tensor(out=t2, in0=PW[:, 0, 1, :], in1=lim, op=ALU.mult), ["PW:0:i", "lim"], ["sq_t2"])
    P.op("dve", lambda e: e.tensor_tensor(out=cfr, in0=t1, in1=t2, op=ALU.add), ["sq_t1", "sq_t2"], ["cfr"])
    P.op("dve", lambda e: e.tensor_tensor(out=t1, in0=PW[:, 0, 1, :], in1=lre, op=ALU.mult), ["PW:0:i", "lre", "cfr"], ["sq_t1"])
    P.op("dve", lambda e: e.tensor_tensor(out=t2, in0=lm1, in1=lim, op=ALU.mult), ["lm1", "lim", "cfr"], ["sq_t2"])
    P.op("dve", lambda e: e.tensor_tensor(out=cfi, in0=t1, in1=t2, op=ALU.subtract), ["sq_t1", "sq_t2"], ["cfi"])
    P.op("dve", lambda e: e.tensor_tensor(out=cfr, in0=cfr, in1=den, op=ALU.mult), ["cfr", "den"], ["cfr"])
    P.op("dve", lambda e: e.tensor_tensor(out=cfi, in0=cfi, in1=den, op=ALU.mult), ["cfi", "den"], ["cfi"])
    braw = a64("braw", 2048).rearrange("p (r g h) -> p r g h", r=2, g=64)
    for gh in range(2):
        dma(braw[:, 0, gh * 32:(gh + 1) * 32, :], b_re[gh * 32:(gh + 1) * 32].rearrange("g p h -> p g h"), writes=["braw"], slow=True)
        dma(braw[:, 1, gh * 32:(gh + 1) * 32, :], b_im[gh * 32:(gh + 1) * 32].rearrange("g p h -> p g h"), writes=["braw"], slow=True)
    Bb = a64("Bb", 2048).rearrange("p (r g h) -> p r g h", r=2, g=64)
    bt1 = a64("bt1", 1024).rearrange("p (g h) -> p g h", g=64); bt2 = a64("bt2", 1024).rearrange("p (g h) -> p g h", g=64)
    cfr_b = cfr.unsqueeze(2).broadcast_to([64, 64, 16]); cfi_b = cfi.unsqueeze(2).broadcast_to([64, 64, 16])
    P.op("dve", lambda e: e.tensor_tensor(out=bt1, in0=braw[:, 0], in1=cfr_b, op=ALU.mult), ["braw", "cfr"], ["bt1"])
    P.op("dve", lambda e: e.tensor_tensor(out=bt2, in0=braw[:, 1], in1=cfi_b, op=ALU.mult), ["braw", "cfi"], ["bt2"])
    P.op("dve", lambda e: e.tensor_tensor(out=Bb[:, 0], in0=bt1, in1=bt2, op=ALU.subtract), ["bt1", "bt2"], ["Bb"])
    P.op("dve", lambda e: e.tensor_tensor(out=bt1, in0=braw[:, 0], in1=cfi_b, op=ALU.mult), ["braw", "cfi", "Bb"], ["bt1"])
    P.op("dve", lambda e: e.tensor_tensor(out=bt2, in0=braw[:, 1], in1=cfr_b, op=ALU.mult), ["braw", "cfr", "Bb"], ["bt2"])
    P.op("dve", lambda e: e.tensor_tensor(out=Bb[:, 1], in0=bt1, in1=bt2, op=ALU.add), ["bt1", "bt2"], ["Bb:i"])
    Cst = A.alloc("Cst", 1024, F32).rearrange("p (g h) -> p g h", g=64)
    for gh in range(2):
        dma(Cst[0:64, gh * 32:(gh + 1) * 32, :], c_re[gh * 32:(gh + 1) * 32].rearrange("g h p -> p g h"), writes=["Cst"], slow=True)
        dma(Cst[64:128, gh * 32:(gh + 1) * 32, :], c_im[gh * 32:(gh + 1) * 32].rearrange("g h p -> p g h"), writes=["Cst"], slow=True)
    P.op("pool", lambda e: e.tensor_scalar(out=Cst[64:128], in0=Cst[64:128], scalar1=-1.0, scalar2=None, op0=ALU.mult), ["Cst"], ["Cst"])
    S1 = A.alloc("S1", NLEV * 64, F32).rearrange("p (l g) -> p l g", l=NLEV)
    S2 = A.alloc("S2", NLEV * 64, F32).rearrange("p (l g) -> p l g", l=NLEV)
    pwkeys = ["PW:%d" % l for l in range(NLEV)] + ["PW:%d:i" % l for l in range(NLEV)]
    dma(S1[0:64], PW[:, :, 0, :], reads=pwkeys, writes=["S1"]); dma(S1[64:128], PW[:, :, 0, :], reads=pwkeys, writes=["S1"])
    dma(S2[0:64], PW[:, :, 1, :], reads=pwkeys, writes=["S2"]); dma(S2[64:128], PW[:, :, 1, :], reads=pwkeys, writes=["S2"])
    P.op("pool", lambda e: e.tensor_scalar(out=S2[64:128], in0=S2[64:128], scalar1=-1.0, scalar2=None, op0=ALU.mult), ["S2"], ["S2"])

    if debug:
        pw_dbg = dscr("pw_dbg", [64, NLEV * 128], F32)
        dma(pw_dbg, PW.rearrange("p l r g -> p (l r g)"), reads=pwkeys, writes=["pw_dbg"])
        bb_dbg = dscr("bb_dbg", [64, 2048], F32)
        dma(bb_dbg, Bb.rearrange("p r g h -> p (r g h)"), reads=["Bb", "Bb:i"], writes=["bb_dbg"])
    for n in ["braw", "bt1", "bt2", "PW", "sq_t1", "sq_t2", "sq_t3", "wr0", "wr1", "wi0", "wi1", "mag", "cr", "ci", "lm1", "den", "cfr", "cfi", "zr", "zi", "dtt", "ldt", "lre", "lim"]:
        A.free(n)
    uT = [A.alloc("uT%d" % i, SEQ, BF16) for i in range(1)]
    X0p = [A.alloc("X0p%d" % i, SEQ, BF16) for i in range(2)]
    Tp = [A.alloc("Tp%d" % i, SEQ, BF16) for i in range(2)]
    Xop = [[A.alloc("Xo%d_%d" % (p_, i), NOWN, BF16).rearrange("p (s t) -> p s t", s=16) for i in range(2)] for p_ in range(2)]
    XBp = [[A.alloc("XB%d_%d" % (p_, i), 64, BF16) for i in range(2)] for p_ in range(2)]
    ysg = A.alloc("ysg", 8 * NOWN, BF16).rearrange("p (k t) -> p k t", k=8)
    WB = [A.alloc("WB%d" % i, 128, BF16) for i in range(4)]
    WC = [A.alloc("WC%d" % i, 128, BF16) for i in range(4)]
    R = [A.alloc("R%d" % i, 128, BF16) for i in range(52)]
    Rt = [A.alloc("Rt%d" % i, 128, F32) for i in range(4)]
    bm = [A.alloc("bm%d" % i, 256, F32)[0:64, :] for i in range(2)]
    gel = [A.alloc("gel%d" % i, NOWN, F32) for i in range(2)]
    evc = [0]
    toff = [0, 4096, 6144, 7168, 7680, 7936, 8064]

    def prep_gen(fc, gl, par, st_):
        g = fc * 8 + gl
        wi_ = st_ * 2 + par
        bmt = bm[par]; bmk = "bm%d" % par
        Bfc = Bb[:, :, fc * 8:(fc + 1) * 8, :]
        gmv64 = gmask[0:64, gl * 128:(gl + 1) * 128].rearrange("p (g h) -> p g h", g=8)
        P.op("dve", lambda e: e.tensor_tensor(out=bmt[:, 0:128].rearrange("p (g h) -> p g h", g=8), in0=Bfc[:, 0], in1=gmv64, op=ALU.mult), ["Bb", "Bb:i", "gmask"], [bmk])
        P.op("dve", lambda e: e.tensor_tensor(out=bmt[:, 128:256].rearrange("p (g h) -> p g h", g=8), in0=Bfc[:, 1], in1=gmv64, op=ALU.mult), ["Bb", "Bb:i", "gmask"], [bmk + ":i"])
        pbw = 7
        P.op("pe", lambda e: e.matmul(out=bank(pbw)[:, 0:64], lhsT=bmt[:, 0:128], rhs=ident_f[0:64, 0:64], start=True, stop=True), [bmk, "ident_f"], [bkey(pbw)])
        P.op("pe", lambda e: e.matmul(out=bank(pbw)[:, 64:128], lhsT=bmt[:, 128:256], rhs=ident_f[0:64, 0:64], start=True, stop=True), [bmk + ":i", "ident_f"], [bkey(pbw)])
        P.op("act", lambda e: e.activation(out=WB[wi_], in_=bank(pbw)[:, 0:128], func=AF.Copy), [bkey(pbw)], ["WB%d" % wi_])
        P.op("pool", lambda e: e.tensor_tensor(out=WC[wi_].rearrange("p (g h) -> p g h", g=8), in0=Cst[:, fc * 8:(fc + 1) * 8, :], in1=gmask[:, gl * 128:(gl + 1) * 128].rearrange("p (g h) -> p g h", g=8), op=ALU.mult), ["Cst", "gmask"], ["WC%d" % wi_])
        yield
        for lev in range(NLEV):
            Rm = R[wi_ * 13 + lev]; Rk = "R%d" % (wi_ * 13 + lev)
            rt = Rt[(lev % 2) * 2 + par]; rtk = "Rt%d" % ((lev % 2) * 2 + par)
            P.op("act", lambda e, lev=lev, rt=rt: e.activation(out=rt, in_=swap_f, func=AF.Copy, scale=S2[:, lev, g:g + 1]), ["swap_f", "S2"], [rtk])
            P.op("dve", lambda e, Rm=Rm, lev=lev, rt=rt: e.scalar_tensor_tensor(out=Rm, in0=ident_f, scalar=S1[:, lev, g:g + 1], in1=rt, op0=ALU.mult, op1=ALU.add), ["ident_f", "S1", rtk], [Rk])
            yield

    def group_gen(fc, gl, par, u, uk, st_):
        g = fc * 8 + gl
        wi_ = st_ * 2 + par
        X0 = X0p[par]; Tb = Tp[par]; Xo = Xop[par]; XB = XBp[par]
        xn = "X0p%d" % par; tn = "Tp%d" % par
        bc = [0]

        def nextbank():
            bc[0] += 1
            return 2 * par + (bc[0] % 2)

        def evac(pb, dst_ap, dkey, n=None):
            src_ap = bank(pb) if n is None else bank(pb)[:, 0:n]
            evc[0] += 1
            if evc[0] % 2 == 0:
                P.op("act", lambda e: e.activation(out=dst_ap, in_=src_ap, func=AF.Copy), [bkey(pb)], [dkey])
            else:
                P.op("dve", lambda e: e.tensor_copy(out=dst_ap, in_=src_ap), [bkey(pb)], [dkey])

        Rg = [(R[wi_ * 13 + lev], "R%d" % (wi_ * 13 + lev)) for lev in range(NLEV)]
        for tt in range(16):
            pb = nextbank()
            P.op("pe", lambda e, pb=pb, tt=tt: e.matmul(out=bank(pb), lhsT=WB[wi_], rhs=u[:, tt * 512:(tt + 1) * 512], start=True, stop=True), ["WB%d" % wi_, uk], [bkey(pb)])
            evac(pb, X0[:, tt * 512:(tt + 1) * 512], xn + ":%d" % tt)
            if tt % 4 == 3:
                yield
        for lev in range(7):
            n_l = 4096 >> lev
            if lev == 0:
                srcv = X0.rearrange("p (i two) -> p i two", two=2); sbase = xn + ":"
            else:
                srcv = Tb[:, toff[lev - 1]:toff[lev - 1] + 2 * n_l].rearrange("p (i two) -> p i two", two=2); sbase = tn + ":%d_" % (lev - 1)
            Rm, Rk = Rg[lev]
            for c0 in range(0, n_l, 512):
                n = min(512, n_l - c0)
                pb = nextbank()
                skeys = sorted(set([sbase + "%d" % ((2 * c0) // 512), sbase + "%d" % ((2 * c0 + 2 * n - 1) // 512)]))
                P.op("pe", lambda e, pb=pb, srcv=srcv, c0=c0, n=n: e.matmul(out=bank(pb)[:, 0:n], lhsT=ident_b, rhs=srcv[:, c0:c0 + n, 1], start=True, stop=False), ["ident_b"] + skeys, [bkey(pb)])
                P.op("pe", lambda e, pb=pb, srcv=srcv, c0=c0, n=n, Rm=Rm: e.matmul(out=bank(pb)[:, 0:n], lhsT=Rm, rhs=srcv[:, c0:c0 + n, 0], start=False, stop=True), [Rk] + skeys, [bkey(pb)])
                evac(pb, Tb[:, toff[lev] + c0:toff[lev] + c0 + n], tn + ":%d_%d" % (lev, c0 // 512), n)
                if (c0 // 512) % 2 == 1:
                    yield
            yield
        xbk = tn + ":6_0"
        xb_src = Tb[:, toff[6]:toff[6] + 64]
        for m_ in range(6):
            sh = 1 << m_
            Rm, Rk = Rg[7 + m_]
            pb = nextbank()
            P.op("pe", lambda e, pb=pb, xb_src=xb_src: e.matmul(out=bank(pb)[:, 0:64], lhsT=ident_b, rhs=xb_src, start=True, stop=False), ["ident_b", xbk], [bkey(pb)])
            P.op("pe", lambda e, pb=pb, xb_src=xb_src, sh=sh, Rm=Rm: e.matmul(out=bank(pb)[:, sh:64], lhsT=Rm, rhs=xb_src[:, 0:64 - sh], start=False, stop=True), [Rk, xbk], [bkey(pb)])
            dstb = XB[m_ % 2]
            xbk = "XB%d_%d" % (par, m_ % 2)
            evac(pb, dstb, xbk, 64)
            xb_src = dstb
            yield
        x0own = X0.rearrange("p (s b t) -> p s b t", s=16, b=4)[:, :, 3, :]
        x0keys = [xn + ":%d" % t for t in range(16)]
        xo0keys = ["Xo%d_0:%d" % (par, t) for t in range(4)]
        P.op("dve", lambda e: e.tensor_copy(out=Xo[0], in_=x0own), x0keys, xo0keys)
        pb = nextbank()
        Rm, Rk = Rg[0]
        P.op("pe", lambda e, pb=pb: e.matmul(out=bank(pb)[:, 0:16], lhsT=ident_b, rhs=x0own[:, :, 0], start=True, stop=False), ["ident_b"] + x0keys, [bkey(pb)])
        P.op("pe", lambda e, pb=pb, xb_src=xb_src, Rm=Rm: e.matmul(out=bank(pb)[:, 0:16], lhsT=Rm, rhs=xb_src.rearrange("p (s b) -> p s b", b=4)[:, :, 2], start=False, stop=True), [Rk, xbk], [bkey(pb)])
        P.op("dve", lambda e, pb=pb: e.tensor_copy(out=Xo[0][:, :, 0], in_=bank(pb)[:, 0:16]), [bkey(pb)] + xo0keys, xo0keys)
        yield
        cur = 0
        for lev in range(7):
            sh = 1 << lev
            Rm, Rk = Rg[lev]
            src = Xo[cur]; dst = Xo[1 - cur]
            for q4 in range(4):
                sk = "Xo%d_%d:%d" % (par, cur, q4); dk = "Xo%d_%d:%d" % (par, 1 - cur, q4)
                pb = nextbank()
                pv = bank(pb).rearrange("p (s t) -> p s t", s=4)
                P.op("pe", lambda e, pb=pb, src=src, q4=q4: e.matmul(out=bank(pb), lhsT=ident_b, rhs=src[:, q4 * 4:(q4 + 1) * 4, :].rearrange("p s t -> p (s t)"), start=True, stop=False), ["ident_b", sk], [bkey(pb)])
                P.op("pe", lambda e, pv=pv, src=src, q4=q4, sh=sh, Rm=Rm: e.matmul(out=pv[:, :, sh:128], lhsT=Rm, rhs=src[:, q4 * 4:(q4 + 1) * 4, 0:128 - sh], start=False, stop=True), [Rk, sk], [bkey(pb)])
                evac(pb, dst[:, q4 * 4:(q4 + 1) * 4, :].rearrange("p s t -> p (s t)"), dk)
                if q4 % 2 == 1:
                    yield
            cur = 1 - cur
        fin = Xo[cur].rearrange("p s t -> p (s t)"); fkb = "Xo%d_%d" % (par, cur)
        if debug and g == 63:
            xf_dbg = dscr("xf_dbg", [128, NOWN])
            dma(xf_dbg, fin, reads=[fkb + ":%d" % t for t in range(4)], writes=["xf_dbg"])
        for ot in range(4):
            yb = 4 + (2 * par + ot) % 3
            P.op("pe", lambda e, ot=ot, yb=yb: e.matmul(out=bank(yb), lhsT=WC[wi_], rhs=fin[:, ot * 512:(ot + 1) * 512], start=True, stop=True), ["WC%d" % wi_, fkb + ":%d" % ot], [bkey(yb)])
            pbk = bkey(yb); pbb = bank(yb)
            yacc = gel[0][:, ot * 512:(ot + 1) * 512]
            if gl == 0:
                uo = u.rearrange("p (s b t) -> p s b t", s=16, b=4)[:, ot * 4:(ot + 1) * 4, 3, :]
                P.op("dve", lambda e, yacc=yacc, uo=uo, pbb=pbb: e.scalar_tensor_tensor(out=yacc.rearrange("p (s t) -> p s t", s=4), in0=uo, scalar=ssmd_pp[:, fc:fc + 1], in1=pbb.rearrange("p (s t) -> p s t", s=4), op0=ALU.mult, op1=ALU.add), [uk, "ssmd_pp", pbk], ["gel0:%d" % ot])
            else:
                P.op("dve", lambda e, yacc=yacc, pbb=pbb: e.tensor_tensor(out=yacc, in0=pbb, in1=yacc, op=ALU.add), [pbk, "gel0:%d" % ot], ["gel0:%d" % ot])
            if ot % 2 == 1:
                yield

    def run_lockstep(gens):
        alive = [True] * len(gens)
        while any(alive):
            for i_ in range(len(gens)):
                if alive[i_]:
                    try:
                        next(gens[i_])
                    except StopIteration:
                        alive[i_] = False

    run_lockstep([prep_gen(0, 0, 0, 0), prep_gen(0, 1, 1, 0)])
    for fc in range(8):
        u = uT[0]; uk = "uT0"
        dma(u, uT_d[fc], reads=["uT_d"], writes=[uk], q="pool")
        for gp in range(4):
            pk = fc * 4 + gp
            st_ = pk % 2
            gens = [group_gen(fc, 2 * gp, 0, u, uk, st_), group_gen(fc, 2 * gp + 1, 1, u, uk, st_)]
            if pk + 1 < 32:
                nfc, ngp = (pk + 1) // 4, (pk + 1) % 4
                gens.append(prep_gen(nfc, 2 * ngp, 0, 1 - st_))
                gens.append(prep_gen(nfc, 2 * ngp + 1, 1, 1 - st_))
            run_lockstep(gens)
        yk = ["gel0:%d" % ot for ot in range(4)]
        P.op("act", lambda e: e.activation(out=gel[1], in_=gel[0], func=AF.Square), yk, ["gel1"])
        P.op("dve", lambda e: e.tensor_scalar(out=gel[1], in0=gel[1], scalar1=0.044715 * 1.5957691216, scalar2=1.5957691216, op0=ALU.mult, op1=ALU.add), ["gel1"], ["gel1"])
        P.op("dve", lambda e: e.tensor_tensor(out=gel[1], in0=gel[1], in1=gel[0], op=ALU.mult), ["gel1"] + yk, ["gel1"])
        P.op("act", lambda e: e.activation(out=gel[1], in_=gel[1], func=AF.Sigmoid), ["gel1"], ["gel1"])
        P.op("dve", lambda e, fc=fc: e.tensor_tensor(out=ysg[:, fc, :], in0=gel[1], in1=gel[0], op=ALU.mult), ["gel1"] + yk, ["ysg"])
    for n in ["uT0", "X0p0", "X0p1", "Tp0", "Tp1", "Xo0_0", "Xo0_1", "Xo1_0", "Xo1_1", "XB0_0", "XB0_1", "XB1_0", "XB1_1", "WB0", "WB1", "WB2", "WB3", "WC0", "WC1", "WC2", "WC3"] + ["R%d" % i for i in range(52)] + ["Rt0", "Rt1", "Rt2", "Rt3", "bm0", "bm1",
              "gel0", "gel1", "S1", "S2", "Cst", "Bb"]:
        A.free(n)

    if debug:
        ysg_dbg = dscr("ysg_dbg", [128, 8 * NOWN])
        dma(ysg_dbg, ysg.rearrange("p k t -> p (k t)"), reads=["ysg"], writes=["ysg_dbg"])
    lq = A.alloc("lq", 256, F32).rearrange("p (a d) -> p a d", a=4)
    for a in range(4):
        dma(lq[:, a, :], lqk[a].partition_broadcast(128), writes=["lq"])
    lamt = A.alloc("lamt", 8, F32)
    lqp = A.alloc("lqp", 128, F32).rearrange("p (a d) -> p a d", a=2)
    P.op("dve", lambda e: e.tensor_tensor(out=lqp[:, 0, :], in0=lq[:, 0, :], in1=lq[:, 1, :], op=ALU.mult), ["lq"], ["lqp"])
    P.op("dve", lambda e: e.tensor_tensor(out=lqp[:, 1, :], in0=lq[:, 2, :], in1=lq[:, 3, :], op=ALU.mult), ["lq"], ["lqp"])
    P.op("dve", lambda e: e.tensor_reduce(out=lamt[:, 0:2], in_=lqp, axis=AX.X, op=ALU.add), ["lqp"], ["lamt"])
    P.op("act", lambda e: e.activation(out=lamt[:, 2:4], in_=lamt[:, 0:2], func=AF.Exp), ["lamt"], ["lamt:e"])
    P.op("dve", lambda e: e.tensor_tensor(out=lamt[:, 4:5], in0=lamt[:, 3:4], in1=lamt[:, 2:3], op=ALU.subtract), ["lamt:e"], ["lamt:d"])
    P.op("dve", lambda e: e.tensor_scalar(out=lamt[:, 5:6], in0=lamt[:, 4:5], scalar1=-0.2, scalar2=None, op0=ALU.add), ["lamt:d"], ["neglam"])
    hn = A.alloc("hn", 128, F32)
    dma(hn, head_norm.partition_broadcast(128), writes=["hn"])
    P.op("dve", lambda e: e.tensor_scalar(out=hn, in0=hn, scalar1=0.8, scalar2=None, op0=ALU.mult), ["hn"], ["hn"])
    kvf = A.alloc("kvf", 64, F32)
    dma(kvf, kvalid.rearrange("(b p) -> p b", p=128), writes=["kvf"], slow=True)

    ya = A.alloc("ya", 16 * 1024, BF16).rearrange("p (s f) -> p s f", s=16)
    Kh = [A.alloc("Kh%d" % i, 2 * SEQ, BF16)[0:64, :].rearrange("p (m t) -> p m t", m=2) for i in range(2)]
    Vh = [A.alloc("Vh%d" % i, 64 * 128, BF16).rearrange("p (b d) -> p b d", b=64) for i in range(1)]
    Qh = [A.alloc("Qh%d" % i, 2 * NOWN, BF16)[0:64, :].rearrange("p (m t) -> p m t", m=2) for i in range(1)]
    PT = [A.alloc("PT%d" % i, 1024, BF16).rearrange("p (m q) -> p m q", m=2) for i in range(2)]
    Esel = A.alloc("Esel", 256, BF16).rearrange("p (m c) -> p m c", m=2)
    Esel_f = A.alloc("Esel_f", 256, F32)
    dma(Esel_f, esel_d, writes=["Esel_f"])
    P.op("pool", lambda e: e.tensor_copy(out=Esel.rearrange("p m c -> p (m c)"), in_=Esel_f), ["Esel_f"], ["Esel"])
    denrow = [A.alloc("denrow%d" % i, 512, F32) for i in range(2)]
    rcol = [A.alloc("rcol%d" % i, 128, F32) for i in range(2)]
    OT = [A.alloc("OT%d" % i, 1024, BF16).rearrange("p (m q) -> p m q", m=2) for i in range(2)]
    ones_f = A.alloc("ones_f", 1, F32)
    P.op("pool", lambda e: e.memset(ones_f, 1.0), [], ["ones_f"])
    ep = [A.alloc("ep%d" % i, 8, F32) for i in range(2)]
    eo = [A.alloc("eo%d" % i, 384, F32) for i in range(2)]
    v_dh = v_d.rearrange("(b p) (h d) -> p b h d", p=128, h=8)
    actr = [0]
    tpv = bank_bf(7).rearrange("p (i m d) -> p i m d", i=4, m=2)
    dcol = bank(6)[:, 0:128]
    for h in range(8):
        K = Kh[h % 2]; V = Vh[0]; Q = Qh[0]
        kk_ = "Kh%d" % (h % 2)
        for m in range(2):
            dma(K[:, m, :], kT_d[h, m * 64:(m + 1) * 64, :], reads=["kT_d"], writes=[kk_], q="sp")
            dma(Q[:, m, :], qT_d[h, m * 64:(m + 1) * 64, :], reads=["qT_d"], writes=["Qh0"], q="sp")
        for vq in range(4):
            dma(V[:, vq * 16:(vq + 1) * 16, :], v_dh[:, vq * 16:(vq + 1) * 16, h, :], reads=["v_d"], writes=["Vh0"], q="sp")
        for G in range(4):
            gj = (h * 4 + G) % 2
            dr = denrow[gj]; drk = "denrow%d" % gj
            ot_ = OT[gj]; otk = "OT%d" % gj
            nkb = 16 * G + 16
            base_i = actr[0]
            actr[0] += nkb

            def emit_scores(kb, G=G, K=K, kk_=kk_, base_i=base_i):
                rel_ = kb - 16 * G - 3
                i0_ = 0 if rel_ <= 0 else (rel_ + 3) // 4
                c0 = i0_ * 128
                idiag = rel_ // 4 if (rel_ >= 0 and rel_ % 4 == 0) else -1
                pj = (base_i + kb) % 2
                pt = PT[pj]; ptk = "PT%d" % pj
                for m in range(2):
                    pb = 2 * pj + m
                    P.op("pe", lambda e, pb=pb, kb=kb, m=m, c0=c0: e.matmul(out=bank(pb)[:, c0:512], lhsT=K[:, m, kb * 128:(kb + 1) * 128], rhs=Q[:, m, G * 512 + c0:(G + 1) * 512], start=True, stop=True), [kk_, "Qh0"], [bkey(pb)])
                    P.op("act", lambda e, pb=pb, pt=pt, m=m, c0=c0: e.activation(out=pt[:, m, c0:512], in_=bank(pb)[:, c0:512], func=AF.Exp, scale=0.125), [bkey(pb)], [ptk + ":%d" % m])
                    if idiag >= 0:
                        P.op("dve", lambda e, pt=pt, m=m, idiag=idiag: e.memset(pt[64:128, m, idiag * 128:idiag * 128 + 64], 0.0), [ptk + ":%d" % m], [ptk + ":%d" % m])
                    if kb < 3:
                        P.op("dve", lambda e, pt=pt, m=m, kb=kb: e.tensor_scalar(out=pt[:, m, :], in0=pt[:, m, :], scalar1=kvf[:, kb:kb + 1], scalar2=None, op0=ALU.mult), [ptk + ":%d" % m, "kvf"], [ptk + ":%d" % m])

            def emit_pv(kb, G=G, V=V, nkb=nkb, base_i=base_i):
                rel_ = kb - 16 * G - 3
                i0_ = 0 if rel_ <= 0 else (rel_ + 3) // 4
                c0 = i0_ * 128
                pj = (base_i + kb) % 2
                pt = PT[pj]; ptk = "PT%d" % pj
                for m in range(2):
                    P.op("pe", lambda e, kb=kb, m=m, pt=pt, c0=c0: e.matmul(out=bank(4 + m)[:, c0:512], lhsT=V[:, kb, :], rhs=pt[:, m, c0:512], start=(kb == 0), stop=(kb == nkb - 1)), [ptk + ":%d" % m, "Vh0"], [bkey(4 + m)])
                    P.op("pe", lambda e, kb=kb, m=m, pt=pt, c0=c0: e.matmul(out=bank(6)[:, c0:512], lhsT=Esel[:, m, :], rhs=pt[:, m, c0:512], start=(kb == 0 and m == 0), stop=(kb == nkb - 1 and m == 1)), [ptk + ":%d" % m, "Esel"], [bkey(6)])

            emit_scores(0)
            for kb in range(nkb):
                if kb + 1 < nkb:
                    emit_scores(kb + 1)
                emit_pv(kb)
            P.op("act", lambda e, ot_=ot_: e.activation(out=ot_[:, 0, :], in_=bank(4), func=AF.Copy), [bkey(4)], [otk + ":0"])
            P.op("dve", lambda e, ot_=ot_: e.tensor_copy(out=ot_[:, 1, :], in_=bank(5)), [bkey(5)], [otk + ":1"])
            P.op("dve", lambda e, dr=dr: e.tensor_copy(out=dr, in_=bank(6)), [bkey(6)], [drk])
            for isl in range(4):
                P.op("pe", lambda e, isl=isl, dr=dr: e.matmul(out=dcol[:, isl * 32:(isl + 1) * 32], lhsT=dr[:, isl * 128:(isl + 1) * 128], rhs=ident_f[:, 0:32], start=True, stop=True), [drk, "ident_f"], [bkey(6)])
            rc = rcol[gj]; rck = "rcol%d" % gj
            P.op("dve", lambda e, rc=rc: e.reciprocal(out=rc, in_=dcol), [bkey(6)], [rck])
            for isl in range(4):
                for m in range(2):
                    P.op("pe", lambda e, isl=isl, m=m, ot_=ot_: e.transpose(out=tpv[:, isl, m, :], in_=ot_[:, m, isl * 128:(isl + 1) * 128], identity=ident_b), [otk + ":%d" % m, "ident_b"], [bkey(7)])
            for isl in range(4):
                s_ = G * 4 + isl
                j = (h * 16 + s_) % 2
                e_ = ep[j]; o_ = eo[j]; ek = "ep%d" % j; ok_ = "eo%d" % j
                P.op("dve", lambda e, e_=e_, isl=isl, rc=rc: e.tensor_copy(out=e_[:, 0:2], in_=rc[:, isl * 32:isl * 32 + 2]), [rck], [ek])
                P.op("dve", lambda e, e_=e_: e.tensor_tensor(out=e_[:, 2:3], in0=e_[:, 1:2], in1=lamt[:, 5:6], op=ALU.mult), [ek, "neglam"], [ek + ":2"])
                P.op("dve", lambda e, e_=e_, o_=o_, isl=isl: e.tensor_scalar(out=o_[:, 0:128], in0=tpv[:, isl, 1, :], scalar1=e_[:, 2:3], scalar2=None, op0=ALU.mult), [bkey(7), ek + ":2"], [ok_])
                P.op("dve", lambda e, e_=e_, o_=o_, isl=isl: e.scalar_tensor_tensor(out=o_[:, 128:256], in0=tpv[:, isl, 0, :], scalar=e_[:, 0:1], in1=o_[:, 0:128], op0=ALU.mult, op1=ALU.add), [bkey(7), ek, ok_], [ok_ + ":1"])
                P.op("act", lambda e, e_=e_, o_=o_: e.activation(out=o_[:, 256:384], in_=o_[:, 128:256], func=AF.Square, accum_out=e_[:, 3:4]), [ok_ + ":1"], [ok_ + ":2", ek + ":3"])
                P.op("act", lambda e, e_=e_: e.activation(out=e_[:, 4:5], in_=e_[:, 3:4], func=AF.Ln, scale=1.0 / 128, bias=epst), [ek + ":3", "epst"], [ek + ":4"])
                P.op("act", lambda e, e_=e_: e.activation(out=e_[:, 5:6], in_=e_[:, 4:5], func=AF.Exp, scale=-0.5), [ek + ":4"], [ek + ":5"])
                P.op("dve", lambda e, e_=e_, o_=o_, s_=s_, h=h: e.scalar_tensor_tensor(out=ya[:, s_, h * 128:(h + 1) * 128], in0=o_[:, 128:256], scalar=e_[:, 5:6], in1=hn, op0=ALU.mult, op1=ALU.mult), [ok_ + ":1", ek + ":5", "hn"], ["ya"])
    for n in ["Kh0", "Kh1", "Vh0", "Qh0", "PT0", "PT1", "denrow0", "denrow1", "rcol0", "rcol1", "Esel_f", "OT0", "OT1", "ep0", "ep1", "eo0", "eo1", "lq", "lqp", "kvf"]:
        A.free(n)

    if debug:
        ya_dbg = dscr("ya_dbg", [128, 16 * 1024])
        dma(ya_dbg, ya.rearrange("p s f -> p (s f)"), reads=["ya"], writes=["ya_dbg"])
    def load_bf16(name):
        dst_d, src_, K_, N_, gn_ = wsc[name]
        KC = K_ // 128
        wt = A.alloc(name, KC * N_, BF16).rearrange("p (k n) -> p k n", k=KC)
        for kc in range(KC):
            dma(wt[:, kc, :], dst_d[kc * 128:(kc + 1) * 128, :], reads=[name + "_d"], writes=[name], q="sp" if kc % 2 == 0 else "act")
        return wt

    gpost = A.alloc("gpost", 1024, F32)
    pst = A.alloc("pst", 8, F32)
    psq = A.alloc("psq", 1024, BF16)
    ptmp = A.alloc("ptmp", 1024, F32)
    ost = [A.alloc("ost%d" % i, 1024, F32) for i in range(2)]
    xres = [A.alloc("xres%d" % i, 1024, F32) for i in range(2)]

    def post_norm_residual(pb0, gain_bc, gkey, res_in, res_in_keys, res_out, res_out_key):
        for half in range(2):
            P.op("act", lambda e, half=half: e.activation(out=psq[:, half * 512:(half + 1) * 512], in_=bank(pb0 + half), func=AF.Square, accum_out=pst[:, half:half + 1]), [bkey(pb0 + half)], ["psq", "pst:%d" % half])
        P.op("dve", lambda e: e.tensor_tensor(out=pst[:, 2:3], in0=pst[:, 0:1], in1=pst[:, 1:2], op=ALU.add), ["pst:0", "pst:1"], ["pst:2"])
        P.op("act", lambda e: e.activation(out=pst[:, 3:4], in_=pst[:, 2:3], func=AF.Ln, scale=1.0 / D, bias=epst), ["pst:2", "epst"], ["pst:3"])
        P.op("act", lambda e: e.activation(out=pst[:, 4:5], in_=pst[:, 3:4], func=AF.Exp, scale=-0.5), ["pst:3"], ["pst:4"])
        for half in range(2):
            P.op("dve", lambda e, half=half: e.scalar_tensor_tensor(out=ptmp[:, half * 512:(half + 1) * 512], in0=bank(pb0 + half), scalar=pst[:, 4:5], in1=gain_bc[:, half * 512:(half + 1) * 512], op0=ALU.mult, op1=ALU.mult), [bkey(pb0 + half), "pst:4", gkey], ["ptmp:%d" % half])
        P.op("dve", lambda e: e.tensor_tensor(out=res_out, in0=ptmp, in1=res_in, op=ALU.add), ["ptmp:0", "ptmp:1"] + res_in_keys, [res_out_key])

    wglu = load_bf16("wglu")
    wssm = load_bf16("wssm")
    ys2 = A.alloc("ys2", 8 * 512, BF16).rearrange("p (k t) -> p k t", k=8)
    gab = A.alloc("gab", 8 * 512, BF16).rearrange("p (k t) -> p k t", k=8)
    sg = [A.alloc("sg%d" % i, 512, BF16) for i in range(2)]
    for tt in range(4):
        dma(gab, g_d[0:8, :, tt * 512:(tt + 1) * 512].rearrange("k p t -> p k t"), reads=["g_d"], writes=["gab"], q="sp")
        for mc in range(8):
            pb = 2 + mc % 2
            for kc in range(8):
                P.op("pe", lambda e, kc=kc, mc=mc, pb=pb, tt=tt: e.matmul(out=bank(pb), lhsT=wglu[:, kc, mc * 128:(mc + 1) * 128], rhs=ysg[:, kc, tt * 512:(tt + 1) * 512], start=(kc == 0), stop=(kc == 7)), ["wglu", "ysg"], [bkey(pb)])
            j = mc % 2
            P.op("act", lambda e, pb=pb, j=j, mc=mc: e.activation(out=sg[j], in_=bank(pb), func=AF.Sigmoid, bias=bglu_pp[:, mc:mc + 1]), [bkey(pb), "bglu_pp"], ["sg%d" % j])
            P.op("dve", lambda e, j=j, mc=mc, tt=tt: e.tensor_tensor(out=ys2[:, mc, :], in0=sg[j], in1=ysg[:, mc, tt * 512:(tt + 1) * 512], op=ALU.mult), ["sg%d" % j, "ysg"], ["ys2"])
        for mc in range(8):
            pa = 4 + (mc % 2)
            for kc in range(8):
                P.op("pe", lambda e, kc=kc, mc=mc, pa=pa: e.matmul(out=bank(pa), lhsT=wssm[:, kc, mc * 128:(mc + 1) * 128], rhs=ys2[:, kc, :], start=(kc == 0), stop=(kc == 7)), ["wssm", "ys2"], [bkey(pa)])
            P.op("dve", lambda e, pa=pa, mc=mc, tt=tt: e.tensor_tensor(out=ysg[:, mc, tt * 512:(tt + 1) * 512], in0=bank(pa), in1=gab[:, mc, :], op=ALU.mult), [bkey(pa), "gab"], ["ysg"])
    for n in ["wglu", "wssm", "ys2", "sg0", "sg1"]:
        A.free(n)
    wda = load_bf16("wda")
    wmix = load_bf16("wmix")
    dma(gpost, gains["norm_mix_post"].partition_broadcast(128), writes=["gpost"])
    yaT = A.alloc("yaT", 8 * 512, BF16).rearrange("p (k t) -> p k t", k=8)
    mrg = A.alloc("mrg", 8 * 512, BF16).rearrange("p (k t) -> p k t", k=8)
    tb = [A.alloc("tb%d" % i, 512, F32) for i in range(2)]
    for tt in range(4):
        for bl in range(4):
            s = tt * 4 + bl
            tpb = bl % 2
            tp = bank_bf(tpb).rearrange("p (k t) -> p k t", k=8)
            for kc in range(8):
                P.op("pe", lambda e, kc=kc, s=s, tp=tp: e.transpose(out=tp[:, kc, :], in_=ya[:, s, kc * 128:(kc + 1) * 128], identity=ident_b), ["ya", "ident_b"], [bkey(tpb)])
            P.op("act", lambda e, tp=tp, bl=bl: e.activation(out=yaT[:, :, bl * 128:(bl + 1) * 128], in_=tp, func=AF.Copy), [bkey(tpb)], ["yaT"])
        dma(gab, g_d[8:16, :, tt * 512:(tt + 1) * 512].rearrange("k p t -> p k t"), reads=["g_d"], writes=["gab"], q="sp")
        for mc in range(8):
            pbb_ = 4 + (mc % 2)
            for kc in range(8):
                P.op("pe", lambda e, kc=kc, mc=mc, pbb_=pbb_: e.matmul(out=bank(pbb_), lhsT=wda[:, kc, mc * 128:(mc + 1) * 128], rhs=yaT[:, kc, :], start=(kc == 0), stop=(kc == 7)), ["wda", "yaT"], [bkey(pbb_)])
            j = mc % 2
            P.op("dve", lambda e, pbb_=pbb_, j=j, mc=mc: e.tensor_tensor(out=tb[j], in0=bank(pbb_), in1=gab[:, mc, :], op=ALU.mult), [bkey(pbb_), "gab"], ["tb%d" % j])
            P.op("dve", lambda e, j=j, mc=mc, tt=tt: e.tensor_tensor(out=mrg[:, mc, :], in0=tb[j], in1=ysg[:, mc, tt * 512:(tt + 1) * 512], op=ALU.add), ["tb%d" % j, "ysg"], ["mrg"])
        for bl in range(4):
            s = tt * 4 + bl
            pb0 = 2 * (bl % 2)
            for half in range(2):
                for kc in range(8):
                    P.op("pe", lambda e, kc=kc, bl=bl, half=half, pb0=pb0: e.matmul(out=bank(pb0 + half), lhsT=mrg[:, kc, bl * 128:(bl + 1) * 128], rhs=wmix[:, kc, half * 512:(half + 1) * 512], start=(kc == 0), stop=(kc == 7)), ["mrg", "wmix"], [bkey(pb0 + half)])
            xr = xres[s % 2]; xrk = "xres%d" % (s % 2)
            dma(xr, xown[s * 128:(s + 1) * 128, :], writes=[xrk], q="sp")
            o_ = ost[s % 2]; ok_ = "ost%d" % (s % 2)
            post_norm_residual(pb0, gpost, "gpost", xr, [xrk], o_, ok_)
            dma(x1_d[s * 128:(s + 1) * 128, :], o_, reads=[ok_], writes=["x1_d"], q="sp")
    for n in ["wda", "wmix", "yaT", "mrg", "gab", "tb0", "tb1", "ya", "ysg"]:
        A.free(n)

    wxkv = load_bf16("wxkv")
    memT = A.alloc("memT", 8 * 256, BF16).rearrange("p (k t) -> p k t", k=8)
    for mb in range(2):
        norm_block_T(mem[mb * 128:(mb + 1) * 128, :], True, memT[:, :, mb * 128:(mb + 1) * 128], "memT")
    mkT = A.alloc("mkT", 8 * 256, BF16).rearrange("p (k t) -> p k t", k=8)
    mv = A.alloc("mv", 2 * 1024, BF16).rearrange("p (m f) -> p m f", m=2)
    for mc in range(8):
        pb = mc % 2
        for kc in range(8):
            P.op("pe", lambda e, kc=kc, mc=mc, pb=pb: e.matmul(out=bank(pb)[:, 0:256], lhsT=wxkv[:, kc, mc * 128:(mc + 1) * 128], rhs=memT[:, kc, :], start=(kc == 0), stop=(kc == 7)), ["wxkv", "memT"], [bkey(pb)])
        P.op("act", lambda e, pb=pb, mc=mc: e.activation(out=mkT[:, mc, :], in_=bank(pb)[:, 0:256], func=AF.Copy), [bkey(pb)], ["mkT"])
    for mt in range(2):
        for half in range(2):
            pb = 2 + half
            for kc in range(8):
                P.op("pe", lambda e, kc=kc, mt=mt, half=half, pb=pb: e.matmul(out=bank(pb), lhsT=memT[:, kc, mt * 128:(mt + 1) * 128], rhs=wxkv[:, kc, 1024 + half * 512:1024 + (half + 1) * 512], start=(kc == 0), stop=(kc == 7)), ["wxkv", "memT"], [bkey(pb)])
            P.op("act", lambda e, pb=pb, mt=mt, half=half: e.activation(out=mv[:, mt, half * 512:(half + 1) * 512], in_=bank(pb), func=AF.Copy), [bkey(pb)], ["mv"])
    A.free("wxkv"); A.free("memT")
    wxq = load_bf16("wxq")
    wxo = load_bf16("wxo")
    dma(gpost, gains["norm_x_post"].partition_broadcast(128), writes=["gpost"])
    ones_b = A.alloc("ones_b", 128, BF16)
    P.op("pool", lambda e: e.memset(ones_b, 1.0), [], ["ones_b"])
    h2T = A.alloc("h2T", 8 * 512, BF16).rearrange("p (k t) -> p k t", k=8)
    xqT = A.alloc("xqT", 8 * 512, BF16).rearrange("p (k t) -> p k t", k=8)
    xoT = A.alloc("xoT", 8 * 512, BF16).rearrange("p (k t) -> p k t", k=8)
    xp = [A.alloc("xp%d" % i, 2 * 512, BF16).rearrange("p (m t) -> p m t", m=2) for i in range(2)]
    rden = [A.alloc("rden%d" % i, 512, F32) for i in range(2)]
    for tt in range(4):
        for bl in range(4):
            s = tt * 4 + bl
            norm_block_T(x1_d[s * 128:(s + 1) * 128, :], True, h2T[:, :, bl * 128:(bl + 1) * 128], "h2T", tp_bank=7)
        for mc in range(8):
            pb = mc % 2
            for kc in range(8):
                P.op("pe", lambda e, kc=kc, mc=mc, pb=pb: e.matmul(out=bank(pb), lhsT=wxq[:, kc, mc * 128:(mc + 1) * 128], rhs=h2T[:, kc, :], start=(kc == 0), stop=(kc == 7)), ["wxq", "h2T"], [bkey(pb)])
            P.op("act", lambda e, pb=pb, mc=mc: e.activation(out=xqT[:, mc, :], in_=bank(pb), func=AF.Copy), [bkey(pb)], ["xqT"])
        for hh in range(4):
            j = hh % 2
            for mt in range(2):
                pb = 2 + mt
                for dc in range(2):
                    P.op("pe", lambda e, hh=hh, mt=mt, dc=dc, pb=pb: e.matmul(out=bank(pb), lhsT=mkT[:, hh * 2 + dc, mt * 128:(mt + 1) * 128], rhs=xqT[:, hh * 2 + dc, :], start=(dc == 0), stop=(dc == 1)), ["mkT", "xqT"], [bkey(pb)])
                P.op("act", lambda e, pb=pb, j=j, mt=mt: e.activation(out=xp[j][:, mt, :], in_=bank(pb), func=AF.Exp, scale=1.0 / 16), [bkey(pb)], ["xp%d" % j])
            for mt in range(2):
                P.op("pe", lambda e, j=j, mt=mt: e.matmul(out=bank(4), lhsT=ones_b, rhs=xp[j][:, mt, :], start=(mt == 0), stop=(mt == 1)), ["ones_b", "xp%d" % j], [bkey(4)])
            P.op("dve", lambda e, j=j: e.reciprocal(out=rden[j], in_=bank(4)), [bkey(4)], ["rden%d" % j])
            for dc in range(2):
                pb = 5 + dc
                for mt in range(2):
                    P.op("pe", lambda e, hh=hh, j=j, mt=mt, dc=dc, pb=pb: e.matmul(out=bank(pb), lhsT=mv[:, mt, (hh * 2 + dc) * 128:(hh * 2 + dc + 1) * 128], rhs=xp[j][:, mt, :], start=(mt == 0), stop=(mt == 1)), ["mv", "xp%d" % j], [bkey(pb)])
                P.op("dve", lambda e, hh=hh, j=j, dc=dc, pb=pb: e.tensor_tensor(out=xoT[:, hh * 2 + dc, :], in0=bank(pb), in1=rden[j], op=ALU.mult), [bkey(pb), "rden%d" % j], ["xoT"])
        for bl in range(4):
            s = tt * 4 + bl
            pb0 = 2 * (bl % 2)
            for half in range(2):
                for kc in range(8):
                    P.op("pe", lambda e, kc=kc, bl=bl, half=half, pb0=pb0: e.matmul(out=bank(pb0 + half), lhsT=xoT[:, kc, bl * 128:(bl + 1) * 128], rhs=wxo[:, kc, half * 512:(half + 1) * 512], start=(kc == 0), stop=(kc == 7)), ["xoT", "wxo"], [bkey(pb0 + half)])
            xr = xres[s % 2]; xrk = "xres%d" % (s % 2)
            dma(xr, x1_d[s * 128:(s + 1) * 128, :], reads=["x1_d"], writes=[xrk], q="sp")
            o_ = ost[s % 2]; ok_ = "ost%d" % (s % 2)
            post_norm_residual(pb0, gpost, "gpost", xr, [xrk], o_, ok_)
            dma(x2_d[s * 128:(s + 1) * 128, :], o_, reads=[ok_], writes=["x2_d"], q="sp")
    for n in ["wxq", "wxo", "mkT", "mv", "h2T", "xqT", "xoT", "xp0", "xp1", "rden0", "rden1"]:
        A.free(n)

    dma(gpost, gains["norm_ff_post"].partition_broadcast(128), writes=["gpost"])
    for n in ["wstage0", "wstage1", "xs0", "xs1", "nsq0", "nsq1", "nh0", "nh1"]:
        if n in A.live:
            A.free(n)
    f1 = A.alloc("f1", 32 * 512, BF16).rearrange("p (k t) -> p k t", k=32)
    wq1 = [A.alloc("wq1_%d" % i, 8 * 1024, BF16).rearrange("p (k n) -> p k n", k=8) for i in range(2)]
    h3T = A.alloc("h3T", 8 * 512, BF16).rearrange("p (k t) -> p k t", k=8)
    fr = [A.alloc("fr%d" % i, 512, BF16) for i in range(2)]
    wctr2 = [0]
    for tt in range(4):
        for bl in range(4):
            s = tt * 4 + bl
            norm_block_T(x2_d[s * 128:(s + 1) * 128, :], True, h3T[:, :, bl * 128:(bl + 1) * 128], "h3T", tp_bank=7)
        for q4 in range(4):
            i = wctr2[0]; wctr2[0] += 1
            w1 = wq1[i % 2]; w1k = "wq1_%d" % (i % 2)
            dma(w1, wf1_d[:, q4 * 1024:(q4 + 1) * 1024].rearrange("(k p) n -> p k n", p=128), reads=["wf_d"], writes=[w1k], q="sp")
            for fl in range(8):
                fc = q4 * 8 + fl
                pb = 4 + fc % 2
                for kc in range(8):
                    P.op("pe", lambda e, kc=kc, fl=fl, pb=pb, w1=w1: e.matmul(out=bank(pb), lhsT=w1[:, kc, fl * 128:(fl + 1) * 128], rhs=h3T[:, kc, :], start=(kc == 0), stop=(kc == 7)), [w1k, "h3T"], [bkey(pb)])
                j = fc % 2
                P.op("act", lambda e, pb=pb, j=j: e.activation(out=fr[j], in_=bank(pb), func=AF.Relu), [bkey(pb)], ["fr%d" % j])
                P.op("dve", lambda e, j=j, fc=fc: e.tensor_tensor(out=f1[:, fc, :], in0=fr[j], in1=fr[j], op=ALU.mult), ["fr%d" % j], ["f1"])
        for q4 in range(4):
            i = wctr2[0]; wctr2[0] += 1
            w2 = wq1[i % 2]; w2k = "wq1_%d" % (i % 2)
            dma(w2, wf2_d[q4 * 1024:(q4 + 1) * 1024, :].rearrange("(k p) n -> p k n", p=128), reads=["wf_d"], writes=[w2k], q="sp")
            for bl in range(4):
                for half in range(2):
                    pbk_ = bl * 2 + half
                    for kcl in range(8):
                        P.op("pe", lambda e, kcl=kcl, bl=bl, half=half, pbk_=pbk_, q4=q4, w2=w2: e.matmul(out=bank(pbk_), lhsT=f1[:, q4 * 8 + kcl, bl * 128:(bl + 1) * 128], rhs=w2[:, kcl, half * 512:(half + 1) * 512], start=(q4 == 0 and kcl == 0), stop=(q4 == 3 and kcl == 7)), ["f1", w2k], [bkey(pbk_)])
        for bl in range(4):
            s = tt * 4 + bl
            pb0 = 2 * bl
            xr = xres[s % 2]; xrk = "xres%d" % (s % 2)
            dma(xr, x2_d[s * 128:(s + 1) * 128, :], reads=["x2_d"], writes=[xrk], q="sp")
            o_ = ost[s % 2]; ok_ = "ost%d" % (s % 2)
            post_norm_residual(pb0, gpost, "gpost", xr, [xrk], o_, ok_)
            dma(out_d[s * 128:(s + 1) * 128, :], o_, reads=[ok_], writes=["out_d"], q="sp")

    P.emit()
    es.close()
    return nc


def _rope_tables(pos):
    inv = (10000.0 ** (-np.arange(0, 64, 2, dtype=np.float32) / 64)).astype(np.float32)
    ang = pos.astype(np.float32)[:, None] * inv[None, :]
    c = np.cos(ang).astype(np.float32).T
    s = np.sin(ang).astype(np.float32).T
    return np.ascontiguousarray(np.tile(c, (4, 1))), np.ascontiguousarray(np.tile(s, (4, 1)))


_NC_CACHE = {}


def make_in_maps(inputs):
    x = np.asarray(inputs["x"], dtype=np.float32)
    memv = np.asarray(inputs["mem"], dtype=np.float32)
    ident = np.eye(128, dtype=np.float32)
    swapm = np.zeros((128, 128), np.float32)
    for p in range(64):
        swapm[p, 64 + p] = 1.0
        swapm[64 + p, p] = 1.0
    gmask = np.zeros((128, 8, 128), np.float32)
    for gl in range(8):
        gmask[:, gl, gl * 16:(gl + 1) * 16] = 1.0
    esel = np.zeros((128, 256), np.float32)
    esel[:, 0] = 1.0
    esel[:, 129] = 1.0
    shared = {}
    for k, v in inputs.items():
        if k in ("x", "mem"):
            continue
        a = np.asarray(v, dtype=np.float32)
        shared[k] = np.ascontiguousarray(a[0])
    in_maps = []
    for c in range(8):
        b, j = c // 4, c % 4
        pad = (3 - j) * 128
        xs = np.zeros((SEQ, D), np.float32)
        xs[pad:] = x[b, :SEQ - pad]
        own_blocks = [4 * s + j for s in range(16)]
        xo = np.concatenate([x[b, r * 128:(r + 1) * 128] for r in own_blocks], axis=0)
        pos_seq = np.arange(SEQ) - pad
        cseq, sseq = _rope_tables(pos_seq)
        pos_own = np.concatenate([np.arange(r * 128, (r + 1) * 128) for r in own_blocks])
        cown, sown = _rope_tables(pos_own)
        kval = (pos_seq >= 0).astype(np.float32)
        m = dict(shared)
        m.update(xseq=xs, xown=np.ascontiguousarray(xo), mem=np.ascontiguousarray(memv[b]), cos_seq=cseq, sin_seq=sseq,
                 cos_own=cown, sin_own=sown, kvalid=kval, esel=esel, ident=ident, swapm=swapm, gmask=gmask)
        in_maps.append(m)
    return in_maps


def kernel(**inputs):
    if "nc" not in _NC_CACHE:
        _NC_CACHE["nc"] = build_program()
    nc = _NC_CACHE["nc"]
    in_maps = make_in_maps(inputs)
    res = run_bass_kernel_spmd(nc, in_maps, core_ids=list(range(8)))
    out = np.zeros((2, SEQ, D), np.float32)
    for c in range(8):
        b, j = c // 4, c % 4
        o = res.results[c]["out"]
        for s in range(16):
            r = 4 * s + j
            out[b, r * 128:(r + 1) * 128] = o[s * 128:(s + 1) * 128]
    return out
```
